# Optimizing a Trainium2 kernel written in Bass

```python
import jax
import jax.numpy as jnp
from jax import lax
import numpy as np

D_MODEL = 1024
BATCH = 32
SEQ = 256
DEPTH = 4
DEC_BATCH = 8
DEC_SEQ = 2048
PAST_LEN = 512

N_GROUPS = 4
GROUP_W = D_MODEL // N_GROUPS
HEAD_DIM = 64
CONV_CH = GROUP_W
CONV_WIDTH = 3
SGU_W = GROUP_W
SGU_HEADS = SGU_W // HEAD_DIM
CHUNK = 128
MLA_HEADS = 4
MLA_NOPE = 64
MLA_ROPE = 32
MLA_V = GROUP_W // MLA_HEADS
MLA_Q_LORA = D_MODEL // 4
MLA_KV_LORA = D_MODEL // 8
SWA_HEADS = GROUP_W // HEAD_DIM
SWA_KV_HEADS = 2
SWA_GROUP = SWA_HEADS // SWA_KV_HEADS
SWA_WINDOW = 128
SWA_BLOCK = 128
Q_BLOCK = 128
MLP_HIDDEN = 4 * D_MODEL
GRID_W = 64
ROPE_THETA = 10000.0
EPS = 1e-6
N_MOD = 6
IN_COLS = 3 * CONV_CH + 2 * SGU_W + MLA_Q_LORA + MLA_KV_LORA + MLA_ROPE + (SWA_HEADS + 2 * SWA_KV_HEADS) * HEAD_DIM
MLA_SCALE = (MLA_NOPE + MLA_ROPE) ** -0.5
SWA_SCALE = HEAD_DIM ** -0.5
NEG_INF = -1e30

kernel_name = 'hybrid_flow_prefix_step'


def rmsnorm(x, g):
    xf = x.astype(jnp.float32)
    y = xf * lax.rsqrt(jnp.mean(xf * xf, axis=-1, keepdims=True) + EPS)
    return (y * g.astype(jnp.float32)).astype(x.dtype)


def axial_rope_tables(n_tokens, rot_dim, dtype):
    rows = n_tokens // GRID_W
    row = jnp.repeat(jnp.arange(rows, dtype=jnp.float32), GRID_W)
    col = jnp.tile(jnp.arange(GRID_W, dtype=jnp.float32), rows)
    half = rot_dim // 2
    inv_freq = ROPE_THETA ** (-jnp.arange(0, half, 2, dtype=jnp.float32) / half)
    ang_r = row[:, None] * inv_freq[None, :]
    ang_c = col[:, None] * inv_freq[None, :]
    ang = jnp.concatenate([ang_r, ang_r, ang_c, ang_c], axis=-1)
    return jnp.cos(ang).astype(dtype), jnp.sin(ang).astype(dtype)


def apply_axial_rope(x, cos, sin):
    bshape = (cos.shape[0],) + (1,) * (x.ndim - 3) + (cos.shape[1],)
    cos = cos.reshape(bshape)
    sin = sin.reshape(bshape)
    x1, x2, x3, x4 = jnp.split(x, 4, axis=-1)
    rot = jnp.concatenate([-x2, x1, -x4, x3], axis=-1)
    return x * cos + rot * sin


def split_in_proj(z):
    sizes = (CONV_CH, CONV_CH, CONV_CH, 2 * SGU_W, MLA_Q_LORA, MLA_KV_LORA, MLA_ROPE,
             SWA_HEADS * HEAD_DIM, SWA_KV_HEADS * HEAD_DIM, SWA_KV_HEADS * HEAD_DIM)
    parts = []
    start = 0
    for size in sizes:
        parts.append(z[..., start:start + size])
        start += size
    return parts


def dwconv_centred(x, w):
    return lax.conv_general_dilated(
        x, w[:, None, :].astype(x.dtype), window_strides=(1,),
        padding=((CONV_WIDTH // 2, CONV_WIDTH // 2),),
        dimension_numbers=('NWC', 'WIO', 'NWC'), feature_group_count=x.shape[-1])


def chunk_sgu(uv, norm_g, w_s, b_s):
    z = jax.nn.gelu(uv)
    u, v = jnp.split(z, 2, axis=-1)
    v = rmsnorm(v, norm_g)
    bsz, n, _ = v.shape
    vh = v.reshape(bsz, n // CHUNK, CHUNK, SGU_HEADS, HEAD_DIM)
    sv = jnp.einsum('hpq,bcqhd->bcphd', w_s, vh) + b_s.T[None, None, :, :, None]
    return u * sv.reshape(bsz, n, SGU_W)


def mla_expand(c_kv, k_pe, w_kv_up):
    bsz, n, _ = c_kv.shape
    kv = (c_kv @ w_kv_up).reshape(bsz, n, MLA_HEADS, MLA_NOPE + MLA_V)
    k_nope, v = kv[..., :MLA_NOPE], kv[..., MLA_NOPE:]
    k_rope = jnp.broadcast_to(k_pe[:, :, None, :], (bsz, n, MLA_HEADS, MLA_ROPE))
    return jnp.concatenate([k_nope, k_rope], axis=-1), v


def blocked_attention(q, k, v, scale):
    bsz, n_q, n_h, d_k = q.shape
    qb = q.reshape(bsz, n_q // Q_BLOCK, Q_BLOCK, n_h, d_k).transpose(1, 0, 2, 3, 4)
    kf = k.astype(jnp.float32)

    def one_block(q_blk):
        s = jnp.einsum('bqhd,bkhd->bhqk', q_blk.astype(jnp.float32), kf) * scale
        p = jax.nn.softmax(s, axis=-1)
        return jnp.einsum('bhqk,bkhd->bqhd', p.astype(v.dtype), v)

    out = lax.map(one_block, qb)
    return out.transpose(1, 0, 2, 3, 4).reshape(bsz, n_q, n_h, v.shape[-1])


def sink_column(sink, like):
    col = sink.astype(jnp.float32).reshape(SWA_KV_HEADS, SWA_GROUP)[:, :, None, None]
    return jnp.broadcast_to(col, like.shape[:-1] + (1,))


def gqa_sink_blocked(q, k, v, sink):
    bsz, n_q = q.shape[:2]
    qb = q.reshape(bsz, n_q // Q_BLOCK, Q_BLOCK, SWA_KV_HEADS, SWA_GROUP, HEAD_DIM).transpose(1, 0, 2, 3, 4, 5)
    kf = k.astype(jnp.float32)

    def one_block(q_blk):
        s = jnp.einsum('bqngd,bsnd->bngqs', q_blk.astype(jnp.float32), kf) * SWA_SCALE
        p = jax.nn.softmax(jnp.concatenate([s, sink_column(sink, s)], axis=-1), axis=-1)[..., :-1]
        return jnp.einsum('bngqs,bsnd->bqngd', p.astype(v.dtype), v)

    out = lax.map(one_block, qb)
    return out.transpose(1, 0, 2, 3, 4, 5).reshape(bsz, n_q, SWA_HEADS * HEAD_DIM)


def banded_sink_attention(q, k, v, k_ctx, v_ctx, sink):
    bsz, n = q.shape[:2]
    nb = n // SWA_BLOCK
    qb = q.reshape(bsz, nb, SWA_BLOCK, SWA_KV_HEADS, SWA_GROUP, HEAD_DIM).astype(jnp.float32)

    def band(t):
        tb = t.reshape(bsz, nb, SWA_BLOCK, SWA_KV_HEADS, HEAD_DIM)
        zero = jnp.zeros_like(tb[:, :1])
        prev = jnp.concatenate([zero, tb[:, :-1]], axis=1)
        nxt = jnp.concatenate([tb[:, 1:], zero], axis=1)
        return jnp.concatenate([prev, tb, nxt], axis=2)

    k_band, v_band = band(k), band(v)
    blk = jnp.arange(nb)[:, None] * SWA_BLOCK
    q_pos = blk + jnp.arange(SWA_BLOCK)[None, :]
    k_pos = blk - SWA_BLOCK + jnp.arange(3 * SWA_BLOCK)[None, :]
    kp = k_pos[:, None, :]
    valid = (jnp.abs(q_pos[:, :, None] - kp) <= SWA_WINDOW) & (kp >= 0) & (kp < n)
    s_band = jnp.einsum('bcqngd,bcknd->bcngqk', qb, k_band.astype(jnp.float32)) * SWA_SCALE
    s_band = jnp.where(valid[None, :, None, None, :, :], s_band, NEG_INF)
    s_ctx = jnp.einsum('bcqngd,bsnd->bcngqs', qb, k_ctx.astype(jnp.float32)) * SWA_SCALE
    logits = jnp.concatenate([s_band, s_ctx, sink_column(sink, s_band)], axis=-1)
    p = jax.nn.softmax(logits, axis=-1).astype(v.dtype)
    n_band = 3 * SWA_BLOCK
    out = (jnp.einsum('bcngqk,bcknd->bcqngd', p[..., :n_band], v_band)
           + jnp.einsum('bcngqs,bsnd->bcqngd', p[..., n_band:-1], v_ctx))
    return out.reshape(bsz, n, SWA_HEADS * HEAD_DIM)


def mixer_front(h, p):
    bsz, n, _ = h.shape
    z = h @ p['w_in']
    a_b, a_c, a_x, sgu_uv, cq, ckv, k_pe, sq, sk, sv = split_in_proj(z)
    y_conv = a_b * dwconv_centred(a_c * a_x, p['conv_w'])
    y_sgu = chunk_sgu(sgu_uv, p['sgu_norm'], p['sgu_w'], p['sgu_b'])
    q_mla = (rmsnorm(cq, p['q_norm']) @ p['w_q_up']).reshape(bsz, n, MLA_HEADS, MLA_NOPE + MLA_ROPE)
    c_kv = rmsnorm(ckv, p['kv_norm'])
    q_swa = sq.reshape(bsz, n, SWA_KV_HEADS, SWA_GROUP, HEAD_DIM)
    k_swa = sk.reshape(bsz, n, SWA_KV_HEADS, HEAD_DIM)
    v_swa = sv.reshape(bsz, n, SWA_KV_HEADS, HEAD_DIM)
    return y_conv, y_sgu, q_mla, c_kv, k_pe, q_swa, k_swa, v_swa


def mix_context(h, p):
    bsz, n, _ = h.shape
    y_conv, y_sgu, q_mla, c_kv, k_pe, q_swa, k_swa, v_swa = mixer_front(h, p)
    k_mla, v_mla = mla_expand(c_kv, k_pe, p['w_kv_up'])
    y_mla = blocked_attention(q_mla, k_mla, v_mla, MLA_SCALE).reshape(bsz, n, MLA_HEADS * MLA_V)
    y_swa = gqa_sink_blocked(q_swa, k_swa, v_swa, p['sink'])
    y = jnp.concatenate([y_conv, y_sgu, y_mla, y_swa], axis=-1) @ p['w_out']
    return y, (c_kv, k_pe, k_swa, v_swa)


def mix_latent(h, p, ckv_ctx, kpe_ctx, k_ctx, v_ctx):
    bsz, n, _ = h.shape
    y_conv, y_sgu, q_mla, c_kv, k_pe, q_swa, k_swa, v_swa = mixer_front(h, p)
    cos_m, sin_m = axial_rope_tables(n, MLA_ROPE, h.dtype)
    q_mla = jnp.concatenate([q_mla[..., :MLA_NOPE], apply_axial_rope(q_mla[..., MLA_NOPE:], cos_m, sin_m)], axis=-1)
    k_lat, v_lat = mla_expand(c_kv, apply_axial_rope(k_pe, cos_m, sin_m), p['w_kv_up'])
    k_cm, v_cm = mla_expand(ckv_ctx, kpe_ctx, p['w_kv_up'])
    y_mla = blocked_attention(q_mla, jnp.concatenate([k_lat, k_cm], axis=1),
                              jnp.concatenate([v_lat, v_cm], axis=1), MLA_SCALE).reshape(bsz, n, MLA_HEADS * MLA_V)
    cos_s, sin_s = axial_rope_tables(n, HEAD_DIM, h.dtype)
    y_swa = banded_sink_attention(apply_axial_rope(q_swa, cos_s, sin_s), apply_axial_rope(k_swa, cos_s, sin_s),
                                  v_swa, k_ctx, v_ctx, p['sink'])
    return jnp.concatenate([y_conv, y_sgu, y_mla, y_swa], axis=-1) @ p['w_out']


def squared_relu_mlp(h, w1, w2):
    return jnp.square(jax.nn.relu(h @ w1)) @ w2


def modulate(x, g, shift, scale):
    return rmsnorm(x, g) * (1 + scale) + shift


def trunk_layer(x, mod, p, mix_fn):
    shift1, scale1, gate1, shift2, scale2, gate2 = jnp.split(mod, N_MOD, axis=-1)
    y, ctx_state = mix_fn(modulate(x, p['norm1'], shift1, scale1))
    x = x + gate1 * y
    x = x + gate2 * squared_relu_mlp(modulate(x, p['norm2'], shift2, scale2), p['w1'], p['w2'])
    return x, ctx_state


def setup_inputs(seed: int = 0) -> dict:
    key = jax.random.key(seed)
    ks = jax.random.split(key, 26)

    def nrm(k, shape, scale=1.0):
        return jax.random.normal(k, shape, dtype=jnp.float32) * scale

    def gain(k, shape):
        return 1.0 + 0.02 * jax.random.normal(k, shape, dtype=jnp.float32)

    return {
        'x_prompt': nrm(ks[0], (BATCH, SEQ, D_MODEL)),
        'x_sample': nrm(ks[1], (DEC_BATCH, DEC_SEQ, D_MODEL)),
        'cache_mla_ckv': nrm(ks[2], (DEC_BATCH, DEPTH, PAST_LEN, MLA_KV_LORA)),
        'cache_mla_kpe': nrm(ks[3], (DEC_BATCH, DEPTH, PAST_LEN, MLA_ROPE)),
        'cache_swa_k': nrm(ks[4], (DEC_BATCH, DEPTH, PAST_LEN, SWA_KV_HEADS, HEAD_DIM)),
        'cache_swa_v': nrm(ks[5], (DEC_BATCH, DEPTH, PAST_LEN, SWA_KV_HEADS, HEAD_DIM)),
        'c': nrm(ks[6], (DEC_BATCH, D_MODEL)),
        'c_ctx': nrm(ks[7], (D_MODEL,)),
        'w_ada': nrm(ks[8], (DEPTH, D_MODEL, N_MOD * D_MODEL), 0.5 * D_MODEL ** -0.5),
        'b_ada': nrm(ks[9], (DEPTH, N_MOD * D_MODEL), 0.01),
        'norm1': gain(ks[10], (DEPTH, D_MODEL)),
        'norm2': gain(ks[11], (DEPTH, D_MODEL)),
        'w_in': nrm(ks[12], (DEPTH, D_MODEL, IN_COLS), D_MODEL ** -0.5),
        'conv_w': nrm(ks[13], (DEPTH, CONV_WIDTH, CONV_CH), CONV_WIDTH ** -0.5),
        'sgu_norm': gain(ks[14], (DEPTH, SGU_W)),
        'sgu_w': nrm(ks[15], (DEPTH, SGU_HEADS, CHUNK, CHUNK), CHUNK ** -0.5),
        'sgu_b': nrm(ks[16], (DEPTH, SGU_HEADS, CHUNK), 0.02),
        'mla_q_norm': gain(ks[17], (DEPTH, MLA_Q_LORA)),
        'mla_w_q_up': nrm(ks[18], (DEPTH, MLA_Q_LORA, MLA_HEADS * (MLA_NOPE + MLA_ROPE)), MLA_Q_LORA ** -0.5),
        'mla_kv_norm': gain(ks[19], (DEPTH, MLA_KV_LORA)),
        'mla_w_kv_up': nrm(ks[20], (DEPTH, MLA_KV_LORA, MLA_HEADS * (MLA_NOPE + MLA_V)), MLA_KV_LORA ** -0.5),
        'swa_sink': nrm(ks[21], (DEPTH, SWA_HEADS)),
        'w_out': nrm(ks[22], (DEPTH, D_MODEL, D_MODEL), D_MODEL ** -0.5),
        'mlp_w1': nrm(ks[23], (DEPTH, D_MODEL, MLP_HIDDEN), D_MODEL ** -0.5),
        'mlp_w2': nrm(ks[24], (DEPTH, MLP_HIDDEN, D_MODEL), MLP_HIDDEN ** -0.5),
        'final_norm': gain(ks[25], (D_MODEL,)),
    }


def reference(x_prompt, x_sample, cache_mla_ckv, cache_mla_kpe, cache_swa_k, cache_swa_v, c, c_ctx,
              w_ada, b_ada, norm1, norm2, w_in, conv_w, sgu_norm, sgu_w, sgu_b, mla_q_norm, mla_w_q_up,
              mla_kv_norm, mla_w_kv_up, swa_sink, w_out, mlp_w1, mlp_w2, final_norm):
    xp = x_prompt
    xs = x_sample
    ckv_list, kpe_list, k_list, v_list = [], [], [], []
    for l in range(DEPTH):
        p = {'norm1': norm1[l], 'norm2': norm2[l], 'w_in': w_in[l], 'conv_w': conv_w[l],
             'sgu_norm': sgu_norm[l], 'sgu_w': sgu_w[l], 'sgu_b': sgu_b[l], 'q_norm': mla_q_norm[l],
             'w_q_up': mla_w_q_up[l], 'kv_norm': mla_kv_norm[l], 'w_kv_up': mla_w_kv_up[l],
             'sink': swa_sink[l], 'w_out': w_out[l], 'w1': mlp_w1[l], 'w2': mlp_w2[l]}
        mod_ctx = (jax.nn.silu(c_ctx) @ w_ada[l] + b_ada[l])[None, None, :]
        mod_lat = (jax.nn.silu(c) @ w_ada[l] + b_ada[l])[:, None, :]
        xp, (ckv_l, kpe_l, k_l, v_l) = trunk_layer(xp, mod_ctx, p, lambda h, p=p: mix_context(h, p))
        ckv_list.append(ckv_l)
        kpe_list.append(kpe_l)
        k_list.append(k_l)
        v_list.append(v_l)
        xs, _ = trunk_layer(
            xs, mod_lat, p,
            lambda h, p=p, l=l: (mix_latent(h, p, cache_mla_ckv[:, l], cache_mla_kpe[:, l],
                                            cache_swa_k[:, l], cache_swa_v[:, l]), None))
    y_prompt = rmsnorm(xp, final_norm)
    y_sample = rmsnorm(xs, final_norm)
    new_mla_ckv = jnp.stack(ckv_list, axis=1)
    new_mla_kpe = jnp.stack(kpe_list, axis=1)
    new_swa_k = jnp.stack(k_list, axis=1)
    new_swa_v = jnp.stack(v_list, axis=1)
    return (y_prompt, y_sample, new_mla_ckv, new_mla_kpe, new_swa_k, new_swa_v)
```

```python
import numpy as np
import concourse.bass as bass
import concourse.mybir as mybir
from concourse.bass_utils import run_bass_kernel_spmd

F32, BF16 = mybir.dt.float32, mybir.dt.bfloat16
AF = mybir.ActivationFunctionType
ALU = mybir.AluOpType

D = 1024
DEPTH = 4
EPS = 1e-6
MLA_SCALE = 96 ** -0.5
SWA_SCALE = 0.125
NEG = -30000.0
NCORES = 8


class Info:
    __slots__ = ("sem", "val", "clock", "eng")

    def __init__(self, eng):
        self.sem = None
        self.val = 0
        self.clock = None
        self.eng = eng


class Prog:
    COMPUTE = ("pe", "act", "dve")
    QUEUES = ("sp", "pool")
    NSL = 6

    def __init__(self, nc):
        self.nc = nc
        self.eng = {"pe": nc.tensor, "act": nc.scalar, "dve": nc.vector, "pool": nc.gpsimd, "sp": nc.sync}
        self.sems = {}
        self.epoch = 0
        self.csem = {}
        self.cnt = {}
        self._new_epoch_sems()
        self.slots = {}
        self.nd = {q: 0 for q in self.QUEUES}
        for q in self.QUEUES:
            self.slots[q] = []
            for i in range(self.NSL):
                nm = f"d_{q}{i}"
                self.sems[nm] = nc.alloc_semaphore(name=nm)
                self.slots[q].append([nm, 0])
        self.clock = {e: {} for e in self.eng}
        self.last_w = {}
        self.readers = {}
        self.pending = {e: [] for e in self.COMPUTE}
        self.out_infos = []
        self.total = {e: 0 for e in self.eng}
        self.marks = []

    def _new_epoch_sems(self):
        for e in self.COMPUTE:
            nm = f"c_{e}{self.epoch}"
            self.sems[nm] = self.nc.alloc_semaphore(name=nm)
            self.csem[e] = nm
            self.cnt[e] = 0
        self.epoch += 1

    def _wait(self, e, info):
        if info.sem is None:
            raise RuntimeError("dependency on unsignaled op")
        ck = self.clock[e]
        if ck.get(info.sem, 0) >= info.val:
            return
        self.eng[e].wait_ge(self.sems[info.sem], info.val)
        for s, v in info.clock.items():
            if ck.get(s, 0) < v:
                ck[s] = v

    def op(self, e, fn, reads=(), writes=(), sig=True, dma=False, is_out=False):
        psr = [k for k in reads if k.startswith("ps")]
        if psr:
            writes = list(writes) + [k for k in psr if k not in writes]
        deps = []
        for k in reads:
            w = self.last_w.get(k)
            if w is not None:
                deps.append((w, True))
        for k in writes:
            w = self.last_w.get(k)
            if w is not None:
                deps.append((w, False))
            rd = self.readers.get(k)
            if rd:
                for r in rd.values():
                    deps.append((r, False))
        for info, raw in deps:
            if info.eng == e and not dma:
                if e == "pe" or not raw:
                    continue
            self._wait(e, info)
        self.total[e] += 1
        info = Info(e)
        if dma:
            n = self.nd[e]
            self.nd[e] += 1
            slot = self.slots[e][n % self.NSL]
            ck = self.clock[e]
            if slot[1] > 0 and ck.get(slot[0], 0) < slot[1]:
                self.eng[e].wait_ge(self.sems[slot[0]], slot[1])
                ck[slot[0]] = slot[1]
            slot[1] += 16
            inst = fn()
            inst.then_inc(self.sems[slot[0]], 16)
            info.sem, info.val = slot[0], slot[1]
            info.clock = dict(ck)
            info.clock[info.sem] = info.val
            info.eng = e + "_dma%d" % n
            if is_out:
                self.out_infos.append(info)
        else:
            inst = fn()
            if e == "pe" and self.marks and self.marks[-1][1] is None:
                nm_ = inst.ins.name
                for mk in reversed(self.marks):
                    if mk[1] is not None:
                        break
                    mk[1] = nm_
            if sig:
                self.cnt[e] += 1
                inst.then_inc(self.sems[self.csem[e]], 1)
                info.sem, info.val = self.csem[e], self.cnt[e]
                info.clock = dict(self.clock[e])
                info.clock[info.sem] = info.val
                for p in self.pending[e]:
                    p.sem, p.val, p.clock = info.sem, info.val, info.clock
                self.pending[e] = []
            else:
                self.pending[e].append(info)
        for k in writes:
            self.last_w[k] = info
            self.readers[k] = {}
        for k in reads:
            self.readers.setdefault(k, {})[info.eng] = info
        return info

    def barrier(self):
        for e in self.COMPUTE:
            assert not self.pending[e]
        for f in self.eng:
            ck = self.clock[f]
            for e in self.COMPUTE:
                nm, v = self.csem[e], self.cnt[e]
                if v > 0 and e != f and ck.get(nm, 0) < v:
                    self.eng[f].wait_ge(self.sems[nm], v)
                if v > 0:
                    ck[nm] = v
            for q in self.QUEUES:
                for nm, v in self.slots[q]:
                    if v > 0 and ck.get(nm, 0) < v:
                        self.eng[f].wait_ge(self.sems[nm], v)
                        ck[nm] = v
        for e in self.COMPUTE:
            if self.cnt[e] > 0:
                self.eng[e].wait_ge(self.sems[self.csem[e]], self.cnt[e])
        self.last_w = {}
        self.readers = {}
        self._new_epoch_sems()

    def mark(self, label):
        self.marks.append([label, None])

    def finish(self):
        for info in self.out_infos:
            self._wait("sp", info)


MARKS = []


class StopBuild(Exception):
    pass


def build_program(depth=DEPTH, groups="AB", debug=False, p0=9):
    nc = bass.Bass("TRN2", target_bir_lowering=False)
    P = Prog(nc)

    def din(name, shape):
        return nc.dram_tensor(name, list(shape), F32, kind="ExternalInput").ap()

    def dout(name, shape):
        return nc.dram_tensor(name, list(shape), F32, kind="ExternalOutput").ap()

    xsT = din("xsT", (D, 2048))
    xpT = din("xpT", (D, 1024))
    ckvT_c = din("ckvT_c", (DEPTH, 128, 512))
    kpeT_c = din("kpeT_c", (DEPTH, 32, 512))
    skT_c = din("skT_c", (DEPTH, 128, 512))
    sv_c = din("sv_c", (DEPTH, 512, 128))
    cfm = din("cfm", (128, 16))
    w_ada = din("w_ada", (DEPTH, D, 6 * D))
    bada_fm = din("bada_fm", (128, DEPTH * 48))
    vecs_fm = din("vecs_fm", (128, 128))
    sgunorm_bc = din("sgunorm_bc", (128, DEPTH * 256))
    kvnorm_bc = din("kvnorm_bc", (128, DEPTH * 128))
    sink_bc = din("sink_bc", (128, 16))
    sgub = din("sgub", (1, DEPTH * 512))
    sgu_wT = din("sgu_wT", (DEPTH, 128, 512))
    w_fm = din("w_fm", (DEPTH, D, 1280))
    w_tm = din("w_tm", (DEPTH, D, 928))
    w_out = din("w_out", (DEPTH, D, D))
    w1 = din("w1", (DEPTH, D, 4096))
    w2 = din("w2", (DEPTH, 4096, D))
    wq = din("wq", (DEPTH, 256, 384))
    wkv = din("wkv", (DEPTH, 128, 512))
    ropes = din("ropes", (128, 16 * 192))
    consts = din("consts", (128, 640))
    ysT = dout("ysT", (D, 2048))
    ypT = dout("ypT", (D, 1024))
    o_ckv = dout("o_ckv", (4, DEPTH, 256, 128))
    o_kpe = dout("o_kpe", (4, DEPTH, 256, 32))
    o_k = dout("o_k", (4, DEPTH, 256, 128))
    o_v = dout("o_v", (4, DEPTH, 256, 128))
    if debug:
        dbg_h = nc.dram_tensor("dbg_h", [128, 8, 512], BF16, kind="ExternalOutput").ap()
        dbg_cat = nc.dram_tensor("dbg_cat", [128, 8, 512], BF16, kind="ExternalOutput").ap()
        dbg_x1 = dout("dbg_x1", (128, 8, 2048))
        dbg_x2 = dout("dbg_x2", (128, 8, 2048))

    A = nc.alloc_sbuf_tensor
    ident_f = A("ident_f", [128, 128], F32)
    cst = A("cst", [128, 512], BF16)
    ident_b, ones_b, mprev, mnext = cst[:, 0:128], cst[:, 128:256], cst[:, 256:384], cst[:, 384:512]
    modt = A("modt", [128, DEPTH * 48 * 2], F32)
    badat = A("badat", [128, DEPTH * 48], F32)
    vecs = A("vecs", [128, 128], F32)
    gs = A("gs", [128, DEPTH * 2 * 8 * 2], F32)
    csil = A("csil", [128, 16], F32)
    sinkexp = A("sinkexp", [128, 16], F32)
    ropet = A("ropet", [128, 16 * 192], BF16)
    sgn = A("sgn", [128, 256], F32)
    kvn = A("kvn", [128, 128], F32)
    wq_t = A("wq_t", [128, 2, 384], BF16)
    wkv_t = A("wkv_t", [128, 512], BF16)
    wsT_t = A("wsT_t", [128, 512], BF16)
    sgub_t = A("sgub_t", [1, 512], BF16)
    x = A("x", [128, 8, 2048], F32)
    R = A("R", [128, 36352], BF16)
    NST = 3
    wst = [A(f"wst{i}", [128, 4096], BF16) for i in range(NST)]
    sqt = [A(f"sqt{i}", [128, 512], BF16) for i in range(2)]
    tt = [A(f"tt{i}", [128, 512], F32) for i in range(2)]
    s_t = A("s_t", [128, 512], F32)
    rstd = s_t
    tmf = [A(f"tmf{i}", [128, 512], F32) for i in range(2)]
    tm2 = [A(f"tm2{i}", [128, 256], F32) for i in range(2)]
    tm3 = [A(f"tm3{i}", [128, 256], F32) for i in range(2)]
    sm = [A(f"sm{i}", [128, 4], F32) for i in range(4)]
    vn_t = [A(f"vn{i}", [128, 256], BF16) for i in range(2)]
    pt = [A(f"pt{i}", [128, 512], BF16) for i in range(3)]
    rden = [A(f"rden{i}", [64, 512], F32) for i in range(2)]
    qh = A("qh", [128, 4, 512], BF16)
    relu_t = [A(f"relu{i}", [128, 512], BF16) for i in range(2)]
    cqraw = relu_t
    yo = tt
    PS = [nc.alloc_psum_tensor(f"ps{i}", [128, 512], F32) for i in range(8)]

    rr = {}

    def ring(name, lst):
        i = rr.get(name, 0)
        rr[name] = i + 1
        j = i % len(lst)
        return lst[j], f"{name}{j}"

    psring = {"lst": list(range(8)), "i": 0}

    def psum():
        lst = psring["lst"]
        b = lst[psring["i"] % len(lst)]
        psring["i"] += 1
        return PS[b], f"ps{b}"

    def set_psring(lst):
        psring["lst"] = list(lst)
        psring["i"] = 0

    def mm(out, lhsT, rhs, start, stop, reads, writes, sig=None):
        if sig is None:
            sig = True
        return P.op("pe", lambda: nc.tensor.matmul(out, lhsT=lhsT, rhs=rhs, start=start, stop=stop),
                    reads=reads, writes=writes, sig=sig)

    def act(out, in_, func, reads, writes, bias=0.0, scale=1.0, accum_out=None):
        kw = {}
        if accum_out is not None:
            kw["accum_out"] = accum_out
        return P.op("act", lambda: nc.scalar.activation(out=out, in_=in_, func=func, bias=bias, scale=scale, **kw),
                    reads=reads, writes=writes)

    def tt_op(out, in0, in1, op, reads, writes):
        return P.op("dve", lambda: nc.vector.tensor_tensor(out=out, in0=in0, in1=in1, op=op), reads=reads, writes=writes)

    def ts_op(out, in0, s1, op0, reads, writes, s2=None, op1=None):
        if op1 is None:
            return P.op("dve", lambda: nc.vector.tensor_scalar(out=out, in0=in0, scalar1=s1, scalar2=None, op0=op0),
                        reads=reads, writes=writes)
        return P.op("dve", lambda: nc.vector.tensor_scalar(out=out, in0=in0, scalar1=s1, scalar2=s2, op0=op0, op1=op1),
                    reads=reads, writes=writes)

    def stt(out, in0, scalar, in1, op0, op1, reads, writes):
        return P.op("dve", lambda: nc.vector.scalar_tensor_tensor(out=out, in0=in0, scalar=scalar, in1=in1, op0=op0, op1=op1),
                    reads=reads, writes=writes)

    def vcopy(out, in_, reads, writes):
        return P.op("dve", lambda: nc.vector.tensor_copy(out=out, in_=in_), reads=reads, writes=writes)

    def recip(out, in_, reads, writes):
        return P.op("dve", lambda: nc.vector.reciprocal(out=out, in_=in_), reads=reads, writes=writes)

    def dma(q, out, in_, reads, writes, is_out=False):
        e = nc.sync if q == "sp" else nc.gpsimd
        return P.op(q, lambda: e.dma_start(out=out, in_=in_), reads=reads, writes=writes, dma=True, is_out=is_out)

    wplan = []
    wstate = {"issued": 0, "used": 0}

    def plan_weights():
        for g in groups:
            NT = 4 if g == "A" else 2
            for l in range(depth):
                for j in range(NT):
                    wplan.append((f"fmK{g}{l}{j}", w_fm[l, :, 0:512], (8, 512)))
                    wplan.append((f"tmK{g}{l}{j}", w_tm[l, :, 0:416], (8, 416)))
                for j in range(NT):
                    wplan.append((f"fmQa{g}{l}{j}", w_fm[l, :, 512:896], (8, 384)))
                    wplan.append((f"fmQb{g}{l}{j}", w_fm[l, :, 896:1280], (8, 384)))
                    wplan.append((f"tmQ{g}{l}{j}", w_tm[l, :, 416:928], (8, 512)))
                    wplan.append((f"wo0{g}{l}{j}", w_out[l, :, 0:512], (8, 512)))
                    wplan.append((f"wo1{g}{l}{j}", w_out[l, :, 512:1024], (8, 512)))
                for jh in range(8):
                    wplan.append((f"w1{g}{l}{jh}", w1[l, :, jh * 512:(jh + 1) * 512], (8, 512)))
                    wplan.append((f"w2{g}{l}{jh}", w2[l, jh * 512:(jh + 1) * 512, :], (4, 1024)))

    def issue_weight():
        i = wstate["issued"]
        if i >= len(wplan):
            return
        name, ap, (nc_, ncol) = wplan[i]
        slot = wst[i % NST]
        dst = slot[:, 0:nc_ * ncol].rearrange("p (c n) -> p c n", c=nc_)
        src = ap.rearrange("(c p) n -> p c n", p=128)
        dma("pool", dst, src, reads=[], writes=[f"wst{i % NST}"])
        wstate["issued"] += 1

    def get_weight(name, ahead=NST - 1):
        i = wstate["used"]
        assert wplan[i][0] == name, (wplan[i][0], name)
        while wstate["issued"] < min(i + ahead + 1, len(wplan)):
            issue_weight()
        wstate["used"] += 1
        nc_, ncol = wplan[i][2]
        return wst[i % NST][:, 0:nc_ * ncol].rearrange("p (c n) -> p c n", c=nc_), f"wst{i % NST}"

    plan_weights()
    MARKS.clear()

    def ckpt(n):
        if p0 == n:
            P.barrier()
            P.finish()
            raise StopBuild()

    with nc.allow_low_precision("bf16 matmuls"), nc.allow_non_contiguous_dma("small strided loads"):
        dma("sp", ident_f[:], consts[:, 0:128], [], ["ident_f"])
        dma("pool", cst[:], consts[:, 0:512], [], ["cst"])
        dma("sp", badat[:], bada_fm[:, :], [], ["badat"])
        dma("sp", vecs[:], vecs_fm[:, :], [], ["vecs"])
        dma("sp", csil[:], cfm[:, :], [], ["csil"])
        dma("sp", sinkexp[:], sink_bc[:, :], [], ["sinkexp"])
        dma("pool", ropet[:], ropes[:, :], [], ["ropet"])
        act(csil[:], csil[:], AF.Silu, ["csil"], ["csil"])
        act(sinkexp[:], sinkexp[:], AF.Exp, ["sinkexp"], ["sinkexp"])
        if p0 == 1:
            P.barrier()
            P.finish()
            return nc
        Rf = R[:, 0:16384].bitcast(F32)
        adabuf = [Rf[:, 0:4096].rearrange("p (c n) -> p c n", c=8), Rf[:, 4096:8192].rearrange("p (c n) -> p c n", c=8)]
        modps, modk = PS[0], "ps0"
        npiece = 0
        for l in range(DEPTH):
            for pc in range(12):
                b = npiece % 2
                dma("sp", adabuf[b], w_ada[l, :, pc * 512:(pc + 1) * 512].rearrange("(c p) n -> p c n", p=128),
                    [], [f"ada{b}"])
                for mi in range(4):
                    m = pc * 4 + mi
                    col = (l * 48 + m) * 2
                    for c in range(8):
                        mm(modps[:, col:col + 2], adabuf[b][:, c, mi * 128:(mi + 1) * 128],
                           csil[:, c * 2:c * 2 + 2], c == 0, c == 7, [f"ada{b}", "csil"], [modk])
                npiece += 1
        if p0 == 2:
            P.barrier()
            P.finish()
            return nc
        tt_op(modt[:].rearrange("p (m v) -> p m v", v=2), modps[:, 0:384].rearrange("p (m v) -> p m v", v=2),
              badat[:].unsqueeze(2).to_broadcast([128, DEPTH * 48, 2]), ALU.add, [modk, "badat"], ["modt"])
        if p0 == 3:
            P.barrier()
            P.finish()
            return nc
        modv = modt[:].rearrange("p (l k c v) -> p l k c v", l=DEPTH, k=6, c=8)
        gsv = gs[:].rearrange("p (l n c v) -> p l n c v", l=DEPTH, n=2, c=8)
        for l in range(DEPTH):
            for n in range(2):
                nv = vecs[:, n * 32 + l * 8:n * 32 + l * 8 + 8]
                ts_op(gsv[:, l, n], modv[:, l, 3 * n + 1], 1.0, ALU.add, ["modt"], ["gs"])
                tt_op(gsv[:, l, n], gsv[:, l, n], nv.unsqueeze(2).to_broadcast([128, 8, 2]), ALU.mult, ["gs", "vecs"], ["gs"])
        P.barrier()

        def run_group(g):
            lat = (g == "A")
            T = 2048 if lat else 1024
            S = 2048 if lat else 256
            NT = T // 512
            NK = 2560 if lat else 1024
            NB = NK // 128
            v = 1 if lat else 0
            xT = xsT if lat else xpT
            yT = ysT if lat else ypT
            o = 0

            def carve(n, c=None):
                nonlocal o
                ap = R[:, o:o + n]
                o += n
                if c is not None:
                    ap = ap.rearrange("p (c t) -> p c t", c=c)
                return ap
            KT = carve(4 * NK, 4)
            Vm = carve(NB * 256, NB)
            ks = carve(NK)
            Vs = carve(NB * 128, NB)
            pbuf = carve(2 * T, 2)
            hj = carve(8 * 512, 8)
            catj = carve(8 * 512, 8)
            cqn = carve(2 * 512, 2)
            qs = carve(2 * 512, 2)
            acc = carve(2 * 512, 2)
            ckT = carve(512)
            assert o <= 36352, o
            h2 = R[:, 0:8 * T].rearrange("p (c t) -> p c t", c=8)
            ub = [R[:, 8 * T + i * 2048:8 * T + (i + 1) * 2048].rearrange("p (c t) -> p c t", c=4) for i in range(2)]

            for c in range(8):
                dma("sp", x[:, c, 0:T], xT[c * 128:(c + 1) * 128, :], [], [f"x{c}_{j}" for j in range(NT)])

            def norm(l, n, j, dst, dkeys):
                ps, pk = psum()
                for c in range(8):
                    sq, sqk = ring("sqt", sqt)
                    act(sq[:], x[:, c, j * 512:(j + 1) * 512], AF.Square, [f"x{c}_{j}"], [sqk])
                    mm(ps[:, :], ones_b, sq[:], c == 0, c == 7, [sqk, "cst"], [pk])
                act(s_t[:], ps[:, :], AF.Sqrt, [pk], ["s_t", "rstd"], bias=EPS, scale=1.0 / D)
                recip(rstd[:], s_t[:], ["s_t"], ["s_t", "rstd"])
                for c in range(8):
                    t, tk = ring("tt", tt)
                    tt_op(t[:], x[:, c, j * 512:(j + 1) * 512], rstd[:], ALU.mult, [f"x{c}_{j}", "rstd"], [tk])
                    act(dst(c), t[:], AF.Identity, [tk, "gs", "modt"], [dkeys(c)],
                        bias=modv[:, l, 3 * n, c, v:v + 1], scale=gsv[:, l, n, c, v:v + 1])

            def attention(NQ, qT, qkeys, chunks, scale, sink_ap, out_ap, out_keys):
                CH = 512 // NQ
                ai = rr.get("acc", 0) % 2
                rr["acc"] = rr.get("acc", 0) + 1
                nump, numk = PS[3 + ai * 2], f"ps{3 + ai * 2}"
                denp, denk = PS[4 + ai * 2], f"ps{4 + ai * 2}"
                ones64 = ones_b[:, 0:64]
                n = len(chunks)
                groups_ = [chunks[g0:g0 + CH] for g0 in range(0, n, CH)]

                def emit_S(grp):
                    sb, sbk = psum()
                    for i, (kT, vv, mask, keys) in enumerate(grp):
                        mm(sb[:, i * NQ:(i + 1) * NQ], kT, qT, True, mask is None, keys + qkeys, [sbk])
                        if mask is not None:
                            mm(sb[:, i * NQ:(i + 1) * NQ], ident_b, mask, False, True, ["cst"], [sbk])
                    return sb, sbk

                nxt = emit_S(groups_[0])
                idx = 0
                for gi, grp in enumerate(groups_):
                    sb, sbk = nxt
                    if gi + 1 < len(groups_):
                        nxt = emit_S(groups_[gi + 1])
                    p_, pk_ = ring("pt", pt)
                    w = len(grp) * NQ
                    act(p_[:, 0:w], sb[:, 0:w], AF.Exp, [sbk], [pk_], scale=scale)
                    for i, (kT, vv, mask, keys) in enumerate(grp):
                        first = (idx == 0)
                        last = (idx == n - 1)
                        idx += 1
                        mm(nump[0:64, 0:NQ], vv, p_[:, i * NQ:(i + 1) * NQ], first, last, keys + [pk_], [numk])
                        mm(denp[0:64, 0:NQ], ones64, p_[:, i * NQ:(i + 1) * NQ], first, last, [pk_, "cst"], [denk])
                rd, rdk = ring("rden", rden)
                if sink_ap is not None:
                    ts_op(rd[:, 0:NQ], denp[0:64, 0:NQ], sink_ap, ALU.add, [denk, "sinkexp"], [rdk])
                    recip(rd[:, 0:NQ], rd[:, 0:NQ], [rdk], [rdk])
                else:
                    recip(rd[:, 0:NQ], denp[0:64, 0:NQ], [denk], [rdk])
                tt_op(out_ap, nump[0:64, 0:NQ], rd[:, 0:NQ], ALU.mult, [numk, rdk], out_keys)

            def rope(dst, src, H, Dh, gb, off, rk, wk):
                Q = Dh // 4
                cos = ropet[:, gb * 192 + off:gb * 192 + off + Dh]
                sin = ropet[:, gb * 192 + off + Dh:gb * 192 + off + 2 * Dh]
                t2, t2k = ring("tm3", tm3)
                s4 = src.rearrange("p (h a b q) -> p h a b q", h=H, a=2, b=2)
                d4 = t2[:, 0:H * Dh].rearrange("p (h a b q) -> p h a b q", h=H, a=2, b=2)
                sn = sin.rearrange("p (a b q) -> p a b q", a=2, b=2)
                for bb in range(2):
                    tt_op(d4[:, :, :, bb, :], s4[:, :, :, 1 - bb, :],
                          sn[:, :, bb, :].unsqueeze(1).to_broadcast([128, H, 2, Q]), ALU.mult,
                          rk + ["ropet"] + ([t2k] if bb else []), [t2k])
                s3 = src.rearrange("p (h d) -> p h d", h=H)
                d3 = dst.rearrange("p (h d) -> p h d", h=H)
                tt_op(d3, s3, cos.unsqueeze(1).to_broadcast([128, H, Dh]), ALU.mult, rk + ["ropet"], wk)
                tt_op(dst, dst, t2[:, 0:H * Dh], ALU.add, wk + [t2k], wk)

            for l in range(depth):
                set_psring(range(8))
                dma("pool", wq_t[:], wq[l].rearrange("(c p) n -> p c n", p=128), [], ["wq_t"])
                dma("pool", wkv_t[:], wkv[l], [], ["wkv_t"])
                dma("pool", wsT_t[:], sgu_wT[l], [], ["wsT_t"])
                dma("pool", sgub_t[:], sgub[:, l * 512:(l + 1) * 512], [], ["sgub_t"])
                dma("sp", sgn[:], sgunorm_bc[:, l * 256:(l + 1) * 256], [], ["sgn"])
                dma("sp", kvn[:], kvnorm_bc[:, l * 128:(l + 1) * 128], [], ["kvn"])
                if lat:
                    dma("pool", ckT[:], ckvT_c[l], [], ["ckT"])
                    for h in range(4):
                        dma("pool", KT[64:96, h, T:T + 512], kpeT_c[l], [], [f"KT{h}_{NT}"])
                    dma("pool", ks[:, T:T + 512], skT_c[l], [], [f"ks_{NT}"])
                    dma("pool", Vs[:, 16:20, :], sv_c[l].rearrange("(b p) d -> p b d", p=128), [], [f"Vs_{NT}"])

                def kv_up(j, nblk):
                    for h in range(4):
                        ps, pk = psum()
                        mm(ps[0:64, 0:nblk * 128], wkv_t[:, h * 128:h * 128 + 64], ckT[:, 0:nblk * 128], True, True,
                           ["wkv_t", "ckT"], [pk])
                        act(KT[0:64, h, j * 512:j * 512 + nblk * 128], ps[0:64, 0:nblk * 128], AF.Copy, [pk], [f"KT{h}_{j}"])
                    for b in range(nblk):
                        ps, pk = psum()
                        mm(ps[:, :], ckT[:, b * 128:(b + 1) * 128], wkv_t[:], True, True, ["wkv_t", "ckT"], [pk])
                        vcopy(Vm[:, j * 4 + b, :].rearrange("p (h d) -> p h d", h=4),
                              ps[:, :].rearrange("p (h t d) -> p h t d", h=4, t=2)[:, :, 1, :], [pk], [f"Vm_{j}"])

                if lat:
                    kv_up(NT, 4)
                ckpt(10)

                for j in range(NT):
                    P.mark(f"{g}{l} K{j} norm")
                    norm(l, 0, j, lambda c: hj[:, c, :], lambda c: f"hj{c}")
                    P.mark(f"{g}{l} K{j} fm")
                    ckpt(11)
                    wt, wk_ = get_weight(f"fmK{g}{l}{j}")
                    for m in range(4):
                        ps, pk = psum()
                        for c in range(8):
                            mm(ps[:, :], wt[:, c, m * 128:(m + 1) * 128], hj[:, c, :], c == 0, c == 7, [wk_, f"hj{c}"], [pk])
                        if m < 2:
                            act(pbuf[:, m, j * 512:(j + 1) * 512], ps[:, :], AF.Copy, [pk], [f"p{m}_{j}"])
                        else:
                            tt_op(pbuf[:, m - 2, j * 512:(j + 1) * 512], ps[:, :], pbuf[:, m - 2, j * 512:(j + 1) * 512],
                                  ALU.mult, [pk, f"p{m - 2}_{j}"], [f"p{m - 2}_{j}"])
                    ckpt(12)
                    P.mark(f"{g}{l} K{j} tm")
                    wt, wk_ = get_weight(f"tmK{g}{l}{j}")
                    for b in range(4):
                        gb = j * 4 + b
                        tok = slice(b * 128, (b + 1) * 128)
                        ps, pk = psum()
                        for c in range(8):
                            mm(ps[:, 0:416], hj[:, c, tok], wt[:, c, :], c == 0, c == 7, [wk_, f"hj{c}"], [pk])
                        st, stk = ring("tmf", tmf)
                        act(st[:, 0:416], ps[:, 0:416], AF.Copy, [pk], [stk])
                        smt, smk = ring("sm", sm)
                        t2, t2k = ring("tm2", tm2)
                        P.op("dve", lambda: nc.vector.memset(smt[:, 0:1], 0.0), [], [smk])
                        act(t2[:, 0:128], st[:, 0:128], AF.Square, [stk, smk], [t2k, smk], accum_out=smt[:, 0:1])
                        act(smt[:, 1:2], smt[:, 0:1], AF.Sqrt, [smk], [smk], bias=EPS, scale=1.0 / 128)
                        recip(smt[:, 2:3], smt[:, 1:2], [smk], [smk])
                        ckpt(130)
                        stt(st[:, 0:128], st[:, 0:128], smt[:, 2:3], kvn[:], ALU.mult, ALU.mult, [stk, smk, "kvn"], [stk])
                        ckpt(131)
                        if lat:
                            t3, t3k = ring("tm2", tm2)
                            rope(t3[:, 0:32], st[:, 128:160], 1, 32, gb, 128, [stk], [t3k])
                            kpe_src, kpek = t3[:, 0:32], t3k
                            t4, t4k = ring("tm2", tm2)
                            rope(t4[:, 0:128], st[:, 160:288], 2, 64, gb, 0, [stk], [t4k])
                            sk_src, skk = t4[:, 0:128], t4k
                        else:
                            kpe_src, kpek = st[:, 128:160], stk
                            sk_src, skk = st[:, 160:288], stk
                            sq_, r0 = divmod(gb * 128, 256)
                            dma("sp", o_ckv[sq_, l, r0:r0 + 128, :], st[:, 0:128], [stk], [], is_out=True)
                            dma("sp", o_kpe[sq_, l, r0:r0 + 128, :], st[:, 128:160], [stk], [], is_out=True)
                            dma("sp", o_k[sq_, l, r0:r0 + 128, :], st[:, 160:288], [stk], [], is_out=True)
                            dma("sp", o_v[sq_, l, r0:r0 + 128, :], st[:, 288:416], [stk], [], is_out=True)
                        vcopy(Vs[:, gb, :], st[:, 288:416], [stk], [f"Vs_{j}"])
                        ckpt(132)
                        ps2, pk2 = psum()
                        P.op("pe", lambda: nc.tensor.transpose(ps2[:, 0:128], st[:, 0:128], ident_f[:]), [stk, "ident_f"], [pk2])
                        P.op("pe", lambda: nc.tensor.transpose(ps2[:, 128:256], sk_src, ident_f[:]), [skk, "ident_f"], [pk2])
                        P.op("pe", lambda: nc.tensor.transpose(ps2[0:32, 256:384], kpe_src, ident_f[:]), [kpek, "ident_f"], [pk2])
                        ckpt(133)
                        act(ckT[:, tok], ps2[:, 0:128], AF.Copy, [pk2], ["ckT"])
                        vcopy(ks[:, gb * 128:(gb + 1) * 128], ps2[:, 128:256], [pk2], [f"ks_{j}"])
                        ckpt(134)
                        for h in range(4):
                            if h % 2 == 0:
                                act(KT[64:96, h, gb * 128:(gb + 1) * 128], ps2[0:32, 256:384], AF.Copy, [pk2], [f"KT{h}_{j}"])
                            else:
                                vcopy(KT[64:96, h, gb * 128:(gb + 1) * 128], ps2[0:32, 256:384], [pk2], [f"KT{h}_{j}"])
                        ckpt(13)
                    P.mark(f"{g}{l} K{j} kvup")
                    kv_up(j, 4)
                    ckpt(14)

                nseq_t = 512 // min(S, 512)
                for j in range(NT):
                    set_psring(range(8))
                    P.mark(f"{g}{l} Q{j} norm")
                    norm(l, 0, j, lambda c: hj[:, c, :], lambda c: f"hj{c}")
                    P.mark(f"{g}{l} Q{j} fm")
                    if debug and l == 0 and j == 0:
                        dma("sp", dbg_h, hj, [f"hj{c}" for c in range(8)], [], is_out=True)
                    wa, wak = get_weight(f"fmQa{g}{l}{j}")
                    wb, wbk = get_weight(f"fmQb{g}{l}{j}", ahead=NST - 2)
                    for m in range(6):
                        wt, wk_ = (wa, wak) if m < 3 else (wb, wbk)
                        mi = m % 3
                        ps, pk = psum()
                        for c in range(8):
                            mm(ps[:, :], wt[:, c, mi * 128:(mi + 1) * 128], hj[:, c, :], c == 0, c == 7, [wk_, f"hj{c}"], [pk])
                        if m < 2:
                            act(catj[:, m, :], ps[:, :], AF.Copy, [pk], [f"cat{m}"])
                        elif m < 4:
                            act(catj[:, m, :], ps[:, :], AF.Gelu_apprx_tanh, [pk], [f"cat{m}"])
                        else:
                            cr, crk = cqraw[m - 4], f"cqraw{m - 4}"
                            act(cr[:], ps[:, :], AF.Copy, [pk], [crk])
                    ckpt(16)
                    P.mark(f"{g}{l} Q{j} cq+conv")
                    ps, pk = psum()
                    for c in range(2):
                        sq, sqk = ring("sqt", sqt)
                        act(sq[:], cqraw[c][:], AF.Square, [f"cqraw{c}"], [sqk])
                        mm(ps[:, :], ones_b, sq[:], c == 0, c == 1, [sqk, "cst"], [pk])
                    act(s_t[:], ps[:, :], AF.Sqrt, [pk], ["s_t", "rstd"], bias=EPS, scale=1.0 / 256)
                    recip(rstd[:], s_t[:], ["s_t"], ["s_t", "rstd"])
                    for c in range(2):
                        stt(cqn[:, c, :], cqraw[c][:], vecs[:, 72 + l * 2 + c:72 + l * 2 + c + 1], rstd[:], ALU.mult, ALU.mult,
                            [f"cqraw{c}", "vecs", "rstd"], [f"cqn{c}"])
                    ckpt(17)
                    Sq = min(S, 512)
                    for c in range(2):
                        cw = lambda k: vecs[:, 80 + (l * 2 + c) * 3 + k:80 + (l * 2 + c) * 3 + k + 1]
                        lo = j * 512
                        pk_all = [f"p{c}_{jj}" for jj in range(NT)]
                        ts_op(acc[:, c, :], pbuf[:, c, lo:lo + 512], cw(1), ALU.mult, pk_all + ["vecs"], [f"acc{c}"])
                        for sidx in range(nseq_t):
                            a0 = sidx * Sq
                            g0 = lo + a0
                            first_in_seq = (g0 % S == 0)
                            last_in_seq = ((g0 + Sq) % S == 0)
                            s0 = 1 if first_in_seq else 0
                            stt(acc[:, c, a0 + s0:a0 + Sq], pbuf[:, c, g0 + s0 - 1:g0 + Sq - 1], cw(0), acc[:, c, a0 + s0:a0 + Sq],
                                ALU.mult, ALU.add, pk_all + ["vecs", f"acc{c}"], [f"acc{c}"])
                            e0 = 1 if last_in_seq else 0
                            stt(acc[:, c, a0:a0 + Sq - e0], pbuf[:, c, g0 + 1:g0 + Sq - e0 + 1], cw(2), acc[:, c, a0:a0 + Sq - e0],
                                ALU.mult, ALU.add, pk_all + ["vecs", f"acc{c}"], [f"acc{c}"])
                        tt_op(catj[:, c, :], catj[:, c, :], acc[:, c, :], ALU.mult, [f"cat{c}", f"acc{c}"], [f"cat{c}"])
                    ckpt(18)
                    P.mark(f"{g}{l} Q{j} tm")
                    wt, wk_ = get_weight(f"tmQ{g}{l}{j}")
                    for b in range(4):
                        gb = j * 4 + b
                        tok = slice(b * 128, (b + 1) * 128)
                        ps, pk = psum()
                        for c in range(8):
                            mm(ps[:, :], hj[:, c, tok], wt[:, c, :], c == 0, c == 7, [wk_, f"hj{c}"], [pk])
                        st, stk = ring("tmf", tmf)
                        act(st[:, 0:256], ps[:, 0:256], AF.Gelu_apprx_tanh, [pk], [stk])
                        act(st[:, 256:512], ps[:, 256:512], AF.Copy, [pk], [stk])
                        smt, smk = ring("sm", sm)
                        t2, t2k = ring("tm2", tm2)
                        P.op("dve", lambda: nc.vector.memset(smt[:, 0:1], 0.0), [], [smk])
                        act(t2[:, 0:256], st[:, 0:256], AF.Square, [stk, smk], [t2k, smk], accum_out=smt[:, 0:1])
                        act(smt[:, 1:2], smt[:, 0:1], AF.Sqrt, [smk], [smk], bias=EPS, scale=1.0 / 256)
                        recip(smt[:, 2:3], smt[:, 1:2], [smk], [smk])
                        vn, vnk = ring("vn", vn_t)
                        stt(vn[:], st[:, 0:256], smt[:, 2:3], sgn[:], ALU.mult, ALU.mult, [stk, smk, "sgn"], [vnk])
                        ps2, pk2 = psum()
                        for hd in range(4):
                            cc, e = divmod(hd, 2)
                            mm(ps2[:, hd * 128:(hd + 1) * 128], vn[:, cc * 128:(cc + 1) * 128], wsT_t[:, hd * 128:(hd + 1) * 128],
                               True, False, [vnk, "wsT_t"], [pk2], sig=False)
                            mm(ps2[:, hd * 128:(hd + 1) * 128], ones_b[0:1, 0:128], sgub_t[0:1, hd * 128:(hd + 1) * 128],
                               False, True, ["cst", "sgub_t"], [pk2], sig=True)
                        for hd in range(4):
                            cc, e = divmod(hd, 2)
                            tt_op(catj[e * 64:(e + 1) * 64, 2 + cc, tok], catj[e * 64:(e + 1) * 64, 2 + cc, tok],
                                  ps2[e * 64:(e + 1) * 64, hd * 128:(hd + 1) * 128], ALU.mult, [f"cat{2 + cc}", pk2], [f"cat{2 + cc}"])
                        if lat:
                            t4, t4k = ring("tm2", tm2)
                            rope(t4[:, 0:256], st[:, 256:512], 4, 64, gb, 0, [stk], [t4k])
                            qsrc, qsk = t4, t4k
                        else:
                            qsrc, qsk = st[:, 256:512], stk
                            qsrc = st
                        ps3, pk3 = psum()
                        for gg in range(2):
                            src_ap = (qsrc[:, gg * 128:(gg + 1) * 128] if lat else st[:, 256 + gg * 128:256 + (gg + 1) * 128])
                            P.op("pe", lambda: nc.tensor.transpose(ps3[:, gg * 128:(gg + 1) * 128], src_ap, ident_f[:]),
                                 [qsk, "ident_f"], [pk3])
                        act(qs[:, :, tok], ps3[:, 0:256].rearrange("p (g t) -> p g t", g=2), AF.Copy, [pk3], ["qs"])

                    ckpt(19)
                    set_psring([0, 1, 2])
                    P.mark(f"{g}{l} Q{j} qmla")
                    for b in range(4):
                        gb = j * 4 + b
                        tokq = slice(b * 128, (b + 1) * 128)
                        ps, pk = PS[7], "ps7"
                        for c in range(2):
                            mm(ps[:, 0:384], cqn[:, c, tokq], wq_t[:, c, :], c == 0, c == 1, [f"cqn{c}", "wq_t"], [pk])
                        st, stk = ring("tmf", tmf)
                        act(st[:, 0:384], ps[:, 0:384], AF.Copy, [pk], [stk])
                        if lat:
                            t3, t3k = ring("tm2", tm2)
                            src4 = st[:, 0:384].rearrange("p (h d) -> p h d", h=4)[:, :, 64:96]
                            t5, t5k = ring("tm2", tm2)
                            vcopy(t5[:, 0:128].rearrange("p (h d) -> p h d", h=4), src4, [stk], [t5k])
                            rope(t3[:, 0:128], t5[:, 0:128], 4, 32, gb, 128, [t5k], [t3k])
                            vcopy(src4, t3[:, 0:128].rearrange("p (h d) -> p h d", h=4), [t3k], [stk])
                        for h in range(4):
                            P.op("pe", lambda: nc.tensor.transpose(ps[0:96, 384:512], st[:, h * 96:(h + 1) * 96], ident_f[:]),
                                 [stk, "ident_f"], [pk])
                            if h % 2 == 0:
                                act(qh[0:96, h, tokq], ps[0:96, 384:512], AF.Copy, [pk], [f"qh{h}"])
                            else:
                                vcopy(qh[0:96, h, tokq], ps[0:96, 384:512], [pk], [f"qh{h}"])
                    ckpt(20)
                    P.mark(f"{g}{l} Q{j} MLA")
                    if lat:
                        for h in range(4):
                            chunks = []
                            for kc in range(20):
                                jt = kc // 4
                                chunks.append((KT[0:96, h, kc * 128:(kc + 1) * 128], Vm[:, kc, h * 64:(h + 1) * 64], None,
                                               [f"KT{h}_{jt}", f"Vm_{jt}"]))
                            cc, e = divmod(h, 2)
                            attention(512, qh[0:96, h, :], [f"qh{h}"], chunks, MLA_SCALE, None,
                                      catj[e * 64:(e + 1) * 64, 4 + cc, :], [f"cat{4 + cc}"])
                    else:
                        for sl in range(2):
                            sidx = (j * 512) // 256 + sl
                            qsl = slice(sl * 256, (sl + 1) * 256)
                            for h in range(4):
                                chunks = []
                                for kc in (2 * sidx, 2 * sidx + 1):
                                    jt = kc // 4
                                    chunks.append((KT[0:96, h, kc * 128:(kc + 1) * 128], Vm[:, kc, h * 64:(h + 1) * 64], None,
                                                   [f"KT{h}_{jt}", f"Vm_{jt}"]))
                                cc, e = divmod(h, 2)
                                attention(256, qh[0:96, h, qsl], [f"qh{h}"], chunks, MLA_SCALE, None,
                                          catj[e * 64:(e + 1) * 64, 4 + cc, qsl], [f"cat{4 + cc}"])
                    ckpt(21)
                    P.mark(f"{g}{l} Q{j} SWA")
                    for n in range(2):
                        for gq in range(2):
                            hidx = n * 2 + gq
                            sink_ap = sinkexp[0:64, l * 4 + hidx:l * 4 + hidx + 1]
                            if lat:
                                for bq in range(4):
                                    blk = j * 4 + bq
                                    tq = slice(bq * 128, (bq + 1) * 128)
                                    chunks = []
                                    if blk >= 1:
                                        chunks.append((ks[n * 64:(n + 1) * 64, (blk - 1) * 128:blk * 128],
                                                       Vs[:, blk - 1, n * 64:(n + 1) * 64], mprev,
                                                       [f"ks_{(blk - 1) // 4}", f"Vs_{(blk - 1) // 4}"]))
                                    chunks.append((ks[n * 64:(n + 1) * 64, blk * 128:(blk + 1) * 128],
                                                   Vs[:, blk, n * 64:(n + 1) * 64], None, [f"ks_{blk // 4}", f"Vs_{blk // 4}"]))
                                    if blk <= 14:
                                        chunks.append((ks[n * 64:(n + 1) * 64, (blk + 1) * 128:(blk + 2) * 128],
                                                       Vs[:, blk + 1, n * 64:(n + 1) * 64], mnext,
                                                       [f"ks_{(blk + 1) // 4}", f"Vs_{(blk + 1) // 4}"]))
                                    for kc in range(16, 20):
                                        chunks.append((ks[n * 64:(n + 1) * 64, kc * 128:(kc + 1) * 128],
                                                       Vs[:, kc, n * 64:(n + 1) * 64], None, [f"ks_{NT}", f"Vs_{NT}"]))
                                    attention(128, qs[n * 64:(n + 1) * 64, gq, tq], ["qs"], chunks, SWA_SCALE, sink_ap,
                                              catj[gq * 64:(gq + 1) * 64, 6 + n, tq], [f"cat{6 + n}"])
                            else:
                                for sl in range(2):
                                    sidx = (j * 512) // 256 + sl
                                    qsl = slice(sl * 256, (sl + 1) * 256)
                                    chunks = []
                                    for kc in (2 * sidx, 2 * sidx + 1):
                                        chunks.append((ks[n * 64:(n + 1) * 64, kc * 128:(kc + 1) * 128],
                                                       Vs[:, kc, n * 64:(n + 1) * 64], None, [f"ks_{kc // 4}", f"Vs_{kc // 4}"]))
                                    attention(256, qs[n * 64:(n + 1) * 64, gq, qsl], ["qs"], chunks, SWA_SCALE, sink_ap,
                                              catj[gq * 64:(gq + 1) * 64, 6 + n, qsl], [f"cat{6 + n}"])
                    ckpt(22)
                    P.mark(f"{g}{l} Q{j} wout")
                    set_psring(range(8))
                    if debug and l == 0 and j == 0:
                        dma("sp", dbg_cat, catj, [f"cat{c}" for c in range(8)], [], is_out=True)
                    for half in range(2):
                        wt, wk_ = get_weight(f"wo{half}{g}{l}{j}")
                        for mi in range(4):
                            m = half * 4 + mi
                            ps, pk = psum()
                            for c in range(8):
                                mm(ps[:, :], wt[:, c, mi * 128:(mi + 1) * 128], catj[:, c, :], c == 0, c == 7, [wk_, f"cat{c}"], [pk])
                            xs_ = x[:, m, j * 512:(j + 1) * 512]
                            stt(xs_, ps[:, :], modv[:, l, 2, m, v:v + 1], xs_, ALU.mult, ALU.add, [pk, "modt", f"x{m}_{j}"], [f"x{m}_{j}"])
                P.barrier()
                if debug and l == 0:
                    dma("sp", dbg_x1[:, :, 0:T], x[:, :, 0:T], [], [], is_out=True)
                    P.barrier()
                ckpt(23)
                P.mark(f"{g}{l} MLP norm")
                set_psring(range(8))
                for j in range(NT):
                    norm(l, 1, j, lambda c: h2[:, c, j * 512:(j + 1) * 512], lambda c: f"h2{c}_{j}")
                ckpt(24)
                P.mark(f"{g}{l} MLP mm")
                for jh in range(8):
                    wa, wak = get_weight(f"w1{g}{l}{jh}")
                    wb, wbk = get_weight(f"w2{g}{l}{jh}", ahead=NST - 2)
                    for j in range(NT):
                        u, uk = ring("ub", ub)
                        for hc in range(4):
                            ps, pk = psum()
                            for c in range(8):
                                mm(ps[:, :], wa[:, c, hc * 128:(hc + 1) * 128], h2[:, c, j * 512:(j + 1) * 512], c == 0, c == 7,
                                   [wak, f"h2{c}_{j}"], [pk])
                            r_, rk_ = ring("relu", relu_t)
                            act(r_[:], ps[:, :], AF.Relu, [pk], [rk_])
                            tt_op(u[:, hc, :], r_[:], r_[:], ALU.mult, [rk_], [f"{uk}_{hc}"])
                        for m in range(8):
                            ps, pk = psum()
                            for hc in range(4):
                                mm(ps[:, :], wb[:, hc, m * 128:(m + 1) * 128], u[:, hc, :], hc == 0, hc == 3, [wbk, f"{uk}_{hc}"], [pk])
                            xs_ = x[:, m, j * 512:(j + 1) * 512]
                            stt(xs_, ps[:, :], modv[:, l, 5, m, v:v + 1], xs_, ALU.mult, ALU.add, [pk, "modt", f"x{m}_{j}"], [f"x{m}_{j}"])
                P.barrier()
                if debug and l == 0:
                    dma("sp", dbg_x2[:, :, 0:T], x[:, :, 0:T], [], [], is_out=True)
                    P.barrier()
            P.mark(f"{g} final")
            for j in range(NT):
                ps, pk = psum()
                for c in range(8):
                    sq, sqk = ring("sqt", sqt)
                    act(sq[:], x[:, c, j * 512:(j + 1) * 512], AF.Square, [f"x{c}_{j}"], [sqk])
                    mm(ps[:, :], ones_b, sq[:], c == 0, c == 7, [sqk, "cst"], [pk])
                act(s_t[:], ps[:, :], AF.Sqrt, [pk], ["s_t", "rstd"], bias=EPS, scale=1.0 / D)
                recip(rstd[:], s_t[:], ["s_t"], ["s_t", "rstd"])
                for c in range(8):
                    y_, yk = ring("tt", tt)
                    stt(y_[:], x[:, c, j * 512:(j + 1) * 512], vecs[:, 64 + c:65 + c], rstd[:], ALU.mult, ALU.mult,
                        [f"x{c}_{j}", "vecs", "rstd"], [yk])
                    dma("sp", yT[c * 128:(c + 1) * 128, j * 512:(j + 1) * 512], y_[:], [yk], [], is_out=True)
            P.barrier()

        try:
            for g_ in groups:
                run_group(g_)
            P.mark("end")
            P.finish()
        except StopBuild:
            pass
        MARKS.extend(P.marks)
    return nc


_CACHE = {}


def _consts():
    ident = np.eye(128, dtype=np.float32)
    ones = np.ones((128, 128), np.float32)
    kk = np.arange(128)[:, None]
    qq = np.arange(128)[None, :]
    mprev = np.where(kk >= qq, 0.0, NEG).astype(np.float32)
    mnext = np.where(kk <= qq, 0.0, NEG).astype(np.float32)
    c = np.concatenate([ident, ones, mprev, mnext, np.zeros((128, 128), np.float32)], axis=1)
    def tables(rot_dim):
        half = rot_dim // 2
        inv = (10000.0 ** (-np.arange(0, half, 2, dtype=np.float32) / half)).astype(np.float32)
        t = np.arange(2048)
        row = (t // 64).astype(np.float32)
        col = (t % 64).astype(np.float32)
        ar = row[:, None] * inv[None, :]
        ac = col[:, None] * inv[None, :]
        ang = np.concatenate([ar, ar, ac, ac], axis=-1).astype(np.float32)
        cos = np.cos(ang).astype(np.float32)
        sin = np.sin(ang).astype(np.float32)
        q = rot_dim // 4
        sgn = np.concatenate([-np.ones(q), np.ones(q), -np.ones(q), np.ones(q)]).astype(np.float32)
        return cos, sin * sgn[None, :]
    cs, ss = tables(64)
    cm, sm_ = tables(32)
    r = np.concatenate([cs, ss, cm, sm_], axis=1)
    r = r.reshape(16, 128, 192).transpose(1, 0, 2).reshape(128, 16 * 192)
    return np.ascontiguousarray(c), np.ascontiguousarray(r.astype(np.float32))


def kernel(x_prompt, x_sample, cache_mla_ckv, cache_mla_kpe, cache_swa_k, cache_swa_v, c, c_ctx,
           w_ada, b_ada, norm1, norm2, w_in, conv_w, sgu_norm, sgu_w, sgu_b, mla_q_norm, mla_w_q_up,
           mla_kv_norm, mla_w_kv_up, swa_sink, w_out, mlp_w1, mlp_w2, final_norm):
    in_maps = pack_inputs(x_prompt, x_sample, cache_mla_ckv, cache_mla_kpe, cache_swa_k, cache_swa_v, c, c_ctx,
                          w_ada, b_ada, norm1, norm2, w_in, conv_w, sgu_norm, sgu_w, sgu_b, mla_q_norm, mla_w_q_up,
                          mla_kv_norm, mla_w_kv_up, swa_sink, w_out, mlp_w1, mlp_w2, final_norm)
    if "nc" not in _CACHE:
        _CACHE["nc"] = build_program()
    nc = _CACHE["nc"]
    res = run_bass_kernel_spmd(nc, in_maps, core_ids=list(range(NCORES)))
    return unpack_outputs(res.results)


def pack_inputs(x_prompt, x_sample, cache_mla_ckv, cache_mla_kpe, cache_swa_k, cache_swa_v, c, c_ctx,
                w_ada, b_ada, norm1, norm2, w_in, conv_w, sgu_norm, sgu_w, sgu_b, mla_q_norm, mla_w_q_up,
                mla_kv_norm, mla_w_kv_up, swa_sink, w_out, mlp_w1, mlp_w2, final_norm, cores=range(NCORES)):
    f = lambda a: np.ascontiguousarray(np.asarray(a, dtype=np.float32))
    x_prompt, x_sample = f(x_prompt), f(x_sample)
    consts, ropes = _consts()
    w_in = f(w_in)
    a_b, a_c, a_x = w_in[:, :, 0:256], w_in[:, :, 256:512], w_in[:, :, 512:768]
    u_, v_ = w_in[:, :, 768:1024], w_in[:, :, 1024:1280]
    cq, ckv, kpe = w_in[:, :, 1280:1536], w_in[:, :, 1536:1664], w_in[:, :, 1664:1696]
    sq, sk, sv = w_in[:, :, 1696:1952], w_in[:, :, 1952:2080], w_in[:, :, 2080:2208]
    sq_g = sq.reshape(DEPTH, D, 2, 2, 64).transpose(0, 1, 3, 2, 4).reshape(DEPTH, D, 256)
    w_fm = f(np.concatenate([a_c, a_x, a_b, u_, cq], axis=2))
    w_tm = f(np.concatenate([ckv, kpe, sk, sv, v_, sq_g], axis=2))
    bada_fm = f(np.asarray(b_ada).reshape(DEPTH, 48, 128).transpose(2, 0, 1).reshape(128, DEPTH * 48))
    vecs = np.zeros((128, 128), np.float32)
    vecs[:, 0:32] = np.asarray(norm1).reshape(DEPTH, 8, 128).transpose(2, 0, 1).reshape(128, 32)
    vecs[:, 32:64] = np.asarray(norm2).reshape(DEPTH, 8, 128).transpose(2, 0, 1).reshape(128, 32)
    vecs[:, 64:72] = np.asarray(final_norm).reshape(8, 128).T
    vecs[:, 72:80] = np.asarray(mla_q_norm).reshape(DEPTH, 2, 128).transpose(2, 0, 1).reshape(128, 8)
    vecs[:, 80:104] = np.asarray(conv_w).reshape(DEPTH, 3, 2, 128).transpose(3, 0, 2, 1).reshape(128, 24)
    sgunorm_bc = f(np.broadcast_to(np.asarray(sgu_norm).reshape(1, DEPTH * 256), (128, DEPTH * 256)))
    kvnorm_bc = f(np.broadcast_to(np.asarray(mla_kv_norm).reshape(1, DEPTH * 128), (128, DEPTH * 128)))
    sink_bc = f(np.broadcast_to(np.asarray(swa_sink).reshape(1, 16), (128, 16)))
    sgub = f(np.asarray(sgu_b).reshape(1, DEPTH * 512))
    sgu_wT = f(np.asarray(sgu_w).transpose(0, 3, 1, 2).reshape(DEPTH, 128, 512))
    shared = dict(w_ada=f(w_ada), bada_fm=bada_fm, vecs_fm=vecs, sgunorm_bc=sgunorm_bc, kvnorm_bc=kvnorm_bc,
                  sink_bc=sink_bc, sgub=sgub, sgu_wT=sgu_wT, w_fm=w_fm, w_tm=w_tm, w_out=f(w_out), w1=f(mlp_w1),
                  w2=f(mlp_w2), wq=f(mla_w_q_up), wkv=f(mla_w_kv_up), ropes=ropes, consts=consts)
    c = np.asarray(c, np.float32)
    c_ctx = np.asarray(c_ctx, np.float32)
    in_maps = []
    for i in cores:
        cv = np.stack([c_ctx, c[i]], axis=0)
        cfm = f(cv.reshape(2, 8, 128).transpose(2, 1, 0).reshape(128, 16))
        m = dict(shared)
        m.update(
            xsT=f(x_sample[i].T),
            xpT=f(x_prompt[4 * i:4 * i + 4].reshape(1024, D).T),
            ckvT_c=f(np.asarray(cache_mla_ckv[i]).transpose(0, 2, 1)),
            kpeT_c=f(np.asarray(cache_mla_kpe[i]).transpose(0, 2, 1)),
            skT_c=f(np.asarray(cache_swa_k[i]).reshape(DEPTH, 512, 128).transpose(0, 2, 1)),
            sv_c=f(np.asarray(cache_swa_v[i]).reshape(DEPTH, 512, 128)),
            cfm=cfm,
        )
        in_maps.append(m)
    return in_maps


def unpack_outputs(rs):
    y_prompt = np.concatenate([r["ypT"].T.reshape(4, 256, D) for r in rs], axis=0).astype(np.float32)
    y_sample = np.stack([r["ysT"].T for r in rs], axis=0).astype(np.float32)
    new_ckv = np.concatenate([r["o_ckv"] for r in rs], axis=0).astype(np.float32)
    new_kpe = np.concatenate([r["o_kpe"] for r in rs], axis=0).astype(np.float32)
    new_k = np.concatenate([r["o_k"] for r in rs], axis=0).reshape(-1, DEPTH, 256, 2, 64).astype(np.float32)
    new_v = np.concatenate([r["o_v"] for r in rs], axis=0).reshape(-1, DEPTH, 256, 2, 64).astype(np.float32)
    return (np.ascontiguousarray(y_prompt), np.ascontiguousarray(y_sample), new_ckv, new_kpe, new_k, new_v)
```

```python
import numpy as np
import concourse.bass as bass
import concourse.mybir as mybir
from concourse.bass_utils import run_bass_kernel_spmd

F32, BF16 = mybir.dt.float32, mybir.dt.bfloat16
AF = mybir.ActivationFunctionType
ALU = mybir.AluOpType

D = 1024
DEPTH = 4
EPS = 1e-6
MLA_SCALE = 96 ** -0.5
SWA_SCALE = 0.125
NEG = -30000.0
NCORES = 8


class Info:
    __slots__ = ("sem", "val", "clock", "eng")

    def __init__(self, eng):
        self.sem = None
        self.val = 0
        self.clock = None
        self.eng = eng


class Prog:
    COMPUTE = ("pe", "act", "dve")
    QUEUES = ("sp", "pool")
    NSL = 6

    def __init__(self, nc):
        self.nc = nc
        self.eng = {"pe": nc.tensor, "act": nc.scalar, "dve": nc.vector, "pool": nc.gpsimd, "sp": nc.sync}
        self.sems = {}
        self.epoch = 0
        self.csem = {}
        self.cnt = {}
        self._new_epoch_sems()
        self.slots = {}
        self.nd = {q: 0 for q in self.QUEUES}
        for q in self.QUEUES:
            self.slots[q] = []
            for i in range(self.NSL):
                nm = f"d_{q}{i}"
                self.sems[nm] = nc.alloc_semaphore(name=nm)
                self.slots[q].append([nm, 0])
        self.clock = {e: {} for e in self.eng}
        self.last_w = {}
        self.readers = {}
        self.pending = {e: [] for e in self.COMPUTE}
        self.out_infos = []
        self.total = {e: 0 for e in self.eng}
        self.ps_open = {}
        self.marks = []

    def _new_epoch_sems(self):
        for e in self.COMPUTE:
            nm = f"c_{e}{self.epoch}"
            self.sems[nm] = self.nc.alloc_semaphore(name=nm)
            self.csem[e] = nm
            self.cnt[e] = 0
        self.epoch += 1

    def _wait(self, e, info):
        if info.sem is None:
            raise RuntimeError("dependency on unsignaled op")
        ck = self.clock[e]
        if ck.get(info.sem, 0) >= info.val:
            return
        self.eng[e].wait_ge(self.sems[info.sem], info.val)
        for s, v in info.clock.items():
            if ck.get(s, 0) < v:
                ck[s] = v

    def op(self, e, fn, reads=(), writes=(), sig=True, dma=False, is_out=False):
        psr = [k for k in reads if k.startswith("ps")]
        for k in psr:
            self.ps_open[k] = False
        if psr:
            writes = list(writes) + [k for k in psr if k not in writes]
        deps = []
        for k in reads:
            w = self.last_w.get(k)
            if w is not None:
                deps.append((w, True))
        for k in writes:
            w = self.last_w.get(k)
            if w is not None:
                deps.append((w, False))
            rd = self.readers.get(k)
            if rd:
                for r in rd.values():
                    deps.append((r, False))
        for info, raw in deps:
            if info.eng == e and not dma:
                if e == "pe":
                    continue
            self._wait(e, info)
        self.total[e] += 1
        info = Info(e)
        if dma:
            n = self.nd[e]
            self.nd[e] += 1
            slot = self.slots[e][n % self.NSL]
            ck = self.clock[e]
            if slot[1] > 0 and ck.get(slot[0], 0) < slot[1]:
                self.eng[e].wait_ge(self.sems[slot[0]], slot[1])
                ck[slot[0]] = slot[1]
            slot[1] += 16
            inst = fn()
            inst.then_inc(self.sems[slot[0]], 16)
            info.sem, info.val = slot[0], slot[1]
            info.clock = dict(ck)
            info.clock[info.sem] = info.val
            info.eng = e + "_dma%d" % n
            if is_out:
                self.out_infos.append(info)
        else:
            inst = fn()
            if e == "pe" and self.marks and self.marks[-1][1] is None:
                nm_ = inst.ins.name
                for mk in reversed(self.marks):
                    if mk[1] is not None:
                        break
                    mk[1] = nm_
            if sig:
                self.cnt[e] += 1
                inst.then_inc(self.sems[self.csem[e]], 1)
                info.sem, info.val = self.csem[e], self.cnt[e]
                info.clock = dict(self.clock[e])
                info.clock[info.sem] = info.val
                for p in self.pending[e]:
                    p.sem, p.val, p.clock = info.sem, info.val, info.clock
                self.pending[e] = []
            else:
                self.pending[e].append(info)
        for k in writes:
            self.last_w[k] = info
            self.readers[k] = {}
        for k in reads:
            self.readers.setdefault(k, {})[info.eng] = info
        return info

    def barrier(self):
        for e in self.COMPUTE:
            assert not self.pending[e]
        for f in self.eng:
            ck = self.clock[f]
            for e in self.COMPUTE:
                nm, v = self.csem[e], self.cnt[e]
                if v > 0 and e != f and ck.get(nm, 0) < v:
                    self.eng[f].wait_ge(self.sems[nm], v)
                if v > 0:
                    ck[nm] = v
            for q in self.QUEUES:
                for nm, v in self.slots[q]:
                    if v > 0 and ck.get(nm, 0) < v:
                        self.eng[f].wait_ge(self.sems[nm], v)
                        ck[nm] = v
        for e in self.COMPUTE:
            if self.cnt[e] > 0:
                self.eng[e].wait_ge(self.sems[self.csem[e]], self.cnt[e])
        self.last_w = {}
        self.readers = {}
        self._new_epoch_sems()

    def mark(self, label):
        self.marks.append([label, None])

    def finish(self):
        for info in self.out_infos:
            self._wait("sp", info)


MARKS = []


class StopBuild(Exception):
    pass


def build_program(depth=DEPTH, groups="AB", debug=False, p0=9):
    nc = bass.Bass("TRN2", target_bir_lowering=False)
    P = Prog(nc)

    def din(name, shape):
        return nc.dram_tensor(name, list(shape), F32, kind="ExternalInput").ap()

    def dout(name, shape):
        return nc.dram_tensor(name, list(shape), F32, kind="ExternalOutput").ap()

    xsT = din("xsT", (D, 2048))
    xpT = din("xpT", (D, 1024))
    ckvT_c = din("ckvT_c", (DEPTH, 128, 512))
    kpeT_c = din("kpeT_c", (DEPTH, 32, 512))
    skT_c = din("skT_c", (DEPTH, 128, 512))
    sv_c = din("sv_c", (DEPTH, 512, 128))
    cfm = din("cfm", (128, 16))
    w_ada = din("w_ada", (DEPTH, D, 6 * D))
    bada_fm = din("bada_fm", (128, DEPTH * 48))
    vecs_fm = din("vecs_fm", (128, 128))
    sgunorm_bc = din("sgunorm_bc", (128, DEPTH * 256))
    kvnorm_bc = din("kvnorm_bc", (128, DEPTH * 128))
    sink_bc = din("sink_bc", (128, 16))
    sgub = din("sgub", (1, DEPTH * 512))
    sgu_wT = din("sgu_wT", (DEPTH, 128, 512))
    w_fm = din("w_fm", (DEPTH, D, 1280))
    w_tm = din("w_tm", (DEPTH, D, 928))
    w_out = din("w_out", (DEPTH, D, D))
    w1 = din("w1", (DEPTH, D, 4096))
    w2 = din("w2", (DEPTH, 4096, D))
    wq = din("wq", (DEPTH, 256, 384))
    wkv = din("wkv", (DEPTH, 128, 512))
    ropes = din("ropes", (128, 16 * 192))
    consts = din("consts", (128, 640))
    ysT = dout("ysT", (D, 2048))
    ypT = dout("ypT", (D, 1024))
    o_ckv = dout("o_ckv", (4, DEPTH, 256, 128))
    o_kpe = dout("o_kpe", (4, DEPTH, 256, 32))
    o_k = dout("o_k", (4, DEPTH, 256, 128))
    o_v = dout("o_v", (4, DEPTH, 256, 128))
    if debug:
        dbg_h = nc.dram_tensor("dbg_h", [128, 8, 512], BF16, kind="ExternalOutput").ap()
        dbg_cat = nc.dram_tensor("dbg_cat", [128, 8, 512], BF16, kind="ExternalOutput").ap()
        dbg_x1 = dout("dbg_x1", (128, 8, 2048))
        dbg_x2 = dout("dbg_x2", (128, 8, 2048))

    A = nc.alloc_sbuf_tensor
    ident_f = A("ident_f", [128, 128], F32)
    cst = A("cst", [128, 512], BF16)
    ident_b, ones_b, mprev, mnext = cst[:, 0:128], cst[:, 128:256], cst[:, 256:384], cst[:, 384:512]
    modt = A("modt", [128, DEPTH * 48 * 2], F32)
    badat = A("badat", [128, DEPTH * 48], F32)
    vecs = A("vecs", [128, 128], F32)
    gs = A("gs", [128, DEPTH * 2 * 8 * 2], F32)
    csil = A("csil", [128, 16], F32)
    csil_b = A("csil_b", [128, 16], BF16)
    sinkexp = A("sinkexp", [128, 16], F32)
    ropet = A("ropet", [128, 16 * 192], BF16)
    sgn = A("sgn", [128, 256], F32)
    kvn = A("kvn", [128, 128], F32)
    wq_t = A("wq_t", [128, 2, 384], BF16)
    wkv_t = A("wkv_t", [128, 512], BF16)
    wsT_t = A("wsT_t", [128, 512], BF16)
    sgub_t = A("sgub_t", [1, 512], BF16)
    x = A("x", [128, 8, 2048], F32)
    R = A("R", [128, 36352], BF16)
    NST = 3
    wst = [A(f"wst{i}", [128, 4096], BF16) for i in range(NST)]
    sqt = [A(f"sqt{i}", [128, 512], BF16) for i in range(2)]
    tt = [A(f"tt{i}", [128, 512], F32) for i in range(2)]
    s_t = A("s_t", [128, 512], F32)
    rstd = s_t
    tmf = [A(f"tmf{i}", [128, 512], F32) for i in range(2)]
    tm2 = [A(f"tm2{i}", [128, 256], F32) for i in range(4)]
    tm3 = [A(f"tm3{i}", [128, 256], F32) for i in range(2)]
    sm = [A(f"sm{i}", [128, 4], F32) for i in range(4)]
    vn_t = [A(f"vn{i}", [128, 256], BF16) for i in range(2)]
    pt = [A(f"pt{i}", [128, 512], BF16) for i in range(3)]
    rden = [A(f"rden{i}", [64, 512], F32) for i in range(1)]
    qh = A("qh", [128, 4, 512], BF16)
    relu_t = [A(f"relu{i}", [128, 512], BF16) for i in range(2)]
    cqraw = relu_t
    yo = tt
    PS = [nc.alloc_psum_tensor(f"ps{i}", [128, 512], F32) for i in range(8)]

    rr = {}

    def ring(name, lst):
        i = rr.get(name, 0)
        rr[name] = i + 1
        j = i % len(lst)
        return lst[j], f"{name}{j}"

    psring = {"lst": list(range(8)), "i": 0}

    def psum():
        lst = psring["lst"]
        b = lst[psring["i"] % len(lst)]
        psring["i"] += 1
        assert not P.ps_open.get(f"ps{b}", False), f"psum bank ps{b} re-allocated before its previous contents were read"
        P.ps_open[f"ps{b}"] = True
        return PS[b], f"ps{b}"

    def set_psring(lst):
        psring["lst"] = list(lst)
        psring["i"] = 0

    def mm(out, lhsT, rhs, start, stop, reads, writes, sig=None):
        if sig is None:
            sig = True
        return P.op("pe", lambda: nc.tensor.matmul(out, lhsT=lhsT, rhs=rhs, start=start, stop=stop),
                    reads=reads, writes=writes, sig=sig)

    def act(out, in_, func, reads, writes, bias=0.0, scale=1.0, accum_out=None):
        kw = {}
        if accum_out is not None:
            kw["accum_out"] = accum_out
        return P.op("act", lambda: nc.scalar.activation(out=out, in_=in_, func=func, bias=bias, scale=scale, **kw),
                    reads=reads, writes=writes)

    def tt_op(out, in0, in1, op, reads, writes):
        return P.op("dve", lambda: nc.vector.tensor_tensor(out=out, in0=in0, in1=in1, op=op), reads=reads, writes=writes)

    def ts_op(out, in0, s1, op0, reads, writes, s2=None, op1=None):
        if op1 is None:
            return P.op("dve", lambda: nc.vector.tensor_scalar(out=out, in0=in0, scalar1=s1, scalar2=None, op0=op0),
                        reads=reads, writes=writes)
        return P.op("dve", lambda: nc.vector.tensor_scalar(out=out, in0=in0, scalar1=s1, scalar2=s2, op0=op0, op1=op1),
                    reads=reads, writes=writes)

    def stt(out, in0, scalar, in1, op0, op1, reads, writes):
        return P.op("dve", lambda: nc.vector.scalar_tensor_tensor(out=out, in0=in0, scalar=scalar, in1=in1, op0=op0, op1=op1),
                    reads=reads, writes=writes)

    def vcopy(out, in_, reads, writes):
        return P.op("dve", lambda: nc.vector.tensor_copy(out=out, in_=in_), reads=reads, writes=writes)

    def recip(out, in_, reads, writes):
        return P.op("dve", lambda: nc.vector.reciprocal(out=out, in_=in_), reads=reads, writes=writes)

    def dma(q, out, in_, reads, writes, is_out=False):
        e = nc.sync if q == "sp" else nc.gpsimd
        return P.op(q, lambda: e.dma_start(out=out, in_=in_), reads=reads, writes=writes, dma=True, is_out=is_out)

    wplan = []
    wstate = {"issued": 0, "used": 0}

    ada_defer = (len(groups) > 0 and groups[0] == "A")
    ada_phase0 = [0] if ada_defer else list(range(DEPTH))
    ADA_SPLIT = [2, 2, 2, 2, 1, 1, 1, 1]

    def plan_weights():
        for l in ada_phase0:
            for pc in range(12):
                wplan.append((f"ada{l}_{pc}", w_ada[l, :, pc * 512:(pc + 1) * 512], (8, 512)))
        for g in groups:
            NT = 4 if g == "A" else 2
            for l in range(depth):
                for j in range(NT):
                    wplan.append((f"fmK{g}{l}{j}", w_fm[l, :, 0:512], (8, 512)))
                    wplan.append((f"tmK{g}{l}{j}", w_tm[l, :, 0:416], (8, 416)))
                for j in range(NT):
                    wplan.append((f"fmQa{g}{l}{j}", w_fm[l, :, 512:896], (8, 384)))
                    wplan.append((f"fmQb{g}{l}{j}", w_fm[l, :, 896:1280], (8, 384)))
                    wplan.append((f"tmQ{g}{l}{j}", w_tm[l, :, 416:928], (8, 512)))
                    wplan.append((f"wo0{g}{l}{j}", w_out[l, :, 0:512], (8, 512)))
                    wplan.append((f"wo1{g}{l}{j}", w_out[l, :, 512:1024], (8, 512)))
                pcn = 0
                for jh in range(8):
                    wplan.append((f"w1{g}{l}{jh}", w1[l, :, jh * 512:(jh + 1) * 512], (8, 512)))
                    wplan.append((f"w2{g}{l}{jh}", w2[l, jh * 512:(jh + 1) * 512, :], (4, 1024)))
                    if ada_defer and g == "A" and l + 1 < DEPTH:
                        for _ in range(ADA_SPLIT[jh]):
                            wplan.append((f"ada{l + 1}_{pcn}", w_ada[l + 1, :, pcn * 512:(pcn + 1) * 512], (8, 512)))
                            pcn += 1

    def issue_weight():
        i = wstate["issued"]
        if i >= len(wplan):
            return
        name, ap, (nc_, ncol) = wplan[i]
        slot = wst[i % NST]
        dst = slot[:, 0:nc_ * ncol].rearrange("p (c n) -> p c n", c=nc_)
        src = ap.rearrange("(c p) n -> p c n", p=128)
        dma("pool", dst, src, reads=[], writes=[f"wst{i % NST}"])
        wstate["issued"] += 1

    def get_weight(name, ahead=NST - 1):
        i = wstate["used"]
        assert wplan[i][0] == name, (wplan[i][0], name)
        while wstate["issued"] < min(i + ahead + 1, len(wplan)):
            issue_weight()
        wstate["used"] += 1
        nc_, ncol = wplan[i][2]
        return wst[i % NST][:, 0:nc_ * ncol].rearrange("p (c n) -> p c n", c=nc_), f"wst{i % NST}"

    plan_weights()
    MARKS.clear()

    def ckpt(n):
        if p0 == n:
            P.barrier()
            P.finish()
            raise StopBuild()

    with nc.allow_low_precision("bf16 matmuls"), nc.allow_non_contiguous_dma("small strided loads"):
        dma("sp", ident_f[:], consts[:, 0:128], [], ["ident_f"])
        dma("pool", cst[:], consts[:, 0:512], [], ["cst"])
        dma("sp", badat[:], bada_fm[:, :], [], ["badat"])
        dma("sp", vecs[:], vecs_fm[:, :], [], ["vecs"])
        dma("sp", csil[:], cfm[:, :], [], ["csil"])
        dma("sp", sinkexp[:], sink_bc[:, :], [], ["sinkexp"])
        dma("pool", ropet[:], ropes[:, :], [], ["ropet"])
        act(csil[:], csil[:], AF.Silu, ["csil"], ["csil"])
        act(sinkexp[:], sinkexp[:], AF.Exp, ["sinkexp"], ["sinkexp"])
        if p0 == 1:
            P.barrier()
            P.finish()
            return nc
        modv = modt[:].rearrange("p (l k c v) -> p l k c v", l=DEPTH, k=6, c=8)
        gsv = gs[:].rearrange("p (l n c v) -> p l n c v", l=DEPTH, n=2, c=8)
        modt3 = modt[:].rearrange("p (m v) -> p m v", v=2)
        vcopy(csil_b[:], csil[:], ["csil"], ["csil_b"])
        P.op("dve", lambda: nc.vector.memset(qh[96:128, :, :], 0.0), [], [f"qh{h}" for h in range(4)])

        def ada_piece(l, pc):
            wt, wk_ = get_weight(f"ada{l}_{pc}")
            ps, pk = psum()
            for c in range(8):
                mm(ps[0:2, :], csil_b[:, c * 2:c * 2 + 2], wt[:, c, :], c == 0, c == 7, [wk_, "csil_b"], [pk])
            t_, tk_ = ring("tt", tt)
            t_ = t_[0:2, :]
            act(t_, ps[0:2, :], AF.Copy, [pk], [tk_])
            ps2, pk2 = psum()
            for mi in range(4):
                P.op("pe", lambda: nc.tensor.transpose(ps2[:, mi * 2:mi * 2 + 2], t_[:, mi * 128:(mi + 1) * 128], ident_f[0:2, 0:2]),
                     [tk_, "ident_f"], [pk2])
            m0 = l * 48 + pc * 4
            tt_op(modt3[:, m0:m0 + 4, :], ps2[:, 0:8].rearrange("p (m v) -> p m v", v=2),
                  badat[:, m0:m0 + 4].unsqueeze(2).to_broadcast([128, 4, 2]), ALU.add, [pk2, "badat"], [f"modt{l}"])

        def ada_finish(l):
            for n in range(2):
                nv = vecs[:, n * 32 + l * 8:n * 32 + l * 8 + 8]
                ts_op(gsv[:, l, n], modv[:, l, 3 * n + 1], 1.0, ALU.add, [f"modt{l}"], [f"gs{l}"])
                tt_op(gsv[:, l, n], gsv[:, l, n], nv.unsqueeze(2).to_broadcast([128, 8, 2]), ALU.mult, [f"gs{l}", "vecs"], [f"gs{l}"])

        for l in ada_phase0:
            for pc in range(12):
                ada_piece(l, pc)
            ada_finish(l)
        P.barrier()

        def run_group(g):
            lat = (g == "A")
            T = 2048 if lat else 1024
            S = 2048 if lat else 256
            NT = T // 512
            NK = 2560 if lat else 1024
            NB = NK // 128
            v = 1 if lat else 0
            xT = xsT if lat else xpT
            yT = ysT if lat else ypT
            o = 0

            def carve(n, c=None):
                nonlocal o
                ap = R[:, o:o + n]
                o += n
                if c is not None:
                    ap = ap.rearrange("p (c t) -> p c t", c=c)
                return ap
            KT = carve(4 * NK, 4)
            Vm = carve(NB * 256, NB)
            ks = carve(NK)
            Vs = carve(NB * 128, NB)
            pbuf = carve(2 * T, 2)
            hj = carve(8 * 512, 8)
            catj = carve(8 * 512, 8)
            cqn = carve(2 * 512, 2)
            qs = carve(2 * 512, 2)
            acc = carve(2 * 512, 2)
            ckT = carve(512)
            assert o <= 36352, o
            h2 = R[:, 0:8 * T].rearrange("p (c t) -> p c t", c=8)
            ub = [R[:, 8 * T + i * 2048:8 * T + (i + 1) * 2048].rearrange("p (c t) -> p c t", c=4) for i in range(2)]

            for c in range(8):
                dma("sp", x[:, c, 0:T], xT[c * 128:(c + 1) * 128, :], [], [f"x{c}_{j}" for j in range(NT)])

            def norm(l, n, j, dst, dkeys):
                ps, pk = psum()
                for c in range(8):
                    sq, sqk = ring("sqt", sqt)
                    act(sq[:], x[:, c, j * 512:(j + 1) * 512], AF.Square, [f"x{c}_{j}"], [sqk])
                    mm(ps[:, :], ones_b, sq[:], c == 0, c == 7, [sqk, "cst"], [pk])
                act(s_t[:], ps[:, :], AF.Sqrt, [pk], ["s_t", "rstd"], bias=EPS, scale=1.0 / D)
                recip(rstd[:], s_t[:], ["s_t"], ["s_t", "rstd"])
                for c in range(8):
                    t, tk = ring("tt", tt)
                    tt_op(t[:], x[:, c, j * 512:(j + 1) * 512], rstd[:], ALU.mult, [f"x{c}_{j}", "rstd"], [tk])
                    act(dst(c), t[:], AF.Identity, [tk, f"gs{l}", f"modt{l}"], [dkeys(c)],
                        bias=modv[:, l, 3 * n, c, v:v + 1], scale=gsv[:, l, n, c, v:v + 1])

            def attention(NQ, qT, qkeys, chunks, scale, sink_ap, out_ap, out_keys, G=1):
                W = G * NQ
                CH = 512 // W
                ai = rr.get("acc", 0) % 2
                rr["acc"] = rr.get("acc", 0) + 1
                nump, numk = PS[3 + ai * 2], f"ps{3 + ai * 2}"
                denp, denk = PS[4 + ai * 2], f"ps{4 + ai * 2}"
                ones64 = ones_b[:, 0:64]
                n = len(chunks)
                groups_ = [chunks[g0:g0 + CH] for g0 in range(0, n, CH)]

                def view(ap2d):
                    return ap2d if G == 1 else ap2d.rearrange("p (g t) -> p g t", g=G)

                def emit_S(grp):
                    sb, sbk = psum()
                    for i, (kT, vv, mask, keys) in enumerate(grp):
                        mm(view(sb[:, i * W:(i + 1) * W]), kT, qT, True, mask is None, keys + qkeys, [sbk])
                        if mask is not None:
                            mrhs = mask if G == 1 else mask.unsqueeze(1).to_broadcast([128, G, NQ])
                            mm(view(sb[:, i * W:(i + 1) * W]), ident_b, mrhs, False, True, ["cst"], [sbk])
                    return sb, sbk

                nxt = emit_S(groups_[0])
                idx = 0
                for gi, grp in enumerate(groups_):
                    sb, sbk = nxt
                    if gi + 1 < len(groups_):
                        nxt = emit_S(groups_[gi + 1])
                    p_, pk_ = ring("pt", pt)
                    w = len(grp) * W
                    act(p_[:, 0:w], sb[:, 0:w], AF.Exp, [sbk], [pk_], scale=scale)
                    for i, (kT, vv, mask, keys) in enumerate(grp):
                        first = (idx == 0)
                        last = (idx == n - 1)
                        idx += 1
                        mm(nump[0:64, 0:W], vv, p_[:, i * W:(i + 1) * W], first, last, keys + [pk_], [numk])
                        mm(denp[0:64, 0:W], ones64, p_[:, i * W:(i + 1) * W], first, last, [pk_, "cst"], [denk])
                rd, rdk = ring("rden", rden)
                sinks = sink_ap if isinstance(sink_ap, list) else [sink_ap] * G
                outs = out_ap if isinstance(out_ap, list) else [out_ap]
                if sinks[0] is not None:
                    for gg in range(G):
                        ts_op(rd[:, gg * NQ:(gg + 1) * NQ], denp[0:64, gg * NQ:(gg + 1) * NQ], sinks[gg], ALU.add,
                              [denk, "sinkexp"], [rdk])
                    recip(rd[:, 0:W], rd[:, 0:W], [rdk], [rdk])
                else:
                    recip(rd[:, 0:W], denp[0:64, 0:W], [denk], [rdk])
                for gg in range(G):
                    tt_op(outs[gg], nump[0:64, gg * NQ:(gg + 1) * NQ], rd[:, gg * NQ:(gg + 1) * NQ], ALU.mult, [numk, rdk], out_keys)

            def rope(dst, src, H, Dh, gb, off, rk, wk):
                Q = Dh // 4
                cos = ropet[:, gb * 192 + off:gb * 192 + off + Dh]
                sin = ropet[:, gb * 192 + off + Dh:gb * 192 + off + 2 * Dh]
                t2, t2k = ring("tm3", tm3)
                s4 = src.rearrange("p (h a b q) -> p h a b q", h=H, a=2, b=2)
                d4 = t2[:, 0:H * Dh].rearrange("p (h a b q) -> p h a b q", h=H, a=2, b=2)
                sn = sin.rearrange("p (a b q) -> p a b q", a=2, b=2)
                for bb in range(2):
                    tt_op(d4[:, :, :, bb, :], s4[:, :, :, 1 - bb, :],
                          sn[:, :, bb, :].unsqueeze(1).to_broadcast([128, H, 2, Q]), ALU.mult,
                          rk + ["ropet"] + ([t2k] if bb else []), [t2k])
                s3 = src.rearrange("p (h d) -> p h d", h=H)
                d3 = dst.rearrange("p (h d) -> p h d", h=H)
                tt_op(d3, s3, cos.unsqueeze(1).to_broadcast([128, H, Dh]), ALU.mult, rk + ["ropet"], wk)
                tt_op(dst, dst, t2[:, 0:H * Dh], ALU.add, wk + [t2k], wk)

            for l in range(depth):
                set_psring(range(8))
                dma("pool", wq_t[:], wq[l].rearrange("(c p) n -> p c n", p=128), [], ["wq_t"])
                dma("pool", wkv_t[:], wkv[l], [], ["wkv_t"])
                dma("pool", wsT_t[:], sgu_wT[l], [], ["wsT_t"])
                dma("pool", sgub_t[:], sgub[:, l * 512:(l + 1) * 512], [], ["sgub_t"])
                dma("sp", sgn[:], sgunorm_bc[:, l * 256:(l + 1) * 256], [], ["sgn"])
                dma("sp", kvn[:], kvnorm_bc[:, l * 128:(l + 1) * 128], [], ["kvn"])
                P.op("dve", lambda: nc.vector.memset(KT[96:128, :, :], 0.0), [],
                     [f"KT{h}_{jj}" for h in range(4) for jj in range(NT + 1)])
                if lat:
                    dma("pool", ckT[:], ckvT_c[l], [], ["ckT"])
                    for h in range(4):
                        dma("pool", KT[64:96, h, T:T + 512], kpeT_c[l], [], [f"KT{h}_{NT}"])
                    dma("pool", ks[:, T:T + 512], skT_c[l], [], [f"ks_{NT}"])
                    dma("pool", Vs[:, 16:20, :], sv_c[l].rearrange("(b p) d -> p b d", p=128), [], [f"Vs_{NT}"])

                def kv_up(j, nblk):
                    for h in range(4):
                        ps, pk = psum()
                        mm(ps[0:64, 0:nblk * 128], wkv_t[:, h * 128:h * 128 + 64], ckT[:, 0:nblk * 128], True, True,
                           ["wkv_t", "ckT"], [pk])
                        act(KT[0:64, h, j * 512:j * 512 + nblk * 128], ps[0:64, 0:nblk * 128], AF.Copy, [pk], [f"KT{h}_{j}"])
                    for b in range(nblk):
                        ps, pk = psum()
                        mm(ps[:, :], ckT[:, b * 128:(b + 1) * 128], wkv_t[:], True, True, ["wkv_t", "ckT"], [pk])
                        vcopy(Vm[:, j * 4 + b, :].rearrange("p (h d) -> p h d", h=4),
                              ps[:, :].rearrange("p (h t d) -> p h t d", h=4, t=2)[:, :, 1, :], [pk], [f"Vm_{j}"])

                if lat:
                    kv_up(NT, 4)
                ckpt(10)

                for j in range(NT):
                    P.mark(f"{g}{l} K{j} norm")
                    norm(l, 0, j, lambda c: hj[:, c, :], lambda c: f"hj{c}")
                    P.mark(f"{g}{l} K{j} fm")
                    wtF, wkF = get_weight(f"fmK{g}{l}{j}")
                    wtT, wkT = get_weight(f"tmK{g}{l}{j}", ahead=NST - 2)

                    def fmK_group(m):
                        ps, pk = psum()
                        for c in range(8):
                            mm(ps[:, :], wtF[:, c, m * 128:(m + 1) * 128], hj[:, c, :], c == 0, c == 7, [wkF, f"hj{c}"], [pk])
                        if m < 2:
                            act(pbuf[:, m, j * 512:(j + 1) * 512], ps[:, :], AF.Copy, [pk], [f"p{m}_{j}"])
                        else:
                            tt_op(pbuf[:, m - 2, j * 512:(j + 1) * 512], ps[:, :], pbuf[:, m - 2, j * 512:(j + 1) * 512],
                                  ALU.mult, [pk, f"p{m - 2}_{j}"], [f"p{m - 2}_{j}"])

                    def tmK_mm(b):
                        tok = slice(b * 128, (b + 1) * 128)
                        ps, pk = psum()
                        for c in range(8):
                            mm(ps[:, 0:416], hj[:, c, tok], wtT[:, c, :], c == 0, c == 7, [wkT, f"hj{c}"], [pk])
                        return ps, pk

                    def tmK_ew(b, ps, pk):
                        gb = j * 4 + b
                        st, stk = ring("tmf", tmf)
                        act(st[:, 0:416], ps[:, 0:416], AF.Copy, [pk], [stk])
                        smt, smk = ring("sm", sm)
                        t2, t2k = ring("tm3", tm3)
                        P.op("dve", lambda: nc.vector.memset(smt[:, 0:1], 0.0), [], [smk])
                        act(t2[:, 0:128], st[:, 0:128], AF.Square, [stk, smk], [t2k, smk], accum_out=smt[:, 0:1])
                        act(smt[:, 1:2], smt[:, 0:1], AF.Sqrt, [smk], [smk], bias=EPS, scale=1.0 / 128)
                        recip(smt[:, 2:3], smt[:, 1:2], [smk], [smk])
                        stt(st[:, 0:128], st[:, 0:128], smt[:, 2:3], kvn[:], ALU.mult, ALU.mult, [stk, smk, "kvn"], [stk])
                        if lat:
                            t3, t3k = ring("tm2", tm2)
                            rope(t3[:, 0:32], st[:, 128:160], 1, 32, gb, 128, [stk], [t3k])
                            kpe_src, kpek = t3[:, 0:32], t3k
                            t4, t4k = ring("tm2", tm2)
                            rope(t4[:, 0:128], st[:, 160:288], 2, 64, gb, 0, [stk], [t4k])
                            sk_src, skk = t4[:, 0:128], t4k
                        else:
                            kpe_src, kpek = st[:, 128:160], stk
                            sk_src, skk = st[:, 160:288], stk
                            sq_, r0 = divmod(gb * 128, 256)
                            dma("sp", o_ckv[sq_, l, r0:r0 + 128, :], st[:, 0:128], [stk], [], is_out=True)
                            dma("sp", o_kpe[sq_, l, r0:r0 + 128, :], st[:, 128:160], [stk], [], is_out=True)
                            dma("sp", o_k[sq_, l, r0:r0 + 128, :], st[:, 160:288], [stk], [], is_out=True)
                            dma("sp", o_v[sq_, l, r0:r0 + 128, :], st[:, 288:416], [stk], [], is_out=True)
                        vcopy(Vs[:, gb, :], st[:, 288:416], [stk], [f"Vs_{j}"])
                        return st, stk, kpe_src, kpek, sk_src, skk

                    def tmK_tr(b, st, stk, kpe_src, kpek, sk_src, skk):
                        gb = j * 4 + b
                        tok = slice(b * 128, (b + 1) * 128)
                        ps2, pk2 = psum()
                        P.op("pe", lambda: nc.tensor.transpose(ps2[:, 0:128], st[:, 0:128], ident_f[:]), [stk, "ident_f"], [pk2])
                        P.op("pe", lambda: nc.tensor.transpose(ps2[:, 128:256], sk_src, ident_f[:]), [skk, "ident_f"], [pk2])
                        P.op("pe", lambda: nc.tensor.transpose(ps2[0:32, 256:384], kpe_src, ident_f[:]), [kpek, "ident_f"], [pk2])
                        act(ckT[:, tok], ps2[:, 0:128], AF.Copy, [pk2], ["ckT"])
                        vcopy(ks[:, gb * 128:(gb + 1) * 128], ps2[:, 128:256], [pk2], [f"ks_{j}"])
                        act(KT[64:96, 0:2, gb * 128:(gb + 1) * 128], ps2[0:32, 256:384].unsqueeze(1).to_broadcast([32, 2, 128]),
                            AF.Copy, [pk2], [f"KT0_{j}", f"KT1_{j}"])
                        vcopy(KT[64:96, 2:4, gb * 128:(gb + 1) * 128], ps2[0:32, 256:384].unsqueeze(1).to_broadcast([32, 2, 128]),
                              [pk2], [f"KT2_{j}", f"KT3_{j}"])

                    P.mark(f"{g}{l} K{j} tm")
                    r = {}
                    e = {}
                    r[0] = tmK_mm(0)
                    r[1] = tmK_mm(1)
                    for b in range(4):
                        e[b] = tmK_ew(b, *r[b])
                        fmK_group(b)
                        tmK_tr(b, *e[b])
                        if b + 2 < 4:
                            r[b + 2] = tmK_mm(b + 2)
                    P.mark(f"{g}{l} K{j} kvup")
                    kv_up(j, 4)
                    ckpt(14)

                nseq_t = 512 // min(S, 512)
                for j in range(NT):
                    set_psring(range(8))
                    if j == 0:
                        P.mark(f"{g}{l} Q{j} norm")
                        norm(l, 0, j, lambda c: hj[:, c, :], lambda c: f"hj{c}")
                    P.mark(f"{g}{l} Q{j} fm")
                    if debug and l == 0 and j == 0:
                        dma("sp", dbg_h, hj, [f"hj{c}" for c in range(8)], [], is_out=True)
                    wa, wak = get_weight(f"fmQa{g}{l}{j}")

                    def fmQ_group(m, wt, wk_):
                        mi = m % 3
                        ps, pk = psum()
                        for c in range(8):
                            mm(ps[:, :], wt[:, c, mi * 128:(mi + 1) * 128], hj[:, c, :], c == 0, c == 7, [wk_, f"hj{c}"], [pk])
                        if m < 2:
                            act(catj[:, m, :], ps[:, :], AF.Copy, [pk], [f"cat{m}"])
                        elif m < 4:
                            act(catj[:, m, :], ps[:, :], AF.Gelu_apprx_tanh, [pk], [f"cat{m}"])
                        else:
                            cr, crk = cqraw[m - 4], f"cqraw{m - 4}"
                            act(cr[:], ps[:, :], AF.Copy, [pk], [crk])

                    for m in range(3):
                        fmQ_group(m, wa, wak)
                    wb, wbk = get_weight(f"fmQb{g}{l}{j}")
                    wtT, wkT = get_weight(f"tmQ{g}{l}{j}", ahead=NST - 2)

                    def cq_conv():
                        ps, pk = psum()
                        for c in range(2):
                            sq, sqk = ring("sqt", sqt)
                            act(sq[:], cqraw[c][:], AF.Square, [f"cqraw{c}"], [sqk])
                            mm(ps[:, :], ones_b, sq[:], c == 0, c == 1, [sqk, "cst"], [pk])
                        act(s_t[:], ps[:, :], AF.Sqrt, [pk], ["s_t", "rstd"], bias=EPS, scale=1.0 / 256)
                        recip(rstd[:], s_t[:], ["s_t"], ["s_t", "rstd"])
                        for c in range(2):
                            stt(cqn[:, c, :], cqraw[c][:], vecs[:, 72 + l * 2 + c:72 + l * 2 + c + 1], rstd[:], ALU.mult, ALU.mult,
                                [f"cqraw{c}", "vecs", "rstd"], [f"cqn{c}"])
                        Sq = min(S, 512)
                        for c in range(2):
                            cw = lambda k: vecs[:, 80 + (l * 2 + c) * 3 + k:80 + (l * 2 + c) * 3 + k + 1]
                            lo = j * 512
                            pk_all = [f"p{c}_{jj}" for jj in range(NT)]
                            ts_op(acc[:, c, :], pbuf[:, c, lo:lo + 512], cw(1), ALU.mult, pk_all + ["vecs"], [f"acc{c}"])
                            for sidx in range(nseq_t):
                                a0 = sidx * Sq
                                g0 = lo + a0
                                first_in_seq = (g0 % S == 0)
                                last_in_seq = ((g0 + Sq) % S == 0)
                                s0 = 1 if first_in_seq else 0
                                stt(acc[:, c, a0 + s0:a0 + Sq], pbuf[:, c, g0 + s0 - 1:g0 + Sq - 1], cw(0), acc[:, c, a0 + s0:a0 + Sq],
                                    ALU.mult, ALU.add, pk_all + ["vecs", f"acc{c}"], [f"acc{c}"])
                                e0 = 1 if last_in_seq else 0
                                stt(acc[:, c, a0:a0 + Sq - e0], pbuf[:, c, g0 + 1:g0 + Sq - e0 + 1], cw(2), acc[:, c, a0:a0 + Sq - e0],
                                    ALU.mult, ALU.add, pk_all + ["vecs", f"acc{c}"], [f"acc{c}"])
                            tt_op(catj[:, c, :], catj[:, c, :], acc[:, c, :], ALU.mult, [f"cat{c}", f"acc{c}"], [f"cat{c}"])

                    def tmQ_mm(b):
                        tok = slice(b * 128, (b + 1) * 128)
                        ps, pk = psum()
                        for c in range(8):
                            mm(ps[:, :], hj[:, c, tok], wtT[:, c, :], c == 0, c == 7, [wkT, f"hj{c}"], [pk])
                        return ps, pk

                    def tmQ_ew(b, ps, pk):
                        gb = j * 4 + b
                        st, stk = ring("tmf", tmf)
                        act(st[:, 0:256], ps[:, 0:256], AF.Gelu_apprx_tanh, [pk], [stk])
                        act(st[:, 256:512], ps[:, 256:512], AF.Copy, [pk], [stk])
                        smt, smk = ring("sm", sm)
                        t2, t2k = ring("tm3", tm3)
                        P.op("dve", lambda: nc.vector.memset(smt[:, 0:1], 0.0), [], [smk])
                        act(t2[:, 0:256], st[:, 0:256], AF.Square, [stk, smk], [t2k, smk], accum_out=smt[:, 0:1])
                        act(smt[:, 1:2], smt[:, 0:1], AF.Sqrt, [smk], [smk], bias=EPS, scale=1.0 / 256)
                        recip(smt[:, 2:3], smt[:, 1:2], [smk], [smk])
                        vn, vnk = ring("vn", vn_t)
                        stt(vn[:], st[:, 0:256], smt[:, 2:3], sgn[:], ALU.mult, ALU.mult, [stk, smk, "sgn"], [vnk])
                        if lat:
                            t4, t4k = ring("tm2", tm2)
                            rope(t4[:, 0:256], st[:, 256:512], 4, 64, gb, 0, [stk], [t4k])
                            return st, stk, vn, vnk, t4, t4k
                        return st, stk, vn, vnk, None, None

                    def tmQ_tail(b, st, stk, vn, vnk, t4, t4k):
                        tok = slice(b * 128, (b + 1) * 128)
                        ps2, pk2 = psum()
                        for hd in range(4):
                            cc, e_ = divmod(hd, 2)
                            mm(ps2[:, hd * 128:(hd + 1) * 128], vn[:, cc * 128:(cc + 1) * 128], wsT_t[:, hd * 128:(hd + 1) * 128],
                               True, False, [vnk, "wsT_t"], [pk2], sig=False)
                            mm(ps2[:, hd * 128:(hd + 1) * 128], ones_b[0:1, 0:128], sgub_t[0:1, hd * 128:(hd + 1) * 128],
                               False, True, ["cst", "sgub_t"], [pk2], sig=True)
                        ps3, pk3 = psum()
                        for gg in range(2):
                            src_ap = (t4[:, gg * 128:(gg + 1) * 128] if lat else st[:, 256 + gg * 128:256 + (gg + 1) * 128])
                            P.op("pe", lambda: nc.tensor.transpose(ps3[:, gg * 128:(gg + 1) * 128], src_ap, ident_f[:]),
                                 [t4k if lat else stk, "ident_f"], [pk3])
                        for hd in range(4):
                            cc, e_ = divmod(hd, 2)
                            tt_op(catj[e_ * 64:(e_ + 1) * 64, 2 + cc, tok], catj[e_ * 64:(e_ + 1) * 64, 2 + cc, tok],
                                  ps2[e_ * 64:(e_ + 1) * 64, hd * 128:(hd + 1) * 128], ALU.mult, [f"cat{2 + cc}", pk2], [f"cat{2 + cc}"])
                        act(qs[:, :, tok], ps3[:, 0:256].rearrange("p (g t) -> p g t", g=2), AF.Copy, [pk3], ["qs"])

                    def qm_mm(b):
                        tokq = slice(b * 128, (b + 1) * 128)
                        ps, pk = psum()
                        for c in range(2):
                            mm(ps[:, 0:384], cqn[:, c, tokq], wq_t[:, c, :], c == 0, c == 1, [f"cqn{c}", "wq_t"], [pk])
                        return ps, pk

                    def qm_ew(b, ps, pk):
                        gb = j * 4 + b
                        st, stk = ring("tmf", tmf)
                        act(st[:, 0:384], ps[:, 0:384], AF.Copy, [pk], [stk])
                        if lat:
                            t3, t3k = ring("tm2", tm2)
                            src4 = st[:, 0:384].rearrange("p (h d) -> p h d", h=4)[:, :, 64:96]
                            t5, t5k = ring("tm2", tm2)
                            vcopy(t5[:, 0:128].rearrange("p (h d) -> p h d", h=4), src4, [stk], [t5k])
                            rope(t3[:, 0:128], t5[:, 0:128], 4, 32, gb, 128, [t5k], [t3k])
                            vcopy(src4, t3[:, 0:128].rearrange("p (h d) -> p h d", h=4), [t3k], [stk])
                        return st, stk

                    def qm_tr(b, st, stk):
                        tokq = slice(b * 128, (b + 1) * 128)
                        ps, pk = psum()
                        for h in range(4):
                            P.op("pe", lambda: nc.tensor.transpose(ps[0:96, h * 128:(h + 1) * 128], st[:, h * 96:(h + 1) * 96], ident_f[:]),
                                 [stk, "ident_f"], [pk])
                        src = ps[0:96, :].rearrange("p (h t) -> p h t", h=4)
                        if b % 2 == 0:
                            act(qh[0:96, :, tokq], src, AF.Copy, [pk], [f"qh{h}" for h in range(4)])
                        else:
                            vcopy(qh[0:96, :, tokq], src, [pk], [f"qh{h}" for h in range(4)])

                    P.mark(f"{g}{l} Q{j} tm")
                    r = {}
                    e = {}
                    rq = {}
                    for m in range(3, 6):
                        fmQ_group(m, wb, wbk)
                    cq_conv()
                    r[0] = tmQ_mm(0)
                    r[1] = tmQ_mm(1)
                    rq[0] = qm_mm(0)
                    for b in range(4):
                        e[b] = tmQ_ew(b, *r[b])
                        eq = qm_ew(b, *rq[b])
                        if b + 1 < 4:
                            rq[b + 1] = qm_mm(b + 1)
                        tmQ_tail(b, *e[b])
                        if b + 2 < 4:
                            r[b + 2] = tmQ_mm(b + 2)
                        qm_tr(b, *eq)
                    ckpt(19)
                    set_psring([0, 1, 2])
                    ckpt(20)
                    P.mark(f"{g}{l} Q{j} MLA")
                    if lat:
                        for h in range(4):
                            chunks = []
                            for kc in range(20):
                                jt = kc // 4
                                chunks.append((KT[:, h, kc * 128:(kc + 1) * 128], Vm[:, kc, h * 64:(h + 1) * 64], None,
                                               [f"KT{h}_{jt}", f"Vm_{jt}"]))
                            cc, e = divmod(h, 2)
                            attention(512, qh[:, h, :], [f"qh{h}"], chunks, MLA_SCALE, None,
                                      catj[e * 64:(e + 1) * 64, 4 + cc, :], [f"cat{4 + cc}"])
                    else:
                        for sl in range(2):
                            sidx = (j * 512) // 256 + sl
                            qsl = slice(sl * 256, (sl + 1) * 256)
                            for h in range(4):
                                chunks = []
                                for kc in (2 * sidx, 2 * sidx + 1):
                                    jt = kc // 4
                                    chunks.append((KT[:, h, kc * 128:(kc + 1) * 128], Vm[:, kc, h * 64:(h + 1) * 64], None,
                                                   [f"KT{h}_{jt}", f"Vm_{jt}"]))
                                cc, e = divmod(h, 2)
                                attention(256, qh[:, h, qsl], [f"qh{h}"], chunks, MLA_SCALE, None,
                                          catj[e * 64:(e + 1) * 64, 4 + cc, qsl], [f"cat{4 + cc}"])
                    ckpt(21)
                    if j + 1 < NT:
                        norm(l, 0, j + 1, lambda c: hj[:, c, :], lambda c: f"hj{c}")
                    P.mark(f"{g}{l} Q{j} SWA")
                    for n in range(2):
                        sink_l = [sinkexp[0:64, l * 4 + n * 2 + gq:l * 4 + n * 2 + gq + 1] for gq in range(2)]
                        if lat:
                            for bq in range(4):
                                blk = j * 4 + bq
                                tq = slice(bq * 128, (bq + 1) * 128)
                                chunks = []
                                if blk >= 1:
                                    chunks.append((ks[n * 64:(n + 1) * 64, (blk - 1) * 128:blk * 128],
                                                   Vs[:, blk - 1, n * 64:(n + 1) * 64], mprev,
                                                   [f"ks_{(blk - 1) // 4}", f"Vs_{(blk - 1) // 4}"]))
                                chunks.append((ks[n * 64:(n + 1) * 64, blk * 128:(blk + 1) * 128],
                                               Vs[:, blk, n * 64:(n + 1) * 64], None, [f"ks_{blk // 4}", f"Vs_{blk // 4}"]))
                                if blk <= 14:
                                    chunks.append((ks[n * 64:(n + 1) * 64, (blk + 1) * 128:(blk + 2) * 128],
                                                   Vs[:, blk + 1, n * 64:(n + 1) * 64], mnext,
                                                   [f"ks_{(blk + 1) // 4}", f"Vs_{(blk + 1) // 4}"]))
                                for kc in range(16, 20):
                                    chunks.append((ks[n * 64:(n + 1) * 64, kc * 128:(kc + 1) * 128],
                                                   Vs[:, kc, n * 64:(n + 1) * 64], None, [f"ks_{NT}", f"Vs_{NT}"]))
                                attention(128, qs[n * 64:(n + 1) * 64, :, tq], ["qs"], chunks, SWA_SCALE, sink_l,
                                          [catj[gq * 64:(gq + 1) * 64, 6 + n, tq] for gq in range(2)], [f"cat{6 + n}"], G=2)
                        else:
                            for sl in range(2):
                                sidx = (j * 512) // 256 + sl
                                qsl = slice(sl * 256, (sl + 1) * 256)
                                chunks = []
                                for kc in (2 * sidx, 2 * sidx + 1):
                                    chunks.append((ks[n * 64:(n + 1) * 64, kc * 128:(kc + 1) * 128],
                                                   Vs[:, kc, n * 64:(n + 1) * 64], None, [f"ks_{kc // 4}", f"Vs_{kc // 4}"]))
                                attention(256, qs[n * 64:(n + 1) * 64, :, qsl], ["qs"], chunks, SWA_SCALE, sink_l,
                                          [catj[gq * 64:(gq + 1) * 64, 6 + n, qsl] for gq in range(2)], [f"cat{6 + n}"], G=2)
                    ckpt(22)
                    P.mark(f"{g}{l} Q{j} wout")
                    set_psring(range(8))
                    if debug and l == 0 and j == 0:
                        dma("sp", dbg_cat, catj, [f"cat{c}" for c in range(8)], [], is_out=True)
                    for half in range(2):
                        wt, wk_ = get_weight(f"wo{half}{g}{l}{j}")
                        for mi in range(4):
                            m = half * 4 + mi
                            ps, pk = psum()
                            for c in range(8):
                                mm(ps[:, :], wt[:, c, mi * 128:(mi + 1) * 128], catj[:, c, :], c == 0, c == 7, [wk_, f"cat{c}"], [pk])
                            xs_ = x[:, m, j * 512:(j + 1) * 512]
                            stt(xs_, ps[:, :], modv[:, l, 2, m, v:v + 1], xs_, ALU.mult, ALU.add, [pk, f"modt{l}", f"x{m}_{j}"], [f"x{m}_{j}"])
                P.barrier()
                if debug and l == 0:
                    dma("sp", dbg_x1[:, :, 0:T], x[:, :, 0:T], [], [], is_out=True)
                    P.barrier()
                ckpt(23)
                P.mark(f"{g}{l} MLP norm")
                set_psring(range(8))
                for j in range(NT):
                    norm(l, 1, j, lambda c: h2[:, c, j * 512:(j + 1) * 512], lambda c: f"h2{c}_{j}")
                ckpt(24)
                P.mark(f"{g}{l} MLP mm")
                ada_pc = [0]
                for jh in range(8):
                    wa, wak = get_weight(f"w1{g}{l}{jh}")
                    wb, wbk = get_weight(f"w2{g}{l}{jh}", ahead=NST - 2)
                    for j in range(NT):
                        u, uk = ring("ub", ub)
                        for hc in range(4):
                            ps, pk = psum()
                            for c in range(8):
                                mm(ps[:, :], wa[:, c, hc * 128:(hc + 1) * 128], h2[:, c, j * 512:(j + 1) * 512], c == 0, c == 7,
                                   [wak, f"h2{c}_{j}"], [pk])
                            r_, rk_ = ring("relu", relu_t)
                            act(r_[:], ps[:, :], AF.Relu, [pk], [rk_])
                            tt_op(u[:, hc, :], r_[:], r_[:], ALU.mult, [rk_], [f"{uk}_{hc}"])
                        for m in range(8):
                            ps, pk = psum()
                            for hc in range(4):
                                mm(ps[:, :], wb[:, hc, m * 128:(m + 1) * 128], u[:, hc, :], hc == 0, hc == 3, [wbk, f"{uk}_{hc}"], [pk])
                            xs_ = x[:, m, j * 512:(j + 1) * 512]
                            stt(xs_, ps[:, :], modv[:, l, 5, m, v:v + 1], xs_, ALU.mult, ALU.add, [pk, f"modt{l}", f"x{m}_{j}"], [f"x{m}_{j}"])
                    if ada_defer and g == "A" and l + 1 < DEPTH:
                        for _ in range(ADA_SPLIT[jh]):
                            ada_piece(l + 1, ada_pc[0])
                            ada_pc[0] += 1
                        if jh == 7:
                            ada_finish(l + 1)
                P.barrier()
                if debug and l == 0:
                    dma("sp", dbg_x2[:, :, 0:T], x[:, :, 0:T], [], [], is_out=True)
                    P.barrier()
            P.mark(f"{g} final")
            for j in range(NT):
                ps, pk = psum()
                for c in range(8):
                    sq, sqk = ring("sqt", sqt)
                    act(sq[:], x[:, c, j * 512:(j + 1) * 512], AF.Square, [f"x{c}_{j}"], [sqk])
                    mm(ps[:, :], ones_b, sq[:], c == 0, c == 7, [sqk, "cst"], [pk])
                act(s_t[:], ps[:, :], AF.Sqrt, [pk], ["s_t", "rstd"], bias=EPS, scale=1.0 / D)
                recip(rstd[:], s_t[:], ["s_t"], ["s_t", "rstd"])
                for c in range(8):
                    y_, yk = ring("tt", tt)
                    stt(y_[:], x[:, c, j * 512:(j + 1) * 512], vecs[:, 64 + c:65 + c], rstd[:], ALU.mult, ALU.mult,
                        [f"x{c}_{j}", "vecs", "rstd"], [yk])
                    dma("sp", yT[c * 128:(c + 1) * 128, j * 512:(j + 1) * 512], y_[:], [yk], [], is_out=True)
            P.barrier()

        try:
            for g_ in groups:
                run_group(g_)
            P.mark("end")
            P.finish()
        except StopBuild:
            pass
        MARKS.extend(P.marks)
    return nc


_CACHE = {}


def _consts():
    ident = np.eye(128, dtype=np.float32)
    ones = np.ones((128, 128), np.float32)
    kk = np.arange(128)[:, None]
    qq = np.arange(128)[None, :]
    mprev = np.where(kk >= qq, 0.0, NEG).astype(np.float32)
    mnext = np.where(kk <= qq, 0.0, NEG).astype(np.float32)
    c = np.concatenate([ident, ones, mprev, mnext, np.zeros((128, 128), np.float32)], axis=1)
    def tables(rot_dim):
        half = rot_dim // 2
        inv = (10000.0 ** (-np.arange(0, half, 2, dtype=np.float32) / half)).astype(np.float32)
        t = np.arange(2048)
        row = (t // 64).astype(np.float32)
        col = (t % 64).astype(np.float32)
        ar = row[:, None] * inv[None, :]
        ac = col[:, None] * inv[None, :]
        ang = np.concatenate([ar, ar, ac, ac], axis=-1).astype(np.float32)
        cos = np.cos(ang).astype(np.float32)
        sin = np.sin(ang).astype(np.float32)
        q = rot_dim // 4
        sgn = np.concatenate([-np.ones(q), np.ones(q), -np.ones(q), np.ones(q)]).astype(np.float32)
        return cos, sin * sgn[None, :]
    cs, ss = tables(64)
    cm, sm_ = tables(32)
    r = np.concatenate([cs, ss, cm, sm_], axis=1)
    r = r.reshape(16, 128, 192).transpose(1, 0, 2).reshape(128, 16 * 192)
    return np.ascontiguousarray(c), np.ascontiguousarray(r.astype(np.float32))


def kernel(x_prompt, x_sample, cache_mla_ckv, cache_mla_kpe, cache_swa_k, cache_swa_v, c, c_ctx,
           w_ada, b_ada, norm1, norm2, w_in, conv_w, sgu_norm, sgu_w, sgu_b, mla_q_norm, mla_w_q_up,
           mla_kv_norm, mla_w_kv_up, swa_sink, w_out, mlp_w1, mlp_w2, final_norm):
    in_maps = pack_inputs(x_prompt, x_sample, cache_mla_ckv, cache_mla_kpe, cache_swa_k, cache_swa_v, c, c_ctx,
                          w_ada, b_ada, norm1, norm2, w_in, conv_w, sgu_norm, sgu_w, sgu_b, mla_q_norm, mla_w_q_up,
                          mla_kv_norm, mla_w_kv_up, swa_sink, w_out, mlp_w1, mlp_w2, final_norm)
    if "nc" not in _CACHE:
        _CACHE["nc"] = build_program()
    nc = _CACHE["nc"]
    res = run_bass_kernel_spmd(nc, in_maps, core_ids=list(range(NCORES)))
    return unpack_outputs(res.results)


def pack_inputs(x_prompt, x_sample, cache_mla_ckv, cache_mla_kpe, cache_swa_k, cache_swa_v, c, c_ctx,
                w_ada, b_ada, norm1, norm2, w_in, conv_w, sgu_norm, sgu_w, sgu_b, mla_q_norm, mla_w_q_up,
                mla_kv_norm, mla_w_kv_up, swa_sink, w_out, mlp_w1, mlp_w2, final_norm, cores=range(NCORES)):
    f = lambda a: np.ascontiguousarray(np.asarray(a, dtype=np.float32))
    x_prompt, x_sample = f(x_prompt), f(x_sample)
    consts, ropes = _consts()
    w_in = f(w_in)
    a_b, a_c, a_x = w_in[:, :, 0:256], w_in[:, :, 256:512], w_in[:, :, 512:768]
    u_, v_ = w_in[:, :, 768:1024], w_in[:, :, 1024:1280]
    cq, ckv, kpe = w_in[:, :, 1280:1536], w_in[:, :, 1536:1664], w_in[:, :, 1664:1696]
    sq, sk, sv = w_in[:, :, 1696:1952], w_in[:, :, 1952:2080], w_in[:, :, 2080:2208]
    sq_g = sq.reshape(DEPTH, D, 2, 2, 64).transpose(0, 1, 3, 2, 4).reshape(DEPTH, D, 256)
    w_fm = f(np.concatenate([a_c, a_x, a_b, u_, cq], axis=2))
    w_tm = f(np.concatenate([ckv, kpe, sk, sv, v_, sq_g], axis=2))
    bada_fm = f(np.asarray(b_ada).reshape(DEPTH, 48, 128).transpose(2, 0, 1).reshape(128, DEPTH * 48))
    vecs = np.zeros((128, 128), np.float32)
    vecs[:, 0:32] = np.asarray(norm1).reshape(DEPTH, 8, 128).transpose(2, 0, 1).reshape(128, 32)
    vecs[:, 32:64] = np.asarray(norm2).reshape(DEPTH, 8, 128).transpose(2, 0, 1).reshape(128, 32)
    vecs[:, 64:72] = np.asarray(final_norm).reshape(8, 128).T
    vecs[:, 72:80] = np.asarray(mla_q_norm).reshape(DEPTH, 2, 128).transpose(2, 0, 1).reshape(128, 8)
    vecs[:, 80:104] = np.asarray(conv_w).reshape(DEPTH, 3, 2, 128).transpose(3, 0, 2, 1).reshape(128, 24)
    sgunorm_bc = f(np.broadcast_to(np.asarray(sgu_norm).reshape(1, DEPTH * 256), (128, DEPTH * 256)))
    kvnorm_bc = f(np.broadcast_to(np.asarray(mla_kv_norm).reshape(1, DEPTH * 128), (128, DEPTH * 128)))
    sink_bc = f(np.broadcast_to(np.asarray(swa_sink).reshape(1, 16), (128, 16)))
    sgub = f(np.asarray(sgu_b).reshape(1, DEPTH * 512))
    sgu_wT = f(np.asarray(sgu_w).transpose(0, 3, 1, 2).reshape(DEPTH, 128, 512))
    shared = dict(w_ada=f(w_ada), bada_fm=bada_fm, vecs_fm=vecs, sgunorm_bc=sgunorm_bc, kvnorm_bc=kvnorm_bc,
                  sink_bc=sink_bc, sgub=sgub, sgu_wT=sgu_wT, w_fm=w_fm, w_tm=w_tm, w_out=f(w_out), w1=f(mlp_w1),
                  w2=f(mlp_w2), wq=f(mla_w_q_up), wkv=f(mla_w_kv_up), ropes=ropes, consts=consts)
    c = np.asarray(c, np.float32)
    c_ctx = np.asarray(c_ctx, np.float32)
    in_maps = []
    for i in cores:
        cv = np.stack([c_ctx, c[i]], axis=0)
        cfm = f(cv.reshape(2, 8, 128).transpose(2, 1, 0).reshape(128, 16))
        m = dict(shared)
        m.update(
            xsT=f(x_sample[i].T),
            xpT=f(x_prompt[4 * i:4 * i + 4].reshape(1024, D).T),
            ckvT_c=f(np.asarray(cache_mla_ckv[i]).transpose(0, 2, 1)),
            kpeT_c=f(np.asarray(cache_mla_kpe[i]).transpose(0, 2, 1)),
            skT_c=f(np.asarray(cache_swa_k[i]).reshape(DEPTH, 512, 128).transpose(0, 2, 1)),
            sv_c=f(np.asarray(cache_swa_v[i]).reshape(DEPTH, 512, 128)),
            cfm=cfm,
        )
        in_maps.append(m)
    return in_maps


def unpack_outputs(rs):
    y_prompt = np.concatenate([r["ypT"].T.reshape(4, 256, D) for r in rs], axis=0).astype(np.float32)
    y_sample = np.stack([r["ysT"].T for r in rs], axis=0).astype(np.float32)
    new_ckv = np.concatenate([r["o_ckv"] for r in rs], axis=0).astype(np.float32)
    new_kpe = np.concatenate([r["o_kpe"] for r in rs], axis=0).astype(np.float32)
    new_k = np.concatenate([r["o_k"] for r in rs], axis=0).reshape(-1, DEPTH, 256, 2, 64).astype(np.float32)
    new_v = np.concatenate([r["o_v"] for r in rs], axis=0).reshape(-1, DEPTH, 256, 2, 64).astype(np.float32)
    return (np.ascontiguousarray(y_prompt), np.ascontiguousarray(y_sample), new_ckv, new_kpe, new_k, new_v)
```

```python
import numpy as np
import concourse.bass as bass
import concourse.mybir as mybir
from concourse.bass_utils import run_bass_kernel_spmd

F32, BF16 = mybir.dt.float32, mybir.dt.bfloat16
AF = mybir.ActivationFunctionType
ALU = mybir.AluOpType

D = 1024
DEPTH = 4
EPS = 1e-6
MLA_SCALE = 96 ** -0.5
SWA_SCALE = 0.125
NEG = -30000.0
NCORES = 8


class Info:
    __slots__ = ("sem", "val", "clock", "eng")

    def __init__(self, eng):
        self.sem = None
        self.val = 0
        self.clock = None
        self.eng = eng


class Prog:
    COMPUTE = ("pe", "act", "dve")
    QUEUES = ("sp", "pool")
    NSL = 6

    def __init__(self, nc):
        self.nc = nc
        self.eng = {"pe": nc.tensor, "act": nc.scalar, "dve": nc.vector, "pool": nc.gpsimd, "sp": nc.sync}
        self.sems = {}
        self.epoch = 0
        self.csem = {}
        self.cnt = {}
        self._new_epoch_sems()
        self.slots = {}
        self.nd = {q: 0 for q in self.QUEUES}
        for q in self.QUEUES:
            self.slots[q] = []
            for i in range(self.NSL):
                nm = f"d_{q}{i}"
                self.sems[nm] = nc.alloc_semaphore(name=nm)
                self.slots[q].append([nm, 0])
        self.clock = {e: {} for e in self.eng}
        self.last_w = {}
        self.readers = {}
        self.pending = {e: [] for e in self.COMPUTE}
        self.out_infos = []
        self.total = {e: 0 for e in self.eng}
        self.ps_open = {}
        self.marks = []

    def _new_epoch_sems(self):
        for e in self.COMPUTE:
            nm = f"c_{e}{self.epoch}"
            self.sems[nm] = self.nc.alloc_semaphore(name=nm)
            self.csem[e] = nm
            self.cnt[e] = 0
        self.epoch += 1

    def _wait(self, e, info):
        if info.sem is None:
            raise RuntimeError("dependency on unsignaled op")
        ck = self.clock[e]
        if ck.get(info.sem, 0) >= info.val:
            return
        self.eng[e].wait_ge(self.sems[info.sem], info.val)
        for s, v in info.clock.items():
            if ck.get(s, 0) < v:
                ck[s] = v

    def op(self, e, fn, reads=(), writes=(), sig=True, dma=False, is_out=False):
        psr = [k for k in reads if k.startswith("ps")]
        for k in psr:
            self.ps_open[k] = False
        if psr:
            writes = list(writes) + [k for k in psr if k not in writes]
        deps = []
        for k in reads:
            w = self.last_w.get(k)
            if w is not None:
                deps.append((w, True))
        for k in writes:
            w = self.last_w.get(k)
            if w is not None:
                deps.append((w, False))
            rd = self.readers.get(k)
            if rd:
                for r in rd.values():
                    deps.append((r, False))
        for info, raw in deps:
            if info.eng == e and not dma:
                if e == "pe":
                    continue
            self._wait(e, info)
        self.total[e] += 1
        info = Info(e)
        if dma:
            n = self.nd[e]
            self.nd[e] += 1
            slot = self.slots[e][n % self.NSL]
            ck = self.clock[e]
            if slot[1] > 0 and ck.get(slot[0], 0) < slot[1]:
                self.eng[e].wait_ge(self.sems[slot[0]], slot[1])
                ck[slot[0]] = slot[1]
            slot[1] += 16
            inst = fn()
            inst.then_inc(self.sems[slot[0]], 16)
            info.sem, info.val = slot[0], slot[1]
            info.clock = dict(ck)
            info.clock[info.sem] = info.val
            info.eng = e + "_dma%d" % n
            if is_out:
                self.out_infos.append(info)
        else:
            inst = fn()
            if e == "pe" and self.marks and self.marks[-1][1] is None:
                nm_ = inst.ins.name
                for mk in reversed(self.marks):
                    if mk[1] is not None:
                        break
                    mk[1] = nm_
            if sig:
                self.cnt[e] += 1
                inst.then_inc(self.sems[self.csem[e]], 1)
                info.sem, info.val = self.csem[e], self.cnt[e]
                info.clock = dict(self.clock[e])
                info.clock[info.sem] = info.val
                for p in self.pending[e]:
                    p.sem, p.val, p.clock = info.sem, info.val, info.clock
                self.pending[e] = []
            else:
                self.pending[e].append(info)
        for k in writes:
            self.last_w[k] = info
            self.readers[k] = {}
        for k in reads:
            self.readers.setdefault(k, {})[info.eng] = info
        return info

    def barrier(self):
        for e in self.COMPUTE:
            assert not self.pending[e]
        for f in self.eng:
            ck = self.clock[f]
            for e in self.COMPUTE:
                nm, v = self.csem[e], self.cnt[e]
                if v > 0 and e != f and ck.get(nm, 0) < v:
                    self.eng[f].wait_ge(self.sems[nm], v)
                if v > 0:
                    ck[nm] = v
            for q in self.QUEUES:
                for nm, v in self.slots[q]:
                    if v > 0 and ck.get(nm, 0) < v:
                        self.eng[f].wait_ge(self.sems[nm], v)
                        ck[nm] = v
        for e in self.COMPUTE:
            if self.cnt[e] > 0:
                self.eng[e].wait_ge(self.sems[self.csem[e]], self.cnt[e])
        self.last_w = {}
        self.readers = {}
        self._new_epoch_sems()

    def mark(self, label):
        self.marks.append([label, None])

    def finish(self):
        for info in self.out_infos:
            self._wait("sp", info)


MARKS = []


class StopBuild(Exception):
    pass


def build_program(depth=DEPTH, groups="AB", debug=False, p0=9):
    nc = bass.Bass("TRN2", target_bir_lowering=False)
    P = Prog(nc)

    def din(name, shape):
        return nc.dram_tensor(name, list(shape), F32, kind="ExternalInput").ap()

    def dout(name, shape):
        return nc.dram_tensor(name, list(shape), F32, kind="ExternalOutput").ap()

    xsT = din("xsT", (D, 2048))
    xpT = din("xpT", (D, 1024))
    ckvT_c = din("ckvT_c", (DEPTH, 128, 512))
    kpeT_c = din("kpeT_c", (DEPTH, 32, 512))
    skT_c = din("skT_c", (DEPTH, 128, 512))
    sv_c = din("sv_c", (DEPTH, 512, 128))
    cfm = din("cfm", (128, 16))
    w_ada = din("w_ada", (DEPTH, D, 6 * D))
    bada_fm = din("bada_fm", (128, DEPTH * 48))
    vecs_fm = din("vecs_fm", (128, 128))
    sgunorm_bc = din("sgunorm_bc", (128, DEPTH * 256))
    kvnorm_bc = din("kvnorm_bc", (128, DEPTH * 128))
    sink_bc = din("sink_bc", (128, 16))
    sgub = din("sgub", (1, DEPTH * 512))
    sgu_wT = din("sgu_wT", (DEPTH, 128, 512))
    w_fm = din("w_fm", (DEPTH, D, 1280))
    w_tm = din("w_tm", (DEPTH, D, 928))
    w_out = din("w_out", (DEPTH, D, D))
    w1 = din("w1", (DEPTH, D, 4096))
    w2 = din("w2", (DEPTH, 4096, D))
    wq = din("wq", (DEPTH, 256, 384))
    wkv = din("wkv", (DEPTH, 128, 512))
    ropes = din("ropes", (128, 16 * 192))
    consts = din("consts", (128, 640))
    ysT = dout("ysT", (D, 2048))
    ypT = dout("ypT", (D, 1024))
    o_ckv = dout("o_ckv", (4, DEPTH, 256, 128))
    o_kpe = dout("o_kpe", (4, DEPTH, 256, 32))
    o_k = dout("o_k", (4, DEPTH, 256, 128))
    o_v = dout("o_v", (4, DEPTH, 256, 128))
    if debug:
        dbg_h = nc.dram_tensor("dbg_h", [128, 8, 512], BF16, kind="ExternalOutput").ap()
        dbg_cat = nc.dram_tensor("dbg_cat", [128, 8, 512], BF16, kind="ExternalOutput").ap()
        dbg_x1 = dout("dbg_x1", (128, 8, 2048))
        dbg_x2 = dout("dbg_x2", (128, 8, 2048))

    A = nc.alloc_sbuf_tensor
    ident_f = A("ident_f", [128, 128], F32)
    cst = A("cst", [128, 512], BF16)
    ident_b, ones_b, mprev, mnext = cst[:, 0:128], cst[:, 128:256], cst[:, 256:384], cst[:, 384:512]
    modt = A("modt", [128, DEPTH * 48 * 2], F32)
    badat = A("badat", [128, DEPTH * 48], F32)
    vecs = A("vecs", [128, 128], F32)
    gs = A("gs", [128, DEPTH * 2 * 8 * 2], F32)
    csil = A("csil", [128, 16], F32)
    csil_b = A("csil_b", [128, 16], BF16)
    sinkexp = A("sinkexp", [128, 16], F32)
    ropet = A("ropet", [128, 16 * 192], BF16)
    sgn = A("sgn", [128, 256], F32)
    kvn = A("kvn", [128, 128], F32)
    wq_t = A("wq_t", [128, 2, 384], BF16)
    wkv_t = A("wkv_t", [128, 512], BF16)
    wsT_t = A("wsT_t", [128, 512], BF16)
    sgub_t = A("sgub_t", [1, 512], BF16)
    x = A("x", [128, 8, 2048], F32)
    R = A("R", [128, 36352], BF16)
    NST = 3
    wst = [A(f"wst{i}", [128, 4096], BF16) for i in range(NST)]
    sqt = [A(f"sqt{i}", [128, 512], BF16) for i in range(2)]
    tt = [A(f"tt{i}", [128, 512], F32) for i in range(2)]
    s_t = A("s_t", [128, 512], F32)
    rstd = s_t
    tmf = [A(f"tmf{i}", [128, 512], F32) for i in range(2)]
    tm2 = [A(f"tm2{i}", [128, 256], F32) for i in range(4)]
    tm3 = [A(f"tm3{i}", [128, 256], F32) for i in range(2)]
    sm = [A(f"sm{i}", [128, 4], F32) for i in range(4)]
    vn_t = [A(f"vn{i}", [128, 256], BF16) for i in range(2)]
    pt = [A(f"pt{i}", [128, 512], BF16) for i in range(3)]
    rden = [A(f"rden{i}", [64, 512], F32) for i in range(1)]
    qh = A("qh", [128, 4, 512], BF16)
    relu_t = [A(f"relu{i}", [128, 512], BF16) for i in range(2)]
    cqraw = relu_t
    yo = tt
    PS = [nc.alloc_psum_tensor(f"ps{i}", [128, 512], F32) for i in range(8)]

    rr = {}

    def ring(name, lst):
        i = rr.get(name, 0)
        rr[name] = i + 1
        j = i % len(lst)
        return lst[j], f"{name}{j}"

    psring = {"lst": list(range(8)), "i": 0}

    def psum():
        lst = psring["lst"]
        b = lst[psring["i"] % len(lst)]
        psring["i"] += 1
        assert not P.ps_open.get(f"ps{b}", False), f"psum bank ps{b} re-allocated before its previous contents were read"
        P.ps_open[f"ps{b}"] = True
        return PS[b], f"ps{b}"

    def set_psring(lst):
        psring["lst"] = list(lst)
        psring["i"] = 0

    def mm(out, lhsT, rhs, start, stop, reads, writes, sig=None):
        if sig is None:
            sig = True
        return P.op("pe", lambda: nc.tensor.matmul(out, lhsT=lhsT, rhs=rhs, start=start, stop=stop),
                    reads=reads, writes=writes, sig=sig)

    def act(out, in_, func, reads, writes, bias=0.0, scale=1.0, accum_out=None):
        kw = {}
        if accum_out is not None:
            kw["accum_out"] = accum_out
        return P.op("act", lambda: nc.scalar.activation(out=out, in_=in_, func=func, bias=bias, scale=scale, **kw),
                    reads=reads, writes=writes)

    def tt_op(out, in0, in1, op, reads, writes):
        return P.op("dve", lambda: nc.vector.tensor_tensor(out=out, in0=in0, in1=in1, op=op), reads=reads, writes=writes)

    def ts_op(out, in0, s1, op0, reads, writes, s2=None, op1=None):
        if op1 is None:
            return P.op("dve", lambda: nc.vector.tensor_scalar(out=out, in0=in0, scalar1=s1, scalar2=None, op0=op0),
                        reads=reads, writes=writes)
        return P.op("dve", lambda: nc.vector.tensor_scalar(out=out, in0=in0, scalar1=s1, scalar2=s2, op0=op0, op1=op1),
                    reads=reads, writes=writes)

    def stt(out, in0, scalar, in1, op0, op1, reads, writes):
        return P.op("dve", lambda: nc.vector.scalar_tensor_tensor(out=out, in0=in0, scalar=scalar, in1=in1, op0=op0, op1=op1),
                    reads=reads, writes=writes)

    def vcopy(out, in_, reads, writes):
        return P.op("dve", lambda: nc.vector.tensor_copy(out=out, in_=in_), reads=reads, writes=writes)

    def recip(out, in_, reads, writes):
        return P.op("dve", lambda: nc.vector.reciprocal(out=out, in_=in_), reads=reads, writes=writes)

    def dma(q, out, in_, reads, writes, is_out=False):
        e = nc.sync if q == "sp" else nc.gpsimd
        return P.op(q, lambda: e.dma_start(out=out, in_=in_), reads=reads, writes=writes, dma=True, is_out=is_out)

    wplan = []
    wstate = {"issued": 0, "used": 0}

    ada_defer = (len(groups) > 0 and groups[0] == "A")
    ada_phase0 = [0] if ada_defer else list(range(DEPTH))
    ADA_SPLIT = [2, 2, 2, 2, 1, 1, 1, 1]

    def plan_weights():
        for l in ada_phase0:
            for pc in range(12):
                wplan.append((f"ada{l}_{pc}", w_ada[l, :, pc * 512:(pc + 1) * 512], (8, 512)))
        for g in groups:
            NT = 4 if g == "A" else 2
            for l in range(depth):
                for j in range(NT):
                    wplan.append((f"fmK{g}{l}{j}", w_fm[l, :, 0:512], (8, 512)))
                    wplan.append((f"tmK{g}{l}{j}", w_tm[l, :, 0:416], (8, 416)))
                for j in range(NT):
                    wplan.append((f"fmQa{g}{l}{j}", w_fm[l, :, 512:896], (8, 384)))
                    wplan.append((f"fmQb{g}{l}{j}", w_fm[l, :, 896:1280], (8, 384)))
                    wplan.append((f"tmQ{g}{l}{j}", w_tm[l, :, 416:928], (8, 512)))
                    wplan.append((f"wo0{g}{l}{j}", w_out[l, :, 0:512], (8, 512)))
                    wplan.append((f"wo1{g}{l}{j}", w_out[l, :, 512:1024], (8, 512)))
                pcn = 0
                for jh in range(8):
                    wplan.append((f"w1{g}{l}{jh}", w1[l, :, jh * 512:(jh + 1) * 512], (8, 512)))
                    wplan.append((f"w2{g}{l}{jh}", w2[l, jh * 512:(jh + 1) * 512, :], (4, 1024)))
                    if ada_defer and g == "A" and l + 1 < DEPTH:
                        for _ in range(ADA_SPLIT[jh]):
                            wplan.append((f"ada{l + 1}_{pcn}", w_ada[l + 1, :, pcn * 512:(pcn + 1) * 512], (8, 512)))
                            pcn += 1

    def issue_weight():
        i = wstate["issued"]
        if i >= len(wplan):
            return
        name, ap, (nc_, ncol) = wplan[i]
        slot = wst[i % NST]
        dst = slot[:, 0:nc_ * ncol].rearrange("p (c n) -> p c n", c=nc_)
        src = ap.rearrange("(c p) n -> p c n", p=128)
        dma("pool", dst, src, reads=[], writes=[f"wst{i % NST}"])
        wstate["issued"] += 1

    def get_weight(name, ahead=NST - 1):
        i = wstate["used"]
        assert wplan[i][0] == name, (wplan[i][0], name)
        while wstate["issued"] < min(i + ahead + 1, len(wplan)):
            issue_weight()
        wstate["used"] += 1
        nc_, ncol = wplan[i][2]
        return wst[i % NST][:, 0:nc_ * ncol].rearrange("p (c n) -> p c n", c=nc_), f"wst{i % NST}"

    plan_weights()
    MARKS.clear()

    def ckpt(n):
        if p0 == n:
            P.barrier()
            P.finish()
            raise StopBuild()

    with nc.allow_low_precision("bf16 matmuls"), nc.allow_non_contiguous_dma("small strided loads"):
        dma("sp", ident_f[:], consts[:, 0:128], [], ["ident_f"])
        dma("pool", cst[:], consts[:, 0:512], [], ["cst"])
        dma("sp", badat[:], bada_fm[:, :], [], ["badat"])
        dma("sp", vecs[:], vecs_fm[:, :], [], ["vecs"])
        dma("sp", csil[:], cfm[:, :], [], ["csil"])
        dma("sp", sinkexp[:], sink_bc[:, :], [], ["sinkexp"])
        dma("pool", ropet[:], ropes[:, :], [], ["ropet"])
        act(csil[:], csil[:], AF.Silu, ["csil"], ["csil"])
        act(sinkexp[:], sinkexp[:], AF.Exp, ["sinkexp"], ["sinkexp"])
        if p0 == 1:
            P.barrier()
            P.finish()
            return nc
        modv = modt[:].rearrange("p (l k c v) -> p l k c v", l=DEPTH, k=6, c=8)
        gsv = gs[:].rearrange("p (l n c v) -> p l n c v", l=DEPTH, n=2, c=8)
        modt3 = modt[:].rearrange("p (m v) -> p m v", v=2)
        vcopy(csil_b[:], csil[:], ["csil"], ["csil_b"])
        P.op("dve", lambda: nc.vector.memset(qh[96:128, :, :], 0.0), [], [f"qh{h}" for h in range(4)])

        def ada_piece(l, pc):
            wt, wk_ = get_weight(f"ada{l}_{pc}")
            ps, pk = psum()
            for c in range(8):
                mm(ps[0:2, :], csil_b[:, c * 2:c * 2 + 2], wt[:, c, :], c == 0, c == 7, [wk_, "csil_b"], [pk])
            t_, tk_ = ring("tt", tt)
            t_ = t_[0:2, :]
            act(t_, ps[0:2, :], AF.Copy, [pk], [tk_])
            ps2, pk2 = psum()
            for mi in range(4):
                P.op("pe", lambda: nc.tensor.transpose(ps2[:, mi * 2:mi * 2 + 2], t_[:, mi * 128:(mi + 1) * 128], ident_f[0:2, 0:2]),
                     [tk_, "ident_f"], [pk2])
            m0 = l * 48 + pc * 4
            tt_op(modt3[:, m0:m0 + 4, :], ps2[:, 0:8].rearrange("p (m v) -> p m v", v=2),
                  badat[:, m0:m0 + 4].unsqueeze(2).to_broadcast([128, 4, 2]), ALU.add, [pk2, "badat"], [f"modt{l}"])

        def ada_finish(l):
            for n in range(2):
                nv = vecs[:, n * 32 + l * 8:n * 32 + l * 8 + 8]
                ts_op(gsv[:, l, n], modv[:, l, 3 * n + 1], 1.0, ALU.add, [f"modt{l}"], [f"gs{l}"])
                tt_op(gsv[:, l, n], gsv[:, l, n], nv.unsqueeze(2).to_broadcast([128, 8, 2]), ALU.mult, [f"gs{l}", "vecs"], [f"gs{l}"])

        for l in ada_phase0:
            for pc in range(12):
                ada_piece(l, pc)
            ada_finish(l)
        P.barrier()

        def run_group(g):
            lat = (g == "A")
            T = 2048 if lat else 1024
            S = 2048 if lat else 256
            NT = T // 512
            NK = 2560 if lat else 1024
            NB = NK // 128
            v = 1 if lat else 0
            xT = xsT if lat else xpT
            yT = ysT if lat else ypT
            o = 0

            def carve(n, c=None):
                nonlocal o
                ap = R[:, o:o + n]
                o += n
                if c is not None:
                    ap = ap.rearrange("p (c t) -> p c t", c=c)
                return ap
            KT = carve(4 * NK, 4)
            Vm = carve(NB * 256, NB)
            ks = carve(NK)
            Vs = carve(NB * 128, NB)
            pbuf = carve(2 * T, 2)
            hjb = [carve(8 * 512, 8)]
            if not lat:
                hjb.append(carve(8 * 512, 8))
            H = {"ap": hjb[0], "k": "hj0_"}

            def use_buf(bi):
                H["ap"] = hjb[bi]
                H["k"] = f"hj{bi}_"

            def norm_to(l, j, bi):
                norm(l, 0, j, lambda c: hjb[bi][:, c, :], lambda c: f"hj{bi}_{c}")
            catj = carve(8 * 512, 8)
            cqn = carve(2 * 512, 2)
            qs = carve(2 * 512, 2)
            acc = carve(2 * 512, 2)
            ckT = carve(512)
            assert o <= 36352, o
            h2 = R[:, 0:8 * T].rearrange("p (c t) -> p c t", c=8)
            ub = [R[:, 8 * T + i * 2048:8 * T + (i + 1) * 2048].rearrange("p (c t) -> p c t", c=4) for i in range(2)]

            for c in range(8):
                dma("sp", x[:, c, 0:T], xT[c * 128:(c + 1) * 128, :], [], [f"x{c}_{j}" for j in range(NT)])

            def norm(l, n, j, dst, dkeys):
                ps, pk = psum()
                for c in range(8):
                    sq, sqk = ring("sqt", sqt)
                    act(sq[:], x[:, c, j * 512:(j + 1) * 512], AF.Square, [f"x{c}_{j}"], [sqk])
                    mm(ps[:, :], ones_b, sq[:], c == 0, c == 7, [sqk, "cst"], [pk])
                act(s_t[:], ps[:, :], AF.Sqrt, [pk], ["s_t", "rstd"], bias=EPS, scale=1.0 / D)
                recip(rstd[:], s_t[:], ["s_t"], ["s_t", "rstd"])
                for c in range(8):
                    t, tk = ring("tt", tt)
                    tt_op(t[:], x[:, c, j * 512:(j + 1) * 512], rstd[:], ALU.mult, [f"x{c}_{j}", "rstd"], [tk])
                    act(dst(c), t[:], AF.Identity, [tk, f"gs{l}", f"modt{l}"], [dkeys(c)],
                        bias=modv[:, l, 3 * n, c, v:v + 1], scale=gsv[:, l, n, c, v:v + 1])

            def attention(NQ, qT, qkeys, chunks, scale, sink_ap, out_ap, out_keys, G=1):
                W = G * NQ
                CH = 512 // W
                ai = rr.get("acc", 0) % 2
                rr["acc"] = rr.get("acc", 0) + 1
                nump, numk = PS[3 + ai * 2], f"ps{3 + ai * 2}"
                denp, denk = PS[4 + ai * 2], f"ps{4 + ai * 2}"
                ones64 = ones_b[:, 0:64]
                n = len(chunks)
                groups_ = [chunks[g0:g0 + CH] for g0 in range(0, n, CH)]

                def view(ap2d):
                    return ap2d if G == 1 else ap2d.rearrange("p (g t) -> p g t", g=G)

                def emit_S(grp):
                    sb, sbk = psum()
                    for i, (kT, vv, mask, keys) in enumerate(grp):
                        mm(view(sb[:, i * W:(i + 1) * W]), kT, qT, True, mask is None, keys + qkeys, [sbk])
                        if mask is not None:
                            mrhs = mask if G == 1 else mask.unsqueeze(1).to_broadcast([128, G, NQ])
                            mm(view(sb[:, i * W:(i + 1) * W]), ident_b, mrhs, False, True, ["cst"], [sbk])
                    return sb, sbk

                nxt = emit_S(groups_[0])
                idx = 0
                for gi, grp in enumerate(groups_):
                    sb, sbk = nxt
                    if gi + 1 < len(groups_):
                        nxt = emit_S(groups_[gi + 1])
                    p_, pk_ = ring("pt", pt)
                    w = len(grp) * W
                    act(p_[:, 0:w], sb[:, 0:w], AF.Exp, [sbk], [pk_], scale=scale)
                    for i, (kT, vv, mask, keys) in enumerate(grp):
                        first = (idx == 0)
                        last = (idx == n - 1)
                        idx += 1
                        mm(nump[0:64, 0:W], vv, p_[:, i * W:(i + 1) * W], first, last, keys + [pk_], [numk])
                        mm(denp[0:64, 0:W], ones64, p_[:, i * W:(i + 1) * W], first, last, [pk_, "cst"], [denk])
                rd, rdk = ring("rden", rden)
                sinks = sink_ap if isinstance(sink_ap, list) else [sink_ap] * G
                outs = out_ap if isinstance(out_ap, list) else [out_ap]
                if sinks[0] is not None:
                    for gg in range(G):
                        ts_op(rd[:, gg * NQ:(gg + 1) * NQ], denp[0:64, gg * NQ:(gg + 1) * NQ], sinks[gg], ALU.add,
                              [denk, "sinkexp"], [rdk])
                    recip(rd[:, 0:W], rd[:, 0:W], [rdk], [rdk])
                else:
                    recip(rd[:, 0:W], denp[0:64, 0:W], [denk], [rdk])
                for gg in range(G):
                    tt_op(outs[gg], nump[0:64, gg * NQ:(gg + 1) * NQ], rd[:, gg * NQ:(gg + 1) * NQ], ALU.mult, [numk, rdk], out_keys)

            def rope(dst, src, H, Dh, gb, off, rk, wk):
                Q = Dh // 4
                cos = ropet[:, gb * 192 + off:gb * 192 + off + Dh]
                sin = ropet[:, gb * 192 + off + Dh:gb * 192 + off + 2 * Dh]
                t2, t2k = ring("tm3", tm3)
                s4 = src.rearrange("p (h a b q) -> p h a b q", h=H, a=2, b=2)
                d4 = t2[:, 0:H * Dh].rearrange("p (h a b q) -> p h a b q", h=H, a=2, b=2)
                sn = sin.rearrange("p (a b q) -> p a b q", a=2, b=2)
                for bb in range(2):
                    tt_op(d4[:, :, :, bb, :], s4[:, :, :, 1 - bb, :],
                          sn[:, :, bb, :].unsqueeze(1).to_broadcast([128, H, 2, Q]), ALU.mult,
                          rk + ["ropet"] + ([t2k] if bb else []), [t2k])
                s3 = src.rearrange("p (h d) -> p h d", h=H)
                d3 = dst.rearrange("p (h d) -> p h d", h=H)
                tt_op(d3, s3, cos.unsqueeze(1).to_broadcast([128, H, Dh]), ALU.mult, rk + ["ropet"], wk)
                tt_op(dst, dst, t2[:, 0:H * Dh], ALU.add, wk + [t2k], wk)

            def small_weights(l_):
                dma("pool", wq_t[:], wq[l_].rearrange("(c p) n -> p c n", p=128), [], ["wq_t"])
                dma("pool", wkv_t[:], wkv[l_], [], ["wkv_t"])
                dma("pool", wsT_t[:], sgu_wT[l_], [], ["wsT_t"])
                dma("pool", sgub_t[:], sgub[:, l_ * 512:(l_ + 1) * 512], [], ["sgub_t"])
                dma("sp", sgn[:], sgunorm_bc[:, l_ * 256:(l_ + 1) * 256], [], ["sgn"])
                dma("sp", kvn[:], kvnorm_bc[:, l_ * 128:(l_ + 1) * 128], [], ["kvn"])

            for l in range(depth):
                set_psring(range(8))
                if l == 0:
                    small_weights(0)
                P.op("dve", lambda: nc.vector.memset(KT[96:128, :, :], 0.0), [],
                     [f"KT{h}_{jj}" for h in range(4) for jj in range(NT + 1)])
                if lat:
                    dma("pool", ckT[:], ckvT_c[l], [], ["ckT"])
                    for h in range(4):
                        dma("pool", KT[64:96, h, T:T + 512], kpeT_c[l], [], [f"KT{h}_{NT}"])
                    dma("pool", ks[:, T:T + 512], skT_c[l], [], [f"ks_{NT}"])
                    dma("pool", Vs[:, 16:20, :], sv_c[l].rearrange("(b p) d -> p b d", p=128), [], [f"Vs_{NT}"])

                def kv_up(j, nblk):
                    for h in range(4):
                        ps, pk = psum()
                        mm(ps[0:64, 0:nblk * 128], wkv_t[:, h * 128:h * 128 + 64], ckT[:, 0:nblk * 128], True, True,
                           ["wkv_t", "ckT"], [pk])
                        act(KT[0:64, h, j * 512:j * 512 + nblk * 128], ps[0:64, 0:nblk * 128], AF.Copy, [pk], [f"KT{h}_{j}"])
                    for b in range(nblk):
                        ps, pk = psum()
                        mm(ps[:, :], ckT[:, b * 128:(b + 1) * 128], wkv_t[:], True, True, ["wkv_t", "ckT"], [pk])
                        vcopy(Vm[:, j * 4 + b, :].rearrange("p (h d) -> p h d", h=4),
                              ps[:, :].rearrange("p (h t d) -> p h t d", h=4, t=2)[:, :, 1, :], [pk], [f"Vm_{j}"])

                if lat:
                    kv_up(NT, 4)
                ckpt(10)

                for j in range(NT):
                    P.mark(f"{g}{l} K{j} norm")
                    if lat:
                        use_buf(0)
                        norm_to(l, j, 0)
                    else:
                        if j == 0:
                            norm_to(l, 0, 0)
                        nxt_j = j + 1 if j + 1 < NT else 0
                        norm_to(l, nxt_j, (j + 1) % 2)
                        use_buf(j % 2)
                    P.mark(f"{g}{l} K{j} fm")
                    wtF, wkF = get_weight(f"fmK{g}{l}{j}")
                    wtT, wkT = get_weight(f"tmK{g}{l}{j}", ahead=NST - 2)

                    def fmK_group(m):
                        ps, pk = psum()
                        for c in range(8):
                            mm(ps[:, :], wtF[:, c, m * 128:(m + 1) * 128], H["ap"][:, c, :], c == 0, c == 7, [wkF, f"{H['k']}{c}"], [pk])
                        if m < 2:
                            act(pbuf[:, m, j * 512:(j + 1) * 512], ps[:, :], AF.Copy, [pk], [f"p{m}_{j}"])
                        else:
                            tt_op(pbuf[:, m - 2, j * 512:(j + 1) * 512], ps[:, :], pbuf[:, m - 2, j * 512:(j + 1) * 512],
                                  ALU.mult, [pk, f"p{m - 2}_{j}"], [f"p{m - 2}_{j}"])

                    def tmK_mm(b):
                        tok = slice(b * 128, (b + 1) * 128)
                        ps, pk = psum()
                        for c in range(8):
                            mm(ps[:, 0:416], H["ap"][:, c, tok], wtT[:, c, :], c == 0, c == 7, [wkT, f"{H['k']}{c}"], [pk])
                        return ps, pk

                    def tmK_ew(b, ps, pk):
                        gb = j * 4 + b
                        st, stk = ring("tmf", tmf)
                        act(st[:, 0:416], ps[:, 0:416], AF.Copy, [pk], [stk])
                        smt, smk = ring("sm", sm)
                        t2, t2k = ring("tm3", tm3)
                        P.op("dve", lambda: nc.vector.memset(smt[:, 0:1], 0.0), [], [smk])
                        act(t2[:, 0:128], st[:, 0:128], AF.Square, [stk, smk], [t2k, smk], accum_out=smt[:, 0:1])
                        act(smt[:, 1:2], smt[:, 0:1], AF.Sqrt, [smk], [smk], bias=EPS, scale=1.0 / 128)
                        recip(smt[:, 2:3], smt[:, 1:2], [smk], [smk])
                        stt(st[:, 0:128], st[:, 0:128], smt[:, 2:3], kvn[:], ALU.mult, ALU.mult, [stk, smk, "kvn"], [stk])
                        if lat:
                            t3, t3k = ring("tm2", tm2)
                            rope(t3[:, 0:32], st[:, 128:160], 1, 32, gb, 128, [stk], [t3k])
                            kpe_src, kpek = t3[:, 0:32], t3k
                            t4, t4k = ring("tm2", tm2)
                            rope(t4[:, 0:128], st[:, 160:288], 2, 64, gb, 0, [stk], [t4k])
                            sk_src, skk = t4[:, 0:128], t4k
                        else:
                            kpe_src, kpek = st[:, 128:160], stk
                            sk_src, skk = st[:, 160:288], stk
                            sq_, r0 = divmod(gb * 128, 256)
                            dma("sp", o_ckv[sq_, l, r0:r0 + 128, :], st[:, 0:128], [stk], [], is_out=True)
                            dma("sp", o_kpe[sq_, l, r0:r0 + 128, :], st[:, 128:160], [stk], [], is_out=True)
                            dma("sp", o_k[sq_, l, r0:r0 + 128, :], st[:, 160:288], [stk], [], is_out=True)
                            dma("sp", o_v[sq_, l, r0:r0 + 128, :], st[:, 288:416], [stk], [], is_out=True)
                        vcopy(Vs[:, gb, :], st[:, 288:416], [stk], [f"Vs_{j}"])
                        return st, stk, kpe_src, kpek, sk_src, skk

                    def tmK_tr(b, st, stk, kpe_src, kpek, sk_src, skk):
                        gb = j * 4 + b
                        tok = slice(b * 128, (b + 1) * 128)
                        ps2, pk2 = psum()
                        P.op("pe", lambda: nc.tensor.transpose(ps2[:, 0:128], st[:, 0:128], ident_f[:]), [stk, "ident_f"], [pk2])
                        P.op("pe", lambda: nc.tensor.transpose(ps2[:, 128:256], sk_src, ident_f[:]), [skk, "ident_f"], [pk2])
                        P.op("pe", lambda: nc.tensor.transpose(ps2[0:32, 256:384], kpe_src, ident_f[:]), [kpek, "ident_f"], [pk2])
                        act(ckT[:, tok], ps2[:, 0:128], AF.Copy, [pk2], ["ckT"])
                        vcopy(ks[:, gb * 128:(gb + 1) * 128], ps2[:, 128:256], [pk2], [f"ks_{j}"])
                        act(KT[64:96, 0:2, gb * 128:(gb + 1) * 128], ps2[0:32, 256:384].unsqueeze(1).to_broadcast([32, 2, 128]),
                            AF.Copy, [pk2], [f"KT0_{j}", f"KT1_{j}"])
                        vcopy(KT[64:96, 2:4, gb * 128:(gb + 1) * 128], ps2[0:32, 256:384].unsqueeze(1).to_broadcast([32, 2, 128]),
                              [pk2], [f"KT2_{j}", f"KT3_{j}"])

                    P.mark(f"{g}{l} K{j} tm")
                    r = {}
                    e = {}
                    r[0] = tmK_mm(0)
                    r[1] = tmK_mm(1)
                    for b in range(4):
                        e[b] = tmK_ew(b, *r[b])
                        fmK_group(b)
                        tmK_tr(b, *e[b])
                        if b + 2 < 4:
                            r[b + 2] = tmK_mm(b + 2)
                    P.mark(f"{g}{l} K{j} kvup")
                    kv_up(j, 4)
                    ckpt(14)

                nseq_t = 512 // min(S, 512)
                for j in range(NT):
                    set_psring(range(8))
                    if lat:
                        use_buf(0)
                        if j == 0:
                            P.mark(f"{g}{l} Q{j} norm")
                            norm_to(l, 0, 0)
                    else:
                        if j + 1 < NT:
                            norm_to(l, j + 1, (NT + j + 1) % 2)
                        use_buf((NT + j) % 2)
                    P.mark(f"{g}{l} Q{j} fm")
                    if debug and l == 0 and j == 0:
                        dma("sp", dbg_h, H["ap"], [f"{H['k']}{c}" for c in range(8)], [], is_out=True)
                    wa, wak = get_weight(f"fmQa{g}{l}{j}")

                    def fmQ_group(m, wt, wk_):
                        mi = m % 3
                        ps, pk = psum()
                        for c in range(8):
                            mm(ps[:, :], wt[:, c, mi * 128:(mi + 1) * 128], H["ap"][:, c, :], c == 0, c == 7, [wk_, f"{H['k']}{c}"], [pk])
                        if m < 2:
                            act(catj[:, m, :], ps[:, :], AF.Copy, [pk], [f"cat{m}"])
                        elif m < 4:
                            act(catj[:, m, :], ps[:, :], AF.Gelu_apprx_tanh, [pk], [f"cat{m}"])
                        else:
                            cr, crk = cqraw[m - 4], f"cqraw{m - 4}"
                            act(cr[:], ps[:, :], AF.Copy, [pk], [crk])

                    for m in range(3):
                        fmQ_group(m, wa, wak)
                    wb, wbk = get_weight(f"fmQb{g}{l}{j}")
                    wtT, wkT = get_weight(f"tmQ{g}{l}{j}", ahead=NST - 2)

                    def cq_conv():
                        ps, pk = psum()
                        for c in range(2):
                            sq, sqk = ring("sqt", sqt)
                            act(sq[:], cqraw[c][:], AF.Square, [f"cqraw{c}"], [sqk])
                            mm(ps[:, :], ones_b, sq[:], c == 0, c == 1, [sqk, "cst"], [pk])
                        act(s_t[:], ps[:, :], AF.Sqrt, [pk], ["s_t", "rstd"], bias=EPS, scale=1.0 / 256)
                        recip(rstd[:], s_t[:], ["s_t"], ["s_t", "rstd"])
                        for c in range(2):
                            stt(cqn[:, c, :], cqraw[c][:], vecs[:, 72 + l * 2 + c:72 + l * 2 + c + 1], rstd[:], ALU.mult, ALU.mult,
                                [f"cqraw{c}", "vecs", "rstd"], [f"cqn{c}"])
                        Sq = min(S, 512)
                        for c in range(2):
                            cw = lambda k: vecs[:, 80 + (l * 2 + c) * 3 + k:80 + (l * 2 + c) * 3 + k + 1]
                            lo = j * 512
                            pk_all = [f"p{c}_{jj}" for jj in range(NT)]
                            ts_op(acc[:, c, :], pbuf[:, c, lo:lo + 512], cw(1), ALU.mult, pk_all + ["vecs"], [f"acc{c}"])
                            for sidx in range(nseq_t):
                                a0 = sidx * Sq
                                g0 = lo + a0
                                first_in_seq = (g0 % S == 0)
                                last_in_seq = ((g0 + Sq) % S == 0)
                                s0 = 1 if first_in_seq else 0
                                stt(acc[:, c, a0 + s0:a0 + Sq], pbuf[:, c, g0 + s0 - 1:g0 + Sq - 1], cw(0), acc[:, c, a0 + s0:a0 + Sq],
                                    ALU.mult, ALU.add, pk_all + ["vecs", f"acc{c}"], [f"acc{c}"])
                                e0 = 1 if last_in_seq else 0
                                stt(acc[:, c, a0:a0 + Sq - e0], pbuf[:, c, g0 + 1:g0 + Sq - e0 + 1], cw(2), acc[:, c, a0:a0 + Sq - e0],
                                    ALU.mult, ALU.add, pk_all + ["vecs", f"acc{c}"], [f"acc{c}"])
                            tt_op(catj[:, c, :], catj[:, c, :], acc[:, c, :], ALU.mult, [f"cat{c}", f"acc{c}"], [f"cat{c}"])

                    def tmQ_mm(b):
                        tok = slice(b * 128, (b + 1) * 128)
                        ps, pk = psum()
                        for c in range(8):
                            mm(ps[:, :], H["ap"][:, c, tok], wtT[:, c, :], c == 0, c == 7, [wkT, f"{H['k']}{c}"], [pk])
                        return ps, pk

                    def tmQ_ew(b, ps, pk):
                        gb = j * 4 + b
                        st, stk = ring("tmf", tmf)
                        act(st[:, 0:256], ps[:, 0:256], AF.Gelu_apprx_tanh, [pk], [stk])
                        act(st[:, 256:512], ps[:, 256:512], AF.Copy, [pk], [stk])
                        smt, smk = ring("sm", sm)
                        t2, t2k = ring("tm3", tm3)
                        P.op("dve", lambda: nc.vector.memset(smt[:, 0:1], 0.0), [], [smk])
                        act(t2[:, 0:256], st[:, 0:256], AF.Square, [stk, smk], [t2k, smk], accum_out=smt[:, 0:1])
                        act(smt[:, 1:2], smt[:, 0:1], AF.Sqrt, [smk], [smk], bias=EPS, scale=1.0 / 256)
                        recip(smt[:, 2:3], smt[:, 1:2], [smk], [smk])
                        vn, vnk = ring("vn", vn_t)
                        stt(vn[:], st[:, 0:256], smt[:, 2:3], sgn[:], ALU.mult, ALU.mult, [stk, smk, "sgn"], [vnk])
                        if lat:
                            t4, t4k = ring("tm2", tm2)
                            rope(t4[:, 0:256], st[:, 256:512], 4, 64, gb, 0, [stk], [t4k])
                            return st, stk, vn, vnk, t4, t4k
                        return st, stk, vn, vnk, None, None

                    def tmQ_tail(b, st, stk, vn, vnk, t4, t4k):
                        tok = slice(b * 128, (b + 1) * 128)
                        ps2, pk2 = psum()
                        for hd in range(4):
                            cc, e_ = divmod(hd, 2)
                            mm(ps2[:, hd * 128:(hd + 1) * 128], vn[:, cc * 128:(cc + 1) * 128], wsT_t[:, hd * 128:(hd + 1) * 128],
                               True, False, [vnk, "wsT_t"], [pk2], sig=False)
                            mm(ps2[:, hd * 128:(hd + 1) * 128], ones_b[0:1, 0:128], sgub_t[0:1, hd * 128:(hd + 1) * 128],
                               False, True, ["cst", "sgub_t"], [pk2], sig=True)
                        ps3, pk3 = psum()
                        for gg in range(2):
                            src_ap = (t4[:, gg * 128:(gg + 1) * 128] if lat else st[:, 256 + gg * 128:256 + (gg + 1) * 128])
                            P.op("pe", lambda: nc.tensor.transpose(ps3[:, gg * 128:(gg + 1) * 128], src_ap, ident_f[:]),
                                 [t4k if lat else stk, "ident_f"], [pk3])
                        for hd in range(4):
                            cc, e_ = divmod(hd, 2)
                            tt_op(catj[e_ * 64:(e_ + 1) * 64, 2 + cc, tok], catj[e_ * 64:(e_ + 1) * 64, 2 + cc, tok],
                                  ps2[e_ * 64:(e_ + 1) * 64, hd * 128:(hd + 1) * 128], ALU.mult, [f"cat{2 + cc}", pk2], [f"cat{2 + cc}"])
                        act(qs[:, :, tok], ps3[:, 0:256].rearrange("p (g t) -> p g t", g=2), AF.Copy, [pk3], ["qs"])

                    def qm_mm(b):
                        tokq = slice(b * 128, (b + 1) * 128)
                        ps, pk = psum()
                        for c in range(2):
                            mm(ps[:, 0:384], cqn[:, c, tokq], wq_t[:, c, :], c == 0, c == 1, [f"cqn{c}", "wq_t"], [pk])
                        return ps, pk

                    def qm_ew(b, ps, pk):
                        gb = j * 4 + b
                        st, stk = ring("tmf", tmf)
                        act(st[:, 0:384], ps[:, 0:384], AF.Copy, [pk], [stk])
                        if lat:
                            t3, t3k = ring("tm2", tm2)
                            src4 = st[:, 0:384].rearrange("p (h d) -> p h d", h=4)[:, :, 64:96]
                            t5, t5k = ring("tm2", tm2)
                            vcopy(t5[:, 0:128].rearrange("p (h d) -> p h d", h=4), src4, [stk], [t5k])
                            rope(t3[:, 0:128], t5[:, 0:128], 4, 32, gb, 128, [t5k], [t3k])
                            vcopy(src4, t3[:, 0:128].rearrange("p (h d) -> p h d", h=4), [t3k], [stk])
                        return st, stk

                    def qm_tr(b, st, stk):
                        tokq = slice(b * 128, (b + 1) * 128)
                        ps, pk = psum()
                        for h in range(4):
                            P.op("pe", lambda: nc.tensor.transpose(ps[0:96, h * 128:(h + 1) * 128], st[:, h * 96:(h + 1) * 96], ident_f[:]),
                                 [stk, "ident_f"], [pk])
                        src = ps[0:96, :].rearrange("p (h t) -> p h t", h=4)
                        if b % 2 == 0:
                            act(qh[0:96, :, tokq], src, AF.Copy, [pk], [f"qh{h}" for h in range(4)])
                        else:
                            vcopy(qh[0:96, :, tokq], src, [pk], [f"qh{h}" for h in range(4)])

                    P.mark(f"{g}{l} Q{j} tm")
                    r = {}
                    e = {}
                    rq = {}
                    for m in range(3, 6):
                        fmQ_group(m, wb, wbk)
                    cq_conv()
                    r[0] = tmQ_mm(0)
                    r[1] = tmQ_mm(1)
                    rq[0] = qm_mm(0)
                    for b in range(4):
                        e[b] = tmQ_ew(b, *r[b])
                        eq = qm_ew(b, *rq[b])
                        if b + 1 < 4:
                            rq[b + 1] = qm_mm(b + 1)
                        tmQ_tail(b, *e[b])
                        if b + 2 < 4:
                            r[b + 2] = tmQ_mm(b + 2)
                        qm_tr(b, *eq)
                    ckpt(19)
                    set_psring([0, 1, 2])
                    ckpt(20)
                    P.mark(f"{g}{l} Q{j} MLA")
                    if lat:
                        for h in range(4):
                            chunks = []
                            for kc in range(20):
                                jt = kc // 4
                                chunks.append((KT[:, h, kc * 128:(kc + 1) * 128], Vm[:, kc, h * 64:(h + 1) * 64], None,
                                               [f"KT{h}_{jt}", f"Vm_{jt}"]))
                            cc, e = divmod(h, 2)
                            attention(512, qh[:, h, :], [f"qh{h}"], chunks, MLA_SCALE, None,
                                      catj[e * 64:(e + 1) * 64, 4 + cc, :], [f"cat{4 + cc}"])
                    else:
                        for sl in range(2):
                            sidx = (j * 512) // 256 + sl
                            qsl = slice(sl * 256, (sl + 1) * 256)
                            for h in range(4):
                                chunks = []
                                for kc in (2 * sidx, 2 * sidx + 1):
                                    jt = kc // 4
                                    chunks.append((KT[:, h, kc * 128:(kc + 1) * 128], Vm[:, kc, h * 64:(h + 1) * 64], None,
                                                   [f"KT{h}_{jt}", f"Vm_{jt}"]))
                                cc, e = divmod(h, 2)
                                attention(256, qh[:, h, qsl], [f"qh{h}"], chunks, MLA_SCALE, None,
                                          catj[e * 64:(e + 1) * 64, 4 + cc, qsl], [f"cat{4 + cc}"])
                    ckpt(21)
                    if lat and j + 1 < NT:
                        norm_to(l, j + 1, 0)
                    P.mark(f"{g}{l} Q{j} SWA")
                    for n in range(2):
                        sink_l = [sinkexp[0:64, l * 4 + n * 2 + gq:l * 4 + n * 2 + gq + 1] for gq in range(2)]
                        if lat:
                            for bq in range(4):
                                blk = j * 4 + bq
                                tq = slice(bq * 128, (bq + 1) * 128)
                                chunks = []
                                if blk >= 1:
                                    chunks.append((ks[n * 64:(n + 1) * 64, (blk - 1) * 128:blk * 128],
                                                   Vs[:, blk - 1, n * 64:(n + 1) * 64], mprev,
                                                   [f"ks_{(blk - 1) // 4}", f"Vs_{(blk - 1) // 4}"]))
                                chunks.append((ks[n * 64:(n + 1) * 64, blk * 128:(blk + 1) * 128],
                                               Vs[:, blk, n * 64:(n + 1) * 64], None, [f"ks_{blk // 4}", f"Vs_{blk // 4}"]))
                                if blk <= 14:
                                    chunks.append((ks[n * 64:(n + 1) * 64, (blk + 1) * 128:(blk + 2) * 128],
                                                   Vs[:, blk + 1, n * 64:(n + 1) * 64], mnext,
                                                   [f"ks_{(blk + 1) // 4}", f"Vs_{(blk + 1) // 4}"]))
                                for kc in range(16, 20):
                                    chunks.append((ks[n * 64:(n + 1) * 64, kc * 128:(kc + 1) * 128],
                                                   Vs[:, kc, n * 64:(n + 1) * 64], None, [f"ks_{NT}", f"Vs_{NT}"]))
                                attention(128, qs[n * 64:(n + 1) * 64, :, tq], ["qs"], chunks, SWA_SCALE, sink_l,
                                          [catj[gq * 64:(gq + 1) * 64, 6 + n, tq] for gq in range(2)], [f"cat{6 + n}"], G=2)
                        else:
                            for sl in range(2):
                                sidx = (j * 512) // 256 + sl
                                qsl = slice(sl * 256, (sl + 1) * 256)
                                chunks = []
                                for kc in (2 * sidx, 2 * sidx + 1):
                                    chunks.append((ks[n * 64:(n + 1) * 64, kc * 128:(kc + 1) * 128],
                                                   Vs[:, kc, n * 64:(n + 1) * 64], None, [f"ks_{kc // 4}", f"Vs_{kc // 4}"]))
                                attention(256, qs[n * 64:(n + 1) * 64, :, qsl], ["qs"], chunks, SWA_SCALE, sink_l,
                                          [catj[gq * 64:(gq + 1) * 64, 6 + n, qsl] for gq in range(2)], [f"cat{6 + n}"], G=2)
                    ckpt(22)
                    P.mark(f"{g}{l} Q{j} wout")
                    set_psring(range(8))
                    if debug and l == 0 and j == 0:
                        dma("sp", dbg_cat, catj, [f"cat{c}" for c in range(8)], [], is_out=True)
                    for half in range(2):
                        wt, wk_ = get_weight(f"wo{half}{g}{l}{j}")
                        for mi in range(4):
                            m = half * 4 + mi
                            ps, pk = psum()
                            for c in range(8):
                                mm(ps[:, :], wt[:, c, mi * 128:(mi + 1) * 128], catj[:, c, :], c == 0, c == 7, [wk_, f"cat{c}"], [pk])
                            xs_ = x[:, m, j * 512:(j + 1) * 512]
                            stt(xs_, ps[:, :], modv[:, l, 2, m, v:v + 1], xs_, ALU.mult, ALU.add, [pk, f"modt{l}", f"x{m}_{j}"], [f"x{m}_{j}"])
                P.barrier()
                if debug and l == 0:
                    dma("sp", dbg_x1[:, :, 0:T], x[:, :, 0:T], [], [], is_out=True)
                    P.barrier()
                ckpt(23)
                P.mark(f"{g}{l} MLP norm")
                set_psring(range(8))
                if l + 1 < depth:
                    small_weights(l + 1)
                ckpt(24)
                P.mark(f"{g}{l} MLP mm")
                ada_pc = [0]
                for jh in range(8):
                    wa, wak = get_weight(f"w1{g}{l}{jh}")
                    wb, wbk = get_weight(f"w2{g}{l}{jh}", ahead=NST - 2)
                    for j in range(NT):
                        if jh == 0:
                            norm(l, 1, j, lambda c: h2[:, c, j * 512:(j + 1) * 512], lambda c: f"h2{c}_{j}")
                        u, uk = ring("ub", ub)
                        for hc in range(4):
                            ps, pk = psum()
                            for c in range(8):
                                mm(ps[:, :], wa[:, c, hc * 128:(hc + 1) * 128], h2[:, c, j * 512:(j + 1) * 512], c == 0, c == 7,
                                   [wak, f"h2{c}_{j}"], [pk])
                            r_, rk_ = ring("relu", relu_t)
                            act(r_[:], ps[:, :], AF.Relu, [pk], [rk_])
                            tt_op(u[:, hc, :], r_[:], r_[:], ALU.mult, [rk_], [f"{uk}_{hc}"])
                        for m in range(8):
                            ps, pk = psum()
                            for hc in range(4):
                                mm(ps[:, :], wb[:, hc, m * 128:(m + 1) * 128], u[:, hc, :], hc == 0, hc == 3, [wbk, f"{uk}_{hc}"], [pk])
                            xs_ = x[:, m, j * 512:(j + 1) * 512]
                            stt(xs_, ps[:, :], modv[:, l, 5, m, v:v + 1], xs_, ALU.mult, ALU.add, [pk, f"modt{l}", f"x{m}_{j}"], [f"x{m}_{j}"])
                    if ada_defer and g == "A" and l + 1 < DEPTH:
                        for _ in range(ADA_SPLIT[jh]):
                            ada_piece(l + 1, ada_pc[0])
                            ada_pc[0] += 1
                        if jh == 7:
                            ada_finish(l + 1)
                P.barrier()
                if debug and l == 0:
                    dma("sp", dbg_x2[:, :, 0:T], x[:, :, 0:T], [], [], is_out=True)
                    P.barrier()
            P.mark(f"{g} final")
            for j in range(NT):
                ps, pk = psum()
                for c in range(8):
                    sq, sqk = ring("sqt", sqt)
                    act(sq[:], x[:, c, j * 512:(j + 1) * 512], AF.Square, [f"x{c}_{j}"], [sqk])
                    mm(ps[:, :], ones_b, sq[:], c == 0, c == 7, [sqk, "cst"], [pk])
                act(s_t[:], ps[:, :], AF.Sqrt, [pk], ["s_t", "rstd"], bias=EPS, scale=1.0 / D)
                recip(rstd[:], s_t[:], ["s_t"], ["s_t", "rstd"])
                for c in range(8):
                    y_, yk = ring("tt", tt)
                    stt(y_[:], x[:, c, j * 512:(j + 1) * 512], vecs[:, 64 + c:65 + c], rstd[:], ALU.mult, ALU.mult,
                        [f"x{c}_{j}", "vecs", "rstd"], [yk])
                    dma("sp", yT[c * 128:(c + 1) * 128, j * 512:(j + 1) * 512], y_[:], [yk], [], is_out=True)
            P.barrier()

        try:
            for g_ in groups:
                run_group(g_)
            P.mark("end")
            P.finish()
        except StopBuild:
            pass
        MARKS.extend(P.marks)
    return nc


_CACHE = {}


def _consts():
    ident = np.eye(128, dtype=np.float32)
    ones = np.ones((128, 128), np.float32)
    kk = np.arange(128)[:, None]
    qq = np.arange(128)[None, :]
    mprev = np.where(kk >= qq, 0.0, NEG).astype(np.float32)
    mnext = np.where(kk <= qq, 0.0, NEG).astype(np.float32)
    c = np.concatenate([ident, ones, mprev, mnext, np.zeros((128, 128), np.float32)], axis=1)
    def tables(rot_dim):
        half = rot_dim // 2
        inv = (10000.0 ** (-np.arange(0, half, 2, dtype=np.float32) / half)).astype(np.float32)
        t = np.arange(2048)
        row = (t // 64).astype(np.float32)
        col = (t % 64).astype(np.float32)
        ar = row[:, None] * inv[None, :]
        ac = col[:, None] * inv[None, :]
        ang = np.concatenate([ar, ar, ac, ac], axis=-1).astype(np.float32)
        cos = np.cos(ang).astype(np.float32)
        sin = np.sin(ang).astype(np.float32)
        q = rot_dim // 4
        sgn = np.concatenate([-np.ones(q), np.ones(q), -np.ones(q), np.ones(q)]).astype(np.float32)
        return cos, sin * sgn[None, :]
    cs, ss = tables(64)
    cm, sm_ = tables(32)
    r = np.concatenate([cs, ss, cm, sm_], axis=1)
    r = r.reshape(16, 128, 192).transpose(1, 0, 2).reshape(128, 16 * 192)
    return np.ascontiguousarray(c), np.ascontiguousarray(r.astype(np.float32))


def kernel(x_prompt, x_sample, cache_mla_ckv, cache_mla_kpe, cache_swa_k, cache_swa_v, c, c_ctx,
           w_ada, b_ada, norm1, norm2, w_in, conv_w, sgu_norm, sgu_w, sgu_b, mla_q_norm, mla_w_q_up,
           mla_kv_norm, mla_w_kv_up, swa_sink, w_out, mlp_w1, mlp_w2, final_norm):
    in_maps = pack_inputs(x_prompt, x_sample, cache_mla_ckv, cache_mla_kpe, cache_swa_k, cache_swa_v, c, c_ctx,
                          w_ada, b_ada, norm1, norm2, w_in, conv_w, sgu_norm, sgu_w, sgu_b, mla_q_norm, mla_w_q_up,
                          mla_kv_norm, mla_w_kv_up, swa_sink, w_out, mlp_w1, mlp_w2, final_norm)
    if "nc" not in _CACHE:
        _CACHE["nc"] = build_program()
    nc = _CACHE["nc"]
    res = run_bass_kernel_spmd(nc, in_maps, core_ids=list(range(NCORES)))
    return unpack_outputs(res.results)


def pack_inputs(x_prompt, x_sample, cache_mla_ckv, cache_mla_kpe, cache_swa_k, cache_swa_v, c, c_ctx,
                w_ada, b_ada, norm1, norm2, w_in, conv_w, sgu_norm, sgu_w, sgu_b, mla_q_norm, mla_w_q_up,
                mla_kv_norm, mla_w_kv_up, swa_sink, w_out, mlp_w1, mlp_w2, final_norm, cores=range(NCORES)):
    f = lambda a: np.ascontiguousarray(np.asarray(a, dtype=np.float32))
    x_prompt, x_sample = f(x_prompt), f(x_sample)
    consts, ropes = _consts()
    w_in = f(w_in)
    a_b, a_c, a_x = w_in[:, :, 0:256], w_in[:, :, 256:512], w_in[:, :, 512:768]
    u_, v_ = w_in[:, :, 768:1024], w_in[:, :, 1024:1280]
    cq, ckv, kpe = w_in[:, :, 1280:1536], w_in[:, :, 1536:1664], w_in[:, :, 1664:1696]
    sq, sk, sv = w_in[:, :, 1696:1952], w_in[:, :, 1952:2080], w_in[:, :, 2080:2208]
    sq_g = sq.reshape(DEPTH, D, 2, 2, 64).transpose(0, 1, 3, 2, 4).reshape(DEPTH, D, 256)
    w_fm = f(np.concatenate([a_c, a_x, a_b, u_, cq], axis=2))
    w_tm = f(np.concatenate([ckv, kpe, sk, sv, v_, sq_g], axis=2))
    bada_fm = f(np.asarray(b_ada).reshape(DEPTH, 48, 128).transpose(2, 0, 1).reshape(128, DEPTH * 48))
    vecs = np.zeros((128, 128), np.float32)
    vecs[:, 0:32] = np.asarray(norm1).reshape(DEPTH, 8, 128).transpose(2, 0, 1).reshape(128, 32)
    vecs[:, 32:64] = np.asarray(norm2).reshape(DEPTH, 8, 128).transpose(2, 0, 1).reshape(128, 32)
    vecs[:, 64:72] = np.asarray(final_norm).reshape(8, 128).T
    vecs[:, 72:80] = np.asarray(mla_q_norm).reshape(DEPTH, 2, 128).transpose(2, 0, 1).reshape(128, 8)
    vecs[:, 80:104] = np.asarray(conv_w).reshape(DEPTH, 3, 2, 128).transpose(3, 0, 2, 1).reshape(128, 24)
    sgunorm_bc = f(np.broadcast_to(np.asarray(sgu_norm).reshape(1, DEPTH * 256), (128, DEPTH * 256)))
    kvnorm_bc = f(np.broadcast_to(np.asarray(mla_kv_norm).reshape(1, DEPTH * 128), (128, DEPTH * 128)))
    sink_bc = f(np.broadcast_to(np.asarray(swa_sink).reshape(1, 16), (128, 16)))
    sgub = f(np.asarray(sgu_b).reshape(1, DEPTH * 512))
    sgu_wT = f(np.asarray(sgu_w).transpose(0, 3, 1, 2).reshape(DEPTH, 128, 512))
    shared = dict(w_ada=f(w_ada), bada_fm=bada_fm, vecs_fm=vecs, sgunorm_bc=sgunorm_bc, kvnorm_bc=kvnorm_bc,
                  sink_bc=sink_bc, sgub=sgub, sgu_wT=sgu_wT, w_fm=w_fm, w_tm=w_tm, w_out=f(w_out), w1=f(mlp_w1),
                  w2=f(mlp_w2), wq=f(mla_w_q_up), wkv=f(mla_w_kv_up), ropes=ropes, consts=consts)
    c = np.asarray(c, np.float32)
    c_ctx = np.asarray(c_ctx, np.float32)
    in_maps = []
    for i in cores:
        cv = np.stack([c_ctx, c[i]], axis=0)
        cfm = f(cv.reshape(2, 8, 128).transpose(2, 1, 0).reshape(128, 16))
        m = dict(shared)
        m.update(
            xsT=f(x_sample[i].T),
            xpT=f(x_prompt[4 * i:4 * i + 4].reshape(1024, D).T),
            ckvT_c=f(np.asarray(cache_mla_ckv[i]).transpose(0, 2, 1)),
            kpeT_c=f(np.asarray(cache_mla_kpe[i]).transpose(0, 2, 1)),
            skT_c=f(np.asarray(cache_swa_k[i]).reshape(DEPTH, 512, 128).transpose(0, 2, 1)),
            sv_c=f(np.asarray(cache_swa_v[i]).reshape(DEPTH, 512, 128)),
            cfm=cfm,
        )
        in_maps.append(m)
    return in_maps


def unpack_outputs(rs):
    y_prompt = np.concatenate([r["ypT"].T.reshape(4, 256, D) for r in rs], axis=0).astype(np.float32)
    y_sample = np.stack([r["ysT"].T for r in rs], axis=0).astype(np.float32)
    new_ckv = np.concatenate([r["o_ckv"] for r in rs], axis=0).astype(np.float32)
    new_kpe = np.concatenate([r["o_kpe"] for r in rs], axis=0).astype(np.float32)
    new_k = np.concatenate([r["o_k"] for r in rs], axis=0).reshape(-1, DEPTH, 256, 2, 64).astype(np.float32)
    new_v = np.concatenate([r["o_v"] for r in rs], axis=0).reshape(-1, DEPTH, 256, 2, 64).astype(np.float32)
    return (np.ascontiguousarray(y_prompt), np.ascontiguousarray(y_sample), new_ckv, new_kpe, new_k, new_v)
```

```python
import numpy as np
import concourse.bass as bass
import concourse.mybir as mybir
from concourse.bass_utils import run_bass_kernel_spmd

F32, BF16 = mybir.dt.float32, mybir.dt.bfloat16
AF = mybir.ActivationFunctionType
ALU = mybir.AluOpType

D = 1024
DEPTH = 4
EPS = 1e-6
MLA_SCALE = 96 ** -0.5
SWA_SCALE = 0.125
NEG = -30000.0
NCORES = 8


class Info:
    __slots__ = ("sem", "val", "clock", "eng")

    def __init__(self, eng):
        self.sem = None
        self.val = 0
        self.clock = None
        self.eng = eng


class Prog:
    COMPUTE = ("pe", "act", "dve")
    QUEUES = ("sp", "pool")
    NSL = 6

    def __init__(self, nc):
        self.nc = nc
        self.eng = {"pe": nc.tensor, "act": nc.scalar, "dve": nc.vector, "pool": nc.gpsimd, "sp": nc.sync}
        self.sems = {}
        self.epoch = 0
        self.csem = {}
        self.cnt = {}
        self._new_epoch_sems()
        self.slots = {}
        self.nd = {q: 0 for q in self.QUEUES}
        for q in self.QUEUES:
            self.slots[q] = []
            for i in range(self.NSL):
                nm = f"d_{q}{i}"
                self.sems[nm] = nc.alloc_semaphore(name=nm)
                self.slots[q].append([nm, 0])
        self.clock = {e: {} for e in self.eng}
        self.last_w = {}
        self.readers = {}
        self.pending = {e: [] for e in self.COMPUTE}
        self.out_infos = []
        self.total = {e: 0 for e in self.eng}
        self.ps_open = {}
        self.marks = []

    def _new_epoch_sems(self):
        for e in self.COMPUTE:
            nm = f"c_{e}{self.epoch}"
            self.sems[nm] = self.nc.alloc_semaphore(name=nm)
            self.csem[e] = nm
            self.cnt[e] = 0
        self.epoch += 1

    def _wait(self, e, info):
        if info.sem is None:
            raise RuntimeError("dependency on unsignaled op")
        ck = self.clock[e]
        if ck.get(info.sem, 0) >= info.val:
            return
        self.eng[e].wait_ge(self.sems[info.sem], info.val)
        for s, v in info.clock.items():
            if ck.get(s, 0) < v:
                ck[s] = v

    def op(self, e, fn, reads=(), writes=(), sig=True, dma=False, is_out=False):
        psr = [k for k in reads if k.startswith("ps")]
        for k in psr:
            self.ps_open[k] = False
        if psr:
            writes = list(writes) + [k for k in psr if k not in writes]
        deps = []
        for k in reads:
            w = self.last_w.get(k)
            if w is not None:
                deps.append((w, True))
        for k in writes:
            w = self.last_w.get(k)
            if w is not None:
                deps.append((w, False))
            rd = self.readers.get(k)
            if rd:
                for r in rd.values():
                    deps.append((r, False))
        for info, raw in deps:
            if info.eng == e and not dma:
                if e == "pe":
                    continue
            self._wait(e, info)
        self.total[e] += 1
        info = Info(e)
        if dma:
            n = self.nd[e]
            self.nd[e] += 1
            slot = self.slots[e][n % self.NSL]
            ck = self.clock[e]
            if slot[1] > 0 and ck.get(slot[0], 0) < slot[1]:
                self.eng[e].wait_ge(self.sems[slot[0]], slot[1])
                ck[slot[0]] = slot[1]
            slot[1] += 16
            inst = fn()
            inst.then_inc(self.sems[slot[0]], 16)
            info.sem, info.val = slot[0], slot[1]
            info.clock = dict(ck)
            info.clock[info.sem] = info.val
            info.eng = e + "_dma%d" % n
            if is_out:
                self.out_infos.append(info)
        else:
            inst = fn()
            if e == "pe" and self.marks and self.marks[-1][1] is None:
                nm_ = inst.ins.name
                for mk in reversed(self.marks):
                    if mk[1] is not None:
                        break
                    mk[1] = nm_
            if sig:
                self.cnt[e] += 1
                inst.then_inc(self.sems[self.csem[e]], 1)
                info.sem, info.val = self.csem[e], self.cnt[e]
                info.clock = dict(self.clock[e])
                info.clock[info.sem] = info.val
                for p in self.pending[e]:
                    p.sem, p.val, p.clock = info.sem, info.val, info.clock
                self.pending[e] = []
            else:
                self.pending[e].append(info)
        for k in writes:
            self.last_w[k] = info
            self.readers[k] = {}
        for k in reads:
            self.readers.setdefault(k, {})[info.eng] = info
        return info

    def barrier(self):
        for e in self.COMPUTE:
            assert not self.pending[e]
        for f in self.eng:
            ck = self.clock[f]
            for e in self.COMPUTE:
                nm, v = self.csem[e], self.cnt[e]
                if v > 0 and e != f and ck.get(nm, 0) < v:
                    self.eng[f].wait_ge(self.sems[nm], v)
                if v > 0:
                    ck[nm] = v
            for q in self.QUEUES:
                for nm, v in self.slots[q]:
                    if v > 0 and ck.get(nm, 0) < v:
                        self.eng[f].wait_ge(self.sems[nm], v)
                        ck[nm] = v
        for e in self.COMPUTE:
            if self.cnt[e] > 0:
                self.eng[e].wait_ge(self.sems[self.csem[e]], self.cnt[e])
        self.last_w = {}
        self.readers = {}
        self._new_epoch_sems()

    def mark(self, label):
        self.marks.append([label, None])

    def finish(self):
        for info in self.out_infos:
            self._wait("sp", info)


MARKS = []


class StopBuild(Exception):
    pass


def build_program(depth=DEPTH, groups="AB", debug=False, p0=9):
    nc = bass.Bass("TRN2", target_bir_lowering=False)
    P = Prog(nc)

    def din(name, shape):
        return nc.dram_tensor(name, list(shape), F32, kind="ExternalInput").ap()

    def dout(name, shape):
        return nc.dram_tensor(name, list(shape), F32, kind="ExternalOutput").ap()

    xsT = din("xsT", (D, 2048))
    xpT = din("xpT", (D, 1024))
    ckvT_c = din("ckvT_c", (DEPTH, 128, 512))
    kpeT_c = din("kpeT_c", (DEPTH, 32, 512))
    skT_c = din("skT_c", (DEPTH, 128, 512))
    sv_c = din("sv_c", (DEPTH, 512, 128))
    cfm = din("cfm", (128, 16))
    w_ada = din("w_ada", (DEPTH, D, 6 * D))
    bada_fm = din("bada_fm", (128, DEPTH * 48))
    vecs_fm = din("vecs_fm", (128, 128))
    sgunorm_bc = din("sgunorm_bc", (128, DEPTH * 256))
    kvnorm_bc = din("kvnorm_bc", (128, DEPTH * 128))
    sink_bc = din("sink_bc", (128, 16))
    sgub = din("sgub", (1, DEPTH * 512))
    sgu_wT = din("sgu_wT", (DEPTH, 128, 512))
    w_fm = din("w_fm", (DEPTH, D, 1280))
    w_tm = din("w_tm", (DEPTH, D, 928))
    w_out = din("w_out", (DEPTH, D, D))
    w1 = din("w1", (DEPTH, D, 4096))
    w2 = din("w2", (DEPTH, 4096, D))
    wq = din("wq", (DEPTH, 256, 384))
    wkv = din("wkv", (DEPTH, 128, 512))
    ropes = din("ropes", (128, 16 * 192))
    consts = din("consts", (128, 640))
    ysT = dout("ysT", (D, 2048))
    ypT = dout("ypT", (D, 1024))
    o_ckv = dout("o_ckv", (4, DEPTH, 256, 128))
    o_kpe = dout("o_kpe", (4, DEPTH, 256, 32))
    o_k = dout("o_k", (4, DEPTH, 256, 128))
    o_v = dout("o_v", (4, DEPTH, 256, 128))
    if debug:
        dbg_h = nc.dram_tensor("dbg_h", [128, 8, 512], BF16, kind="ExternalOutput").ap()
        dbg_cat = nc.dram_tensor("dbg_cat", [128, 8, 512], BF16, kind="ExternalOutput").ap()
        dbg_x1 = dout("dbg_x1", (128, 8, 2048))
        dbg_x2 = dout("dbg_x2", (128, 8, 2048))

    A = nc.alloc_sbuf_tensor
    ident_f = A("ident_f", [128, 128], F32)
    cst = A("cst", [128, 512], BF16)
    ident_b, ones_b, mprev, mnext = cst[:, 0:128], cst[:, 128:256], cst[:, 256:384], cst[:, 384:512]
    modt = A("modt", [128, DEPTH * 48 * 2], F32)
    badat = A("badat", [128, DEPTH * 48], F32)
    vecs = A("vecs", [128, 128], F32)
    gs = A("gs", [128, DEPTH * 2 * 8 * 2], F32)
    csil = A("csil", [128, 16], F32)
    csil_b = A("csil_b", [128, 16], BF16)
    sinkexp = A("sinkexp", [128, 16], F32)
    ropet = A("ropet", [128, 16 * 192], BF16)
    sgn = A("sgn", [128, 256], F32)
    kvn = A("kvn", [128, 128], F32)
    wq_t = A("wq_t", [128, 2, 384], BF16)
    wkv_t = A("wkv_t", [128, 512], BF16)
    wsT_t = A("wsT_t", [128, 512], BF16)
    sgub_t = A("sgub_t", [1, 512], BF16)
    x = A("x", [128, 8, 2048], F32)
    R = A("R", [128, 36352], BF16)
    NST = 3
    wst = [A(f"wst{i}", [128, 4096], BF16) for i in range(NST)]
    sqt = [A(f"sqt{i}", [128, 512], BF16) for i in range(2)]
    tt = [A(f"tt{i}", [128, 512], F32) for i in range(2)]
    s_t = A("s_t", [128, 512], F32)
    rstd = s_t
    tmf = [A(f"tmf{i}", [128, 512], F32) for i in range(2)]
    tm2 = [A(f"tm2{i}", [128, 256], F32) for i in range(4)]
    tm3 = [A(f"tm3{i}", [128, 256], F32) for i in range(2)]
    sm = [A(f"sm{i}", [128, 4], F32) for i in range(4)]
    vn_t = [A(f"vn{i}", [128, 256], BF16) for i in range(2)]
    pt = [A(f"pt{i}", [128, 512], BF16) for i in range(4)]
    rden = [A(f"rden{i}", [64, 512], F32) for i in range(1)]
    qh = A("qh", [128, 4, 512], BF16)
    relu_t = [A(f"relu{i}", [128, 512], BF16) for i in range(2)]
    cqraw = relu_t
    yo = tt
    PS = [nc.alloc_psum_tensor(f"ps{i}", [128, 512], F32) for i in range(8)]

    rr = {}

    def ring(name, lst):
        i = rr.get(name, 0)
        rr[name] = i + 1
        j = i % len(lst)
        return lst[j], f"{name}{j}"

    psring = {"lst": list(range(8)), "i": 0}

    def psum():
        lst = psring["lst"]
        b = lst[psring["i"] % len(lst)]
        psring["i"] += 1
        assert not P.ps_open.get(f"ps{b}", False), f"psum bank ps{b} re-allocated before its previous contents were read"
        P.ps_open[f"ps{b}"] = True
        return PS[b], f"ps{b}"

    def set_psring(lst):
        psring["lst"] = list(lst)
        psring["i"] = 0

    def mm(out, lhsT, rhs, start, stop, reads, writes, sig=None):
        if sig is None:
            sig = True
        return P.op("pe", lambda: nc.tensor.matmul(out, lhsT=lhsT, rhs=rhs, start=start, stop=stop),
                    reads=reads, writes=writes, sig=sig)

    def act(out, in_, func, reads, writes, bias=0.0, scale=1.0, accum_out=None):
        kw = {}
        if accum_out is not None:
            kw["accum_out"] = accum_out
        return P.op("act", lambda: nc.scalar.activation(out=out, in_=in_, func=func, bias=bias, scale=scale, **kw),
                    reads=reads, writes=writes)

    def tt_op(out, in0, in1, op, reads, writes):
        return P.op("dve", lambda: nc.vector.tensor_tensor(out=out, in0=in0, in1=in1, op=op), reads=reads, writes=writes)

    def ts_op(out, in0, s1, op0, reads, writes, s2=None, op1=None):
        if op1 is None:
            return P.op("dve", lambda: nc.vector.tensor_scalar(out=out, in0=in0, scalar1=s1, scalar2=None, op0=op0),
                        reads=reads, writes=writes)
        return P.op("dve", lambda: nc.vector.tensor_scalar(out=out, in0=in0, scalar1=s1, scalar2=s2, op0=op0, op1=op1),
                    reads=reads, writes=writes)

    def stt(out, in0, scalar, in1, op0, op1, reads, writes):
        return P.op("dve", lambda: nc.vector.scalar_tensor_tensor(out=out, in0=in0, scalar=scalar, in1=in1, op0=op0, op1=op1),
                    reads=reads, writes=writes)

    def vcopy(out, in_, reads, writes):
        return P.op("dve", lambda: nc.vector.tensor_copy(out=out, in_=in_), reads=reads, writes=writes)

    def recip(out, in_, reads, writes):
        return P.op("dve", lambda: nc.vector.reciprocal(out=out, in_=in_), reads=reads, writes=writes)

    def dma(q, out, in_, reads, writes, is_out=False):
        e = nc.sync if q == "sp" else nc.gpsimd
        return P.op(q, lambda: e.dma_start(out=out, in_=in_), reads=reads, writes=writes, dma=True, is_out=is_out)

    wplan = []
    wstate = {"issued": 0, "used": 0}

    ada_defer = (len(groups) > 0 and groups[0] == "A")
    ada_phase0 = [0] if ada_defer else list(range(DEPTH))
    ADA_SPLIT = [2, 2, 2, 2, 1, 1, 1, 1]

    def plan_weights():
        for l in ada_phase0:
            for pc in range(12):
                wplan.append((f"ada{l}_{pc}", w_ada[l, :, pc * 512:(pc + 1) * 512], (8, 512)))
        for g in groups:
            NT = 4 if g == "A" else 2
            for l in range(depth):
                for j in range(NT):
                    wplan.append((f"fmK{g}{l}{j}", w_fm[l, :, 0:512], (8, 512)))
                    wplan.append((f"tmK{g}{l}{j}", w_tm[l, :, 0:416], (8, 416)))
                for j in range(NT):
                    wplan.append((f"fmQa{g}{l}{j}", w_fm[l, :, 512:896], (8, 384)))
                    wplan.append((f"fmQb{g}{l}{j}", w_fm[l, :, 896:1280], (8, 384)))
                    wplan.append((f"tmQ{g}{l}{j}", w_tm[l, :, 416:928], (8, 512)))
                    wplan.append((f"wo0{g}{l}{j}", w_out[l, :, 0:512], (8, 512)))
                    wplan.append((f"wo1{g}{l}{j}", w_out[l, :, 512:1024], (8, 512)))
                pcn = 0
                for jh in range(8):
                    wplan.append((f"w1{g}{l}{jh}", w1[l, :, jh * 512:(jh + 1) * 512], (8, 512)))
                    wplan.append((f"w2{g}{l}{jh}", w2[l, jh * 512:(jh + 1) * 512, :], (4, 1024)))
                    if ada_defer and g == "A" and l + 1 < DEPTH:
                        for _ in range(ADA_SPLIT[jh]):
                            wplan.append((f"ada{l + 1}_{pcn}", w_ada[l + 1, :, pcn * 512:(pcn + 1) * 512], (8, 512)))
                            pcn += 1

    def issue_weight():
        i = wstate["issued"]
        if i >= len(wplan):
            return
        name, ap, (nc_, ncol) = wplan[i]
        slot = wst[i % NST]
        dst = slot[:, 0:nc_ * ncol].rearrange("p (c n) -> p c n", c=nc_)
        src = ap.rearrange("(c p) n -> p c n", p=128)
        dma("pool", dst, src, reads=[], writes=[f"wst{i % NST}"])
        wstate["issued"] += 1

    def get_weight(name, ahead=NST - 1):
        i = wstate["used"]
        assert wplan[i][0] == name, (wplan[i][0], name)
        while wstate["issued"] < min(i + ahead + 1, len(wplan)):
            issue_weight()
        wstate["used"] += 1
        nc_, ncol = wplan[i][2]
        return wst[i % NST][:, 0:nc_ * ncol].rearrange("p (c n) -> p c n", c=nc_), f"wst{i % NST}"

    plan_weights()
    MARKS.clear()

    def ckpt(n):
        if p0 == n:
            P.barrier()
            P.finish()
            raise StopBuild()

    with nc.allow_low_precision("bf16 matmuls"), nc.allow_non_contiguous_dma("small strided loads"):
        dma("sp", ident_f[:], consts[:, 0:128], [], ["ident_f"])
        dma("pool", cst[:], consts[:, 0:512], [], ["cst"])
        dma("sp", badat[:], bada_fm[:, :], [], ["badat"])
        dma("sp", vecs[:], vecs_fm[:, :], [], ["vecs"])
        dma("sp", csil[:], cfm[:, :], [], ["csil"])
        dma("sp", sinkexp[:], sink_bc[:, :], [], ["sinkexp"])
        dma("pool", ropet[:], ropes[:, :], [], ["ropet"])
        act(csil[:], csil[:], AF.Silu, ["csil"], ["csil"])
        act(sinkexp[:], sinkexp[:], AF.Exp, ["sinkexp"], ["sinkexp"])
        if p0 == 1:
            P.barrier()
            P.finish()
            return nc
        modv = modt[:].rearrange("p (l k c v) -> p l k c v", l=DEPTH, k=6, c=8)
        gsv = gs[:].rearrange("p (l n c v) -> p l n c v", l=DEPTH, n=2, c=8)
        modt3 = modt[:].rearrange("p (m v) -> p m v", v=2)
        vcopy(csil_b[:], csil[:], ["csil"], ["csil_b"])
        P.op("dve", lambda: nc.vector.memset(qh[96:128, :, :], 0.0), [], [f"qh{h}" for h in range(4)])

        def ada_piece(l, pc):
            wt, wk_ = get_weight(f"ada{l}_{pc}")
            ps, pk = psum()
            for c in range(8):
                mm(ps[0:2, :], csil_b[:, c * 2:c * 2 + 2], wt[:, c, :], c == 0, c == 7, [wk_, "csil_b"], [pk])
            t_, tk_ = ring("tt", tt)
            t_ = t_[0:2, :]
            act(t_, ps[0:2, :], AF.Copy, [pk], [tk_])
            ps2, pk2 = psum()
            for mi in range(4):
                P.op("pe", lambda: nc.tensor.transpose(ps2[:, mi * 2:mi * 2 + 2], t_[:, mi * 128:(mi + 1) * 128], ident_f[0:2, 0:2]),
                     [tk_, "ident_f"], [pk2])
            m0 = l * 48 + pc * 4
            tt_op(modt3[:, m0:m0 + 4, :], ps2[:, 0:8].rearrange("p (m v) -> p m v", v=2),
                  badat[:, m0:m0 + 4].unsqueeze(2).to_broadcast([128, 4, 2]), ALU.add, [pk2, "badat"], [f"modt{l}"])

        def ada_finish(l):
            for n in range(2):
                nv = vecs[:, n * 32 + l * 8:n * 32 + l * 8 + 8]
                ts_op(gsv[:, l, n], modv[:, l, 3 * n + 1], 1.0, ALU.add, [f"modt{l}"], [f"gs{l}"])
                tt_op(gsv[:, l, n], gsv[:, l, n], nv.unsqueeze(2).to_broadcast([128, 8, 2]), ALU.mult, [f"gs{l}", "vecs"], [f"gs{l}"])

        for l in ada_phase0:
            for pc in range(12):
                ada_piece(l, pc)
            ada_finish(l)
        P.barrier()

        def run_group(g):
            lat = (g == "A")
            T = 2048 if lat else 1024
            S = 2048 if lat else 256
            NT = T // 512
            NK = 2560 if lat else 1024
            NB = NK // 128
            v = 1 if lat else 0
            xT = xsT if lat else xpT
            yT = ysT if lat else ypT
            o = 0

            def carve(n, c=None):
                nonlocal o
                ap = R[:, o:o + n]
                o += n
                if c is not None:
                    ap = ap.rearrange("p (c t) -> p c t", c=c)
                return ap
            KT = carve(4 * NK, 4)
            Vm = carve(NB * 256, NB)
            ks = carve(NK)
            Vs = carve(NB * 128, NB)
            pbuf = carve(2 * T, 2)
            hjb = [carve(8 * 512, 8)]
            if not lat:
                hjb.append(carve(8 * 512, 8))
            H = {"ap": hjb[0], "k": "hj0_"}

            def use_buf(bi):
                H["ap"] = hjb[bi]
                H["k"] = f"hj{bi}_"

            def norm_to(l, j, bi):
                norm(l, 0, j, lambda c: hjb[bi][:, c, :], lambda c: f"hj{bi}_{c}")
            catj = carve(8 * 512, 8)
            cqn = carve(2 * 512, 2)
            qs = carve(2 * 512, 2)
            acc = carve(2 * 512, 2)
            ckT = carve(512)
            assert o <= 36352, o
            h2 = R[:, 0:8 * T].rearrange("p (c t) -> p c t", c=8)
            ub = [R[:, 8 * T + i * 2048:8 * T + (i + 1) * 2048].rearrange("p (c t) -> p c t", c=4) for i in range(2)]

            for c in range(8):
                dma("sp", x[:, c, 0:T], xT[c * 128:(c + 1) * 128, :], [], [f"x{c}_{j}" for j in range(NT)])

            def norm(l, n, j, dst, dkeys):
                ps, pk = psum()
                for c in range(8):
                    sq, sqk = ring("sqt", sqt)
                    act(sq[:], x[:, c, j * 512:(j + 1) * 512], AF.Square, [f"x{c}_{j}"], [sqk])
                    mm(ps[:, :], ones_b, sq[:], c == 0, c == 7, [sqk, "cst"], [pk])
                act(s_t[:], ps[:, :], AF.Sqrt, [pk], ["s_t", "rstd"], bias=EPS, scale=1.0 / D)
                recip(rstd[:], s_t[:], ["s_t"], ["s_t", "rstd"])
                for c in range(8):
                    t, tk = ring("tt", tt)
                    tt_op(t[:], x[:, c, j * 512:(j + 1) * 512], rstd[:], ALU.mult, [f"x{c}_{j}", "rstd"], [tk])
                    act(dst(c), t[:], AF.Identity, [tk, f"gs{l}", f"modt{l}"], [dkeys(c)],
                        bias=modv[:, l, 3 * n, c, v:v + 1], scale=gsv[:, l, n, c, v:v + 1])

            def attention(NQ, qT, qkeys, chunks, scale, sink_ap, out_ap, out_keys, G=1, LA=2):
                W = G * NQ
                CH = 512 // W
                ai = rr.get("acc", 0) % 2
                rr["acc"] = rr.get("acc", 0) + 1
                nump, numk = PS[3 + ai * 2], f"ps{3 + ai * 2}"
                denp, denk = PS[4 + ai * 2], f"ps{4 + ai * 2}"
                ones64 = ones_b[:, 0:64]
                n = len(chunks)
                groups_ = [chunks[g0:g0 + CH] for g0 in range(0, n, CH)]

                def view(ap2d):
                    return ap2d if G == 1 else ap2d.rearrange("p (g t) -> p g t", g=G)

                def emit_S(grp):
                    sb, sbk = psum()
                    for i, (kT, vv, mask, keys) in enumerate(grp):
                        mm(view(sb[:, i * W:(i + 1) * W]), kT, qT, True, mask is None, keys + qkeys, [sbk])
                        if mask is not None:
                            mrhs = mask if G == 1 else mask.unsqueeze(1).to_broadcast([128, G, NQ])
                            mm(view(sb[:, i * W:(i + 1) * W]), ident_b, mrhs, False, True, ["cst"], [sbk])
                    return sb, sbk

                ng = len(groups_)
                sbs = {}
                for gi in range(min(LA, ng)):
                    sbs[gi] = emit_S(groups_[gi])
                idx = 0
                for g0 in range(0, ng, LA):
                    cur = list(range(g0, min(g0 + LA, ng)))
                    for gi in range(g0 + LA, min(g0 + 2 * LA, ng)):
                        sbs[gi] = emit_S(groups_[gi])
                    pts = {}
                    for gi in cur:
                        sb, sbk = sbs.pop(gi)
                        p_, pk_ = ring("pt", pt)
                        w = len(groups_[gi]) * W
                        act(p_[:, 0:w], sb[:, 0:w], AF.Exp, [sbk], [pk_], scale=scale)
                        pts[gi] = (p_, pk_)
                    for gi in cur:
                        p_, pk_ = pts[gi]
                        for i, (kT, vv, mask, keys) in enumerate(groups_[gi]):
                            first = (idx == 0)
                            last = (idx == n - 1)
                            idx += 1
                            mm(nump[0:64, 0:W], vv, p_[:, i * W:(i + 1) * W], first, last, keys + [pk_], [numk])
                            mm(denp[0:64, 0:W], ones64, p_[:, i * W:(i + 1) * W], first, last, [pk_, "cst"], [denk])
                rd, rdk = ring("rden", rden)
                sinks = sink_ap if isinstance(sink_ap, list) else [sink_ap] * G
                outs = out_ap if isinstance(out_ap, list) else [out_ap]
                if sinks[0] is not None:
                    for gg in range(G):
                        ts_op(rd[:, gg * NQ:(gg + 1) * NQ], denp[0:64, gg * NQ:(gg + 1) * NQ], sinks[gg], ALU.add,
                              [denk, "sinkexp"], [rdk])
                    recip(rd[:, 0:W], rd[:, 0:W], [rdk], [rdk])
                else:
                    recip(rd[:, 0:W], denp[0:64, 0:W], [denk], [rdk])
                for gg in range(G):
                    tt_op(outs[gg], nump[0:64, gg * NQ:(gg + 1) * NQ], rd[:, gg * NQ:(gg + 1) * NQ], ALU.mult, [numk, rdk], out_keys)

            def rope(dst, src, H, Dh, gb, off, rk, wk):
                Q = Dh // 4
                cos = ropet[:, gb * 192 + off:gb * 192 + off + Dh]
                sin = ropet[:, gb * 192 + off + Dh:gb * 192 + off + 2 * Dh]
                t2, t2k = ring("tm3", tm3)
                s4 = src.rearrange("p (h a b q) -> p h a b q", h=H, a=2, b=2)
                d4 = t2[:, 0:H * Dh].rearrange("p (h a b q) -> p h a b q", h=H, a=2, b=2)
                sn = sin.rearrange("p (a b q) -> p a b q", a=2, b=2)
                for bb in range(2):
                    tt_op(d4[:, :, :, bb, :], s4[:, :, :, 1 - bb, :],
                          sn[:, :, bb, :].unsqueeze(1).to_broadcast([128, H, 2, Q]), ALU.mult,
                          rk + ["ropet"] + ([t2k] if bb else []), [t2k])
                s3 = src.rearrange("p (h d) -> p h d", h=H)
                d3 = dst.rearrange("p (h d) -> p h d", h=H)
                tt_op(d3, s3, cos.unsqueeze(1).to_broadcast([128, H, Dh]), ALU.mult, rk + ["ropet"], wk)
                tt_op(dst, dst, t2[:, 0:H * Dh], ALU.add, wk + [t2k], wk)

            def small_weights(l_):
                dma("pool", wq_t[:], wq[l_].rearrange("(c p) n -> p c n", p=128), [], ["wq_t"])
                dma("pool", wkv_t[:], wkv[l_], [], ["wkv_t"])
                dma("pool", wsT_t[:], sgu_wT[l_], [], ["wsT_t"])
                dma("pool", sgub_t[:], sgub[:, l_ * 512:(l_ + 1) * 512], [], ["sgub_t"])
                dma("sp", sgn[:], sgunorm_bc[:, l_ * 256:(l_ + 1) * 256], [], ["sgn"])
                dma("sp", kvn[:], kvnorm_bc[:, l_ * 128:(l_ + 1) * 128], [], ["kvn"])

            for l in range(depth):
                set_psring(range(8))
                if l == 0:
                    small_weights(0)
                P.op("dve", lambda: nc.vector.memset(KT[96:128, :, :], 0.0), [],
                     [f"KT{h}_{jj}" for h in range(4) for jj in range(NT + 1)])
                if lat:
                    dma("pool", ckT[:], ckvT_c[l], [], ["ckT"])
                    for h in range(4):
                        dma("pool", KT[64:96, h, T:T + 512], kpeT_c[l], [], [f"KT{h}_{NT}"])
                    dma("pool", ks[:, T:T + 512], skT_c[l], [], [f"ks_{NT}"])
                    dma("pool", Vs[:, 16:20, :], sv_c[l].rearrange("(b p) d -> p b d", p=128), [], [f"Vs_{NT}"])

                def kv_up(j, nblk):
                    for h in range(4):
                        ps, pk = psum()
                        mm(ps[0:64, 0:nblk * 128], wkv_t[:, h * 128:h * 128 + 64], ckT[:, 0:nblk * 128], True, True,
                           ["wkv_t", "ckT"], [pk])
                        act(KT[0:64, h, j * 512:j * 512 + nblk * 128], ps[0:64, 0:nblk * 128], AF.Copy, [pk], [f"KT{h}_{j}"])
                    for b in range(nblk):
                        ps, pk = psum()
                        mm(ps[:, :], ckT[:, b * 128:(b + 1) * 128], wkv_t[:], True, True, ["wkv_t", "ckT"], [pk])
                        vcopy(Vm[:, j * 4 + b, :].rearrange("p (h d) -> p h d", h=4),
                              ps[:, :].rearrange("p (h t d) -> p h t d", h=4, t=2)[:, :, 1, :], [pk], [f"Vm_{j}"])

                if lat:
                    kv_up(NT, 4)
                ckpt(10)

                for j in range(NT):
                    P.mark(f"{g}{l} K{j} norm")
                    if lat:
                        use_buf(0)
                        norm_to(l, j, 0)
                    else:
                        if j == 0:
                            norm_to(l, 0, 0)
                        nxt_j = j + 1 if j + 1 < NT else 0
                        norm_to(l, nxt_j, (j + 1) % 2)
                        use_buf(j % 2)
                    P.mark(f"{g}{l} K{j} fm")
                    wtF, wkF = get_weight(f"fmK{g}{l}{j}")
                    wtT, wkT = get_weight(f"tmK{g}{l}{j}", ahead=NST - 2)

                    def fmK_group(m):
                        ps, pk = psum()
                        for c in range(8):
                            mm(ps[:, :], wtF[:, c, m * 128:(m + 1) * 128], H["ap"][:, c, :], c == 0, c == 7, [wkF, f"{H['k']}{c}"], [pk])
                        if m < 2:
                            act(pbuf[:, m, j * 512:(j + 1) * 512], ps[:, :], AF.Copy, [pk], [f"p{m}_{j}"])
                        else:
                            tt_op(pbuf[:, m - 2, j * 512:(j + 1) * 512], ps[:, :], pbuf[:, m - 2, j * 512:(j + 1) * 512],
                                  ALU.mult, [pk, f"p{m - 2}_{j}"], [f"p{m - 2}_{j}"])

                    def tmK_mm(b):
                        tok = slice(b * 128, (b + 1) * 128)
                        ps, pk = psum()
                        for c in range(8):
                            mm(ps[:, 0:416], H["ap"][:, c, tok], wtT[:, c, :], c == 0, c == 7, [wkT, f"{H['k']}{c}"], [pk])
                        return ps, pk

                    def tmK_ew(b, ps, pk):
                        gb = j * 4 + b
                        st, stk = ring("tmf", tmf)
                        act(st[:, 0:416], ps[:, 0:416], AF.Copy, [pk], [stk])
                        smt, smk = ring("sm", sm)
                        t2, t2k = ring("tm3", tm3)
                        P.op("dve", lambda: nc.vector.memset(smt[:, 0:1], 0.0), [], [smk])
                        act(t2[:, 0:128], st[:, 0:128], AF.Square, [stk, smk], [t2k, smk], accum_out=smt[:, 0:1])
                        act(smt[:, 1:2], smt[:, 0:1], AF.Sqrt, [smk], [smk], bias=EPS, scale=1.0 / 128)
                        recip(smt[:, 2:3], smt[:, 1:2], [smk], [smk])
                        stt(st[:, 0:128], st[:, 0:128], smt[:, 2:3], kvn[:], ALU.mult, ALU.mult, [stk, smk, "kvn"], [stk])
                        if lat:
                            t3, t3k = ring("tm2", tm2)
                            rope(t3[:, 0:32], st[:, 128:160], 1, 32, gb, 128, [stk], [t3k])
                            kpe_src, kpek = t3[:, 0:32], t3k
                            t4, t4k = ring("tm2", tm2)
                            rope(t4[:, 0:128], st[:, 160:288], 2, 64, gb, 0, [stk], [t4k])
                            sk_src, skk = t4[:, 0:128], t4k
                        else:
                            kpe_src, kpek = st[:, 128:160], stk
                            sk_src, skk = st[:, 160:288], stk
                            sq_, r0 = divmod(gb * 128, 256)
                            dma("sp", o_ckv[sq_, l, r0:r0 + 128, :], st[:, 0:128], [stk], [], is_out=True)
                            dma("sp", o_kpe[sq_, l, r0:r0 + 128, :], st[:, 128:160], [stk], [], is_out=True)
                            dma("sp", o_k[sq_, l, r0:r0 + 128, :], st[:, 160:288], [stk], [], is_out=True)
                            dma("sp", o_v[sq_, l, r0:r0 + 128, :], st[:, 288:416], [stk], [], is_out=True)
                        vcopy(Vs[:, gb, :], st[:, 288:416], [stk], [f"Vs_{j}"])
                        return st, stk, kpe_src, kpek, sk_src, skk

                    def tmK_tr(b, st, stk, kpe_src, kpek, sk_src, skk):
                        gb = j * 4 + b
                        tok = slice(b * 128, (b + 1) * 128)
                        ps2, pk2 = psum()
                        P.op("pe", lambda: nc.tensor.transpose(ps2[:, 0:128], st[:, 0:128], ident_f[:]), [stk, "ident_f"], [pk2])
                        P.op("pe", lambda: nc.tensor.transpose(ps2[:, 128:256], sk_src, ident_f[:]), [skk, "ident_f"], [pk2])
                        P.op("pe", lambda: nc.tensor.transpose(ps2[0:32, 256:384], kpe_src, ident_f[:]), [kpek, "ident_f"], [pk2])
                        act(ckT[:, tok], ps2[:, 0:128], AF.Copy, [pk2], ["ckT"])
                        vcopy(ks[:, gb * 128:(gb + 1) * 128], ps2[:, 128:256], [pk2], [f"ks_{j}"])
                        act(KT[64:96, 0:2, gb * 128:(gb + 1) * 128], ps2[0:32, 256:384].unsqueeze(1).to_broadcast([32, 2, 128]),
                            AF.Copy, [pk2], [f"KT0_{j}", f"KT1_{j}"])
                        vcopy(KT[64:96, 2:4, gb * 128:(gb + 1) * 128], ps2[0:32, 256:384].unsqueeze(1).to_broadcast([32, 2, 128]),
                              [pk2], [f"KT2_{j}", f"KT3_{j}"])

                    P.mark(f"{g}{l} K{j} tm")
                    r = {}
                    e = {}
                    r[0] = tmK_mm(0)
                    r[1] = tmK_mm(1)
                    for b in range(4):
                        e[b] = tmK_ew(b, *r[b])
                        fmK_group(b)
                        tmK_tr(b, *e[b])
                        if b + 2 < 4:
                            r[b + 2] = tmK_mm(b + 2)
                    P.mark(f"{g}{l} K{j} kvup")
                    kv_up(j, 4)
                    ckpt(14)

                nseq_t = 512 // min(S, 512)
                for j in range(NT):
                    set_psring(range(8))
                    if lat:
                        use_buf(0)
                        if j == 0:
                            P.mark(f"{g}{l} Q{j} norm")
                            norm_to(l, 0, 0)
                    else:
                        if j + 1 < NT:
                            norm_to(l, j + 1, (NT + j + 1) % 2)
                        use_buf((NT + j) % 2)
                    P.mark(f"{g}{l} Q{j} fm")
                    if debug and l == 0 and j == 0:
                        dma("sp", dbg_h, H["ap"], [f"{H['k']}{c}" for c in range(8)], [], is_out=True)
                    wa, wak = get_weight(f"fmQa{g}{l}{j}")

                    def fmQ_group(m, wt, wk_):
                        mi = m % 3
                        ps, pk = psum()
                        for c in range(8):
                            mm(ps[:, :], wt[:, c, mi * 128:(mi + 1) * 128], H["ap"][:, c, :], c == 0, c == 7, [wk_, f"{H['k']}{c}"], [pk])
                        if m < 2:
                            act(catj[:, m, :], ps[:, :], AF.Copy, [pk], [f"cat{m}"])
                        elif m < 4:
                            act(catj[:, m, :], ps[:, :], AF.Gelu_apprx_tanh, [pk], [f"cat{m}"])
                        else:
                            cr, crk = cqraw[m - 4], f"cqraw{m - 4}"
                            act(cr[:], ps[:, :], AF.Copy, [pk], [crk])

                    for m in range(3):
                        fmQ_group(m, wa, wak)
                    wb, wbk = get_weight(f"fmQb{g}{l}{j}")
                    wtT, wkT = get_weight(f"tmQ{g}{l}{j}", ahead=NST - 2)

                    def cq_conv():
                        ps, pk = psum()
                        for c in range(2):
                            sq, sqk = ring("sqt", sqt)
                            act(sq[:], cqraw[c][:], AF.Square, [f"cqraw{c}"], [sqk])
                            mm(ps[:, :], ones_b, sq[:], c == 0, c == 1, [sqk, "cst"], [pk])
                        act(s_t[:], ps[:, :], AF.Sqrt, [pk], ["s_t", "rstd"], bias=EPS, scale=1.0 / 256)
                        recip(rstd[:], s_t[:], ["s_t"], ["s_t", "rstd"])
                        for c in range(2):
                            stt(cqn[:, c, :], cqraw[c][:], vecs[:, 72 + l * 2 + c:72 + l * 2 + c + 1], rstd[:], ALU.mult, ALU.mult,
                                [f"cqraw{c}", "vecs", "rstd"], [f"cqn{c}"])
                        Sq = min(S, 512)
                        for c in range(2):
                            cw = lambda k: vecs[:, 80 + (l * 2 + c) * 3 + k:80 + (l * 2 + c) * 3 + k + 1]
                            lo = j * 512
                            pk_all = [f"p{c}_{jj}" for jj in range(NT)]
                            ts_op(acc[:, c, :], pbuf[:, c, lo:lo + 512], cw(1), ALU.mult, pk_all + ["vecs"], [f"acc{c}"])
                            for sidx in range(nseq_t):
                                a0 = sidx * Sq
                                g0 = lo + a0
                                first_in_seq = (g0 % S == 0)
                                last_in_seq = ((g0 + Sq) % S == 0)
                                s0 = 1 if first_in_seq else 0
                                stt(acc[:, c, a0 + s0:a0 + Sq], pbuf[:, c, g0 + s0 - 1:g0 + Sq - 1], cw(0), acc[:, c, a0 + s0:a0 + Sq],
                                    ALU.mult, ALU.add, pk_all + ["vecs", f"acc{c}"], [f"acc{c}"])
                                e0 = 1 if last_in_seq else 0
                                stt(acc[:, c, a0:a0 + Sq - e0], pbuf[:, c, g0 + 1:g0 + Sq - e0 + 1], cw(2), acc[:, c, a0:a0 + Sq - e0],
                                    ALU.mult, ALU.add, pk_all + ["vecs", f"acc{c}"], [f"acc{c}"])
                            tt_op(catj[:, c, :], catj[:, c, :], acc[:, c, :], ALU.mult, [f"cat{c}", f"acc{c}"], [f"cat{c}"])

                    def tmQ_mm(b):
                        tok = slice(b * 128, (b + 1) * 128)
                        ps, pk = psum()
                        for c in range(8):
                            mm(ps[:, :], H["ap"][:, c, tok], wtT[:, c, :], c == 0, c == 7, [wkT, f"{H['k']}{c}"], [pk])
                        return ps, pk

                    def tmQ_ew(b, ps, pk):
                        gb = j * 4 + b
                        st, stk = ring("tmf", tmf)
                        act(st[:, 0:256], ps[:, 0:256], AF.Gelu_apprx_tanh, [pk], [stk])
                        act(st[:, 256:512], ps[:, 256:512], AF.Copy, [pk], [stk])
                        smt, smk = ring("sm", sm)
                        t2, t2k = ring("tm3", tm3)
                        P.op("dve", lambda: nc.vector.memset(smt[:, 0:1], 0.0), [], [smk])
                        act(t2[:, 0:256], st[:, 0:256], AF.Square, [stk, smk], [t2k, smk], accum_out=smt[:, 0:1])
                        act(smt[:, 1:2], smt[:, 0:1], AF.Sqrt, [smk], [smk], bias=EPS, scale=1.0 / 256)
                        recip(smt[:, 2:3], smt[:, 1:2], [smk], [smk])
                        vn, vnk = ring("vn", vn_t)
                        stt(vn[:], st[:, 0:256], smt[:, 2:3], sgn[:], ALU.mult, ALU.mult, [stk, smk, "sgn"], [vnk])
                        if lat:
                            t4, t4k = ring("tm2", tm2)
                            rope(t4[:, 0:256], st[:, 256:512], 4, 64, gb, 0, [stk], [t4k])
                            return st, stk, vn, vnk, t4, t4k
                        return st, stk, vn, vnk, None, None

                    def tmQ_tail(b, st, stk, vn, vnk, t4, t4k):
                        tok = slice(b * 128, (b + 1) * 128)
                        ps2, pk2 = psum()
                        for hd in range(4):
                            cc, e_ = divmod(hd, 2)
                            mm(ps2[:, hd * 128:(hd + 1) * 128], vn[:, cc * 128:(cc + 1) * 128], wsT_t[:, hd * 128:(hd + 1) * 128],
                               True, False, [vnk, "wsT_t"], [pk2], sig=False)
                            mm(ps2[:, hd * 128:(hd + 1) * 128], ones_b[0:1, 0:128], sgub_t[0:1, hd * 128:(hd + 1) * 128],
                               False, True, ["cst", "sgub_t"], [pk2], sig=True)
                        ps3, pk3 = psum()
                        for gg in range(2):
                            src_ap = (t4[:, gg * 128:(gg + 1) * 128] if lat else st[:, 256 + gg * 128:256 + (gg + 1) * 128])
                            P.op("pe", lambda: nc.tensor.transpose(ps3[:, gg * 128:(gg + 1) * 128], src_ap, ident_f[:]),
                                 [t4k if lat else stk, "ident_f"], [pk3])
                        for hd in range(4):
                            cc, e_ = divmod(hd, 2)
                            tt_op(catj[e_ * 64:(e_ + 1) * 64, 2 + cc, tok], catj[e_ * 64:(e_ + 1) * 64, 2 + cc, tok],
                                  ps2[e_ * 64:(e_ + 1) * 64, hd * 128:(hd + 1) * 128], ALU.mult, [f"cat{2 + cc}", pk2], [f"cat{2 + cc}"])
                        act(qs[:, :, tok], ps3[:, 0:256].rearrange("p (g t) -> p g t", g=2), AF.Copy, [pk3], ["qs"])

                    def qm_mm(b):
                        tokq = slice(b * 128, (b + 1) * 128)
                        ps, pk = psum()
                        for c in range(2):
                            mm(ps[:, 0:384], cqn[:, c, tokq], wq_t[:, c, :], c == 0, c == 1, [f"cqn{c}", "wq_t"], [pk])
                        return ps, pk

                    def qm_ew(b, ps, pk):
                        gb = j * 4 + b
                        st, stk = ring("tmf", tmf)
                        act(st[:, 0:384], ps[:, 0:384], AF.Copy, [pk], [stk])
                        if lat:
                            t3, t3k = ring("tm2", tm2)
                            src4 = st[:, 0:384].rearrange("p (h d) -> p h d", h=4)[:, :, 64:96]
                            t5, t5k = ring("tm2", tm2)
                            vcopy(t5[:, 0:128].rearrange("p (h d) -> p h d", h=4), src4, [stk], [t5k])
                            rope(t3[:, 0:128], t5[:, 0:128], 4, 32, gb, 128, [t5k], [t3k])
                            vcopy(src4, t3[:, 0:128].rearrange("p (h d) -> p h d", h=4), [t3k], [stk])
                        return st, stk

                    def qm_tr(b, st, stk):
                        tokq = slice(b * 128, (b + 1) * 128)
                        ps, pk = psum()
                        for h in range(4):
                            P.op("pe", lambda: nc.tensor.transpose(ps[0:96, h * 128:(h + 1) * 128], st[:, h * 96:(h + 1) * 96], ident_f[:]),
                                 [stk, "ident_f"], [pk])
                        src = ps[0:96, :].rearrange("p (h t) -> p h t", h=4)
                        if b % 2 == 0:
                            act(qh[0:96, :, tokq], src, AF.Copy, [pk], [f"qh{h}" for h in range(4)])
                        else:
                            vcopy(qh[0:96, :, tokq], src, [pk], [f"qh{h}" for h in range(4)])

                    P.mark(f"{g}{l} Q{j} tm")
                    r = {}
                    e = {}
                    rq = {}
                    for m in range(3, 6):
                        fmQ_group(m, wb, wbk)
                    cq_conv()
                    r[0] = tmQ_mm(0)
                    r[1] = tmQ_mm(1)
                    rq[0] = qm_mm(0)
                    for b in range(4):
                        e[b] = tmQ_ew(b, *r[b])
                        eq = qm_ew(b, *rq[b])
                        if b + 1 < 4:
                            rq[b + 1] = qm_mm(b + 1)
                        tmQ_tail(b, *e[b])
                        if b + 2 < 4:
                            r[b + 2] = tmQ_mm(b + 2)
                        qm_tr(b, *eq)
                    ckpt(19)
                    set_psring([0, 1, 2, 7])
                    ckpt(20)
                    P.mark(f"{g}{l} Q{j} MLA")
                    if lat:
                        for h in range(4):
                            chunks = []
                            for kc in range(20):
                                jt = kc // 4
                                chunks.append((KT[:, h, kc * 128:(kc + 1) * 128], Vm[:, kc, h * 64:(h + 1) * 64], None,
                                               [f"KT{h}_{jt}", f"Vm_{jt}"]))
                            cc, e = divmod(h, 2)
                            attention(512, qh[:, h, :], [f"qh{h}"], chunks, MLA_SCALE, None,
                                      catj[e * 64:(e + 1) * 64, 4 + cc, :], [f"cat{4 + cc}"])
                    else:
                        for sl in range(2):
                            sidx = (j * 512) // 256 + sl
                            qsl = slice(sl * 256, (sl + 1) * 256)
                            for h in range(4):
                                chunks = []
                                for kc in (2 * sidx, 2 * sidx + 1):
                                    jt = kc // 4
                                    chunks.append((KT[:, h, kc * 128:(kc + 1) * 128], Vm[:, kc, h * 64:(h + 1) * 64], None,
                                                   [f"KT{h}_{jt}", f"Vm_{jt}"]))
                                cc, e = divmod(h, 2)
                                attention(256, qh[:, h, qsl], [f"qh{h}"], chunks, MLA_SCALE, None,
                                          catj[e * 64:(e + 1) * 64, 4 + cc, qsl], [f"cat{4 + cc}"])
                    ckpt(21)
                    if lat and j + 1 < NT:
                        norm_to(l, j + 1, 0)
                    P.mark(f"{g}{l} Q{j} SWA")
                    for n in range(2):
                        sink_l = [sinkexp[0:64, l * 4 + n * 2 + gq:l * 4 + n * 2 + gq + 1] for gq in range(2)]
                        if lat:
                            for bq in range(4):
                                blk = j * 4 + bq
                                tq = slice(bq * 128, (bq + 1) * 128)
                                chunks = []
                                if blk >= 1:
                                    chunks.append((ks[n * 64:(n + 1) * 64, (blk - 1) * 128:blk * 128],
                                                   Vs[:, blk - 1, n * 64:(n + 1) * 64], mprev,
                                                   [f"ks_{(blk - 1) // 4}", f"Vs_{(blk - 1) // 4}"]))
                                chunks.append((ks[n * 64:(n + 1) * 64, blk * 128:(blk + 1) * 128],
                                               Vs[:, blk, n * 64:(n + 1) * 64], None, [f"ks_{blk // 4}", f"Vs_{blk // 4}"]))
                                if blk <= 14:
                                    chunks.append((ks[n * 64:(n + 1) * 64, (blk + 1) * 128:(blk + 2) * 128],
                                                   Vs[:, blk + 1, n * 64:(n + 1) * 64], mnext,
                                                   [f"ks_{(blk + 1) // 4}", f"Vs_{(blk + 1) // 4}"]))
                                for kc in range(16, 20):
                                    chunks.append((ks[n * 64:(n + 1) * 64, kc * 128:(kc + 1) * 128],
                                                   Vs[:, kc, n * 64:(n + 1) * 64], None, [f"ks_{NT}", f"Vs_{NT}"]))
                                attention(128, qs[n * 64:(n + 1) * 64, :, tq], ["qs"], chunks, SWA_SCALE, sink_l,
                                          [catj[gq * 64:(gq + 1) * 64, 6 + n, tq] for gq in range(2)], [f"cat{6 + n}"], G=2)
                        else:
                            for sl in range(2):
                                sidx = (j * 512) // 256 + sl
                                qsl = slice(sl * 256, (sl + 1) * 256)
                                chunks = []
                                for kc in (2 * sidx, 2 * sidx + 1):
                                    chunks.append((ks[n * 64:(n + 1) * 64, kc * 128:(kc + 1) * 128],
                                                   Vs[:, kc, n * 64:(n + 1) * 64], None, [f"ks_{kc // 4}", f"Vs_{kc // 4}"]))
                                attention(256, qs[n * 64:(n + 1) * 64, :, qsl], ["qs"], chunks, SWA_SCALE, sink_l,
                                          [catj[gq * 64:(gq + 1) * 64, 6 + n, qsl] for gq in range(2)], [f"cat{6 + n}"], G=2)
                    ckpt(22)
                    P.mark(f"{g}{l} Q{j} wout")
                    set_psring(range(8))
                    if debug and l == 0 and j == 0:
                        dma("sp", dbg_cat, catj, [f"cat{c}" for c in range(8)], [], is_out=True)
                    for half in range(2):
                        wt, wk_ = get_weight(f"wo{half}{g}{l}{j}")
                        for mi in range(4):
                            m = half * 4 + mi
                            ps, pk = psum()
                            for c in range(8):
                                mm(ps[:, :], wt[:, c, mi * 128:(mi + 1) * 128], catj[:, c, :], c == 0, c == 7, [wk_, f"cat{c}"], [pk])
                            xs_ = x[:, m, j * 512:(j + 1) * 512]
                            stt(xs_, ps[:, :], modv[:, l, 2, m, v:v + 1], xs_, ALU.mult, ALU.add, [pk, f"modt{l}", f"x{m}_{j}"], [f"x{m}_{j}"])
                P.barrier()
                if debug and l == 0:
                    dma("sp", dbg_x1[:, :, 0:T], x[:, :, 0:T], [], [], is_out=True)
                    P.barrier()
                ckpt(23)
                P.mark(f"{g}{l} MLP norm")
                set_psring(range(8))
                if l + 1 < depth:
                    small_weights(l + 1)
                ckpt(24)
                P.mark(f"{g}{l} MLP mm")
                ada_pc = [0]
                for jh in range(8):
                    wa, wak = get_weight(f"w1{g}{l}{jh}")
                    wb, wbk = get_weight(f"w2{g}{l}{jh}", ahead=NST - 2)
                    for j in range(NT):
                        if jh == 0:
                            norm(l, 1, j, lambda c: h2[:, c, j * 512:(j + 1) * 512], lambda c: f"h2{c}_{j}")
                        u, uk = ring("ub", ub)
                        for hc in range(4):
                            ps, pk = psum()
                            for c in range(8):
                                mm(ps[:, :], wa[:, c, hc * 128:(hc + 1) * 128], h2[:, c, j * 512:(j + 1) * 512], c == 0, c == 7,
                                   [wak, f"h2{c}_{j}"], [pk])
                            r_, rk_ = ring("relu", relu_t)
                            act(r_[:], ps[:, :], AF.Relu, [pk], [rk_])
                            tt_op(u[:, hc, :], r_[:], r_[:], ALU.mult, [rk_], [f"{uk}_{hc}"])
                        for m in range(8):
                            ps, pk = psum()
                            for hc in range(4):
                                mm(ps[:, :], wb[:, hc, m * 128:(m + 1) * 128], u[:, hc, :], hc == 0, hc == 3, [wbk, f"{uk}_{hc}"], [pk])
                            xs_ = x[:, m, j * 512:(j + 1) * 512]
                            stt(xs_, ps[:, :], modv[:, l, 5, m, v:v + 1], xs_, ALU.mult, ALU.add, [pk, f"modt{l}", f"x{m}_{j}"], [f"x{m}_{j}"])
                    if ada_defer and g == "A" and l + 1 < DEPTH:
                        for _ in range(ADA_SPLIT[jh]):
                            ada_piece(l + 1, ada_pc[0])
                            ada_pc[0] += 1
                        if jh == 7:
                            ada_finish(l + 1)
                P.barrier()
                if debug and l == 0:
                    dma("sp", dbg_x2[:, :, 0:T], x[:, :, 0:T], [], [], is_out=True)
                    P.barrier()
            P.mark(f"{g} final")
            for j in range(NT):
                ps, pk = psum()
                for c in range(8):
                    sq, sqk = ring("sqt", sqt)
                    act(sq[:], x[:, c, j * 512:(j + 1) * 512], AF.Square, [f"x{c}_{j}"], [sqk])
                    mm(ps[:, :], ones_b, sq[:], c == 0, c == 7, [sqk, "cst"], [pk])
                act(s_t[:], ps[:, :], AF.Sqrt, [pk], ["s_t", "rstd"], bias=EPS, scale=1.0 / D)
                recip(rstd[:], s_t[:], ["s_t"], ["s_t", "rstd"])
                for c in range(8):
                    y_, yk = ring("tt", tt)
                    stt(y_[:], x[:, c, j * 512:(j + 1) * 512], vecs[:, 64 + c:65 + c], rstd[:], ALU.mult, ALU.mult,
                        [f"x{c}_{j}", "vecs", "rstd"], [yk])
                    dma("sp", yT[c * 128:(c + 1) * 128, j * 512:(j + 1) * 512], y_[:], [yk], [], is_out=True)
            P.barrier()

        try:
            for g_ in groups:
                run_group(g_)
            P.mark("end")
            P.finish()
        except StopBuild:
            pass
        MARKS.extend(P.marks)
    return nc


_CACHE = {}


def _consts():
    ident = np.eye(128, dtype=np.float32)
    ones = np.ones((128, 128), np.float32)
    kk = np.arange(128)[:, None]
    qq = np.arange(128)[None, :]
    mprev = np.where(kk >= qq, 0.0, NEG).astype(np.float32)
    mnext = np.where(kk <= qq, 0.0, NEG).astype(np.float32)
    c = np.concatenate([ident, ones, mprev, mnext, np.zeros((128, 128), np.float32)], axis=1)
    def tables(rot_dim):
        half = rot_dim // 2
        inv = (10000.0 ** (-np.arange(0, half, 2, dtype=np.float32) / half)).astype(np.float32)
        t = np.arange(2048)
        row = (t // 64).astype(np.float32)
        col = (t % 64).astype(np.float32)
        ar = row[:, None] * inv[None, :]
        ac = col[:, None] * inv[None, :]
        ang = np.concatenate([ar, ar, ac, ac], axis=-1).astype(np.float32)
        cos = np.cos(ang).astype(np.float32)
        sin = np.sin(ang).astype(np.float32)
        q = rot_dim // 4
        sgn = np.concatenate([-np.ones(q), np.ones(q), -np.ones(q), np.ones(q)]).astype(np.float32)
        return cos, sin * sgn[None, :]
    cs, ss = tables(64)
    cm, sm_ = tables(32)
    r = np.concatenate([cs, ss, cm, sm_], axis=1)
    r = r.reshape(16, 128, 192).transpose(1, 0, 2).reshape(128, 16 * 192)
    return np.ascontiguousarray(c), np.ascontiguousarray(r.astype(np.float32))


def kernel(x_prompt, x_sample, cache_mla_ckv, cache_mla_kpe, cache_swa_k, cache_swa_v, c, c_ctx,
           w_ada, b_ada, norm1, norm2, w_in, conv_w, sgu_norm, sgu_w, sgu_b, mla_q_norm, mla_w_q_up,
           mla_kv_norm, mla_w_kv_up, swa_sink, w_out, mlp_w1, mlp_w2, final_norm):
    in_maps = pack_inputs(x_prompt, x_sample, cache_mla_ckv, cache_mla_kpe, cache_swa_k, cache_swa_v, c, c_ctx,
                          w_ada, b_ada, norm1, norm2, w_in, conv_w, sgu_norm, sgu_w, sgu_b, mla_q_norm, mla_w_q_up,
                          mla_kv_norm, mla_w_kv_up, swa_sink, w_out, mlp_w1, mlp_w2, final_norm)
    if "nc" not in _CACHE:
        _CACHE["nc"] = build_program()
    nc = _CACHE["nc"]
    res = run_bass_kernel_spmd(nc, in_maps, core_ids=list(range(NCORES)))
    return unpack_outputs(res.results)


def pack_inputs(x_prompt, x_sample, cache_mla_ckv, cache_mla_kpe, cache_swa_k, cache_swa_v, c, c_ctx,
                w_ada, b_ada, norm1, norm2, w_in, conv_w, sgu_norm, sgu_w, sgu_b, mla_q_norm, mla_w_q_up,
                mla_kv_norm, mla_w_kv_up, swa_sink, w_out, mlp_w1, mlp_w2, final_norm, cores=range(NCORES)):
    f = lambda a: np.ascontiguousarray(np.asarray(a, dtype=np.float32))
    x_prompt, x_sample = f(x_prompt), f(x_sample)
    consts, ropes = _consts()
    w_in = f(w_in)
    a_b, a_c, a_x = w_in[:, :, 0:256], w_in[:, :, 256:512], w_in[:, :, 512:768]
    u_, v_ = w_in[:, :, 768:1024], w_in[:, :, 1024:1280]
    cq, ckv, kpe = w_in[:, :, 1280:1536], w_in[:, :, 1536:1664], w_in[:, :, 1664:1696]
    sq, sk, sv = w_in[:, :, 1696:1952], w_in[:, :, 1952:2080], w_in[:, :, 2080:2208]
    sq_g = sq.reshape(DEPTH, D, 2, 2, 64).transpose(0, 1, 3, 2, 4).reshape(DEPTH, D, 256)
    w_fm = f(np.concatenate([a_c, a_x, a_b, u_, cq], axis=2))
    w_tm = f(np.concatenate([ckv, kpe, sk, sv, v_, sq_g], axis=2))
    bada_fm = f(np.asarray(b_ada).reshape(DEPTH, 48, 128).transpose(2, 0, 1).reshape(128, DEPTH * 48))
    vecs = np.zeros((128, 128), np.float32)
    vecs[:, 0:32] = np.asarray(norm1).reshape(DEPTH, 8, 128).transpose(2, 0, 1).reshape(128, 32)
    vecs[:, 32:64] = np.asarray(norm2).reshape(DEPTH, 8, 128).transpose(2, 0, 1).reshape(128, 32)
    vecs[:, 64:72] = np.asarray(final_norm).reshape(8, 128).T
    vecs[:, 72:80] = np.asarray(mla_q_norm).reshape(DEPTH, 2, 128).transpose(2, 0, 1).reshape(128, 8)
    vecs[:, 80:104] = np.asarray(conv_w).reshape(DEPTH, 3, 2, 128).transpose(3, 0, 2, 1).reshape(128, 24)
    sgunorm_bc = f(np.broadcast_to(np.asarray(sgu_norm).reshape(1, DEPTH * 256), (128, DEPTH * 256)))
    kvnorm_bc = f(np.broadcast_to(np.asarray(mla_kv_norm).reshape(1, DEPTH * 128), (128, DEPTH * 128)))
    sink_bc = f(np.broadcast_to(np.asarray(swa_sink).reshape(1, 16), (128, 16)))
    sgub = f(np.asarray(sgu_b).reshape(1, DEPTH * 512))
    sgu_wT = f(np.asarray(sgu_w).transpose(0, 3, 1, 2).reshape(DEPTH, 128, 512))
    shared = dict(w_ada=f(w_ada), bada_fm=bada_fm, vecs_fm=vecs, sgunorm_bc=sgunorm_bc, kvnorm_bc=kvnorm_bc,
                  sink_bc=sink_bc, sgub=sgub, sgu_wT=sgu_wT, w_fm=w_fm, w_tm=w_tm, w_out=f(w_out), w1=f(mlp_w1),
                  w2=f(mlp_w2), wq=f(mla_w_q_up), wkv=f(mla_w_kv_up), ropes=ropes, consts=consts)
    c = np.asarray(c, np.float32)
    c_ctx = np.asarray(c_ctx, np.float32)
    in_maps = []
    for i in cores:
        cv = np.stack([c_ctx, c[i]], axis=0)
        cfm = f(cv.reshape(2, 8, 128).transpose(2, 1, 0).reshape(128, 16))
        m = dict(shared)
        m.update(
            xsT=f(x_sample[i].T),
            xpT=f(x_prompt[4 * i:4 * i + 4].reshape(1024, D).T),
            ckvT_c=f(np.asarray(cache_mla_ckv[i]).transpose(0, 2, 1)),
            kpeT_c=f(np.asarray(cache_mla_kpe[i]).transpose(0, 2, 1)),
            skT_c=f(np.asarray(cache_swa_k[i]).reshape(DEPTH, 512, 128).transpose(0, 2, 1)),
            sv_c=f(np.asarray(cache_swa_v[i]).reshape(DEPTH, 512, 128)),
            cfm=cfm,
        )
        in_maps.append(m)
    return in_maps


def unpack_outputs(rs):
    y_prompt = np.concatenate([r["ypT"].T.reshape(4, 256, D) for r in rs], axis=0).astype(np.float32)
    y_sample = np.stack([r["ysT"].T for r in rs], axis=0).astype(np.float32)
    new_ckv = np.concatenate([r["o_ckv"] for r in rs], axis=0).astype(np.float32)
    new_kpe = np.concatenate([r["o_kpe"] for r in rs], axis=0).astype(np.float32)
    new_k = np.concatenate([r["o_k"] for r in rs], axis=0).reshape(-1, DEPTH, 256, 2, 64).astype(np.float32)
    new_v = np.concatenate([r["o_v"] for r in rs], axis=0).reshape(-1, DEPTH, 256, 2, 64).astype(np.float32)
    return (np.ascontiguousarray(y_prompt), np.ascontiguousarray(y_sample), new_ckv, new_kpe, new_k, new_v)
```

```python
import numpy as np
import concourse.bass as bass
import concourse.mybir as mybir
from concourse.bass_utils import run_bass_kernel_spmd

F32, BF16 = mybir.dt.float32, mybir.dt.bfloat16
AF = mybir.ActivationFunctionType
ALU = mybir.AluOpType

D = 1024
DEPTH = 4
EPS = 1e-6
MLA_SCALE = 96 ** -0.5
SWA_SCALE = 0.125
NEG = -30000.0
NCORES = 8


class Info:
    __slots__ = ("sem", "val", "clock", "eng")

    def __init__(self, eng):
        self.sem = None
        self.val = 0
        self.clock = None
        self.eng = eng


class Prog:
    COMPUTE = ("pe", "act", "dve")
    QUEUES = ("sp", "pool")
    NSL = 6

    def __init__(self, nc):
        self.nc = nc
        self.eng = {"pe": nc.tensor, "act": nc.scalar, "dve": nc.vector, "pool": nc.gpsimd, "sp": nc.sync}
        self.sems = {}
        self.epoch = 0
        self.csem = {}
        self.cnt = {}
        self._new_epoch_sems()
        self.slots = {}
        self.nd = {q: 0 for q in self.QUEUES}
        for q in self.QUEUES:
            self.slots[q] = []
            for i in range(self.NSL):
                nm = f"d_{q}{i}"
                self.sems[nm] = nc.alloc_semaphore(name=nm)
                self.slots[q].append([nm, 0])
        self.clock = {e: {} for e in self.eng}
        self.last_w = {}
        self.readers = {}
        self.pending = {e: [] for e in self.COMPUTE}
        self.out_infos = []
        self.total = {e: 0 for e in self.eng}
        self.ps_open = {}
        self.marks = []

    def _new_epoch_sems(self):
        for e in self.COMPUTE:
            nm = f"c_{e}{self.epoch}"
            self.sems[nm] = self.nc.alloc_semaphore(name=nm)
            self.csem[e] = nm
            self.cnt[e] = 0
        self.epoch += 1

    def _wait(self, e, info):
        if info.sem is None:
            raise RuntimeError("dependency on unsignaled op")
        ck = self.clock[e]
        if ck.get(info.sem, 0) >= info.val:
            return
        self.eng[e].wait_ge(self.sems[info.sem], info.val)
        for s, v in info.clock.items():
            if ck.get(s, 0) < v:
                ck[s] = v

    def op(self, e, fn, reads=(), writes=(), sig=True, dma=False, is_out=False):
        psr = [k for k in reads if k.startswith("ps")]
        for k in psr:
            self.ps_open[k] = False
        if psr:
            writes = list(writes) + [k for k in psr if k not in writes]
        deps = []
        for k in reads:
            w = self.last_w.get(k)
            if w is not None:
                deps.append((w, True))
        for k in writes:
            w = self.last_w.get(k)
            if w is not None:
                deps.append((w, False))
            rd = self.readers.get(k)
            if rd:
                for r in rd.values():
                    deps.append((r, False))
        for info, raw in deps:
            if info.eng == e and not dma:
                if e == "pe":
                    continue
            self._wait(e, info)
        self.total[e] += 1
        info = Info(e)
        if dma:
            n = self.nd[e]
            self.nd[e] += 1
            slot = self.slots[e][n % self.NSL]
            ck = self.clock[e]
            if slot[1] > 0 and ck.get(slot[0], 0) < slot[1]:
                self.eng[e].wait_ge(self.sems[slot[0]], slot[1])
                ck[slot[0]] = slot[1]
            slot[1] += 16
            inst = fn()
            inst.then_inc(self.sems[slot[0]], 16)
            info.sem, info.val = slot[0], slot[1]
            info.clock = dict(ck)
            info.clock[info.sem] = info.val
            info.eng = e + "_dma%d" % n
            if is_out:
                self.out_infos.append(info)
        else:
            inst = fn()
            if e == "pe" and self.marks and self.marks[-1][1] is None:
                nm_ = inst.ins.name
                for mk in reversed(self.marks):
                    if mk[1] is not None:
                        break
                    mk[1] = nm_
            if sig:
                self.cnt[e] += 1
                inst.then_inc(self.sems[self.csem[e]], 1)
                info.sem, info.val = self.csem[e], self.cnt[e]
                info.clock = dict(self.clock[e])
                info.clock[info.sem] = info.val
                for p in self.pending[e]:
                    p.sem, p.val, p.clock = info.sem, info.val, info.clock
                self.pending[e] = []
            else:
                self.pending[e].append(info)
        for k in writes:
            self.last_w[k] = info
            self.readers[k] = {}
        for k in reads:
            self.readers.setdefault(k, {})[info.eng] = info
        return info

    def barrier(self):
        for e in self.COMPUTE:
            assert not self.pending[e]
        for f in self.eng:
            ck = self.clock[f]
            for e in self.COMPUTE:
                nm, v = self.csem[e], self.cnt[e]
                if v > 0 and e != f and ck.get(nm, 0) < v:
                    self.eng[f].wait_ge(self.sems[nm], v)
                if v > 0:
                    ck[nm] = v
            for q in self.QUEUES:
                for nm, v in self.slots[q]:
                    if v > 0 and ck.get(nm, 0) < v:
                        self.eng[f].wait_ge(self.sems[nm], v)
                        ck[nm] = v
        for e in self.COMPUTE:
            if self.cnt[e] > 0:
                self.eng[e].wait_ge(self.sems[self.csem[e]], self.cnt[e])
        self.last_w = {}
        self.readers = {}
        self._new_epoch_sems()

    def mark(self, label):
        self.marks.append([label, None])

    def finish(self):
        for info in self.out_infos:
            self._wait("sp", info)


MARKS = []


class StopBuild(Exception):
    pass


def build_program(depth=DEPTH, groups="AB", debug=False, p0=9):
    nc = bass.Bass("TRN2", target_bir_lowering=False)
    P = Prog(nc)

    def din(name, shape):
        return nc.dram_tensor(name, list(shape), F32, kind="ExternalInput").ap()

    def dout(name, shape):
        return nc.dram_tensor(name, list(shape), F32, kind="ExternalOutput").ap()

    xsT = din("xsT", (D, 2048))
    xpT = din("xpT", (D, 1024))
    ckvT_c = din("ckvT_c", (DEPTH, 128, 512))
    kpeT_c = din("kpeT_c", (DEPTH, 32, 512))
    skT_c = din("skT_c", (DEPTH, 128, 512))
    sv_c = din("sv_c", (DEPTH, 512, 128))
    cfm = din("cfm", (128, 16))
    w_ada = din("w_ada", (DEPTH, D, 6 * D))
    bada_fm = din("bada_fm", (128, DEPTH * 48))
    vecs_fm = din("vecs_fm", (128, 128))
    sgunorm_bc = din("sgunorm_bc", (128, DEPTH * 256))
    kvnorm_bc = din("kvnorm_bc", (128, DEPTH * 128))
    sink_bc = din("sink_bc", (128, 16))
    sgub = din("sgub", (1, DEPTH * 512))
    sgu_wT = din("sgu_wT", (DEPTH, 128, 512))
    w_fm = din("w_fm", (DEPTH, D, 1280))
    w_tm = din("w_tm", (DEPTH, D, 928))
    w_out = din("w_out", (DEPTH, D, D))
    w1 = din("w1", (DEPTH, D, 4096))
    w2 = din("w2", (DEPTH, 4096, D))
    wq = din("wq", (DEPTH, 256, 384))
    wkv = din("wkv", (DEPTH, 128, 512))
    ropes = din("ropes", (128, 16 * 192))
    consts = din("consts", (128, 640))
    ysT = dout("ysT", (D, 2048))
    ypT = dout("ypT", (D, 1024))
    o_ckv = dout("o_ckv", (4, DEPTH, 256, 128))
    o_kpe = dout("o_kpe", (4, DEPTH, 256, 32))
    o_k = dout("o_k", (4, DEPTH, 256, 128))
    o_v = dout("o_v", (4, DEPTH, 256, 128))
    if debug:
        dbg_h = nc.dram_tensor("dbg_h", [128, 8, 512], BF16, kind="ExternalOutput").ap()
        dbg_cat = nc.dram_tensor("dbg_cat", [128, 8, 512], BF16, kind="ExternalOutput").ap()
        dbg_x1 = dout("dbg_x1", (128, 8, 2048))
        dbg_x2 = dout("dbg_x2", (128, 8, 2048))

    A = nc.alloc_sbuf_tensor
    ident_f = A("ident_f", [128, 128], F32)
    cst = A("cst", [128, 512], BF16)
    ident_b, ones_b, mprev, mnext = cst[:, 0:128], cst[:, 128:256], cst[:, 256:384], cst[:, 384:512]
    modt = A("modt", [128, DEPTH * 48 * 2], F32)
    badat = A("badat", [128, DEPTH * 48], F32)
    vecs = A("vecs", [128, 128], F32)
    gs = A("gs", [128, DEPTH * 2 * 8 * 2], F32)
    csil = A("csil", [128, 16], F32)
    csil_b = A("csil_b", [128, 16], BF16)
    sinkexp = A("sinkexp", [128, 16], F32)
    ropet = A("ropet", [128, 16 * 192], BF16)
    sgn = A("sgn", [128, 256], F32)
    kvn = A("kvn", [128, 128], F32)
    wq_t = A("wq_t", [128, 2, 384], BF16)
    wkv_t = A("wkv_t", [128, 512], BF16)
    wsT_t = A("wsT_t", [128, 512], BF16)
    sgub_t = A("sgub_t", [1, 512], BF16)
    x = A("x", [128, 8, 2048], F32)
    R = A("R", [128, 36352], BF16)
    NST = 3
    wst = [A(f"wst{i}", [128, 4096], BF16) for i in range(NST)]
    sqt = [A(f"sqt{i}", [128, 512], BF16) for i in range(2)]
    tt = [A(f"tt{i}", [128, 512], F32) for i in range(2)]
    s_t = A("s_t", [128, 512], F32)
    rstd = s_t
    tmf = [A(f"tmf{i}", [128, 512], F32) for i in range(2)]
    tm2 = [A(f"tm2{i}", [128, 256], F32) for i in range(4)]
    tm3 = [A(f"tm3{i}", [128, 256], F32) for i in range(2)]
    sm = [A(f"sm{i}", [128, 4], F32) for i in range(4)]
    vn_t = [A(f"vn{i}", [128, 256], BF16) for i in range(2)]
    pt = [A(f"pt{i}", [128, 512], BF16) for i in range(4)]
    rden = [A(f"rden{i}", [64, 512], F32) for i in range(1)]
    qh = A("qh", [128, 4, 512], BF16)
    relu_t = [A(f"relu{i}", [128, 512], BF16) for i in range(2)]
    cqraw = relu_t
    yo = tt
    PS = [nc.alloc_psum_tensor(f"ps{i}", [128, 512], F32) for i in range(8)]

    rr = {}

    def ring(name, lst):
        i = rr.get(name, 0)
        rr[name] = i + 1
        j = i % len(lst)
        return lst[j], f"{name}{j}"

    psring = {"lst": list(range(8)), "i": 0}

    def psum():
        lst = psring["lst"]
        b = lst[psring["i"] % len(lst)]
        psring["i"] += 1
        assert not P.ps_open.get(f"ps{b}", False), f"psum bank ps{b} re-allocated before its previous contents were read"
        P.ps_open[f"ps{b}"] = True
        return PS[b], f"ps{b}"

    def set_psring(lst):
        psring["lst"] = list(lst)
        psring["i"] = 0

    def mm(out, lhsT, rhs, start, stop, reads, writes, sig=None):
        if sig is None:
            sig = True
        return P.op("pe", lambda: nc.tensor.matmul(out, lhsT=lhsT, rhs=rhs, start=start, stop=stop),
                    reads=reads, writes=writes, sig=sig)

    def act(out, in_, func, reads, writes, bias=0.0, scale=1.0, accum_out=None):
        kw = {}
        if accum_out is not None:
            kw["accum_out"] = accum_out
        return P.op("act", lambda: nc.scalar.activation(out=out, in_=in_, func=func, bias=bias, scale=scale, **kw),
                    reads=reads, writes=writes)

    def tt_op(out, in0, in1, op, reads, writes):
        return P.op("dve", lambda: nc.vector.tensor_tensor(out=out, in0=in0, in1=in1, op=op), reads=reads, writes=writes)

    def ts_op(out, in0, s1, op0, reads, writes, s2=None, op1=None):
        if op1 is None:
            return P.op("dve", lambda: nc.vector.tensor_scalar(out=out, in0=in0, scalar1=s1, scalar2=None, op0=op0),
                        reads=reads, writes=writes)
        return P.op("dve", lambda: nc.vector.tensor_scalar(out=out, in0=in0, scalar1=s1, scalar2=s2, op0=op0, op1=op1),
                    reads=reads, writes=writes)

    def stt(out, in0, scalar, in1, op0, op1, reads, writes):
        return P.op("dve", lambda: nc.vector.scalar_tensor_tensor(out=out, in0=in0, scalar=scalar, in1=in1, op0=op0, op1=op1),
                    reads=reads, writes=writes)

    def vcopy(out, in_, reads, writes):
        return P.op("dve", lambda: nc.vector.tensor_copy(out=out, in_=in_), reads=reads, writes=writes)

    def recip(out, in_, reads, writes):
        return P.op("dve", lambda: nc.vector.reciprocal(out=out, in_=in_), reads=reads, writes=writes)

    def dma(q, out, in_, reads, writes, is_out=False):
        e = nc.sync if q == "sp" else nc.gpsimd
        return P.op(q, lambda: e.dma_start(out=out, in_=in_), reads=reads, writes=writes, dma=True, is_out=is_out)

    wplan = []
    wstate = {"issued": 0, "used": 0}

    ada_defer = (len(groups) > 0 and groups[0] == "A")
    ada_phase0 = [0] if ada_defer else list(range(DEPTH))
    ADA_SPLIT = [2, 2, 2, 2, 1, 1, 1, 1]

    def plan_weights():
        for l in ada_phase0:
            for pc in range(12):
                wplan.append((f"ada{l}_{pc}", w_ada[l, :, pc * 512:(pc + 1) * 512], (8, 512)))
        for g in groups:
            NT = 4 if g == "A" else 2
            for l in range(depth):
                for j in range(NT):
                    wplan.append((f"fmK{g}{l}{j}", w_fm[l, :, 0:512], (8, 512)))
                    wplan.append((f"tmK{g}{l}{j}", w_tm[l, :, 0:416], (8, 416)))
                for j in range(NT):
                    wplan.append((f"fmQa{g}{l}{j}", w_fm[l, :, 512:896], (8, 384)))
                    wplan.append((f"fmQb{g}{l}{j}", w_fm[l, :, 896:1280], (8, 384)))
                    wplan.append((f"tmQ{g}{l}{j}", w_tm[l, :, 416:928], (8, 512)))
                    wplan.append((f"wo0{g}{l}{j}", w_out[l, :, 0:512], (8, 512)))
                    wplan.append((f"wo1{g}{l}{j}", w_out[l, :, 512:1024], (8, 512)))
                pcn = 0
                for jh in range(8):
                    wplan.append((f"w1{g}{l}{jh}", w1[l, :, jh * 512:(jh + 1) * 512], (8, 512)))
                    wplan.append((f"w2{g}{l}{jh}", w2[l, jh * 512:(jh + 1) * 512, :], (4, 1024)))
                    if ada_defer and g == "A" and l + 1 < DEPTH:
                        for _ in range(ADA_SPLIT[jh]):
                            wplan.append((f"ada{l + 1}_{pcn}", w_ada[l + 1, :, pcn * 512:(pcn + 1) * 512], (8, 512)))
                            pcn += 1

    def issue_weight():
        i = wstate["issued"]
        if i >= len(wplan):
            return
        name, ap, (nc_, ncol) = wplan[i]
        slot = wst[i % NST]
        dst = slot[:, 0:nc_ * ncol].rearrange("p (c n) -> p c n", c=nc_)
        src = ap.rearrange("(c p) n -> p c n", p=128)
        dma("pool", dst, src, reads=[], writes=[f"wst{i % NST}"])
        wstate["issued"] += 1

    def get_weight(name, ahead=NST - 1):
        i = wstate["used"]
        assert wplan[i][0] == name, (wplan[i][0], name)
        while wstate["issued"] < min(i + ahead + 1, len(wplan)):
            issue_weight()
        wstate["used"] += 1
        nc_, ncol = wplan[i][2]
        return wst[i % NST][:, 0:nc_ * ncol].rearrange("p (c n) -> p c n", c=nc_), f"wst{i % NST}"

    plan_weights()
    MARKS.clear()

    def ckpt(n):
        if p0 == n:
            P.barrier()
            P.finish()
            raise StopBuild()

    with nc.allow_low_precision("bf16 matmuls"), nc.allow_non_contiguous_dma("small strided loads"):
        dma("sp", ident_f[:], consts[:, 0:128], [], ["ident_f"])
        dma("pool", cst[:], consts[:, 0:512], [], ["cst"])
        dma("sp", badat[:], bada_fm[:, :], [], ["badat"])
        dma("sp", vecs[:], vecs_fm[:, :], [], ["vecs"])
        dma("sp", csil[:], cfm[:, :], [], ["csil"])
        dma("sp", sinkexp[:], sink_bc[:, :], [], ["sinkexp"])
        dma("pool", ropet[:], ropes[:, :], [], ["ropet"])
        act(csil[:], csil[:], AF.Silu, ["csil"], ["csil"])
        act(sinkexp[:], sinkexp[:], AF.Exp, ["sinkexp"], ["sinkexp"])
        if p0 == 1:
            P.barrier()
            P.finish()
            return nc
        modv = modt[:].rearrange("p (l k c v) -> p l k c v", l=DEPTH, k=6, c=8)
        gsv = gs[:].rearrange("p (l n c v) -> p l n c v", l=DEPTH, n=2, c=8)
        modt3 = modt[:].rearrange("p (m v) -> p m v", v=2)
        vcopy(csil_b[:], csil[:], ["csil"], ["csil_b"])
        P.op("dve", lambda: nc.vector.memset(qh[96:128, :, :], 0.0), [], [f"qh{h}" for h in range(4)])

        def ada_piece(l, pc):
            wt, wk_ = get_weight(f"ada{l}_{pc}")
            ps, pk = psum()
            for c in range(8):
                mm(ps[0:2, :], csil_b[:, c * 2:c * 2 + 2], wt[:, c, :], c == 0, c == 7, [wk_, "csil_b"], [pk])
            t_, tk_ = ring("tt", tt)
            t_ = t_[0:2, :]
            act(t_, ps[0:2, :], AF.Copy, [pk], [tk_])
            ps2, pk2 = psum()
            for mi in range(4):
                P.op("pe", lambda: nc.tensor.transpose(ps2[:, mi * 2:mi * 2 + 2], t_[:, mi * 128:(mi + 1) * 128], ident_f[0:2, 0:2]),
                     [tk_, "ident_f"], [pk2])
            m0 = l * 48 + pc * 4
            tt_op(modt3[:, m0:m0 + 4, :], ps2[:, 0:8].rearrange("p (m v) -> p m v", v=2),
                  badat[:, m0:m0 + 4].unsqueeze(2).to_broadcast([128, 4, 2]), ALU.add, [pk2, "badat"], [f"modt{l}"])

        def ada_finish(l):
            for n in range(2):
                nv = vecs[:, n * 32 + l * 8:n * 32 + l * 8 + 8]
                ts_op(gsv[:, l, n], modv[:, l, 3 * n + 1], 1.0, ALU.add, [f"modt{l}"], [f"gs{l}"])
                tt_op(gsv[:, l, n], gsv[:, l, n], nv.unsqueeze(2).to_broadcast([128, 8, 2]), ALU.mult, [f"gs{l}", "vecs"], [f"gs{l}"])

        for l in ada_phase0:
            for pc in range(12):
                ada_piece(l, pc)
            ada_finish(l)
        P.barrier()

        def run_group(g):
            lat = (g == "A")
            T = 2048 if lat else 1024
            S = 2048 if lat else 256
            NT = T // 512
            NK = 2560 if lat else 1024
            NB = NK // 128
            v = 1 if lat else 0
            xT = xsT if lat else xpT
            yT = ysT if lat else ypT
            o = 0

            def carve(n, c=None):
                nonlocal o
                ap = R[:, o:o + n]
                o += n
                if c is not None:
                    ap = ap.rearrange("p (c t) -> p c t", c=c)
                return ap
            KT = carve(4 * NK, 4)
            Vm = carve(NB * 256, NB)
            ks = carve(NK)
            Vs = carve(NB * 128, NB)
            pbuf = carve(2 * T, 2)
            hjb = [carve(8 * 512, 8)]
            if not lat:
                hjb.append(carve(8 * 512, 8))
            H = {"ap": hjb[0], "k": "hj0_"}

            def use_buf(bi):
                H["ap"] = hjb[bi]
                H["k"] = f"hj{bi}_"

            def norm_to(l, j, bi):
                norm(l, 0, j, lambda c: hjb[bi][:, c, :], lambda c: f"hj{bi}_{c}")
            catj = carve(8 * 512, 8)
            cqn = carve(2 * 512, 2)
            qs = carve(2 * 512, 2)
            acc = carve(2 * 512, 2)
            ckT = carve(512)
            assert o <= 36352, o
            h2 = R[:, 0:8 * T].rearrange("p (c t) -> p c t", c=8)
            ub = [R[:, 8 * T + i * 2048:8 * T + (i + 1) * 2048].rearrange("p (c t) -> p c t", c=4) for i in range(2)]

            for c in range(8):
                dma("sp", x[:, c, 0:T], xT[c * 128:(c + 1) * 128, :], [], [f"x{c}_{j}" for j in range(NT)])

            def norm(l, n, j, dst, dkeys):
                ps, pk = psum()
                for c in range(8):
                    sq, sqk = ring("sqt", sqt)
                    act(sq[:], x[:, c, j * 512:(j + 1) * 512], AF.Square, [f"x{c}_{j}"], [sqk])
                    mm(ps[:, :], ones_b, sq[:], c == 0, c == 7, [sqk, "cst"], [pk])
                act(s_t[:], ps[:, :], AF.Sqrt, [pk], ["s_t", "rstd"], bias=EPS, scale=1.0 / D)
                recip(rstd[:], s_t[:], ["s_t"], ["s_t", "rstd"])
                for c in range(8):
                    t, tk = ring("tt", tt)
                    tt_op(t[:], x[:, c, j * 512:(j + 1) * 512], rstd[:], ALU.mult, [f"x{c}_{j}", "rstd"], [tk])
                    act(dst(c), t[:], AF.Identity, [tk, f"gs{l}", f"modt{l}"], [dkeys(c)],
                        bias=modv[:, l, 3 * n, c, v:v + 1], scale=gsv[:, l, n, c, v:v + 1])

            def attention(NQ, qT, qkeys, chunks, scale, sink_ap, out_ap, out_keys, G=1, LA=2):
                W = G * NQ
                CH = 512 // W
                ai = rr.get("acc", 0) % 2
                rr["acc"] = rr.get("acc", 0) + 1
                nump, numk = PS[3 + ai * 2], f"ps{3 + ai * 2}"
                denp, denk = PS[4 + ai * 2], f"ps{4 + ai * 2}"
                ones64 = ones_b[:, 0:64]
                n = len(chunks)
                groups_ = [chunks[g0:g0 + CH] for g0 in range(0, n, CH)]

                def view(ap2d):
                    return ap2d if G == 1 else ap2d.rearrange("p (g t) -> p g t", g=G)

                def emit_S(grp):
                    sb, sbk = psum()
                    for i, (kT, vv, mask, keys) in enumerate(grp):
                        mm(view(sb[:, i * W:(i + 1) * W]), kT, qT, True, mask is None, keys + qkeys, [sbk])
                        if mask is not None:
                            mrhs = mask if G == 1 else mask.unsqueeze(1).to_broadcast([128, G, NQ])
                            mm(view(sb[:, i * W:(i + 1) * W]), ident_b, mrhs, False, True, ["cst"], [sbk])
                    return sb, sbk

                ng = len(groups_)
                sbs = {}
                for gi in range(min(LA, ng)):
                    sbs[gi] = emit_S(groups_[gi])
                idx = 0
                for g0 in range(0, ng, LA):
                    cur = list(range(g0, min(g0 + LA, ng)))
                    for gi in range(g0 + LA, min(g0 + 2 * LA, ng)):
                        sbs[gi] = emit_S(groups_[gi])
                    pts = {}
                    for gi in cur:
                        sb, sbk = sbs.pop(gi)
                        p_, pk_ = ring("pt", pt)
                        w = len(groups_[gi]) * W
                        act(p_[:, 0:w], sb[:, 0:w], AF.Exp, [sbk], [pk_], scale=scale)
                        pts[gi] = (p_, pk_)
                    for gi in cur:
                        p_, pk_ = pts[gi]
                        for i, (kT, vv, mask, keys) in enumerate(groups_[gi]):
                            first = (idx == 0)
                            last = (idx == n - 1)
                            idx += 1
                            mm(nump[0:64, 0:W], vv, p_[:, i * W:(i + 1) * W], first, last, keys + [pk_], [numk])
                            mm(denp[0:64, 0:W], ones64, p_[:, i * W:(i + 1) * W], first, last, [pk_, "cst"], [denk])
                rd, rdk = ring("rden", rden)
                sinks = sink_ap if isinstance(sink_ap, list) else [sink_ap] * G
                outs = out_ap if isinstance(out_ap, list) else [out_ap]
                if sinks[0] is not None:
                    for gg in range(G):
                        ts_op(rd[:, gg * NQ:(gg + 1) * NQ], denp[0:64, gg * NQ:(gg + 1) * NQ], sinks[gg], ALU.add,
                              [denk, "sinkexp"], [rdk])
                    recip(rd[:, 0:W], rd[:, 0:W], [rdk], [rdk])
                else:
                    recip(rd[:, 0:W], denp[0:64, 0:W], [denk], [rdk])
                for gg in range(G):
                    tt_op(outs[gg], nump[0:64, gg * NQ:(gg + 1) * NQ], rd[:, gg * NQ:(gg + 1) * NQ], ALU.mult, [numk, rdk], out_keys)

            def rope(dst, src, H, Dh, gb, off, rk, wk):
                Q = Dh // 4
                cos = ropet[:, gb * 192 + off:gb * 192 + off + Dh]
                sin = ropet[:, gb * 192 + off + Dh:gb * 192 + off + 2 * Dh]
                t2, t2k = ring("tm3", tm3)
                s4 = src.rearrange("p (h a b q) -> p h a b q", h=H, a=2, b=2)
                d4 = t2[:, 0:H * Dh].rearrange("p (h a b q) -> p h a b q", h=H, a=2, b=2)
                sn = sin.rearrange("p (a b q) -> p a b q", a=2, b=2)
                for bb in range(2):
                    tt_op(d4[:, :, :, bb, :], s4[:, :, :, 1 - bb, :],
                          sn[:, :, bb, :].unsqueeze(1).to_broadcast([128, H, 2, Q]), ALU.mult,
                          rk + ["ropet"] + ([t2k] if bb else []), [t2k])
                s3 = src.rearrange("p (h d) -> p h d", h=H)
                d3 = dst.rearrange("p (h d) -> p h d", h=H)
                tt_op(d3, s3, cos.unsqueeze(1).to_broadcast([128, H, Dh]), ALU.mult, rk + ["ropet"], wk)
                tt_op(dst, dst, t2[:, 0:H * Dh], ALU.add, wk + [t2k], wk)

            def small_weights(l_):
                dma("pool", wq_t[:], wq[l_].rearrange("(c p) n -> p c n", p=128), [], ["wq_t"])
                dma("pool", wkv_t[:], wkv[l_], [], ["wkv_t"])
                dma("pool", wsT_t[:], sgu_wT[l_], [], ["wsT_t"])
                dma("pool", sgub_t[:], sgub[:, l_ * 512:(l_ + 1) * 512], [], ["sgub_t"])
                dma("sp", sgn[:], sgunorm_bc[:, l_ * 256:(l_ + 1) * 256], [], ["sgn"])
                dma("sp", kvn[:], kvnorm_bc[:, l_ * 128:(l_ + 1) * 128], [], ["kvn"])

            for l in range(depth):
                set_psring(range(8))
                if l == 0:
                    small_weights(0)
                P.op("dve", lambda: nc.vector.memset(KT[96:128, :, :], 0.0), [],
                     [f"KT{h}_{jj}" for h in range(4) for jj in range(NT + 1)])
                if lat:
                    dma("pool", ckT[:], ckvT_c[l], [], ["ckT"])
                    for h in range(4):
                        dma("pool", KT[64:96, h, T:T + 512], kpeT_c[l], [], [f"KT{h}_{NT}"])
                    dma("pool", ks[:, T:T + 512], skT_c[l], [], [f"ks_{NT}"])
                    dma("pool", Vs[:, 16:20, :], sv_c[l].rearrange("(b p) d -> p b d", p=128), [], [f"Vs_{NT}"])

                def kv_up(j, nblk):
                    for h in range(4):
                        ps, pk = psum()
                        mm(ps[0:64, 0:nblk * 128], wkv_t[:, h * 128:h * 128 + 64], ckT[:, 0:nblk * 128], True, True,
                           ["wkv_t", "ckT"], [pk])
                        act(KT[0:64, h, j * 512:j * 512 + nblk * 128], ps[0:64, 0:nblk * 128], AF.Copy, [pk], [f"KT{h}_{j}"])
                    for b in range(nblk):
                        ps, pk = psum()
                        mm(ps[:, :], ckT[:, b * 128:(b + 1) * 128], wkv_t[:], True, True, ["wkv_t", "ckT"], [pk])
                        vcopy(Vm[:, j * 4 + b, :].rearrange("p (h d) -> p h d", h=4),
                              ps[:, :].rearrange("p (h t d) -> p h t d", h=4, t=2)[:, :, 1, :], [pk], [f"Vm_{j}"])

                if lat:
                    kv_up(NT, 4)
                ckpt(10)

                for j in range(NT):
                    P.mark(f"{g}{l} K{j} norm")
                    if lat:
                        use_buf(0)
                        norm_to(l, j, 0)
                    else:
                        if j == 0:
                            norm_to(l, 0, 0)
                        if j + 1 < NT:
                            norm_to(l, j + 1, (j + 1) % 2)
                        use_buf(j % 2)
                    P.mark(f"{g}{l} K{j} fm")
                    wtF, wkF = get_weight(f"fmK{g}{l}{j}")
                    wtT, wkT = get_weight(f"tmK{g}{l}{j}", ahead=NST - 2)

                    def fmK_group(m):
                        ps, pk = psum()
                        for c in range(8):
                            mm(ps[:, :], wtF[:, c, m * 128:(m + 1) * 128], H["ap"][:, c, :], c == 0, c == 7, [wkF, f"{H['k']}{c}"], [pk])
                        if m < 2:
                            act(pbuf[:, m, j * 512:(j + 1) * 512], ps[:, :], AF.Copy, [pk], [f"p{m}_{j}"])
                        else:
                            tt_op(pbuf[:, m - 2, j * 512:(j + 1) * 512], ps[:, :], pbuf[:, m - 2, j * 512:(j + 1) * 512],
                                  ALU.mult, [pk, f"p{m - 2}_{j}"], [f"p{m - 2}_{j}"])

                    def tmK_mm(b):
                        tok = slice(b * 128, (b + 1) * 128)
                        ps, pk = psum()
                        for c in range(8):
                            mm(ps[:, 0:416], H["ap"][:, c, tok], wtT[:, c, :], c == 0, c == 7, [wkT, f"{H['k']}{c}"], [pk])
                        return ps, pk

                    def tmK_ew(b, ps, pk):
                        gb = j * 4 + b
                        st, stk = ring("tmf", tmf)
                        act(st[:, 0:416], ps[:, 0:416], AF.Copy, [pk], [stk])
                        smt, smk = ring("sm", sm)
                        t2, t2k = ring("tm3", tm3)
                        P.op("dve", lambda: nc.vector.memset(smt[:, 0:1], 0.0), [], [smk])
                        act(t2[:, 0:128], st[:, 0:128], AF.Square, [stk, smk], [t2k, smk], accum_out=smt[:, 0:1])
                        act(smt[:, 1:2], smt[:, 0:1], AF.Sqrt, [smk], [smk], bias=EPS, scale=1.0 / 128)
                        recip(smt[:, 2:3], smt[:, 1:2], [smk], [smk])
                        stt(st[:, 0:128], st[:, 0:128], smt[:, 2:3], kvn[:], ALU.mult, ALU.mult, [stk, smk, "kvn"], [stk])
                        if lat:
                            t3, t3k = ring("tm2", tm2)
                            rope(t3[:, 0:32], st[:, 128:160], 1, 32, gb, 128, [stk], [t3k])
                            kpe_src, kpek = t3[:, 0:32], t3k
                            t4, t4k = ring("tm2", tm2)
                            rope(t4[:, 0:128], st[:, 160:288], 2, 64, gb, 0, [stk], [t4k])
                            sk_src, skk = t4[:, 0:128], t4k
                        else:
                            kpe_src, kpek = st[:, 128:160], stk
                            sk_src, skk = st[:, 160:288], stk
                            sq_, r0 = divmod(gb * 128, 256)
                            dma("sp", o_ckv[sq_, l, r0:r0 + 128, :], st[:, 0:128], [stk], [], is_out=True)
                            dma("sp", o_kpe[sq_, l, r0:r0 + 128, :], st[:, 128:160], [stk], [], is_out=True)
                            dma("sp", o_k[sq_, l, r0:r0 + 128, :], st[:, 160:288], [stk], [], is_out=True)
                            dma("sp", o_v[sq_, l, r0:r0 + 128, :], st[:, 288:416], [stk], [], is_out=True)
                        vcopy(Vs[:, gb, :], st[:, 288:416], [stk], [f"Vs_{j}"])
                        return st, stk, kpe_src, kpek, sk_src, skk

                    def tmK_tr(b, st, stk, kpe_src, kpek, sk_src, skk):
                        gb = j * 4 + b
                        tok = slice(b * 128, (b + 1) * 128)
                        ps2, pk2 = psum()
                        P.op("pe", lambda: nc.tensor.transpose(ps2[:, 0:128], st[:, 0:128], ident_f[:]), [stk, "ident_f"], [pk2])
                        P.op("pe", lambda: nc.tensor.transpose(ps2[:, 128:256], sk_src, ident_f[:]), [skk, "ident_f"], [pk2])
                        P.op("pe", lambda: nc.tensor.transpose(ps2[0:32, 256:384], kpe_src, ident_f[:]), [kpek, "ident_f"], [pk2])
                        act(ckT[:, tok], ps2[:, 0:128], AF.Copy, [pk2], ["ckT"])
                        vcopy(ks[:, gb * 128:(gb + 1) * 128], ps2[:, 128:256], [pk2], [f"ks_{j}"])
                        act(KT[64:96, 0:2, gb * 128:(gb + 1) * 128], ps2[0:32, 256:384].unsqueeze(1).to_broadcast([32, 2, 128]),
                            AF.Copy, [pk2], [f"KT0_{j}", f"KT1_{j}"])
                        vcopy(KT[64:96, 2:4, gb * 128:(gb + 1) * 128], ps2[0:32, 256:384].unsqueeze(1).to_broadcast([32, 2, 128]),
                              [pk2], [f"KT2_{j}", f"KT3_{j}"])

                    P.mark(f"{g}{l} K{j} tm")
                    r = {}
                    e = {}
                    r[0] = tmK_mm(0)
                    r[1] = tmK_mm(1)
                    for b in range(4):
                        e[b] = tmK_ew(b, *r[b])
                        fmK_group(b)
                        tmK_tr(b, *e[b])
                        if b + 2 < 4:
                            r[b + 2] = tmK_mm(b + 2)
                    P.mark(f"{g}{l} K{j} kvup")
                    kv_up(j, 4)
                    ckpt(14)

                nseq_t = 512 // min(S, 512)
                for j in range(NT):
                    set_psring(range(8))
                    if lat:
                        use_buf(0)
                        if j == 0:
                            P.mark(f"{g}{l} Q{j} norm")
                            norm_to(l, 0, 0)
                    else:
                        use_buf(j % 2)
                    P.mark(f"{g}{l} Q{j} fm")
                    if debug and l == 0 and j == 0:
                        dma("sp", dbg_h, H["ap"], [f"{H['k']}{c}" for c in range(8)], [], is_out=True)
                    wa, wak = get_weight(f"fmQa{g}{l}{j}")

                    def fmQ_group(m, wt, wk_):
                        mi = m % 3
                        ps, pk = psum()
                        for c in range(8):
                            mm(ps[:, :], wt[:, c, mi * 128:(mi + 1) * 128], H["ap"][:, c, :], c == 0, c == 7, [wk_, f"{H['k']}{c}"], [pk])
                        if m < 2:
                            act(catj[:, m, :], ps[:, :], AF.Copy, [pk], [f"cat{m}"])
                        elif m < 4:
                            act(catj[:, m, :], ps[:, :], AF.Gelu_apprx_tanh, [pk], [f"cat{m}"])
                        else:
                            cr, crk = cqraw[m - 4], f"cqraw{m - 4}"
                            act(cr[:], ps[:, :], AF.Copy, [pk], [crk])

                    for m in range(3):
                        fmQ_group(m, wa, wak)
                    wb, wbk = get_weight(f"fmQb{g}{l}{j}")
                    wtT, wkT = get_weight(f"tmQ{g}{l}{j}", ahead=NST - 2)

                    def cq_conv():
                        ps, pk = psum()
                        for c in range(2):
                            sq, sqk = ring("sqt", sqt)
                            act(sq[:], cqraw[c][:], AF.Square, [f"cqraw{c}"], [sqk])
                            mm(ps[:, :], ones_b, sq[:], c == 0, c == 1, [sqk, "cst"], [pk])
                        act(s_t[:], ps[:, :], AF.Sqrt, [pk], ["s_t", "rstd"], bias=EPS, scale=1.0 / 256)
                        recip(rstd[:], s_t[:], ["s_t"], ["s_t", "rstd"])
                        for c in range(2):
                            stt(cqn[:, c, :], cqraw[c][:], vecs[:, 72 + l * 2 + c:72 + l * 2 + c + 1], rstd[:], ALU.mult, ALU.mult,
                                [f"cqraw{c}", "vecs", "rstd"], [f"cqn{c}"])
                        Sq = min(S, 512)
                        for c in range(2):
                            cw = lambda k: vecs[:, 80 + (l * 2 + c) * 3 + k:80 + (l * 2 + c) * 3 + k + 1]
                            lo = j * 512
                            pk_all = [f"p{c}_{jj}" for jj in range(NT)]
                            ts_op(acc[:, c, :], pbuf[:, c, lo:lo + 512], cw(1), ALU.mult, pk_all + ["vecs"], [f"acc{c}"])
                            for sidx in range(nseq_t):
                                a0 = sidx * Sq
                                g0 = lo + a0
                                first_in_seq = (g0 % S == 0)
                                last_in_seq = ((g0 + Sq) % S == 0)
                                s0 = 1 if first_in_seq else 0
                                stt(acc[:, c, a0 + s0:a0 + Sq], pbuf[:, c, g0 + s0 - 1:g0 + Sq - 1], cw(0), acc[:, c, a0 + s0:a0 + Sq],
                                    ALU.mult, ALU.add, pk_all + ["vecs", f"acc{c}"], [f"acc{c}"])
                                e0 = 1 if last_in_seq else 0
                                stt(acc[:, c, a0:a0 + Sq - e0], pbuf[:, c, g0 + 1:g0 + Sq - e0 + 1], cw(2), acc[:, c, a0:a0 + Sq - e0],
                                    ALU.mult, ALU.add, pk_all + ["vecs", f"acc{c}"], [f"acc{c}"])
                            tt_op(catj[:, c, :], catj[:, c, :], acc[:, c, :], ALU.mult, [f"cat{c}", f"acc{c}"], [f"cat{c}"])

                    def tmQ_mm(b):
                        tok = slice(b * 128, (b + 1) * 128)
                        ps, pk = psum()
                        for c in range(8):
                            mm(ps[:, :], H["ap"][:, c, tok], wtT[:, c, :], c == 0, c == 7, [wkT, f"{H['k']}{c}"], [pk])
                        return ps, pk

                    def tmQ_ew(b, ps, pk):
                        gb = j * 4 + b
                        st, stk = ring("tmf", tmf)
                        act(st[:, 0:256], ps[:, 0:256], AF.Gelu_apprx_tanh, [pk], [stk])
                        act(st[:, 256:512], ps[:, 256:512], AF.Copy, [pk], [stk])
                        smt, smk = ring("sm", sm)
                        t2, t2k = ring("tm3", tm3)
                        P.op("dve", lambda: nc.vector.memset(smt[:, 0:1], 0.0), [], [smk])
                        act(t2[:, 0:256], st[:, 0:256], AF.Square, [stk, smk], [t2k, smk], accum_out=smt[:, 0:1])
                        act(smt[:, 1:2], smt[:, 0:1], AF.Sqrt, [smk], [smk], bias=EPS, scale=1.0 / 256)
                        recip(smt[:, 2:3], smt[:, 1:2], [smk], [smk])
                        vn, vnk = ring("vn", vn_t)
                        stt(vn[:], st[:, 0:256], smt[:, 2:3], sgn[:], ALU.mult, ALU.mult, [stk, smk, "sgn"], [vnk])
                        if lat:
                            t4, t4k = ring("tm2", tm2)
                            rope(t4[:, 0:256], st[:, 256:512], 4, 64, gb, 0, [stk], [t4k])
                            return st, stk, vn, vnk, t4, t4k
                        return st, stk, vn, vnk, None, None

                    def tmQ_tail(b, st, stk, vn, vnk, t4, t4k):
                        tok = slice(b * 128, (b + 1) * 128)
                        ps2, pk2 = psum()
                        for hd in range(4):
                            cc, e_ = divmod(hd, 2)
                            mm(ps2[:, hd * 128:(hd + 1) * 128], vn[:, cc * 128:(cc + 1) * 128], wsT_t[:, hd * 128:(hd + 1) * 128],
                               True, False, [vnk, "wsT_t"], [pk2], sig=False)
                            mm(ps2[:, hd * 128:(hd + 1) * 128], ones_b[0:1, 0:128], sgub_t[0:1, hd * 128:(hd + 1) * 128],
                               False, True, ["cst", "sgub_t"], [pk2], sig=True)
                        ps3, pk3 = psum()
                        for gg in range(2):
                            src_ap = (t4[:, gg * 128:(gg + 1) * 128] if lat else st[:, 256 + gg * 128:256 + (gg + 1) * 128])
                            P.op("pe", lambda: nc.tensor.transpose(ps3[:, gg * 128:(gg + 1) * 128], src_ap, ident_f[:]),
                                 [t4k if lat else stk, "ident_f"], [pk3])
                        for hd in range(4):
                            cc, e_ = divmod(hd, 2)
                            tt_op(catj[e_ * 64:(e_ + 1) * 64, 2 + cc, tok], catj[e_ * 64:(e_ + 1) * 64, 2 + cc, tok],
                                  ps2[e_ * 64:(e_ + 1) * 64, hd * 128:(hd + 1) * 128], ALU.mult, [f"cat{2 + cc}", pk2], [f"cat{2 + cc}"])
                        act(qs[:, :, tok], ps3[:, 0:256].rearrange("p (g t) -> p g t", g=2), AF.Copy, [pk3], ["qs"])

                    def qm_mm(b):
                        tokq = slice(b * 128, (b + 1) * 128)
                        ps, pk = psum()
                        for c in range(2):
                            mm(ps[:, 0:384], cqn[:, c, tokq], wq_t[:, c, :], c == 0, c == 1, [f"cqn{c}", "wq_t"], [pk])
                        return ps, pk

                    def qm_ew(b, ps, pk):
                        gb = j * 4 + b
                        st, stk = ring("tmf", tmf)
                        act(st[:, 0:384], ps[:, 0:384], AF.Copy, [pk], [stk])
                        if lat:
                            t3, t3k = ring("tm2", tm2)
                            src4 = st[:, 0:384].rearrange("p (h d) -> p h d", h=4)[:, :, 64:96]
                            t5, t5k = ring("tm2", tm2)
                            vcopy(t5[:, 0:128].rearrange("p (h d) -> p h d", h=4), src4, [stk], [t5k])
                            rope(t3[:, 0:128], t5[:, 0:128], 4, 32, gb, 128, [t5k], [t3k])
                            vcopy(src4, t3[:, 0:128].rearrange("p (h d) -> p h d", h=4), [t3k], [stk])
                        return st, stk

                    def qm_tr(b, st, stk):
                        tokq = slice(b * 128, (b + 1) * 128)
                        ps, pk = psum()
                        for h in range(4):
                            P.op("pe", lambda: nc.tensor.transpose(ps[0:96, h * 128:(h + 1) * 128], st[:, h * 96:(h + 1) * 96], ident_f[:]),
                                 [stk, "ident_f"], [pk])
                        src = ps[0:96, :].rearrange("p (h t) -> p h t", h=4)
                        if b % 2 == 0:
                            act(qh[0:96, :, tokq], src, AF.Copy, [pk], [f"qh{h}" for h in range(4)])
                        else:
                            vcopy(qh[0:96, :, tokq], src, [pk], [f"qh{h}" for h in range(4)])

                    P.mark(f"{g}{l} Q{j} tm")
                    r = {}
                    e = {}
                    rq = {}
                    for m in range(3, 6):
                        fmQ_group(m, wb, wbk)
                    cq_conv()
                    r[0] = tmQ_mm(0)
                    r[1] = tmQ_mm(1)
                    rq[0] = qm_mm(0)
                    for b in range(4):
                        e[b] = tmQ_ew(b, *r[b])
                        eq = qm_ew(b, *rq[b])
                        if b + 1 < 4:
                            rq[b + 1] = qm_mm(b + 1)
                        tmQ_tail(b, *e[b])
                        if b + 2 < 4:
                            r[b + 2] = tmQ_mm(b + 2)
                        qm_tr(b, *eq)
                    ckpt(19)
                    set_psring([0, 1, 2, 7])
                    ckpt(20)
                    P.mark(f"{g}{l} Q{j} MLA")
                    if lat:
                        for h in range(4):
                            chunks = []
                            for kc in range(20):
                                jt = kc // 4
                                chunks.append((KT[:, h, kc * 128:(kc + 1) * 128], Vm[:, kc, h * 64:(h + 1) * 64], None,
                                               [f"KT{h}_{jt}", f"Vm_{jt}"]))
                            cc, e = divmod(h, 2)
                            attention(512, qh[:, h, :], [f"qh{h}"], chunks, MLA_SCALE, None,
                                      catj[e * 64:(e + 1) * 64, 4 + cc, :], [f"cat{4 + cc}"])
                    else:
                        for sl in range(2):
                            sidx = (j * 512) // 256 + sl
                            qsl = slice(sl * 256, (sl + 1) * 256)
                            for h in range(4):
                                chunks = []
                                for kc in (2 * sidx, 2 * sidx + 1):
                                    jt = kc // 4
                                    chunks.append((KT[:, h, kc * 128:(kc + 1) * 128], Vm[:, kc, h * 64:(h + 1) * 64], None,
                                                   [f"KT{h}_{jt}", f"Vm_{jt}"]))
                                cc, e = divmod(h, 2)
                                attention(256, qh[:, h, qsl], [f"qh{h}"], chunks, MLA_SCALE, None,
                                          catj[e * 64:(e + 1) * 64, 4 + cc, qsl], [f"cat{4 + cc}"])
                    ckpt(21)
                    if lat and j + 1 < NT:
                        norm_to(l, j + 1, 0)
                    P.mark(f"{g}{l} Q{j} SWA")
                    for n in range(2):
                        sink_l = [sinkexp[0:64, l * 4 + n * 2 + gq:l * 4 + n * 2 + gq + 1] for gq in range(2)]
                        if lat:
                            for bq in range(4):
                                blk = j * 4 + bq
                                tq = slice(bq * 128, (bq + 1) * 128)
                                chunks = []
                                if blk >= 1:
                                    chunks.append((ks[n * 64:(n + 1) * 64, (blk - 1) * 128:blk * 128],
                                                   Vs[:, blk - 1, n * 64:(n + 1) * 64], mprev,
                                                   [f"ks_{(blk - 1) // 4}", f"Vs_{(blk - 1) // 4}"]))
                                chunks.append((ks[n * 64:(n + 1) * 64, blk * 128:(blk + 1) * 128],
                                               Vs[:, blk, n * 64:(n + 1) * 64], None, [f"ks_{blk // 4}", f"Vs_{blk // 4}"]))
                                if blk <= 14:
                                    chunks.append((ks[n * 64:(n + 1) * 64, (blk + 1) * 128:(blk + 2) * 128],
                                                   Vs[:, blk + 1, n * 64:(n + 1) * 64], mnext,
                                                   [f"ks_{(blk + 1) // 4}", f"Vs_{(blk + 1) // 4}"]))
                                for kc in range(16, 20):
                                    chunks.append((ks[n * 64:(n + 1) * 64, kc * 128:(kc + 1) * 128],
                                                   Vs[:, kc, n * 64:(n + 1) * 64], None, [f"ks_{NT}", f"Vs_{NT}"]))
                                attention(128, qs[n * 64:(n + 1) * 64, :, tq], ["qs"], chunks, SWA_SCALE, sink_l,
                                          [catj[gq * 64:(gq + 1) * 64, 6 + n, tq] for gq in range(2)], [f"cat{6 + n}"], G=2)
                        else:
                            for sl in range(2):
                                sidx = (j * 512) // 256 + sl
                                qsl = slice(sl * 256, (sl + 1) * 256)
                                chunks = []
                                for kc in (2 * sidx, 2 * sidx + 1):
                                    chunks.append((ks[n * 64:(n + 1) * 64, kc * 128:(kc + 1) * 128],
                                                   Vs[:, kc, n * 64:(n + 1) * 64], None, [f"ks_{kc // 4}", f"Vs_{kc // 4}"]))
                                attention(256, qs[n * 64:(n + 1) * 64, :, qsl], ["qs"], chunks, SWA_SCALE, sink_l,
                                          [catj[gq * 64:(gq + 1) * 64, 6 + n, qsl] for gq in range(2)], [f"cat{6 + n}"], G=2)
                    ckpt(22)
                    P.mark(f"{g}{l} Q{j} wout")
                    set_psring(range(8))
                    if debug and l == 0 and j == 0:
                        dma("sp", dbg_cat, catj, [f"cat{c}" for c in range(8)], [], is_out=True)
                    for half in range(2):
                        wt, wk_ = get_weight(f"wo{half}{g}{l}{j}")
                        for mi in range(4):
                            m = half * 4 + mi
                            ps, pk = psum()
                            for c in range(8):
                                mm(ps[:, :], wt[:, c, mi * 128:(mi + 1) * 128], catj[:, c, :], c == 0, c == 7, [wk_, f"cat{c}"], [pk])
                            xs_ = x[:, m, j * 512:(j + 1) * 512]
                            stt(xs_, ps[:, :], modv[:, l, 2, m, v:v + 1], xs_, ALU.mult, ALU.add, [pk, f"modt{l}", f"x{m}_{j}"], [f"x{m}_{j}"])
                P.barrier()
                if debug and l == 0:
                    dma("sp", dbg_x1[:, :, 0:T], x[:, :, 0:T], [], [], is_out=True)
                    P.barrier()
                ckpt(23)
                P.mark(f"{g}{l} MLP norm")
                set_psring(range(8))
                if l + 1 < depth:
                    small_weights(l + 1)
                ckpt(24)
                P.mark(f"{g}{l} MLP mm")
                ada_pc = [0]
                for jh in range(8):
                    wa, wak = get_weight(f"w1{g}{l}{jh}")
                    wb, wbk = get_weight(f"w2{g}{l}{jh}", ahead=NST - 2)
                    def mlp_up(j):
                        if jh == 0:
                            norm(l, 1, j, lambda c: h2[:, c, j * 512:(j + 1) * 512], lambda c: f"h2{c}_{j}")
                        u, uk = ring("ub", ub)
                        for hc in range(4):
                            ps, pk = psum()
                            for c in range(8):
                                mm(ps[:, :], wa[:, c, hc * 128:(hc + 1) * 128], h2[:, c, j * 512:(j + 1) * 512], c == 0, c == 7,
                                   [wak, f"h2{c}_{j}"], [pk])
                            r_, rk_ = ring("relu", relu_t)
                            act(r_[:], ps[:, :], AF.Relu, [pk], [rk_])
                            tt_op(u[:, hc, :], r_[:], r_[:], ALU.mult, [rk_], [f"{uk}_{hc}"])
                        return u, uk

                    def mlp_down(j, u, uk):
                        for m in range(8):
                            ps, pk = psum()
                            for hc in range(4):
                                mm(ps[:, :], wb[:, hc, m * 128:(m + 1) * 128], u[:, hc, :], hc == 0, hc == 3, [wbk, f"{uk}_{hc}"], [pk])
                            xs_ = x[:, m, j * 512:(j + 1) * 512]
                            stt(xs_, ps[:, :], modv[:, l, 5, m, v:v + 1], xs_, ALU.mult, ALU.add, [pk, f"modt{l}", f"x{m}_{j}"], [f"x{m}_{j}"])

                    us = {0: mlp_up(0)}
                    for j in range(NT):
                        if j + 1 < NT:
                            us[j + 1] = mlp_up(j + 1)
                        mlp_down(j, *us.pop(j))
                    if ada_defer and g == "A" and l + 1 < DEPTH:
                        for _ in range(ADA_SPLIT[jh]):
                            ada_piece(l + 1, ada_pc[0])
                            ada_pc[0] += 1
                        if jh == 7:
                            ada_finish(l + 1)
                P.barrier()
                if debug and l == 0:
                    dma("sp", dbg_x2[:, :, 0:T], x[:, :, 0:T], [], [], is_out=True)
                    P.barrier()
            P.mark(f"{g} final")
            for j in range(NT):
                ps, pk = psum()
                for c in range(8):
                    sq, sqk = ring("sqt", sqt)
                    act(sq[:], x[:, c, j * 512:(j + 1) * 512], AF.Square, [f"x{c}_{j}"], [sqk])
                    mm(ps[:, :], ones_b, sq[:], c == 0, c == 7, [sqk, "cst"], [pk])
                act(s_t[:], ps[:, :], AF.Sqrt, [pk], ["s_t", "rstd"], bias=EPS, scale=1.0 / D)
                recip(rstd[:], s_t[:], ["s_t"], ["s_t", "rstd"])
                for c in range(8):
                    y_, yk = ring("tt", tt)
                    stt(y_[:], x[:, c, j * 512:(j + 1) * 512], vecs[:, 64 + c:65 + c], rstd[:], ALU.mult, ALU.mult,
                        [f"x{c}_{j}", "vecs", "rstd"], [yk])
                    dma("sp", yT[c * 128:(c + 1) * 128, j * 512:(j + 1) * 512], y_[:], [yk], [], is_out=True)
            P.barrier()

        try:
            for g_ in groups:
                run_group(g_)
            P.mark("end")
            P.finish()
        except StopBuild:
            pass
        MARKS.extend(P.marks)
    return nc


_CACHE = {}


def _consts():
    ident = np.eye(128, dtype=np.float32)
    ones = np.ones((128, 128), np.float32)
    kk = np.arange(128)[:, None]
    qq = np.arange(128)[None, :]
    mprev = np.where(kk >= qq, 0.0, NEG).astype(np.float32)
    mnext = np.where(kk <= qq, 0.0, NEG).astype(np.float32)
    c = np.concatenate([ident, ones, mprev, mnext, np.zeros((128, 128), np.float32)], axis=1)
    def tables(rot_dim):
        half = rot_dim // 2
        inv = (10000.0 ** (-np.arange(0, half, 2, dtype=np.float32) / half)).astype(np.float32)
        t = np.arange(2048)
        row = (t // 64).astype(np.float32)
        col = (t % 64).astype(np.float32)
        ar = row[:, None] * inv[None, :]
        ac = col[:, None] * inv[None, :]
        ang = np.concatenate([ar, ar, ac, ac], axis=-1).astype(np.float32)
        cos = np.cos(ang).astype(np.float32)
        sin = np.sin(ang).astype(np.float32)
        q = rot_dim // 4
        sgn = np.concatenate([-np.ones(q), np.ones(q), -np.ones(q), np.ones(q)]).astype(np.float32)
        return cos, sin * sgn[None, :]
    cs, ss = tables(64)
    cm, sm_ = tables(32)
    r = np.concatenate([cs, ss, cm, sm_], axis=1)
    r = r.reshape(16, 128, 192).transpose(1, 0, 2).reshape(128, 16 * 192)
    return np.ascontiguousarray(c), np.ascontiguousarray(r.astype(np.float32))


def kernel(x_prompt, x_sample, cache_mla_ckv, cache_mla_kpe, cache_swa_k, cache_swa_v, c, c_ctx,
           w_ada, b_ada, norm1, norm2, w_in, conv_w, sgu_norm, sgu_w, sgu_b, mla_q_norm, mla_w_q_up,
           mla_kv_norm, mla_w_kv_up, swa_sink, w_out, mlp_w1, mlp_w2, final_norm):
    in_maps = pack_inputs(x_prompt, x_sample, cache_mla_ckv, cache_mla_kpe, cache_swa_k, cache_swa_v, c, c_ctx,
                          w_ada, b_ada, norm1, norm2, w_in, conv_w, sgu_norm, sgu_w, sgu_b, mla_q_norm, mla_w_q_up,
                          mla_kv_norm, mla_w_kv_up, swa_sink, w_out, mlp_w1, mlp_w2, final_norm)
    if "nc" not in _CACHE:
        _CACHE["nc"] = build_program()
    nc = _CACHE["nc"]
    res = run_bass_kernel_spmd(nc, in_maps, core_ids=list(range(NCORES)))
    return unpack_outputs(res.results)


def pack_inputs(x_prompt, x_sample, cache_mla_ckv, cache_mla_kpe, cache_swa_k, cache_swa_v, c, c_ctx,
                w_ada, b_ada, norm1, norm2, w_in, conv_w, sgu_norm, sgu_w, sgu_b, mla_q_norm, mla_w_q_up,
                mla_kv_norm, mla_w_kv_up, swa_sink, w_out, mlp_w1, mlp_w2, final_norm, cores=range(NCORES)):
    f = lambda a: np.ascontiguousarray(np.asarray(a, dtype=np.float32))
    x_prompt, x_sample = f(x_prompt), f(x_sample)
    consts, ropes = _consts()
    w_in = f(w_in)
    a_b, a_c, a_x = w_in[:, :, 0:256], w_in[:, :, 256:512], w_in[:, :, 512:768]
    u_, v_ = w_in[:, :, 768:1024], w_in[:, :, 1024:1280]
    cq, ckv, kpe = w_in[:, :, 1280:1536], w_in[:, :, 1536:1664], w_in[:, :, 1664:1696]
    sq, sk, sv = w_in[:, :, 1696:1952], w_in[:, :, 1952:2080], w_in[:, :, 2080:2208]
    sq_g = sq.reshape(DEPTH, D, 2, 2, 64).transpose(0, 1, 3, 2, 4).reshape(DEPTH, D, 256)
    w_fm = f(np.concatenate([a_c, a_x, a_b, u_, cq], axis=2))
    w_tm = f(np.concatenate([ckv, kpe, sk, sv, v_, sq_g], axis=2))
    bada_fm = f(np.asarray(b_ada).reshape(DEPTH, 48, 128).transpose(2, 0, 1).reshape(128, DEPTH * 48))
    vecs = np.zeros((128, 128), np.float32)
    vecs[:, 0:32] = np.asarray(norm1).reshape(DEPTH, 8, 128).transpose(2, 0, 1).reshape(128, 32)
    vecs[:, 32:64] = np.asarray(norm2).reshape(DEPTH, 8, 128).transpose(2, 0, 1).reshape(128, 32)
    vecs[:, 64:72] = np.asarray(final_norm).reshape(8, 128).T
    vecs[:, 72:80] = np.asarray(mla_q_norm).reshape(DEPTH, 2, 128).transpose(2, 0, 1).reshape(128, 8)
    vecs[:, 80:104] = np.asarray(conv_w).reshape(DEPTH, 3, 2, 128).transpose(3, 0, 2, 1).reshape(128, 24)
    sgunorm_bc = f(np.broadcast_to(np.asarray(sgu_norm).reshape(1, DEPTH * 256), (128, DEPTH * 256)))
    kvnorm_bc = f(np.broadcast_to(np.asarray(mla_kv_norm).reshape(1, DEPTH * 128), (128, DEPTH * 128)))
    sink_bc = f(np.broadcast_to(np.asarray(swa_sink).reshape(1, 16), (128, 16)))
    sgub = f(np.asarray(sgu_b).reshape(1, DEPTH * 512))
    sgu_wT = f(np.asarray(sgu_w).transpose(0, 3, 1, 2).reshape(DEPTH, 128, 512))
    shared = dict(w_ada=f(w_ada), bada_fm=bada_fm, vecs_fm=vecs, sgunorm_bc=sgunorm_bc, kvnorm_bc=kvnorm_bc,
                  sink_bc=sink_bc, sgub=sgub, sgu_wT=sgu_wT, w_fm=w_fm, w_tm=w_tm, w_out=f(w_out), w1=f(mlp_w1),
                  w2=f(mlp_w2), wq=f(mla_w_q_up), wkv=f(mla_w_kv_up), ropes=ropes, consts=consts)
    c = np.asarray(c, np.float32)
    c_ctx = np.asarray(c_ctx, np.float32)
    in_maps = []
    for i in cores:
        cv = np.stack([c_ctx, c[i]], axis=0)
        cfm = f(cv.reshape(2, 8, 128).transpose(2, 1, 0).reshape(128, 16))
        m = dict(shared)
        m.update(
            xsT=f(x_sample[i].T),
            xpT=f(x_prompt[4 * i:4 * i + 4].reshape(1024, D).T),
            ckvT_c=f(np.asarray(cache_mla_ckv[i]).transpose(0, 2, 1)),
            kpeT_c=f(np.asarray(cache_mla_kpe[i]).transpose(0, 2, 1)),
            skT_c=f(np.asarray(cache_swa_k[i]).reshape(DEPTH, 512, 128).transpose(0, 2, 1)),
            sv_c=f(np.asarray(cache_swa_v[i]).reshape(DEPTH, 512, 128)),
            cfm=cfm,
        )
        in_maps.append(m)
    return in_maps


def unpack_outputs(rs):
    y_prompt = np.concatenate([r["ypT"].T.reshape(4, 256, D) for r in rs], axis=0).astype(np.float32)
    y_sample = np.stack([r["ysT"].T for r in rs], axis=0).astype(np.float32)
    new_ckv = np.concatenate([r["o_ckv"] for r in rs], axis=0).astype(np.float32)
    new_kpe = np.concatenate([r["o_kpe"] for r in rs], axis=0).astype(np.float32)
    new_k = np.concatenate([r["o_k"] for r in rs], axis=0).reshape(-1, DEPTH, 256, 2, 64).astype(np.float32)
    new_v = np.concatenate([r["o_v"] for r in rs], axis=0).reshape(-1, DEPTH, 256, 2, 64).astype(np.float32)
    return (np.ascontiguousarray(y_prompt), np.ascontiguousarray(y_sample), new_ckv, new_kpe, new_k, new_v)
```

```python
import numpy as np
import concourse.bass as bass
import concourse.mybir as mybir
from concourse.bass_utils import run_bass_kernel_spmd

F32, BF16 = mybir.dt.float32, mybir.dt.bfloat16
AF = mybir.ActivationFunctionType
ALU = mybir.AluOpType

D = 1024
DEPTH = 4
EPS = 1e-6
MLA_SCALE = 96 ** -0.5
SWA_SCALE = 0.125
NEG = -30000.0
NCORES = 8


class Info:
    __slots__ = ("sem", "val", "clock", "eng")

    def __init__(self, eng):
        self.sem = None
        self.val = 0
        self.clock = None
        self.eng = eng


class Prog:
    COMPUTE = ("pe", "act", "dve")
    QUEUES = ("sp", "pool")
    NSL = 6

    def __init__(self, nc):
        self.nc = nc
        self.eng = {"pe": nc.tensor, "act": nc.scalar, "dve": nc.vector, "pool": nc.gpsimd, "sp": nc.sync}
        self.sems = {}
        self.epoch = 0
        self.csem = {}
        self.cnt = {}
        self._new_epoch_sems()
        self.slots = {}
        self.nd = {q: 0 for q in self.QUEUES}
        for q in self.QUEUES:
            self.slots[q] = []
            for i in range(self.NSL):
                nm = f"d_{q}{i}"
                self.sems[nm] = nc.alloc_semaphore(name=nm)
                self.slots[q].append([nm, 0])
        self.clock = {e: {} for e in self.eng}
        self.last_w = {}
        self.readers = {}
        self.pending = {e: [] for e in self.COMPUTE}
        self.out_infos = []
        self.total = {e: 0 for e in self.eng}
        self.ps_open = {}
        self.marks = []

    def _new_epoch_sems(self):
        for e in self.COMPUTE:
            nm = f"c_{e}{self.epoch}"
            self.sems[nm] = self.nc.alloc_semaphore(name=nm)
            self.csem[e] = nm
            self.cnt[e] = 0
        self.epoch += 1

    def _wait(self, e, info):
        if info.sem is None:
            raise RuntimeError("dependency on unsignaled op")
        ck = self.clock[e]
        if ck.get(info.sem, 0) >= info.val:
            return
        self.eng[e].wait_ge(self.sems[info.sem], info.val)
        for s, v in info.clock.items():
            if ck.get(s, 0) < v:
                ck[s] = v

    def op(self, e, fn, reads=(), writes=(), sig=True, dma=False, is_out=False):
        psr = [k for k in reads if k.startswith("ps")]
        for k in psr:
            self.ps_open[k] = False
        if psr:
            writes = list(writes) + [k for k in psr if k not in writes]
        deps = []
        for k in reads:
            w = self.last_w.get(k)
            if w is not None:
                deps.append((w, True))
        for k in writes:
            w = self.last_w.get(k)
            if w is not None:
                deps.append((w, False))
            rd = self.readers.get(k)
            if rd:
                for r in rd.values():
                    deps.append((r, False))
        for info, raw in deps:
            if info.eng == e and not dma:
                if e == "pe":
                    continue
            self._wait(e, info)
        self.total[e] += 1
        info = Info(e)
        if dma:
            n = self.nd[e]
            self.nd[e] += 1
            slot = self.slots[e][n % self.NSL]
            ck = self.clock[e]
            if slot[1] > 0 and ck.get(slot[0], 0) < slot[1]:
                self.eng[e].wait_ge(self.sems[slot[0]], slot[1])
                ck[slot[0]] = slot[1]
            slot[1] += 16
            inst = fn()
            inst.then_inc(self.sems[slot[0]], 16)
            info.sem, info.val = slot[0], slot[1]
            info.clock = dict(ck)
            info.clock[info.sem] = info.val
            info.eng = e + "_dma%d" % n
            if is_out:
                self.out_infos.append(info)
        else:
            inst = fn()
            if e == "pe" and self.marks and self.marks[-1][1] is None:
                nm_ = inst.ins.name
                for mk in reversed(self.marks):
                    if mk[1] is not None:
                        break
                    mk[1] = nm_
            if sig:
                self.cnt[e] += 1
                inst.then_inc(self.sems[self.csem[e]], 1)
                info.sem, info.val = self.csem[e], self.cnt[e]
                info.clock = dict(self.clock[e])
                info.clock[info.sem] = info.val
                for p in self.pending[e]:
                    p.sem, p.val, p.clock = info.sem, info.val, info.clock
                self.pending[e] = []
            else:
                self.pending[e].append(info)
        for k in writes:
            self.last_w[k] = info
            self.readers[k] = {}
        for k in reads:
            self.readers.setdefault(k, {})[info.eng] = info
        return info

    def barrier(self):
        for e in self.COMPUTE:
            assert not self.pending[e]
        for f in self.eng:
            ck = self.clock[f]
            for e in self.COMPUTE:
                nm, v = self.csem[e], self.cnt[e]
                if v > 0 and e != f and ck.get(nm, 0) < v:
                    self.eng[f].wait_ge(self.sems[nm], v)
                if v > 0:
                    ck[nm] = v
            for q in self.QUEUES:
                for nm, v in self.slots[q]:
                    if v > 0 and ck.get(nm, 0) < v:
                        self.eng[f].wait_ge(self.sems[nm], v)
                        ck[nm] = v
        for e in self.COMPUTE:
            if self.cnt[e] > 0:
                self.eng[e].wait_ge(self.sems[self.csem[e]], self.cnt[e])
        self.last_w = {}
        self.readers = {}
        self._new_epoch_sems()

    def mark(self, label):
        self.marks.append([label, None])

    def finish(self):
        for info in self.out_infos:
            self._wait("sp", info)


MARKS = []


class StopBuild(Exception):
    pass


def build_program(depth=DEPTH, groups="AB", debug=False, p0=9):
    nc = bass.Bass("TRN2", target_bir_lowering=False)
    P = Prog(nc)

    def din(name, shape):
        return nc.dram_tensor(name, list(shape), F32, kind="ExternalInput").ap()

    def dout(name, shape):
        return nc.dram_tensor(name, list(shape), F32, kind="ExternalOutput").ap()

    xsT = din("xsT", (D, 2048))
    xpT = din("xpT", (D, 1024))
    ckvT_c = din("ckvT_c", (DEPTH, 128, 512))
    kpeT_c = din("kpeT_c", (DEPTH, 32, 512))
    skT_c = din("skT_c", (DEPTH, 128, 512))
    sv_c = din("sv_c", (DEPTH, 512, 128))
    cfm = din("cfm", (128, 16))
    w_ada = din("w_ada", (DEPTH, D, 6 * D))
    bada_fm = din("bada_fm", (128, DEPTH * 48))
    vecs_fm = din("vecs_fm", (128, 128))
    sgunorm_bc = din("sgunorm_bc", (128, DEPTH * 256))
    kvnorm_bc = din("kvnorm_bc", (128, DEPTH * 128))
    sink_bc = din("sink_bc", (128, 16))
    sgub = din("sgub", (1, DEPTH * 512))
    sgu_wT = din("sgu_wT", (DEPTH, 128, 512))
    w_fm = din("w_fm", (DEPTH, D, 1280))
    w_tm = din("w_tm", (DEPTH, D, 928))
    w_out = din("w_out", (DEPTH, D, D))
    w1 = din("w1", (DEPTH, D, 4096))
    w2 = din("w2", (DEPTH, 4096, D))
    wq = din("wq", (DEPTH, 256, 384))
    wkv = din("wkv", (DEPTH, 128, 512))
    ropes = din("ropes", (128, 16 * 192))
    consts = din("consts", (128, 640))
    ysT = dout("ysT", (D, 2048))
    ypT = dout("ypT", (D, 1024))
    o_ckv = dout("o_ckv", (4, DEPTH, 256, 128))
    o_kpe = dout("o_kpe", (4, DEPTH, 256, 32))
    o_k = dout("o_k", (4, DEPTH, 256, 128))
    o_v = dout("o_v", (4, DEPTH, 256, 128))
    if debug:
        dbg_h = nc.dram_tensor("dbg_h", [128, 8, 512], BF16, kind="ExternalOutput").ap()
        dbg_cat = nc.dram_tensor("dbg_cat", [128, 8, 512], BF16, kind="ExternalOutput").ap()
        dbg_x1 = dout("dbg_x1", (128, 8, 2048))
        dbg_x2 = dout("dbg_x2", (128, 8, 2048))

    A = nc.alloc_sbuf_tensor
    ident_f = A("ident_f", [128, 128], F32)
    cst = A("cst", [128, 512], BF16)
    ident_b, ones_b, mprev, mnext = cst[:, 0:128], cst[:, 128:256], cst[:, 256:384], cst[:, 384:512]
    modt = A("modt", [128, DEPTH * 48 * 2], F32)
    badat = A("badat", [128, DEPTH * 48], F32)
    vecs = A("vecs", [128, 128], F32)
    gs = A("gs", [128, DEPTH * 2 * 8 * 2], F32)
    csil = A("csil", [128, 16], F32)
    csil_b = A("csil_b", [128, 16], BF16)
    sinkexp = A("sinkexp", [128, 16], F32)
    ropet = A("ropet", [128, 16 * 192], BF16)
    sgn = A("sgn", [128, 256], F32)
    kvn = A("kvn", [128, 128], F32)
    wq_t = A("wq_t", [128, 2, 384], BF16)
    wkv_t = A("wkv_t", [128, 512], BF16)
    wsT_t = A("wsT_t", [128, 512], BF16)
    sgub_t = A("sgub_t", [1, 512], BF16)
    x = A("x", [128, 8, 2048], F32)
    R = A("R", [128, 36352], BF16)
    NST = 3
    wst = [A(f"wst{i}", [128, 4096], BF16) for i in range(NST)]
    sqt = [A(f"sqt{i}", [128, 512], BF16) for i in range(2)]
    tt = [A(f"tt{i}", [128, 512], F32) for i in range(2)]
    s_t = A("s_t", [128, 512], F32)
    rstd = s_t
    tmf = [A(f"tmf{i}", [128, 512], F32) for i in range(2)]
    tm2 = [A(f"tm2{i}", [128, 256], F32) for i in range(4)]
    tm3 = [A(f"tm3{i}", [128, 256], F32) for i in range(2)]
    sm = [A(f"sm{i}", [128, 4], F32) for i in range(4)]
    vn_t = [A(f"vn{i}", [128, 256], BF16) for i in range(2)]
    pt = [A(f"pt{i}", [128, 512], BF16) for i in range(4)]
    rden = [A(f"rden{i}", [64, 512], F32) for i in range(1)]
    qh = A("qh", [128, 4, 512], BF16)
    relu_t = [A(f"relu{i}", [128, 512], BF16) for i in range(2)]
    cqraw = relu_t
    yo = tt
    PS = [nc.alloc_psum_tensor(f"ps{i}", [128, 512], F32) for i in range(8)]

    rr = {}

    def ring(name, lst):
        i = rr.get(name, 0)
        rr[name] = i + 1
        j = i % len(lst)
        return lst[j], f"{name}{j}"

    psring = {"lst": list(range(8)), "i": 0}

    def psum():
        lst = psring["lst"]
        b = lst[psring["i"] % len(lst)]
        psring["i"] += 1
        assert not P.ps_open.get(f"ps{b}", False), f"psum bank ps{b} re-allocated before its previous contents were read"
        P.ps_open[f"ps{b}"] = True
        return PS[b], f"ps{b}"

    def set_psring(lst):
        psring["lst"] = list(lst)
        psring["i"] = 0

    def mm(out, lhsT, rhs, start, stop, reads, writes, sig=None):
        if sig is None:
            sig = True
        return P.op("pe", lambda: nc.tensor.matmul(out, lhsT=lhsT, rhs=rhs, start=start, stop=stop),
                    reads=reads, writes=writes, sig=sig)

    def act(out, in_, func, reads, writes, bias=0.0, scale=1.0, accum_out=None):
        kw = {}
        if accum_out is not None:
            kw["accum_out"] = accum_out
        return P.op("act", lambda: nc.scalar.activation(out=out, in_=in_, func=func, bias=bias, scale=scale, **kw),
                    reads=reads, writes=writes)

    def tt_op(out, in0, in1, op, reads, writes):
        return P.op("dve", lambda: nc.vector.tensor_tensor(out=out, in0=in0, in1=in1, op=op), reads=reads, writes=writes)

    def ts_op(out, in0, s1, op0, reads, writes, s2=None, op1=None):
        if op1 is None:
            return P.op("dve", lambda: nc.vector.tensor_scalar(out=out, in0=in0, scalar1=s1, scalar2=None, op0=op0),
                        reads=reads, writes=writes)
        return P.op("dve", lambda: nc.vector.tensor_scalar(out=out, in0=in0, scalar1=s1, scalar2=s2, op0=op0, op1=op1),
                    reads=reads, writes=writes)

    def stt(out, in0, scalar, in1, op0, op1, reads, writes):
        return P.op("dve", lambda: nc.vector.scalar_tensor_tensor(out=out, in0=in0, scalar=scalar, in1=in1, op0=op0, op1=op1),
                    reads=reads, writes=writes)

    def vcopy(out, in_, reads, writes):
        return P.op("dve", lambda: nc.vector.tensor_copy(out=out, in_=in_), reads=reads, writes=writes)

    def recip(out, in_, reads, writes):
        return P.op("dve", lambda: nc.vector.reciprocal(out=out, in_=in_), reads=reads, writes=writes)

    def dma(q, out, in_, reads, writes, is_out=False):
        e = nc.sync if q == "sp" else nc.gpsimd
        return P.op(q, lambda: e.dma_start(out=out, in_=in_), reads=reads, writes=writes, dma=True, is_out=is_out)

    wplan = []
    wstate = {"issued": 0, "used": 0}

    ada_defer = (len(groups) > 0 and groups[0] == "A")
    ada_phase0 = [0] if ada_defer else list(range(DEPTH))
    ADA_SPLIT = [2, 2, 2, 2, 1, 1, 1, 1]

    def plan_weights():
        for l in ada_phase0:
            for pc in range(12):
                wplan.append((f"ada{l}_{pc}", w_ada[l, :, pc * 512:(pc + 1) * 512], (8, 512)))
        for g in groups:
            NT = 4 if g == "A" else 2
            for l in range(depth):
                for j in range(NT):
                    wplan.append((f"fmK{g}{l}{j}", w_fm[l, :, 0:512], (8, 512)))
                    wplan.append((f"tmK{g}{l}{j}", w_tm[l, :, 0:416], (8, 416)))
                for j in range(NT):
                    wplan.append((f"fmQa{g}{l}{j}", w_fm[l, :, 512:896], (8, 384)))
                    wplan.append((f"fmQb{g}{l}{j}", w_fm[l, :, 896:1280], (8, 384)))
                    wplan.append((f"tmQ{g}{l}{j}", w_tm[l, :, 416:928], (8, 512)))
                    wplan.append((f"wo0{g}{l}{j}", w_out[l, :, 0:512], (8, 512)))
                    wplan.append((f"wo1{g}{l}{j}", w_out[l, :, 512:1024], (8, 512)))
                pcn = 0
                for jh in range(8):
                    wplan.append((f"w1{g}{l}{jh}", w1[l, :, jh * 512:(jh + 1) * 512], (8, 512)))
                    wplan.append((f"w2{g}{l}{jh}", w2[l, jh * 512:(jh + 1) * 512, :], (4, 1024)))
                    if ada_defer and g == "A" and l + 1 < DEPTH:
                        for _ in range(ADA_SPLIT[jh]):
                            wplan.append((f"ada{l + 1}_{pcn}", w_ada[l + 1, :, pcn * 512:(pcn + 1) * 512], (8, 512)))
                            pcn += 1

    def issue_weight():
        i = wstate["issued"]
        if i >= len(wplan):
            return
        name, ap, (nc_, ncol) = wplan[i]
        slot = wst[i % NST]
        dst = slot[:, 0:nc_ * ncol].rearrange("p (c n) -> p c n", c=nc_)
        src = ap.rearrange("(c p) n -> p c n", p=128)
        dma("pool", dst, src, reads=[], writes=[f"wst{i % NST}"])
        wstate["issued"] += 1

    def get_weight(name, ahead=NST - 1):
        i = wstate["used"]
        assert wplan[i][0] == name, (wplan[i][0], name)
        while wstate["issued"] < min(i + ahead + 1, len(wplan)):
            issue_weight()
        wstate["used"] += 1
        nc_, ncol = wplan[i][2]
        return wst[i % NST][:, 0:nc_ * ncol].rearrange("p (c n) -> p c n", c=nc_), f"wst{i % NST}"

    plan_weights()
    MARKS.clear()

    def ckpt(n):
        if p0 == n:
            P.barrier()
            P.finish()
            raise StopBuild()

    with nc.allow_low_precision("bf16 matmuls"), nc.allow_non_contiguous_dma("small strided loads"):
        dma("sp", ident_f[:], consts[:, 0:128], [], ["ident_f"])
        dma("pool", cst[:], consts[:, 0:512], [], ["cst"])
        dma("sp", badat[:], bada_fm[:, :], [], ["badat"])
        dma("sp", vecs[:], vecs_fm[:, :], [], ["vecs"])
        dma("sp", csil[:], cfm[:, :], [], ["csil"])
        dma("sp", sinkexp[:], sink_bc[:, :], [], ["sinkexp"])
        dma("pool", ropet[:], ropes[:, :], [], ["ropet"])
        act(csil[:], csil[:], AF.Silu, ["csil"], ["csil"])
        act(sinkexp[:], sinkexp[:], AF.Exp, ["sinkexp"], ["sinkexp"])
        if p0 == 1:
            P.barrier()
            P.finish()
            return nc
        modv = modt[:].rearrange("p (l k c v) -> p l k c v", l=DEPTH, k=6, c=8)
        gsv = gs[:].rearrange("p (l n c v) -> p l n c v", l=DEPTH, n=2, c=8)
        modt3 = modt[:].rearrange("p (m v) -> p m v", v=2)
        vcopy(csil_b[:], csil[:], ["csil"], ["csil_b"])
        P.op("dve", lambda: nc.vector.memset(qh[96:128, :, :], 0.0), [], [f"qh{h}" for h in range(4)])

        def ada_piece(l, pc):
            wt, wk_ = get_weight(f"ada{l}_{pc}")
            ps, pk = psum()
            for c in range(8):
                mm(ps[0:2, :], csil_b[:, c * 2:c * 2 + 2], wt[:, c, :], c == 0, c == 7, [wk_, "csil_b"], [pk])
            t_, tk_ = ring("tt", tt)
            t_ = t_[0:2, :]
            act(t_, ps[0:2, :], AF.Copy, [pk], [tk_])
            ps2, pk2 = psum()
            for mi in range(4):
                P.op("pe", lambda: nc.tensor.transpose(ps2[:, mi * 2:mi * 2 + 2], t_[:, mi * 128:(mi + 1) * 128], ident_f[0:2, 0:2]),
                     [tk_, "ident_f"], [pk2])
            m0 = l * 48 + pc * 4
            tt_op(modt3[:, m0:m0 + 4, :], ps2[:, 0:8].rearrange("p (m v) -> p m v", v=2),
                  badat[:, m0:m0 + 4].unsqueeze(2).to_broadcast([128, 4, 2]), ALU.add, [pk2, "badat"], [f"modt{l}"])

        def ada_finish(l):
            for n in range(2):
                nv = vecs[:, n * 32 + l * 8:n * 32 + l * 8 + 8]
                ts_op(gsv[:, l, n], modv[:, l, 3 * n + 1], 1.0, ALU.add, [f"modt{l}"], [f"gs{l}"])
                tt_op(gsv[:, l, n], gsv[:, l, n], nv.unsqueeze(2).to_broadcast([128, 8, 2]), ALU.mult, [f"gs{l}", "vecs"], [f"gs{l}"])

        for l in ada_phase0:
            for pc in range(12):
                ada_piece(l, pc)
            ada_finish(l)
        P.barrier()

        def run_group(g):
            lat = (g == "A")
            T = 2048 if lat else 1024
            S = 2048 if lat else 256
            NT = T // 512
            NK = 2560 if lat else 1024
            NB = NK // 128
            v = 1 if lat else 0
            xT = xsT if lat else xpT
            yT = ysT if lat else ypT
            o = 0

            def carve(n, c=None):
                nonlocal o
                ap = R[:, o:o + n]
                o += n
                if c is not None:
                    ap = ap.rearrange("p (c t) -> p c t", c=c)
                return ap
            KT = carve(4 * NK, 4)
            Vm = carve(NB * 256, NB)
            ks = carve(NK)
            Vs = carve(NB * 128, NB)
            pbuf = carve(2 * T, 2)
            hjb = [carve(8 * 512, 8)]
            if not lat:
                hjb.append(carve(8 * 512, 8))
            H = {"ap": hjb[0], "k": "hj0_"}

            def use_buf(bi):
                H["ap"] = hjb[bi]
                H["k"] = f"hj{bi}_"

            def norm_to(l, j, bi):
                norm(l, 0, j, lambda c: hjb[bi][:, c, :], lambda c: f"hj{bi}_{c}")
            catj = carve(8 * 512, 8)
            cqn = carve(2 * 512, 2)
            qs = carve(2 * 512, 2)
            acc = carve(2 * 512, 2)
            ckT = carve(512)
            assert o <= 36352, o
            h2 = R[:, 0:8 * T].rearrange("p (c t) -> p c t", c=8)
            ub = [R[:, 8 * T + i * 2048:8 * T + (i + 1) * 2048].rearrange("p (c t) -> p c t", c=4) for i in range(2)]

            for c in range(8):
                dma("sp", x[:, c, 0:T], xT[c * 128:(c + 1) * 128, :], [], [f"x{c}_{j}" for j in range(NT)])

            def norm(l, n, j, dst, dkeys):
                ps, pk = psum()
                for c in range(8):
                    sq, sqk = ring("sqt", sqt)
                    act(sq[:], x[:, c, j * 512:(j + 1) * 512], AF.Square, [f"x{c}_{j}"], [sqk])
                    mm(ps[:, :], ones_b, sq[:], c == 0, c == 7, [sqk, "cst"], [pk])
                act(s_t[:], ps[:, :], AF.Sqrt, [pk], ["s_t", "rstd"], bias=EPS, scale=1.0 / D)
                recip(rstd[:], s_t[:], ["s_t"], ["s_t", "rstd"])
                for c in range(8):
                    t, tk = ring("tt", tt)
                    tt_op(t[:], x[:, c, j * 512:(j + 1) * 512], rstd[:], ALU.mult, [f"x{c}_{j}", "rstd"], [tk])
                    act(dst(c), t[:], AF.Identity, [tk, f"gs{l}", f"modt{l}"], [dkeys(c)],
                        bias=modv[:, l, 3 * n, c, v:v + 1], scale=gsv[:, l, n, c, v:v + 1])

            def attention_start(NQ, qT, qkeys, chunks, scale, sink_ap, out_ap, out_keys, G=1, LA=2):
                W = G * NQ
                CH = 512 // W
                ai = rr.get("acc", 0) % 2
                rr["acc"] = rr.get("acc", 0) + 1
                nump, numk = PS[3 + ai * 2], f"ps{3 + ai * 2}"
                denp, denk = PS[4 + ai * 2], f"ps{4 + ai * 2}"
                ones64 = ones_b[:, 0:64]
                n = len(chunks)
                groups_ = [chunks[g0:g0 + CH] for g0 in range(0, n, CH)]

                def view(ap2d):
                    return ap2d if G == 1 else ap2d.rearrange("p (g t) -> p g t", g=G)

                def emit_S(grp):
                    sb, sbk = psum()
                    for i, (kT, vv, mask, keys) in enumerate(grp):
                        mm(view(sb[:, i * W:(i + 1) * W]), kT, qT, True, mask is None, keys + qkeys, [sbk])
                        if mask is not None:
                            mrhs = mask if G == 1 else mask.unsqueeze(1).to_broadcast([128, G, NQ])
                            mm(view(sb[:, i * W:(i + 1) * W]), ident_b, mrhs, False, True, ["cst"], [sbk])
                    return sb, sbk

                ng = len(groups_)
                sbs = {}
                for gi in range(min(LA, ng)):
                    sbs[gi] = emit_S(groups_[gi])

                def run(next_start=None):
                    nxt_run = None
                    idx = 0
                    for g0 in range(0, ng, LA):
                        cur = list(range(g0, min(g0 + LA, ng)))
                        for gi in range(g0 + LA, min(g0 + 2 * LA, ng)):
                            sbs[gi] = emit_S(groups_[gi])
                        if g0 + LA >= ng and next_start is not None:
                            nxt_run = next_start()
                        pts = {}
                        for gi in cur:
                            sb, sbk = sbs.pop(gi)
                            p_, pk_ = ring("pt", pt)
                            w = len(groups_[gi]) * W
                            act(p_[:, 0:w], sb[:, 0:w], AF.Exp, [sbk], [pk_], scale=scale)
                            pts[gi] = (p_, pk_)
                        for gi in cur:
                            p_, pk_ = pts[gi]
                            for i, (kT, vv, mask, keys) in enumerate(groups_[gi]):
                                first = (idx == 0)
                                last = (idx == n - 1)
                                idx += 1
                                mm(nump[0:64, 0:W], vv, p_[:, i * W:(i + 1) * W], first, last, keys + [pk_], [numk])
                                mm(denp[0:64, 0:W], ones64, p_[:, i * W:(i + 1) * W], first, last, [pk_, "cst"], [denk])
                    rd, rdk = ring("rden", rden)
                    sinks = sink_ap if isinstance(sink_ap, list) else [sink_ap] * G
                    outs = out_ap if isinstance(out_ap, list) else [out_ap]
                    if sinks[0] is not None:
                        for gg in range(G):
                            ts_op(rd[:, gg * NQ:(gg + 1) * NQ], denp[0:64, gg * NQ:(gg + 1) * NQ], sinks[gg], ALU.add,
                                  [denk, "sinkexp"], [rdk])
                        recip(rd[:, 0:W], rd[:, 0:W], [rdk], [rdk])
                    else:
                        recip(rd[:, 0:W], denp[0:64, 0:W], [denk], [rdk])
                    for gg in range(G):
                        tt_op(outs[gg], nump[0:64, gg * NQ:(gg + 1) * NQ], rd[:, gg * NQ:(gg + 1) * NQ], ALU.mult,
                              [numk, rdk], out_keys)
                    return nxt_run

                return run

            def attention_seq(calls):
                run = attention_start(*calls[0][0], **calls[0][1])
                for i in range(len(calls)):
                    if i + 1 < len(calls):
                        na, nk = calls[i + 1]
                        run = run(lambda na=na, nk=nk: attention_start(*na, **nk))
                    else:
                        run(None)

            def rope(dst, src, H, Dh, gb, off, rk, wk):
                Q = Dh // 4
                cos = ropet[:, gb * 192 + off:gb * 192 + off + Dh]
                sin = ropet[:, gb * 192 + off + Dh:gb * 192 + off + 2 * Dh]
                t2, t2k = ring("tm3", tm3)
                s4 = src.rearrange("p (h a b q) -> p h a b q", h=H, a=2, b=2)
                d4 = t2[:, 0:H * Dh].rearrange("p (h a b q) -> p h a b q", h=H, a=2, b=2)
                sn = sin.rearrange("p (a b q) -> p a b q", a=2, b=2)
                for bb in range(2):
                    tt_op(d4[:, :, :, bb, :], s4[:, :, :, 1 - bb, :],
                          sn[:, :, bb, :].unsqueeze(1).to_broadcast([128, H, 2, Q]), ALU.mult,
                          rk + ["ropet"] + ([t2k] if bb else []), [t2k])
                s3 = src.rearrange("p (h d) -> p h d", h=H)
                d3 = dst.rearrange("p (h d) -> p h d", h=H)
                tt_op(d3, s3, cos.unsqueeze(1).to_broadcast([128, H, Dh]), ALU.mult, rk + ["ropet"], wk)
                tt_op(dst, dst, t2[:, 0:H * Dh], ALU.add, wk + [t2k], wk)

            def small_weights(l_):
                dma("pool", wq_t[:], wq[l_].rearrange("(c p) n -> p c n", p=128), [], ["wq_t"])
                dma("pool", wkv_t[:], wkv[l_], [], ["wkv_t"])
                dma("pool", wsT_t[:], sgu_wT[l_], [], ["wsT_t"])
                dma("pool", sgub_t[:], sgub[:, l_ * 512:(l_ + 1) * 512], [], ["sgub_t"])
                dma("sp", sgn[:], sgunorm_bc[:, l_ * 256:(l_ + 1) * 256], [], ["sgn"])
                dma("sp", kvn[:], kvnorm_bc[:, l_ * 128:(l_ + 1) * 128], [], ["kvn"])

            for l in range(depth):
                set_psring(range(8))
                if l == 0:
                    small_weights(0)
                P.op("dve", lambda: nc.vector.memset(KT[96:128, :, :], 0.0), [],
                     [f"KT{h}_{jj}" for h in range(4) for jj in range(NT + 1)])
                if lat:
                    dma("pool", ckT[:], ckvT_c[l], [], ["ckT"])
                    for h in range(4):
                        dma("pool", KT[64:96, h, T:T + 512], kpeT_c[l], [], [f"KT{h}_{NT}"])
                    dma("pool", ks[:, T:T + 512], skT_c[l], [], [f"ks_{NT}"])
                    dma("pool", Vs[:, 16:20, :], sv_c[l].rearrange("(b p) d -> p b d", p=128), [], [f"Vs_{NT}"])

                def kv_up(j, nblk):
                    for h in range(4):
                        ps, pk = psum()
                        mm(ps[0:64, 0:nblk * 128], wkv_t[:, h * 128:h * 128 + 64], ckT[:, 0:nblk * 128], True, True,
                           ["wkv_t", "ckT"], [pk])
                        act(KT[0:64, h, j * 512:j * 512 + nblk * 128], ps[0:64, 0:nblk * 128], AF.Copy, [pk], [f"KT{h}_{j}"])
                    for b in range(nblk):
                        ps, pk = psum()
                        mm(ps[:, :], ckT[:, b * 128:(b + 1) * 128], wkv_t[:], True, True, ["wkv_t", "ckT"], [pk])
                        vcopy(Vm[:, j * 4 + b, :].rearrange("p (h d) -> p h d", h=4),
                              ps[:, :].rearrange("p (h t d) -> p h t d", h=4, t=2)[:, :, 1, :], [pk], [f"Vm_{j}"])

                if lat:
                    kv_up(NT, 4)
                ckpt(10)

                for j in range(NT):
                    P.mark(f"{g}{l} K{j} norm")
                    if lat:
                        use_buf(0)
                        norm_to(l, j, 0)
                    else:
                        if j == 0:
                            norm_to(l, 0, 0)
                        if j + 1 < NT:
                            norm_to(l, j + 1, (j + 1) % 2)
                        use_buf(j % 2)
                    P.mark(f"{g}{l} K{j} fm")
                    wtF, wkF = get_weight(f"fmK{g}{l}{j}")
                    wtT, wkT = get_weight(f"tmK{g}{l}{j}", ahead=NST - 2)

                    def fmK_group(m):
                        ps, pk = psum()
                        for c in range(8):
                            mm(ps[:, :], wtF[:, c, m * 128:(m + 1) * 128], H["ap"][:, c, :], c == 0, c == 7, [wkF, f"{H['k']}{c}"], [pk])
                        if m < 2:
                            act(pbuf[:, m, j * 512:(j + 1) * 512], ps[:, :], AF.Copy, [pk], [f"p{m}_{j}"])
                        else:
                            tt_op(pbuf[:, m - 2, j * 512:(j + 1) * 512], ps[:, :], pbuf[:, m - 2, j * 512:(j + 1) * 512],
                                  ALU.mult, [pk, f"p{m - 2}_{j}"], [f"p{m - 2}_{j}"])

                    def tmK_mm(b):
                        tok = slice(b * 128, (b + 1) * 128)
                        ps, pk = psum()
                        for c in range(8):
                            mm(ps[:, 0:416], H["ap"][:, c, tok], wtT[:, c, :], c == 0, c == 7, [wkT, f"{H['k']}{c}"], [pk])
                        return ps, pk

                    def tmK_ew(b, ps, pk):
                        gb = j * 4 + b
                        st, stk = ring("tmf", tmf)
                        act(st[:, 0:416], ps[:, 0:416], AF.Copy, [pk], [stk])
                        smt, smk = ring("sm", sm)
                        t2, t2k = ring("tm3", tm3)
                        P.op("dve", lambda: nc.vector.memset(smt[:, 0:1], 0.0), [], [smk])
                        act(t2[:, 0:128], st[:, 0:128], AF.Square, [stk, smk], [t2k, smk], accum_out=smt[:, 0:1])
                        act(smt[:, 1:2], smt[:, 0:1], AF.Sqrt, [smk], [smk], bias=EPS, scale=1.0 / 128)
                        recip(smt[:, 2:3], smt[:, 1:2], [smk], [smk])
                        stt(st[:, 0:128], st[:, 0:128], smt[:, 2:3], kvn[:], ALU.mult, ALU.mult, [stk, smk, "kvn"], [stk])
                        if lat:
                            t3, t3k = ring("tm2", tm2)
                            rope(t3[:, 0:32], st[:, 128:160], 1, 32, gb, 128, [stk], [t3k])
                            kpe_src, kpek = t3[:, 0:32], t3k
                            t4, t4k = ring("tm2", tm2)
                            rope(t4[:, 0:128], st[:, 160:288], 2, 64, gb, 0, [stk], [t4k])
                            sk_src, skk = t4[:, 0:128], t4k
                        else:
                            kpe_src, kpek = st[:, 128:160], stk
                            sk_src, skk = st[:, 160:288], stk
                            sq_, r0 = divmod(gb * 128, 256)
                            dma("sp", o_ckv[sq_, l, r0:r0 + 128, :], st[:, 0:128], [stk], [], is_out=True)
                            dma("sp", o_kpe[sq_, l, r0:r0 + 128, :], st[:, 128:160], [stk], [], is_out=True)
                            dma("sp", o_k[sq_, l, r0:r0 + 128, :], st[:, 160:288], [stk], [], is_out=True)
                            dma("sp", o_v[sq_, l, r0:r0 + 128, :], st[:, 288:416], [stk], [], is_out=True)
                        vcopy(Vs[:, gb, :], st[:, 288:416], [stk], [f"Vs_{j}"])
                        return st, stk, kpe_src, kpek, sk_src, skk

                    def tmK_tr(b, st, stk, kpe_src, kpek, sk_src, skk):
                        gb = j * 4 + b
                        tok = slice(b * 128, (b + 1) * 128)
                        ps2, pk2 = psum()
                        P.op("pe", lambda: nc.tensor.transpose(ps2[:, 0:128], st[:, 0:128], ident_f[:]), [stk, "ident_f"], [pk2])
                        P.op("pe", lambda: nc.tensor.transpose(ps2[:, 128:256], sk_src, ident_f[:]), [skk, "ident_f"], [pk2])
                        P.op("pe", lambda: nc.tensor.transpose(ps2[0:32, 256:384], kpe_src, ident_f[:]), [kpek, "ident_f"], [pk2])
                        act(ckT[:, tok], ps2[:, 0:128], AF.Copy, [pk2], ["ckT"])
                        vcopy(ks[:, gb * 128:(gb + 1) * 128], ps2[:, 128:256], [pk2], [f"ks_{j}"])
                        act(KT[64:96, 0:2, gb * 128:(gb + 1) * 128], ps2[0:32, 256:384].unsqueeze(1).to_broadcast([32, 2, 128]),
                            AF.Copy, [pk2], [f"KT0_{j}", f"KT1_{j}"])
                        vcopy(KT[64:96, 2:4, gb * 128:(gb + 1) * 128], ps2[0:32, 256:384].unsqueeze(1).to_broadcast([32, 2, 128]),
                              [pk2], [f"KT2_{j}", f"KT3_{j}"])

                    P.mark(f"{g}{l} K{j} tm")
                    r = {}
                    e = {}
                    r[0] = tmK_mm(0)
                    r[1] = tmK_mm(1)
                    for b in range(4):
                        e[b] = tmK_ew(b, *r[b])
                        fmK_group(b)
                        tmK_tr(b, *e[b])
                        if b + 2 < 4:
                            r[b + 2] = tmK_mm(b + 2)
                    P.mark(f"{g}{l} K{j} kvup")
                    kv_up(j, 4)
                    ckpt(14)

                nseq_t = 512 // min(S, 512)
                for j in range(NT):
                    set_psring(range(8))
                    if lat:
                        use_buf(0)
                        if j == 0:
                            P.mark(f"{g}{l} Q{j} norm")
                            norm_to(l, 0, 0)
                    else:
                        use_buf(j % 2)
                    P.mark(f"{g}{l} Q{j} fm")
                    if debug and l == 0 and j == 0:
                        dma("sp", dbg_h, H["ap"], [f"{H['k']}{c}" for c in range(8)], [], is_out=True)
                    wa, wak = get_weight(f"fmQa{g}{l}{j}")

                    def fmQ_group(m, wt, wk_):
                        mi = m % 3
                        ps, pk = psum()
                        for c in range(8):
                            mm(ps[:, :], wt[:, c, mi * 128:(mi + 1) * 128], H["ap"][:, c, :], c == 0, c == 7, [wk_, f"{H['k']}{c}"], [pk])
                        if m < 2:
                            act(catj[:, m, :], ps[:, :], AF.Copy, [pk], [f"cat{m}"])
                        elif m < 4:
                            act(catj[:, m, :], ps[:, :], AF.Gelu_apprx_tanh, [pk], [f"cat{m}"])
                        else:
                            cr, crk = cqraw[m - 4], f"cqraw{m - 4}"
                            act(cr[:], ps[:, :], AF.Copy, [pk], [crk])

                    for m in range(3):
                        fmQ_group(m, wa, wak)
                    wb, wbk = get_weight(f"fmQb{g}{l}{j}")
                    wtT, wkT = get_weight(f"tmQ{g}{l}{j}", ahead=NST - 2)

                    def cq_conv():
                        ps, pk = psum()
                        for c in range(2):
                            sq, sqk = ring("sqt", sqt)
                            act(sq[:], cqraw[c][:], AF.Square, [f"cqraw{c}"], [sqk])
                            mm(ps[:, :], ones_b, sq[:], c == 0, c == 1, [sqk, "cst"], [pk])
                        act(s_t[:], ps[:, :], AF.Sqrt, [pk], ["s_t", "rstd"], bias=EPS, scale=1.0 / 256)
                        recip(rstd[:], s_t[:], ["s_t"], ["s_t", "rstd"])
                        for c in range(2):
                            stt(cqn[:, c, :], cqraw[c][:], vecs[:, 72 + l * 2 + c:72 + l * 2 + c + 1], rstd[:], ALU.mult, ALU.mult,
                                [f"cqraw{c}", "vecs", "rstd"], [f"cqn{c}"])
                        Sq = min(S, 512)
                        for c in range(2):
                            cw = lambda k: vecs[:, 80 + (l * 2 + c) * 3 + k:80 + (l * 2 + c) * 3 + k + 1]
                            lo = j * 512
                            pk_all = [f"p{c}_{jj}" for jj in range(NT)]
                            ts_op(acc[:, c, :], pbuf[:, c, lo:lo + 512], cw(1), ALU.mult, pk_all + ["vecs"], [f"acc{c}"])
                            for sidx in range(nseq_t):
                                a0 = sidx * Sq
                                g0 = lo + a0
                                first_in_seq = (g0 % S == 0)
                                last_in_seq = ((g0 + Sq) % S == 0)
                                s0 = 1 if first_in_seq else 0
                                stt(acc[:, c, a0 + s0:a0 + Sq], pbuf[:, c, g0 + s0 - 1:g0 + Sq - 1], cw(0), acc[:, c, a0 + s0:a0 + Sq],
                                    ALU.mult, ALU.add, pk_all + ["vecs", f"acc{c}"], [f"acc{c}"])
                                e0 = 1 if last_in_seq else 0
                                stt(acc[:, c, a0:a0 + Sq - e0], pbuf[:, c, g0 + 1:g0 + Sq - e0 + 1], cw(2), acc[:, c, a0:a0 + Sq - e0],
                                    ALU.mult, ALU.add, pk_all + ["vecs", f"acc{c}"], [f"acc{c}"])
                            tt_op(catj[:, c, :], catj[:, c, :], acc[:, c, :], ALU.mult, [f"cat{c}", f"acc{c}"], [f"cat{c}"])

                    def tmQ_mm(b):
                        tok = slice(b * 128, (b + 1) * 128)
                        ps, pk = psum()
                        for c in range(8):
                            mm(ps[:, :], H["ap"][:, c, tok], wtT[:, c, :], c == 0, c == 7, [wkT, f"{H['k']}{c}"], [pk])
                        return ps, pk

                    def tmQ_ew(b, ps, pk):
                        gb = j * 4 + b
                        st, stk = ring("tmf", tmf)
                        act(st[:, 0:256], ps[:, 0:256], AF.Gelu_apprx_tanh, [pk], [stk])
                        act(st[:, 256:512], ps[:, 256:512], AF.Copy, [pk], [stk])
                        smt, smk = ring("sm", sm)
                        t2, t2k = ring("tm3", tm3)
                        P.op("dve", lambda: nc.vector.memset(smt[:, 0:1], 0.0), [], [smk])
                        act(t2[:, 0:256], st[:, 0:256], AF.Square, [stk, smk], [t2k, smk], accum_out=smt[:, 0:1])
                        act(smt[:, 1:2], smt[:, 0:1], AF.Sqrt, [smk], [smk], bias=EPS, scale=1.0 / 256)
                        recip(smt[:, 2:3], smt[:, 1:2], [smk], [smk])
                        vn, vnk = ring("vn", vn_t)
                        stt(vn[:], st[:, 0:256], smt[:, 2:3], sgn[:], ALU.mult, ALU.mult, [stk, smk, "sgn"], [vnk])
                        if lat:
                            t4, t4k = ring("tm2", tm2)
                            rope(t4[:, 0:256], st[:, 256:512], 4, 64, gb, 0, [stk], [t4k])
                            return st, stk, vn, vnk, t4, t4k
                        return st, stk, vn, vnk, None, None

                    def tmQ_tail(b, st, stk, vn, vnk, t4, t4k):
                        tok = slice(b * 128, (b + 1) * 128)
                        ps2, pk2 = psum()
                        for hd in range(4):
                            cc, e_ = divmod(hd, 2)
                            mm(ps2[:, hd * 128:(hd + 1) * 128], vn[:, cc * 128:(cc + 1) * 128], wsT_t[:, hd * 128:(hd + 1) * 128],
                               True, False, [vnk, "wsT_t"], [pk2], sig=False)
                            mm(ps2[:, hd * 128:(hd + 1) * 128], ones_b[0:1, 0:128], sgub_t[0:1, hd * 128:(hd + 1) * 128],
                               False, True, ["cst", "sgub_t"], [pk2], sig=True)
                        ps3, pk3 = psum()
                        for gg in range(2):
                            src_ap = (t4[:, gg * 128:(gg + 1) * 128] if lat else st[:, 256 + gg * 128:256 + (gg + 1) * 128])
                            P.op("pe", lambda: nc.tensor.transpose(ps3[:, gg * 128:(gg + 1) * 128], src_ap, ident_f[:]),
                                 [t4k if lat else stk, "ident_f"], [pk3])
                        for hd in range(4):
                            cc, e_ = divmod(hd, 2)
                            tt_op(catj[e_ * 64:(e_ + 1) * 64, 2 + cc, tok], catj[e_ * 64:(e_ + 1) * 64, 2 + cc, tok],
                                  ps2[e_ * 64:(e_ + 1) * 64, hd * 128:(hd + 1) * 128], ALU.mult, [f"cat{2 + cc}", pk2], [f"cat{2 + cc}"])
                        act(qs[:, :, tok], ps3[:, 0:256].rearrange("p (g t) -> p g t", g=2), AF.Copy, [pk3], ["qs"])

                    def qm_mm(b):
                        tokq = slice(b * 128, (b + 1) * 128)
                        ps, pk = psum()
                        for c in range(2):
                            mm(ps[:, 0:384], cqn[:, c, tokq], wq_t[:, c, :], c == 0, c == 1, [f"cqn{c}", "wq_t"], [pk])
                        return ps, pk

                    def qm_ew(b, ps, pk):
                        gb = j * 4 + b
                        st, stk = ring("tmf", tmf)
                        act(st[:, 0:384], ps[:, 0:384], AF.Copy, [pk], [stk])
                        if lat:
                            t3, t3k = ring("tm2", tm2)
                            src4 = st[:, 0:384].rearrange("p (h d) -> p h d", h=4)[:, :, 64:96]
                            t5, t5k = ring("tm2", tm2)
                            vcopy(t5[:, 0:128].rearrange("p (h d) -> p h d", h=4), src4, [stk], [t5k])
                            rope(t3[:, 0:128], t5[:, 0:128], 4, 32, gb, 128, [t5k], [t3k])
                            vcopy(src4, t3[:, 0:128].rearrange("p (h d) -> p h d", h=4), [t3k], [stk])
                        return st, stk

                    def qm_tr(b, st, stk):
                        tokq = slice(b * 128, (b + 1) * 128)
                        ps, pk = psum()
                        for h in range(4):
                            P.op("pe", lambda: nc.tensor.transpose(ps[0:96, h * 128:(h + 1) * 128], st[:, h * 96:(h + 1) * 96], ident_f[:]),
                                 [stk, "ident_f"], [pk])
                        src = ps[0:96, :].rearrange("p (h t) -> p h t", h=4)
                        if b % 2 == 0:
                            act(qh[0:96, :, tokq], src, AF.Copy, [pk], [f"qh{h}" for h in range(4)])
                        else:
                            vcopy(qh[0:96, :, tokq], src, [pk], [f"qh{h}" for h in range(4)])

                    P.mark(f"{g}{l} Q{j} tm")
                    r = {}
                    e = {}
                    rq = {}
                    for m in range(3, 6):
                        fmQ_group(m, wb, wbk)
                    cq_conv()
                    r[0] = tmQ_mm(0)
                    r[1] = tmQ_mm(1)
                    rq[0] = qm_mm(0)
                    for b in range(4):
                        e[b] = tmQ_ew(b, *r[b])
                        eq = qm_ew(b, *rq[b])
                        if b + 1 < 4:
                            rq[b + 1] = qm_mm(b + 1)
                        tmQ_tail(b, *e[b])
                        if b + 2 < 4:
                            r[b + 2] = tmQ_mm(b + 2)
                        qm_tr(b, *eq)
                    ckpt(19)
                    set_psring([0, 1, 2, 7])
                    ckpt(20)
                    P.mark(f"{g}{l} Q{j} MLA")
                    acalls = []
                    if lat:
                        for h in range(4):
                            chunks = []
                            for kc in range(20):
                                jt = kc // 4
                                chunks.append((KT[:, h, kc * 128:(kc + 1) * 128], Vm[:, kc, h * 64:(h + 1) * 64], None,
                                               [f"KT{h}_{jt}", f"Vm_{jt}"]))
                            cc, e = divmod(h, 2)
                            acalls.append(((512, qh[:, h, :], [f"qh{h}"], chunks, MLA_SCALE, None,
                                            catj[e * 64:(e + 1) * 64, 4 + cc, :], [f"cat{4 + cc}"]), {}))
                    else:
                        for sl in range(2):
                            sidx = (j * 512) // 256 + sl
                            qsl = slice(sl * 256, (sl + 1) * 256)
                            for h in range(4):
                                chunks = []
                                for kc in (2 * sidx, 2 * sidx + 1):
                                    jt = kc // 4
                                    chunks.append((KT[:, h, kc * 128:(kc + 1) * 128], Vm[:, kc, h * 64:(h + 1) * 64], None,
                                                   [f"KT{h}_{jt}", f"Vm_{jt}"]))
                                cc, e = divmod(h, 2)
                                acalls.append(((256, qh[:, h, qsl], [f"qh{h}"], chunks, MLA_SCALE, None,
                                                catj[e * 64:(e + 1) * 64, 4 + cc, qsl], [f"cat{4 + cc}"]), {}))
                    ckpt(21)
                    if lat and j + 1 < NT:
                        norm_to(l, j + 1, 0)
                    P.mark(f"{g}{l} Q{j} SWA")
                    for n in range(2):
                        sink_l = [sinkexp[0:64, l * 4 + n * 2 + gq:l * 4 + n * 2 + gq + 1] for gq in range(2)]
                        if lat:
                            for bq in range(4):
                                blk = j * 4 + bq
                                tq = slice(bq * 128, (bq + 1) * 128)
                                chunks = []
                                if blk >= 1:
                                    chunks.append((ks[n * 64:(n + 1) * 64, (blk - 1) * 128:blk * 128],
                                                   Vs[:, blk - 1, n * 64:(n + 1) * 64], mprev,
                                                   [f"ks_{(blk - 1) // 4}", f"Vs_{(blk - 1) // 4}"]))
                                chunks.append((ks[n * 64:(n + 1) * 64, blk * 128:(blk + 1) * 128],
                                               Vs[:, blk, n * 64:(n + 1) * 64], None, [f"ks_{blk // 4}", f"Vs_{blk // 4}"]))
                                if blk <= 14:
                                    chunks.append((ks[n * 64:(n + 1) * 64, (blk + 1) * 128:(blk + 2) * 128],
                                                   Vs[:, blk + 1, n * 64:(n + 1) * 64], mnext,
                                                   [f"ks_{(blk + 1) // 4}", f"Vs_{(blk + 1) // 4}"]))
                                for kc in range(16, 20):
                                    chunks.append((ks[n * 64:(n + 1) * 64, kc * 128:(kc + 1) * 128],
                                                   Vs[:, kc, n * 64:(n + 1) * 64], None, [f"ks_{NT}", f"Vs_{NT}"]))
                                acalls.append(((128, qs[n * 64:(n + 1) * 64, :, tq], ["qs"], chunks, SWA_SCALE, sink_l,
                                                [catj[gq * 64:(gq + 1) * 64, 6 + n, tq] for gq in range(2)], [f"cat{6 + n}"]),
                                               {"G": 2}))
                        else:
                            for sl in range(2):
                                sidx = (j * 512) // 256 + sl
                                qsl = slice(sl * 256, (sl + 1) * 256)
                                chunks = []
                                for kc in (2 * sidx, 2 * sidx + 1):
                                    chunks.append((ks[n * 64:(n + 1) * 64, kc * 128:(kc + 1) * 128],
                                                   Vs[:, kc, n * 64:(n + 1) * 64], None, [f"ks_{kc // 4}", f"Vs_{kc // 4}"]))
                                acalls.append(((256, qs[n * 64:(n + 1) * 64, :, qsl], ["qs"], chunks, SWA_SCALE, sink_l,
                                                [catj[gq * 64:(gq + 1) * 64, 6 + n, qsl] for gq in range(2)], [f"cat{6 + n}"]),
                                               {"G": 2}))
                    attention_seq(acalls)
                    ckpt(22)
                    P.mark(f"{g}{l} Q{j} wout")
                    set_psring(range(8))
                    if debug and l == 0 and j == 0:
                        dma("sp", dbg_cat, catj, [f"cat{c}" for c in range(8)], [], is_out=True)
                    for half in range(2):
                        wt, wk_ = get_weight(f"wo{half}{g}{l}{j}")
                        for mi in range(4):
                            m = half * 4 + mi
                            ps, pk = psum()
                            for c in range(8):
                                mm(ps[:, :], wt[:, c, mi * 128:(mi + 1) * 128], catj[:, c, :], c == 0, c == 7, [wk_, f"cat{c}"], [pk])
                            xs_ = x[:, m, j * 512:(j + 1) * 512]
                            stt(xs_, ps[:, :], modv[:, l, 2, m, v:v + 1], xs_, ALU.mult, ALU.add, [pk, f"modt{l}", f"x{m}_{j}"], [f"x{m}_{j}"])
                P.barrier()
                if debug and l == 0:
                    dma("sp", dbg_x1[:, :, 0:T], x[:, :, 0:T], [], [], is_out=True)
                    P.barrier()
                ckpt(23)
                P.mark(f"{g}{l} MLP norm")
                set_psring(range(8))
                if l + 1 < depth:
                    small_weights(l + 1)
                ckpt(24)
                P.mark(f"{g}{l} MLP mm")
                ada_pc = [0]
                for jh in range(8):
                    wa, wak = get_weight(f"w1{g}{l}{jh}")
                    wb, wbk = get_weight(f"w2{g}{l}{jh}", ahead=NST - 2)
                    def mlp_up(j):
                        if jh == 0:
                            norm(l, 1, j, lambda c: h2[:, c, j * 512:(j + 1) * 512], lambda c: f"h2{c}_{j}")
                        u, uk = ring("ub", ub)
                        for hc in range(4):
                            ps, pk = psum()
                            for c in range(8):
                                mm(ps[:, :], wa[:, c, hc * 128:(hc + 1) * 128], h2[:, c, j * 512:(j + 1) * 512], c == 0, c == 7,
                                   [wak, f"h2{c}_{j}"], [pk])
                            r_, rk_ = ring("relu", relu_t)
                            act(r_[:], ps[:, :], AF.Relu, [pk], [rk_])
                            tt_op(u[:, hc, :], r_[:], r_[:], ALU.mult, [rk_], [f"{uk}_{hc}"])
                        return u, uk

                    def mlp_down(j, u, uk):
                        for m in range(8):
                            ps, pk = psum()
                            for hc in range(4):
                                mm(ps[:, :], wb[:, hc, m * 128:(m + 1) * 128], u[:, hc, :], hc == 0, hc == 3, [wbk, f"{uk}_{hc}"], [pk])
                            xs_ = x[:, m, j * 512:(j + 1) * 512]
                            stt(xs_, ps[:, :], modv[:, l, 5, m, v:v + 1], xs_, ALU.mult, ALU.add, [pk, f"modt{l}", f"x{m}_{j}"], [f"x{m}_{j}"])

                    us = {0: mlp_up(0)}
                    for j in range(NT):
                        if j + 1 < NT:
                            us[j + 1] = mlp_up(j + 1)
                        mlp_down(j, *us.pop(j))
                    if ada_defer and g == "A" and l + 1 < DEPTH:
                        for _ in range(ADA_SPLIT[jh]):
                            ada_piece(l + 1, ada_pc[0])
                            ada_pc[0] += 1
                        if jh == 7:
                            ada_finish(l + 1)
                P.barrier()
                if debug and l == 0:
                    dma("sp", dbg_x2[:, :, 0:T], x[:, :, 0:T], [], [], is_out=True)
                    P.barrier()
            P.mark(f"{g} final")
            for j in range(NT):
                ps, pk = psum()
                for c in range(8):
                    sq, sqk = ring("sqt", sqt)
                    act(sq[:], x[:, c, j * 512:(j + 1) * 512], AF.Square, [f"x{c}_{j}"], [sqk])
                    mm(ps[:, :], ones_b, sq[:], c == 0, c == 7, [sqk, "cst"], [pk])
                act(s_t[:], ps[:, :], AF.Sqrt, [pk], ["s_t", "rstd"], bias=EPS, scale=1.0 / D)
                recip(rstd[:], s_t[:], ["s_t"], ["s_t", "rstd"])
                for c in range(8):
                    y_, yk = ring("tt", tt)
                    stt(y_[:], x[:, c, j * 512:(j + 1) * 512], vecs[:, 64 + c:65 + c], rstd[:], ALU.mult, ALU.mult,
                        [f"x{c}_{j}", "vecs", "rstd"], [yk])
                    dma("sp", yT[c * 128:(c + 1) * 128, j * 512:(j + 1) * 512], y_[:], [yk], [], is_out=True)
            P.barrier()

        try:
            for g_ in groups:
                run_group(g_)
            P.mark("end")
            P.finish()
        except StopBuild:
            pass
        MARKS.extend(P.marks)
    return nc


_CACHE = {}


def _consts():
    ident = np.eye(128, dtype=np.float32)
    ones = np.ones((128, 128), np.float32)
    kk = np.arange(128)[:, None]
    qq = np.arange(128)[None, :]
    mprev = np.where(kk >= qq, 0.0, NEG).astype(np.float32)
    mnext = np.where(kk <= qq, 0.0, NEG).astype(np.float32)
    c = np.concatenate([ident, ones, mprev, mnext, np.zeros((128, 128), np.float32)], axis=1)
    def tables(rot_dim):
        half = rot_dim // 2
        inv = (10000.0 ** (-np.arange(0, half, 2, dtype=np.float32) / half)).astype(np.float32)
        t = np.arange(2048)
        row = (t // 64).astype(np.float32)
        col = (t % 64).astype(np.float32)
        ar = row[:, None] * inv[None, :]
        ac = col[:, None] * inv[None, :]
        ang = np.concatenate([ar, ar, ac, ac], axis=-1).astype(np.float32)
        cos = np.cos(ang).astype(np.float32)
        sin = np.sin(ang).astype(np.float32)
        q = rot_dim // 4
        sgn = np.concatenate([-np.ones(q), np.ones(q), -np.ones(q), np.ones(q)]).astype(np.float32)
        return cos, sin * sgn[None, :]
    cs, ss = tables(64)
    cm, sm_ = tables(32)
    r = np.concatenate([cs, ss, cm, sm_], axis=1)
    r = r.reshape(16, 128, 192).transpose(1, 0, 2).reshape(128, 16 * 192)
    return np.ascontiguousarray(c), np.ascontiguousarray(r.astype(np.float32))


def kernel(x_prompt, x_sample, cache_mla_ckv, cache_mla_kpe, cache_swa_k, cache_swa_v, c, c_ctx,
           w_ada, b_ada, norm1, norm2, w_in, conv_w, sgu_norm, sgu_w, sgu_b, mla_q_norm, mla_w_q_up,
           mla_kv_norm, mla_w_kv_up, swa_sink, w_out, mlp_w1, mlp_w2, final_norm):
    in_maps = pack_inputs(x_prompt, x_sample, cache_mla_ckv, cache_mla_kpe, cache_swa_k, cache_swa_v, c, c_ctx,
                          w_ada, b_ada, norm1, norm2, w_in, conv_w, sgu_norm, sgu_w, sgu_b, mla_q_norm, mla_w_q_up,
                          mla_kv_norm, mla_w_kv_up, swa_sink, w_out, mlp_w1, mlp_w2, final_norm)
    if "nc" not in _CACHE:
        _CACHE["nc"] = build_program()
    nc = _CACHE["nc"]
    res = run_bass_kernel_spmd(nc, in_maps, core_ids=list(range(NCORES)))
    return unpack_outputs(res.results)


def pack_inputs(x_prompt, x_sample, cache_mla_ckv, cache_mla_kpe, cache_swa_k, cache_swa_v, c, c_ctx,
                w_ada, b_ada, norm1, norm2, w_in, conv_w, sgu_norm, sgu_w, sgu_b, mla_q_norm, mla_w_q_up,
                mla_kv_norm, mla_w_kv_up, swa_sink, w_out, mlp_w1, mlp_w2, final_norm, cores=range(NCORES)):
    f = lambda a: np.ascontiguousarray(np.asarray(a, dtype=np.float32))
    x_prompt, x_sample = f(x_prompt), f(x_sample)
    consts, ropes = _consts()
    w_in = f(w_in)
    a_b, a_c, a_x = w_in[:, :, 0:256], w_in[:, :, 256:512], w_in[:, :, 512:768]
    u_, v_ = w_in[:, :, 768:1024], w_in[:, :, 1024:1280]
    cq, ckv, kpe = w_in[:, :, 1280:1536], w_in[:, :, 1536:1664], w_in[:, :, 1664:1696]
    sq, sk, sv = w_in[:, :, 1696:1952], w_in[:, :, 1952:2080], w_in[:, :, 2080:2208]
    sq_g = sq.reshape(DEPTH, D, 2, 2, 64).transpose(0, 1, 3, 2, 4).reshape(DEPTH, D, 256)
    w_fm = f(np.concatenate([a_c, a_x, a_b, u_, cq], axis=2))
    w_tm = f(np.concatenate([ckv, kpe, sk, sv, v_, sq_g], axis=2))
    bada_fm = f(np.asarray(b_ada).reshape(DEPTH, 48, 128).transpose(2, 0, 1).reshape(128, DEPTH * 48))
    vecs = np.zeros((128, 128), np.float32)
    vecs[:, 0:32] = np.asarray(norm1).reshape(DEPTH, 8, 128).transpose(2, 0, 1).reshape(128, 32)
    vecs[:, 32:64] = np.asarray(norm2).reshape(DEPTH, 8, 128).transpose(2, 0, 1).reshape(128, 32)
    vecs[:, 64:72] = np.asarray(final_norm).reshape(8, 128).T
    vecs[:, 72:80] = np.asarray(mla_q_norm).reshape(DEPTH, 2, 128).transpose(2, 0, 1).reshape(128, 8)
    vecs[:, 80:104] = np.asarray(conv_w).reshape(DEPTH, 3, 2, 128).transpose(3, 0, 2, 1).reshape(128, 24)
    sgunorm_bc = f(np.broadcast_to(np.asarray(sgu_norm).reshape(1, DEPTH * 256), (128, DEPTH * 256)))
    kvnorm_bc = f(np.broadcast_to(np.asarray(mla_kv_norm).reshape(1, DEPTH * 128), (128, DEPTH * 128)))
    sink_bc = f(np.broadcast_to(np.asarray(swa_sink).reshape(1, 16), (128, 16)))
    sgub = f(np.asarray(sgu_b).reshape(1, DEPTH * 512))
    sgu_wT = f(np.asarray(sgu_w).transpose(0, 3, 1, 2).reshape(DEPTH, 128, 512))
    shared = dict(w_ada=f(w_ada), bada_fm=bada_fm, vecs_fm=vecs, sgunorm_bc=sgunorm_bc, kvnorm_bc=kvnorm_bc,
                  sink_bc=sink_bc, sgub=sgub, sgu_wT=sgu_wT, w_fm=w_fm, w_tm=w_tm, w_out=f(w_out), w1=f(mlp_w1),
                  w2=f(mlp_w2), wq=f(mla_w_q_up), wkv=f(mla_w_kv_up), ropes=ropes, consts=consts)
    c = np.asarray(c, np.float32)
    c_ctx = np.asarray(c_ctx, np.float32)
    in_maps = []
    for i in cores:
        cv = np.stack([c_ctx, c[i]], axis=0)
        cfm = f(cv.reshape(2, 8, 128).transpose(2, 1, 0).reshape(128, 16))
        m = dict(shared)
        m.update(
            xsT=f(x_sample[i].T),
            xpT=f(x_prompt[4 * i:4 * i + 4].reshape(1024, D).T),
            ckvT_c=f(np.asarray(cache_mla_ckv[i]).transpose(0, 2, 1)),
            kpeT_c=f(np.asarray(cache_mla_kpe[i]).transpose(0, 2, 1)),
            skT_c=f(np.asarray(cache_swa_k[i]).reshape(DEPTH, 512, 128).transpose(0, 2, 1)),
            sv_c=f(np.asarray(cache_swa_v[i]).reshape(DEPTH, 512, 128)),
            cfm=cfm,
        )
        in_maps.append(m)
    return in_maps


def unpack_outputs(rs):
    y_prompt = np.concatenate([r["ypT"].T.reshape(4, 256, D) for r in rs], axis=0).astype(np.float32)
    y_sample = np.stack([r["ysT"].T for r in rs], axis=0).astype(np.float32)
    new_ckv = np.concatenate([r["o_ckv"] for r in rs], axis=0).astype(np.float32)
    new_kpe = np.concatenate([r["o_kpe"] for r in rs], axis=0).astype(np.float32)
    new_k = np.concatenate([r["o_k"] for r in rs], axis=0).reshape(-1, DEPTH, 256, 2, 64).astype(np.float32)
    new_v = np.concatenate([r["o_v"] for r in rs], axis=0).reshape(-1, DEPTH, 256, 2, 64).astype(np.float32)
    return (np.ascontiguousarray(y_prompt), np.ascontiguousarray(y_sample), new_ckv, new_kpe, new_k, new_v)
```

```python
import numpy as np
import concourse.bass as bass
import concourse.mybir as mybir
from concourse.bass_utils import run_bass_kernel_spmd

F32, BF16 = mybir.dt.float32, mybir.dt.bfloat16
AF = mybir.ActivationFunctionType
ALU = mybir.AluOpType

D = 1024
DEPTH = 4
EPS = 1e-6
MLA_SCALE = 96 ** -0.5
SWA_SCALE = 0.125
NEG = -30000.0
NCORES = 8


class Info:
    __slots__ = ("sem", "val", "clock", "eng")

    def __init__(self, eng):
        self.sem = None
        self.val = 0
        self.clock = None
        self.eng = eng


class Prog:
    COMPUTE = ("pe", "act", "dve")
    QUEUES = ("sp", "pool")
    NSL = 6

    def __init__(self, nc):
        self.nc = nc
        self.eng = {"pe": nc.tensor, "act": nc.scalar, "dve": nc.vector, "pool": nc.gpsimd, "sp": nc.sync}
        self.sems = {}
        self.epoch = 0
        self.csem = {}
        self.cnt = {}
        self._new_epoch_sems()
        self.slots = {}
        self.nd = {q: 0 for q in self.QUEUES}
        for q in self.QUEUES:
            self.slots[q] = []
            for i in range(self.NSL):
                nm = f"d_{q}{i}"
                self.sems[nm] = nc.alloc_semaphore(name=nm)
                self.slots[q].append([nm, 0])
        self.clock = {e: {} for e in self.eng}
        self.last_w = {}
        self.readers = {}
        self.pending = {e: [] for e in self.COMPUTE}
        self.out_infos = []
        self.total = {e: 0 for e in self.eng}
        self.ps_open = {}
        self.marks = []

    def _new_epoch_sems(self):
        for e in self.COMPUTE:
            nm = f"c_{e}{self.epoch}"
            self.sems[nm] = self.nc.alloc_semaphore(name=nm)
            self.csem[e] = nm
            self.cnt[e] = 0
        self.epoch += 1

    def _wait(self, e, info):
        if info.sem is None:
            raise RuntimeError("dependency on unsignaled op")
        ck = self.clock[e]
        if ck.get(info.sem, 0) >= info.val:
            return
        self.eng[e].wait_ge(self.sems[info.sem], info.val)
        for s, v in info.clock.items():
            if ck.get(s, 0) < v:
                ck[s] = v

    def op(self, e, fn, reads=(), writes=(), sig=True, dma=False, is_out=False):
        psr = [k for k in reads if k.startswith("ps")]
        for k in psr:
            self.ps_open[k] = False
        if psr:
            writes = list(writes) + [k for k in psr if k not in writes]
        deps = []
        for k in reads:
            w = self.last_w.get(k)
            if w is not None:
                deps.append((w, True))
        for k in writes:
            w = self.last_w.get(k)
            if w is not None:
                deps.append((w, False))
            rd = self.readers.get(k)
            if rd:
                for r in rd.values():
                    deps.append((r, False))
        for info, raw in deps:
            if info.eng == e and not dma:
                if e == "pe":
                    continue
            self._wait(e, info)
        self.total[e] += 1
        info = Info(e)
        if dma:
            n = self.nd[e]
            self.nd[e] += 1
            slot = self.slots[e][n % self.NSL]
            ck = self.clock[e]
            if slot[1] > 0 and ck.get(slot[0], 0) < slot[1]:
                self.eng[e].wait_ge(self.sems[slot[0]], slot[1])
                ck[slot[0]] = slot[1]
            slot[1] += 16
            inst = fn()
            inst.then_inc(self.sems[slot[0]], 16)
            info.sem, info.val = slot[0], slot[1]
            info.clock = dict(ck)
            info.clock[info.sem] = info.val
            info.eng = e + "_dma%d" % n
            if is_out:
                self.out_infos.append(info)
        else:
            inst = fn()
            if e == "pe" and self.marks and self.marks[-1][1] is None:
                nm_ = inst.ins.name
                for mk in reversed(self.marks):
                    if mk[1] is not None:
                        break
                    mk[1] = nm_
            if sig:
                self.cnt[e] += 1
                inst.then_inc(self.sems[self.csem[e]], 1)
                info.sem, info.val = self.csem[e], self.cnt[e]
                info.clock = dict(self.clock[e])
                info.clock[info.sem] = info.val
                for p in self.pending[e]:
                    p.sem, p.val, p.clock = info.sem, info.val, info.clock
                self.pending[e] = []
            else:
                self.pending[e].append(info)
        for k in writes:
            self.last_w[k] = info
            self.readers[k] = {}
        for k in reads:
            self.readers.setdefault(k, {})[info.eng] = info
        return info

    def barrier(self):
        for e in self.COMPUTE:
            assert not self.pending[e]
        for f in self.eng:
            ck = self.clock[f]
            for e in self.COMPUTE:
                nm, v = self.csem[e], self.cnt[e]
                if v > 0 and e != f and ck.get(nm, 0) < v:
                    self.eng[f].wait_ge(self.sems[nm], v)
                if v > 0:
                    ck[nm] = v
            for q in self.QUEUES:
                for nm, v in self.slots[q]:
                    if v > 0 and ck.get(nm, 0) < v:
                        self.eng[f].wait_ge(self.sems[nm], v)
                        ck[nm] = v
        for e in self.COMPUTE:
            if self.cnt[e] > 0:
                self.eng[e].wait_ge(self.sems[self.csem[e]], self.cnt[e])
        self.last_w = {}
        self.readers = {}
        self._new_epoch_sems()

    def mark(self, label):
        self.marks.append([label, None])

    def finish(self):
        for info in self.out_infos:
            self._wait("sp", info)


MARKS = []


class StopBuild(Exception):
    pass


def build_program(depth=DEPTH, groups="AB", debug=False, p0=9):
    nc = bass.Bass("TRN2", target_bir_lowering=False)
    P = Prog(nc)

    def din(name, shape):
        return nc.dram_tensor(name, list(shape), F32, kind="ExternalInput").ap()

    def dout(name, shape):
        return nc.dram_tensor(name, list(shape), F32, kind="ExternalOutput").ap()

    xsT = din("xsT", (D, 2048))
    xpT = din("xpT", (D, 1024))
    ckvT_c = din("ckvT_c", (DEPTH, 128, 512))
    kpeT_c = din("kpeT_c", (DEPTH, 32, 512))
    skT_c = din("skT_c", (DEPTH, 128, 512))
    sv_c = din("sv_c", (DEPTH, 512, 128))
    cfm = din("cfm", (128, 16))
    w_ada = din("w_ada", (DEPTH, D, 6 * D))
    bada_fm = din("bada_fm", (128, DEPTH * 48))
    vecs_fm = din("vecs_fm", (128, 128))
    sgunorm_bc = din("sgunorm_bc", (128, DEPTH * 256))
    kvnorm_bc = din("kvnorm_bc", (128, DEPTH * 128))
    sink_bc = din("sink_bc", (128, 16))
    sgub = din("sgub", (1, DEPTH * 512))
    sgu_wT = din("sgu_wT", (DEPTH, 128, 512))
    w_fm = din("w_fm", (DEPTH, D, 1280))
    w_tm = din("w_tm", (DEPTH, D, 928))
    w_out = din("w_out", (DEPTH, D, D))
    w1 = din("w1", (DEPTH, D, 4096))
    w2 = din("w2", (DEPTH, 4096, D))
    wq = din("wq", (DEPTH, 256, 384))
    wkv = din("wkv", (DEPTH, 128, 512))
    ropes = din("ropes", (128, 16 * 192))
    consts = din("consts", (128, 640))
    ysT = dout("ysT", (D, 2048))
    ypT = dout("ypT", (D, 1024))
    o_ckv = dout("o_ckv", (4, DEPTH, 256, 128))
    o_kpe = dout("o_kpe", (4, DEPTH, 256, 32))
    o_k = dout("o_k", (4, DEPTH, 256, 128))
    o_v = dout("o_v", (4, DEPTH, 256, 128))
    if debug:
        dbg_h = nc.dram_tensor("dbg_h", [128, 8, 512], BF16, kind="ExternalOutput").ap()
        dbg_cat = nc.dram_tensor("dbg_cat", [128, 8, 512], BF16, kind="ExternalOutput").ap()
        dbg_x1 = dout("dbg_x1", (128, 8, 2048))
        dbg_x2 = dout("dbg_x2", (128, 8, 2048))

    A = nc.alloc_sbuf_tensor
    ident_f = A("ident_f", [128, 128], F32)
    cst = A("cst", [128, 512], BF16)
    ident_b, ones_b, mprev, mnext = cst[:, 0:128], cst[:, 128:256], cst[:, 256:384], cst[:, 384:512]
    modt = A("modt", [128, DEPTH * 48 * 2], F32)
    badat = A("badat", [128, DEPTH * 48], F32)
    vecs = A("vecs", [128, 128], F32)
    gs = A("gs", [128, DEPTH * 2 * 8 * 2], F32)
    csil = A("csil", [128, 16], F32)
    csil_b = A("csil_b", [128, 16], BF16)
    sinkexp = A("sinkexp", [128, 16], F32)
    ropet = A("ropet", [128, 16 * 192], BF16)
    sgn = A("sgn", [128, 256], F32)
    kvn = A("kvn", [128, 128], F32)
    wq_t = A("wq_t", [128, 2, 384], BF16)
    wkv_t = A("wkv_t", [128, 512], BF16)
    wsT_t = A("wsT_t", [128, 512], BF16)
    sgub_t = A("sgub_t", [1, 512], BF16)
    x = A("x", [128, 8, 2048], F32)
    R = A("R", [128, 36352], BF16)
    NST = 3
    wst = [A(f"wst{i}", [128, 4096], BF16) for i in range(NST)]
    sqt = [A(f"sqt{i}", [128, 512], BF16) for i in range(2)]
    tt = [A(f"tt{i}", [128, 512], F32) for i in range(2)]
    s_t = A("s_t", [128, 512], F32)
    rstd = s_t
    tmf = [A(f"tmf{i}", [128, 512], F32) for i in range(2)]
    tm2 = [A(f"tm2{i}", [128, 256], F32) for i in range(4)]
    tm3 = [A(f"tm3{i}", [128, 256], F32) for i in range(2)]
    sm = [A(f"sm{i}", [128, 4], F32) for i in range(4)]
    vn_t = [A(f"vn{i}", [128, 256], BF16) for i in range(2)]
    pt = [A(f"pt{i}", [128, 512], BF16) for i in range(4)]
    rden = [A(f"rden{i}", [64, 512], F32) for i in range(1)]
    qh = A("qh", [128, 4, 512], BF16)
    relu_t = [A(f"relu{i}", [128, 512], BF16) for i in range(2)]
    cqraw = relu_t
    yo = tt
    PS = [nc.alloc_psum_tensor(f"ps{i}", [128, 512], F32) for i in range(8)]

    rr = {}

    def ring(name, lst):
        i = rr.get(name, 0)
        rr[name] = i + 1
        j = i % len(lst)
        return lst[j], f"{name}{j}"

    psring = {"lst": list(range(8)), "i": 0}

    def psum():
        lst = psring["lst"]
        b = lst[psring["i"] % len(lst)]
        psring["i"] += 1
        assert not P.ps_open.get(f"ps{b}", False), f"psum bank ps{b} re-allocated before its previous contents were read"
        P.ps_open[f"ps{b}"] = True
        return PS[b], f"ps{b}"

    def set_psring(lst):
        psring["lst"] = list(lst)
        psring["i"] = 0

    def mm(out, lhsT, rhs, start, stop, reads, writes, sig=None):
        if sig is None:
            sig = True
        return P.op("pe", lambda: nc.tensor.matmul(out, lhsT=lhsT, rhs=rhs, start=start, stop=stop),
                    reads=reads, writes=writes, sig=sig)

    def act(out, in_, func, reads, writes, bias=0.0, scale=1.0, accum_out=None):
        kw = {}
        if accum_out is not None:
            kw["accum_out"] = accum_out
        return P.op("act", lambda: nc.scalar.activation(out=out, in_=in_, func=func, bias=bias, scale=scale, **kw),
                    reads=reads, writes=writes)

    def tt_op(out, in0, in1, op, reads, writes):
        return P.op("dve", lambda: nc.vector.tensor_tensor(out=out, in0=in0, in1=in1, op=op), reads=reads, writes=writes)

    def ts_op(out, in0, s1, op0, reads, writes, s2=None, op1=None):
        if op1 is None:
            return P.op("dve", lambda: nc.vector.tensor_scalar(out=out, in0=in0, scalar1=s1, scalar2=None, op0=op0),
                        reads=reads, writes=writes)
        return P.op("dve", lambda: nc.vector.tensor_scalar(out=out, in0=in0, scalar1=s1, scalar2=s2, op0=op0, op1=op1),
                    reads=reads, writes=writes)

    def stt(out, in0, scalar, in1, op0, op1, reads, writes):
        return P.op("dve", lambda: nc.vector.scalar_tensor_tensor(out=out, in0=in0, scalar=scalar, in1=in1, op0=op0, op1=op1),
                    reads=reads, writes=writes)

    def vcopy(out, in_, reads, writes):
        return P.op("dve", lambda: nc.vector.tensor_copy(out=out, in_=in_), reads=reads, writes=writes)

    def recip(out, in_, reads, writes):
        return P.op("dve", lambda: nc.vector.reciprocal(out=out, in_=in_), reads=reads, writes=writes)

    def dma(q, out, in_, reads, writes, is_out=False):
        e = nc.sync if q == "sp" else nc.gpsimd
        return P.op(q, lambda: e.dma_start(out=out, in_=in_), reads=reads, writes=writes, dma=True, is_out=is_out)

    wplan = []
    wstate = {"issued": 0, "used": 0}

    ada_defer = (len(groups) > 0 and groups[0] == "A")
    ada_phase0 = [0] if ada_defer else list(range(DEPTH))
    ADA_SPLIT = [2, 2, 2, 2, 1, 1, 1, 1]

    def plan_weights():
        for l in ada_phase0:
            for pc in range(12):
                wplan.append((f"ada{l}_{pc}", w_ada[l, :, pc * 512:(pc + 1) * 512], (8, 512)))
        for g in groups:
            NT = 4 if g == "A" else 2
            for l in range(depth):
                for j in range(NT):
                    wplan.append((f"fmK{g}{l}{j}", w_fm[l, :, 0:512], (8, 512)))
                    wplan.append((f"tmK{g}{l}{j}", w_tm[l, :, 0:416], (8, 416)))
                for j in range(NT):
                    wplan.append((f"fmQa{g}{l}{j}", w_fm[l, :, 512:896], (8, 384)))
                    wplan.append((f"fmQb{g}{l}{j}", w_fm[l, :, 896:1280], (8, 384)))
                    wplan.append((f"tmQ{g}{l}{j}", w_tm[l, :, 416:928], (8, 512)))
                    if ada_defer and g == "A" and l + 1 < DEPTH:
                        for pcn in range(3 * j, 3 * j + 3):
                            wplan.append((f"ada{l + 1}_{pcn}", w_ada[l + 1, :, pcn * 512:(pcn + 1) * 512], (8, 512)))
                    wplan.append((f"wo0{g}{l}{j}", w_out[l, :, 0:512], (8, 512)))
                    wplan.append((f"wo1{g}{l}{j}", w_out[l, :, 512:1024], (8, 512)))
                pcn = 0
                for jh in range(8):
                    wplan.append((f"w1{g}{l}{jh}", w1[l, :, jh * 512:(jh + 1) * 512], (8, 512)))
                    wplan.append((f"w2{g}{l}{jh}", w2[l, jh * 512:(jh + 1) * 512, :], (4, 1024)))


    def issue_weight():
        i = wstate["issued"]
        if i >= len(wplan):
            return
        name, ap, (nc_, ncol) = wplan[i]
        slot = wst[i % NST]
        dst = slot[:, 0:nc_ * ncol].rearrange("p (c n) -> p c n", c=nc_)
        src = ap.rearrange("(c p) n -> p c n", p=128)
        dma("pool", dst, src, reads=[], writes=[f"wst{i % NST}"])
        wstate["issued"] += 1

    def get_weight(name, ahead=NST - 1):
        i = wstate["used"]
        assert wplan[i][0] == name, (wplan[i][0], name)
        while wstate["issued"] < min(i + ahead + 1, len(wplan)):
            issue_weight()
        wstate["used"] += 1
        nc_, ncol = wplan[i][2]
        return wst[i % NST][:, 0:nc_ * ncol].rearrange("p (c n) -> p c n", c=nc_), f"wst{i % NST}"

    plan_weights()
    MARKS.clear()

    def ckpt(n):
        if p0 == n:
            P.barrier()
            P.finish()
            raise StopBuild()

    with nc.allow_low_precision("bf16 matmuls"), nc.allow_non_contiguous_dma("small strided loads"):
        dma("sp", ident_f[:], consts[:, 0:128], [], ["ident_f"])
        dma("pool", cst[:], consts[:, 0:512], [], ["cst"])
        dma("sp", badat[:], bada_fm[:, :], [], ["badat"])
        dma("sp", vecs[:], vecs_fm[:, :], [], ["vecs"])
        dma("sp", csil[:], cfm[:, :], [], ["csil"])
        dma("sp", sinkexp[:], sink_bc[:, :], [], ["sinkexp"])
        dma("pool", ropet[:], ropes[:, :], [], ["ropet"])
        act(csil[:], csil[:], AF.Silu, ["csil"], ["csil"])
        act(sinkexp[:], sinkexp[:], AF.Exp, ["sinkexp"], ["sinkexp"])
        if p0 == 1:
            P.barrier()
            P.finish()
            return nc
        modv = modt[:].rearrange("p (l k c v) -> p l k c v", l=DEPTH, k=6, c=8)
        gsv = gs[:].rearrange("p (l n c v) -> p l n c v", l=DEPTH, n=2, c=8)
        modt3 = modt[:].rearrange("p (m v) -> p m v", v=2)
        vcopy(csil_b[:], csil[:], ["csil"], ["csil_b"])
        P.op("dve", lambda: nc.vector.memset(qh[96:128, :, :], 0.0), [], [f"qh{h}" for h in range(4)])

        def ada_piece(l, pc):
            wt, wk_ = get_weight(f"ada{l}_{pc}")
            ps, pk = psum()
            for c in range(8):
                mm(ps[0:2, :], csil_b[:, c * 2:c * 2 + 2], wt[:, c, :], c == 0, c == 7, [wk_, "csil_b"], [pk])
            t_, tk_ = ring("tt", tt)
            t_ = t_[0:2, :]
            act(t_, ps[0:2, :], AF.Copy, [pk], [tk_])
            ps2, pk2 = psum()
            for mi in range(4):
                P.op("pe", lambda: nc.tensor.transpose(ps2[:, mi * 2:mi * 2 + 2], t_[:, mi * 128:(mi + 1) * 128], ident_f[0:2, 0:2]),
                     [tk_, "ident_f"], [pk2])
            m0 = l * 48 + pc * 4
            tt_op(modt3[:, m0:m0 + 4, :], ps2[:, 0:8].rearrange("p (m v) -> p m v", v=2),
                  badat[:, m0:m0 + 4].unsqueeze(2).to_broadcast([128, 4, 2]), ALU.add, [pk2, "badat"], [f"modt{l}"])

        def ada_finish(l):
            for n in range(2):
                nv = vecs[:, n * 32 + l * 8:n * 32 + l * 8 + 8]
                ts_op(gsv[:, l, n], modv[:, l, 3 * n + 1], 1.0, ALU.add, [f"modt{l}"], [f"gs{l}"])
                tt_op(gsv[:, l, n], gsv[:, l, n], nv.unsqueeze(2).to_broadcast([128, 8, 2]), ALU.mult, [f"gs{l}", "vecs"], [f"gs{l}"])

        for l in ada_phase0:
            for pc in range(12):
                ada_piece(l, pc)
            ada_finish(l)
        P.barrier()

        def run_group(g):
            lat = (g == "A")
            T = 2048 if lat else 1024
            S = 2048 if lat else 256
            NT = T // 512
            NK = 2560 if lat else 1024
            NB = NK // 128
            v = 1 if lat else 0
            xT = xsT if lat else xpT
            yT = ysT if lat else ypT
            o = 0

            def carve(n, c=None):
                nonlocal o
                ap = R[:, o:o + n]
                o += n
                if c is not None:
                    ap = ap.rearrange("p (c t) -> p c t", c=c)
                return ap
            KT = carve(4 * NK, 4)
            Vm = carve(NB * 256, NB)
            ks = carve(NK)
            Vs = carve(NB * 128, NB)
            pbuf = carve(2 * T, 2)
            hjb = [carve(8 * 512, 8)]
            if not lat:
                hjb.append(carve(8 * 512, 8))
            H = {"ap": hjb[0], "k": "hj0_"}

            def use_buf(bi):
                H["ap"] = hjb[bi]
                H["k"] = f"hj{bi}_"

            def norm_to(l, j, bi):
                norm(l, 0, j, lambda c: hjb[bi][:, c, :], lambda c: f"hj{bi}_{c}")
            catj = carve(8 * 512, 8)
            cqn = carve(2 * 512, 2)
            qs = carve(2 * 512, 2)
            acc = carve(2 * 512, 2)
            ckT = carve(512)
            assert o <= 36352, o
            h2 = R[:, 0:8 * T].rearrange("p (c t) -> p c t", c=8)
            ub = [R[:, 8 * T + i * 2048:8 * T + (i + 1) * 2048].rearrange("p (c t) -> p c t", c=4) for i in range(2)]

            for c in range(8):
                dma("sp", x[:, c, 0:T], xT[c * 128:(c + 1) * 128, :], [], [f"x{c}_{j}" for j in range(NT)])

            def norm(l, n, j, dst, dkeys):
                ps, pk = psum()
                for c in range(8):
                    sq, sqk = ring("sqt", sqt)
                    act(sq[:], x[:, c, j * 512:(j + 1) * 512], AF.Square, [f"x{c}_{j}"], [sqk])
                    mm(ps[:, :], ones_b, sq[:], c == 0, c == 7, [sqk, "cst"], [pk])
                act(s_t[:], ps[:, :], AF.Sqrt, [pk], ["s_t", "rstd"], bias=EPS, scale=1.0 / D)
                recip(rstd[:], s_t[:], ["s_t"], ["s_t", "rstd"])
                for c in range(8):
                    t, tk = ring("tt", tt)
                    tt_op(t[:], x[:, c, j * 512:(j + 1) * 512], rstd[:], ALU.mult, [f"x{c}_{j}", "rstd"], [tk])
                    act(dst(c), t[:], AF.Identity, [tk, f"gs{l}", f"modt{l}"], [dkeys(c)],
                        bias=modv[:, l, 3 * n, c, v:v + 1], scale=gsv[:, l, n, c, v:v + 1])

            def attention_start(NQ, qT, qkeys, chunks, scale, sink_ap, out_ap, out_keys, G=1, LA=2):
                W = G * NQ
                CH = 512 // W
                ai = rr.get("acc", 0) % 2
                rr["acc"] = rr.get("acc", 0) + 1
                nump, numk = PS[3 + ai * 2], f"ps{3 + ai * 2}"
                denp, denk = PS[4 + ai * 2], f"ps{4 + ai * 2}"
                ones64 = ones_b[:, 0:64]
                n = len(chunks)
                groups_ = [chunks[g0:g0 + CH] for g0 in range(0, n, CH)]

                def view(ap2d):
                    return ap2d if G == 1 else ap2d.rearrange("p (g t) -> p g t", g=G)

                def emit_S(grp):
                    sb, sbk = psum()
                    for i, (kT, vv, mask, keys) in enumerate(grp):
                        mm(view(sb[:, i * W:(i + 1) * W]), kT, qT, True, mask is None, keys + qkeys, [sbk])
                        if mask is not None:
                            mrhs = mask if G == 1 else mask.unsqueeze(1).to_broadcast([128, G, NQ])
                            mm(view(sb[:, i * W:(i + 1) * W]), ident_b, mrhs, False, True, ["cst"], [sbk])
                    return sb, sbk

                ng = len(groups_)
                sbs = {}
                for gi in range(min(LA, ng)):
                    sbs[gi] = emit_S(groups_[gi])

                def run(next_start=None):
                    nxt_run = None
                    idx = 0
                    for g0 in range(0, ng, LA):
                        cur = list(range(g0, min(g0 + LA, ng)))
                        for gi in range(g0 + LA, min(g0 + 2 * LA, ng)):
                            sbs[gi] = emit_S(groups_[gi])
                        if g0 + LA >= ng and next_start is not None:
                            nxt_run = next_start()
                        pts = {}
                        for gi in cur:
                            sb, sbk = sbs.pop(gi)
                            p_, pk_ = ring("pt", pt)
                            w = len(groups_[gi]) * W
                            act(p_[:, 0:w], sb[:, 0:w], AF.Exp, [sbk], [pk_], scale=scale)
                            pts[gi] = (p_, pk_)
                        for gi in cur:
                            p_, pk_ = pts[gi]
                            for i, (kT, vv, mask, keys) in enumerate(groups_[gi]):
                                first = (idx == 0)
                                last = (idx == n - 1)
                                idx += 1
                                mm(nump[0:64, 0:W], vv, p_[:, i * W:(i + 1) * W], first, last, keys + [pk_], [numk])
                                mm(denp[0:64, 0:W], ones64, p_[:, i * W:(i + 1) * W], first, last, [pk_, "cst"], [denk])
                    rd, rdk = ring("rden", rden)
                    sinks = sink_ap if isinstance(sink_ap, list) else [sink_ap] * G
                    outs = out_ap if isinstance(out_ap, list) else [out_ap]
                    if sinks[0] is not None:
                        for gg in range(G):
                            ts_op(rd[:, gg * NQ:(gg + 1) * NQ], denp[0:64, gg * NQ:(gg + 1) * NQ], sinks[gg], ALU.add,
                                  [denk, "sinkexp"], [rdk])
                        recip(rd[:, 0:W], rd[:, 0:W], [rdk], [rdk])
                    else:
                        recip(rd[:, 0:W], denp[0:64, 0:W], [denk], [rdk])
                    for gg in range(G):
                        tt_op(outs[gg], nump[0:64, gg * NQ:(gg + 1) * NQ], rd[:, gg * NQ:(gg + 1) * NQ], ALU.mult,
                              [numk, rdk], out_keys)
                    return nxt_run

                return run

            def attention_seq(calls):
                run = attention_start(*calls[0][0], **calls[0][1])
                for i in range(len(calls)):
                    if i + 1 < len(calls):
                        na, nk = calls[i + 1]
                        run = run(lambda na=na, nk=nk: attention_start(*na, **nk))
                    else:
                        run(None)

            def rope(dst, src, H, Dh, gb, off, rk, wk):
                Q = Dh // 4
                cos = ropet[:, gb * 192 + off:gb * 192 + off + Dh]
                sin = ropet[:, gb * 192 + off + Dh:gb * 192 + off + 2 * Dh]
                t2, t2k = ring("tm3", tm3)
                s4 = src.rearrange("p (h a b q) -> p h a b q", h=H, a=2, b=2)
                d4 = t2[:, 0:H * Dh].rearrange("p (h a b q) -> p h a b q", h=H, a=2, b=2)
                sn = sin.rearrange("p (a b q) -> p a b q", a=2, b=2)
                for bb in range(2):
                    tt_op(d4[:, :, :, bb, :], s4[:, :, :, 1 - bb, :],
                          sn[:, :, bb, :].unsqueeze(1).to_broadcast([128, H, 2, Q]), ALU.mult,
                          rk + ["ropet"] + ([t2k] if bb else []), [t2k])
                s3 = src.rearrange("p (h d) -> p h d", h=H)
                d3 = dst.rearrange("p (h d) -> p h d", h=H)
                tt_op(d3, s3, cos.unsqueeze(1).to_broadcast([128, H, Dh]), ALU.mult, rk + ["ropet"], wk)
                tt_op(dst, dst, t2[:, 0:H * Dh], ALU.add, wk + [t2k], wk)

            def small_weights(l_):
                dma("pool", wq_t[:], wq[l_].rearrange("(c p) n -> p c n", p=128), [], ["wq_t"])
                dma("pool", wkv_t[:], wkv[l_], [], ["wkv_t"])
                dma("pool", wsT_t[:], sgu_wT[l_], [], ["wsT_t"])
                dma("pool", sgub_t[:], sgub[:, l_ * 512:(l_ + 1) * 512], [], ["sgub_t"])
                dma("sp", sgn[:], sgunorm_bc[:, l_ * 256:(l_ + 1) * 256], [], ["sgn"])
                dma("sp", kvn[:], kvnorm_bc[:, l_ * 128:(l_ + 1) * 128], [], ["kvn"])

            for l in range(depth):
                set_psring(range(8))
                if l == 0:
                    small_weights(0)
                P.op("dve", lambda: nc.vector.memset(KT[96:128, :, :], 0.0), [],
                     [f"KT{h}_{jj}" for h in range(4) for jj in range(NT + 1)])
                if lat:
                    dma("pool", ckT[:], ckvT_c[l], [], ["ckT"])
                    for h in range(4):
                        dma("pool", KT[64:96, h, T:T + 512], kpeT_c[l], [], [f"KT{h}_{NT}"])
                    dma("pool", ks[:, T:T + 512], skT_c[l], [], [f"ks_{NT}"])
                    dma("pool", Vs[:, 16:20, :], sv_c[l].rearrange("(b p) d -> p b d", p=128), [], [f"Vs_{NT}"])

                def kv_up(j, nblk):
                    for h in range(4):
                        ps, pk = psum()
                        mm(ps[0:64, 0:nblk * 128], wkv_t[:, h * 128:h * 128 + 64], ckT[:, 0:nblk * 128], True, True,
                           ["wkv_t", "ckT"], [pk])
                        act(KT[0:64, h, j * 512:j * 512 + nblk * 128], ps[0:64, 0:nblk * 128], AF.Copy, [pk], [f"KT{h}_{j}"])
                    for b in range(nblk):
                        ps, pk = psum()
                        mm(ps[:, :], ckT[:, b * 128:(b + 1) * 128], wkv_t[:], True, True, ["wkv_t", "ckT"], [pk])
                        vcopy(Vm[:, j * 4 + b, :].rearrange("p (h d) -> p h d", h=4),
                              ps[:, :].rearrange("p (h t d) -> p h t d", h=4, t=2)[:, :, 1, :], [pk], [f"Vm_{j}"])

                if lat:
                    kv_up(NT, 4)
                ckpt(10)

                for j in range(NT):
                    P.mark(f"{g}{l} K{j} norm")
                    if lat:
                        use_buf(0)
                        norm_to(l, j, 0)
                    else:
                        if j == 0:
                            norm_to(l, 0, 0)
                        if j + 1 < NT:
                            norm_to(l, j + 1, (j + 1) % 2)
                        use_buf(j % 2)
                    P.mark(f"{g}{l} K{j} fm")
                    wtF, wkF = get_weight(f"fmK{g}{l}{j}")
                    wtT, wkT = get_weight(f"tmK{g}{l}{j}", ahead=NST - 2)

                    def fmK_group(m):
                        ps, pk = psum()
                        for c in range(8):
                            mm(ps[:, :], wtF[:, c, m * 128:(m + 1) * 128], H["ap"][:, c, :], c == 0, c == 7, [wkF, f"{H['k']}{c}"], [pk])
                        if m < 2:
                            act(pbuf[:, m, j * 512:(j + 1) * 512], ps[:, :], AF.Copy, [pk], [f"p{m}_{j}"])
                        else:
                            tt_op(pbuf[:, m - 2, j * 512:(j + 1) * 512], ps[:, :], pbuf[:, m - 2, j * 512:(j + 1) * 512],
                                  ALU.mult, [pk, f"p{m - 2}_{j}"], [f"p{m - 2}_{j}"])

                    def tmK_mm(b):
                        tok = slice(b * 128, (b + 1) * 128)
                        ps, pk = psum()
                        for c in range(8):
                            mm(ps[:, 0:416], H["ap"][:, c, tok], wtT[:, c, :], c == 0, c == 7, [wkT, f"{H['k']}{c}"], [pk])
                        return ps, pk

                    def tmK_ew(b, ps, pk):
                        gb = j * 4 + b
                        st, stk = ring("tmf", tmf)
                        act(st[:, 0:416], ps[:, 0:416], AF.Copy, [pk], [stk])
                        smt, smk = ring("sm", sm)
                        t2, t2k = ring("tm3", tm3)
                        P.op("dve", lambda: nc.vector.memset(smt[:, 0:1], 0.0), [], [smk])
                        act(t2[:, 0:128], st[:, 0:128], AF.Square, [stk, smk], [t2k, smk], accum_out=smt[:, 0:1])
                        act(smt[:, 1:2], smt[:, 0:1], AF.Sqrt, [smk], [smk], bias=EPS, scale=1.0 / 128)
                        recip(smt[:, 2:3], smt[:, 1:2], [smk], [smk])
                        stt(st[:, 0:128], st[:, 0:128], smt[:, 2:3], kvn[:], ALU.mult, ALU.mult, [stk, smk, "kvn"], [stk])
                        if lat:
                            t3, t3k = ring("tm2", tm2)
                            rope(t3[:, 0:32], st[:, 128:160], 1, 32, gb, 128, [stk], [t3k])
                            kpe_src, kpek = t3[:, 0:32], t3k
                            t4, t4k = ring("tm2", tm2)
                            rope(t4[:, 0:128], st[:, 160:288], 2, 64, gb, 0, [stk], [t4k])
                            sk_src, skk = t4[:, 0:128], t4k
                        else:
                            kpe_src, kpek = st[:, 128:160], stk
                            sk_src, skk = st[:, 160:288], stk
                            sq_, r0 = divmod(gb * 128, 256)
                            dma("sp", o_ckv[sq_, l, r0:r0 + 128, :], st[:, 0:128], [stk], [], is_out=True)
                            dma("sp", o_kpe[sq_, l, r0:r0 + 128, :], st[:, 128:160], [stk], [], is_out=True)
                            dma("sp", o_k[sq_, l, r0:r0 + 128, :], st[:, 160:288], [stk], [], is_out=True)
                            dma("sp", o_v[sq_, l, r0:r0 + 128, :], st[:, 288:416], [stk], [], is_out=True)
                        vcopy(Vs[:, gb, :], st[:, 288:416], [stk], [f"Vs_{j}"])
                        return st, stk, kpe_src, kpek, sk_src, skk

                    def tmK_tr(b, st, stk, kpe_src, kpek, sk_src, skk):
                        gb = j * 4 + b
                        tok = slice(b * 128, (b + 1) * 128)
                        ps2, pk2 = psum()
                        P.op("pe", lambda: nc.tensor.transpose(ps2[:, 0:128], st[:, 0:128], ident_f[:]), [stk, "ident_f"], [pk2])
                        P.op("pe", lambda: nc.tensor.transpose(ps2[:, 128:256], sk_src, ident_f[:]), [skk, "ident_f"], [pk2])
                        P.op("pe", lambda: nc.tensor.transpose(ps2[0:32, 256:384], kpe_src, ident_f[:]), [kpek, "ident_f"], [pk2])
                        act(ckT[:, tok], ps2[:, 0:128], AF.Copy, [pk2], ["ckT"])
                        vcopy(ks[:, gb * 128:(gb + 1) * 128], ps2[:, 128:256], [pk2], [f"ks_{j}"])
                        act(KT[64:96, 0:2, gb * 128:(gb + 1) * 128], ps2[0:32, 256:384].unsqueeze(1).to_broadcast([32, 2, 128]),
                            AF.Copy, [pk2], [f"KT0_{j}", f"KT1_{j}"])
                        vcopy(KT[64:96, 2:4, gb * 128:(gb + 1) * 128], ps2[0:32, 256:384].unsqueeze(1).to_broadcast([32, 2, 128]),
                              [pk2], [f"KT2_{j}", f"KT3_{j}"])

                    P.mark(f"{g}{l} K{j} tm")
                    r = {}
                    e = {}
                    r[0] = tmK_mm(0)
                    r[1] = tmK_mm(1)
                    for b in range(4):
                        e[b] = tmK_ew(b, *r[b])
                        fmK_group(b)
                        tmK_tr(b, *e[b])
                        if b + 2 < 4:
                            r[b + 2] = tmK_mm(b + 2)
                    P.mark(f"{g}{l} K{j} kvup")
                    kv_up(j, 4)
                    ckpt(14)

                nseq_t = 512 // min(S, 512)
                for j in range(NT):
                    set_psring(range(8))
                    if lat:
                        use_buf(0)
                        if j == 0:
                            P.mark(f"{g}{l} Q{j} norm")
                            norm_to(l, 0, 0)
                    else:
                        use_buf(j % 2)
                    P.mark(f"{g}{l} Q{j} fm")
                    if debug and l == 0 and j == 0:
                        dma("sp", dbg_h, H["ap"], [f"{H['k']}{c}" for c in range(8)], [], is_out=True)
                    wa, wak = get_weight(f"fmQa{g}{l}{j}")

                    def fmQ_group(m, wt, wk_):
                        mi = m % 3
                        ps, pk = psum()
                        for c in range(8):
                            mm(ps[:, :], wt[:, c, mi * 128:(mi + 1) * 128], H["ap"][:, c, :], c == 0, c == 7, [wk_, f"{H['k']}{c}"], [pk])
                        if m < 2:
                            act(catj[:, m, :], ps[:, :], AF.Copy, [pk], [f"cat{m}"])
                        elif m < 4:
                            act(catj[:, m, :], ps[:, :], AF.Gelu_apprx_tanh, [pk], [f"cat{m}"])
                        else:
                            cr, crk = cqraw[m - 4], f"cqraw{m - 4}"
                            act(cr[:], ps[:, :], AF.Copy, [pk], [crk])

                    for m in range(3):
                        fmQ_group(m, wa, wak)
                    wb, wbk = get_weight(f"fmQb{g}{l}{j}")
                    wtT, wkT = get_weight(f"tmQ{g}{l}{j}", ahead=NST - 2)

                    def cq_conv():
                        ps, pk = psum()
                        for c in range(2):
                            sq, sqk = ring("sqt", sqt)
                            act(sq[:], cqraw[c][:], AF.Square, [f"cqraw{c}"], [sqk])
                            mm(ps[:, :], ones_b, sq[:], c == 0, c == 1, [sqk, "cst"], [pk])
                        act(s_t[:], ps[:, :], AF.Sqrt, [pk], ["s_t", "rstd"], bias=EPS, scale=1.0 / 256)
                        recip(rstd[:], s_t[:], ["s_t"], ["s_t", "rstd"])
                        for c in range(2):
                            stt(cqn[:, c, :], cqraw[c][:], vecs[:, 72 + l * 2 + c:72 + l * 2 + c + 1], rstd[:], ALU.mult, ALU.mult,
                                [f"cqraw{c}", "vecs", "rstd"], [f"cqn{c}"])
                        Sq = min(S, 512)
                        for c in range(2):
                            cw = lambda k: vecs[:, 80 + (l * 2 + c) * 3 + k:80 + (l * 2 + c) * 3 + k + 1]
                            lo = j * 512
                            pk_all = [f"p{c}_{jj}" for jj in range(NT)]
                            ts_op(acc[:, c, :], pbuf[:, c, lo:lo + 512], cw(1), ALU.mult, pk_all + ["vecs"], [f"acc{c}"])
                            for sidx in range(nseq_t):
                                a0 = sidx * Sq
                                g0 = lo + a0
                                first_in_seq = (g0 % S == 0)
                                last_in_seq = ((g0 + Sq) % S == 0)
                                s0 = 1 if first_in_seq else 0
                                stt(acc[:, c, a0 + s0:a0 + Sq], pbuf[:, c, g0 + s0 - 1:g0 + Sq - 1], cw(0), acc[:, c, a0 + s0:a0 + Sq],
                                    ALU.mult, ALU.add, pk_all + ["vecs", f"acc{c}"], [f"acc{c}"])
                                e0 = 1 if last_in_seq else 0
                                stt(acc[:, c, a0:a0 + Sq - e0], pbuf[:, c, g0 + 1:g0 + Sq - e0 + 1], cw(2), acc[:, c, a0:a0 + Sq - e0],
                                    ALU.mult, ALU.add, pk_all + ["vecs", f"acc{c}"], [f"acc{c}"])
                            tt_op(catj[:, c, :], catj[:, c, :], acc[:, c, :], ALU.mult, [f"cat{c}", f"acc{c}"], [f"cat{c}"])

                    def tmQ_mm(b):
                        tok = slice(b * 128, (b + 1) * 128)
                        ps, pk = psum()
                        for c in range(8):
                            mm(ps[:, :], H["ap"][:, c, tok], wtT[:, c, :], c == 0, c == 7, [wkT, f"{H['k']}{c}"], [pk])
                        return ps, pk

                    def tmQ_ew(b, ps, pk):
                        gb = j * 4 + b
                        st, stk = ring("tmf", tmf)
                        act(st[:, 0:256], ps[:, 0:256], AF.Gelu_apprx_tanh, [pk], [stk])
                        act(st[:, 256:512], ps[:, 256:512], AF.Copy, [pk], [stk])
                        smt, smk = ring("sm", sm)
                        t2, t2k = ring("tm3", tm3)
                        P.op("dve", lambda: nc.vector.memset(smt[:, 0:1], 0.0), [], [smk])
                        act(t2[:, 0:256], st[:, 0:256], AF.Square, [stk, smk], [t2k, smk], accum_out=smt[:, 0:1])
                        act(smt[:, 1:2], smt[:, 0:1], AF.Sqrt, [smk], [smk], bias=EPS, scale=1.0 / 256)
                        recip(smt[:, 2:3], smt[:, 1:2], [smk], [smk])
                        vn, vnk = ring("vn", vn_t)
                        stt(vn[:], st[:, 0:256], smt[:, 2:3], sgn[:], ALU.mult, ALU.mult, [stk, smk, "sgn"], [vnk])
                        if lat:
                            t4, t4k = ring("tm2", tm2)
                            rope(t4[:, 0:256], st[:, 256:512], 4, 64, gb, 0, [stk], [t4k])
                            return st, stk, vn, vnk, t4, t4k
                        return st, stk, vn, vnk, None, None

                    def tmQ_tail(b, st, stk, vn, vnk, t4, t4k):
                        tok = slice(b * 128, (b + 1) * 128)
                        ps2, pk2 = psum()
                        for hd in range(4):
                            cc, e_ = divmod(hd, 2)
                            mm(ps2[:, hd * 128:(hd + 1) * 128], vn[:, cc * 128:(cc + 1) * 128], wsT_t[:, hd * 128:(hd + 1) * 128],
                               True, False, [vnk, "wsT_t"], [pk2], sig=False)
                            mm(ps2[:, hd * 128:(hd + 1) * 128], ones_b[0:1, 0:128], sgub_t[0:1, hd * 128:(hd + 1) * 128],
                               False, True, ["cst", "sgub_t"], [pk2], sig=True)
                        ps3, pk3 = psum()
                        for gg in range(2):
                            src_ap = (t4[:, gg * 128:(gg + 1) * 128] if lat else st[:, 256 + gg * 128:256 + (gg + 1) * 128])
                            P.op("pe", lambda: nc.tensor.transpose(ps3[:, gg * 128:(gg + 1) * 128], src_ap, ident_f[:]),
                                 [t4k if lat else stk, "ident_f"], [pk3])
                        for hd in range(4):
                            cc, e_ = divmod(hd, 2)
                            tt_op(catj[e_ * 64:(e_ + 1) * 64, 2 + cc, tok], catj[e_ * 64:(e_ + 1) * 64, 2 + cc, tok],
                                  ps2[e_ * 64:(e_ + 1) * 64, hd * 128:(hd + 1) * 128], ALU.mult, [f"cat{2 + cc}", pk2], [f"cat{2 + cc}"])
                        act(qs[:, :, tok], ps3[:, 0:256].rearrange("p (g t) -> p g t", g=2), AF.Copy, [pk3], ["qs"])

                    def qm_mm(b):
                        tokq = slice(b * 128, (b + 1) * 128)
                        ps, pk = psum()
                        for c in range(2):
                            mm(ps[:, 0:384], cqn[:, c, tokq], wq_t[:, c, :], c == 0, c == 1, [f"cqn{c}", "wq_t"], [pk])
                        return ps, pk

                    def qm_ew(b, ps, pk):
                        gb = j * 4 + b
                        st, stk = ring("tmf", tmf)
                        act(st[:, 0:384], ps[:, 0:384], AF.Copy, [pk], [stk])
                        if lat:
                            t3, t3k = ring("tm2", tm2)
                            src4 = st[:, 0:384].rearrange("p (h d) -> p h d", h=4)[:, :, 64:96]
                            t5, t5k = ring("tm2", tm2)
                            vcopy(t5[:, 0:128].rearrange("p (h d) -> p h d", h=4), src4, [stk], [t5k])
                            rope(t3[:, 0:128], t5[:, 0:128], 4, 32, gb, 128, [t5k], [t3k])
                            vcopy(src4, t3[:, 0:128].rearrange("p (h d) -> p h d", h=4), [t3k], [stk])
                        return st, stk

                    def qm_tr(b, st, stk):
                        tokq = slice(b * 128, (b + 1) * 128)
                        ps, pk = psum()
                        for h in range(4):
                            P.op("pe", lambda: nc.tensor.transpose(ps[0:96, h * 128:(h + 1) * 128], st[:, h * 96:(h + 1) * 96], ident_f[:]),
                                 [stk, "ident_f"], [pk])
                        src = ps[0:96, :].rearrange("p (h t) -> p h t", h=4)
                        if b % 2 == 0:
                            act(qh[0:96, :, tokq], src, AF.Copy, [pk], [f"qh{h}" for h in range(4)])
                        else:
                            vcopy(qh[0:96, :, tokq], src, [pk], [f"qh{h}" for h in range(4)])

                    P.mark(f"{g}{l} Q{j} tm")
                    r = {}
                    e = {}
                    rq = {}
                    for m in range(3, 6):
                        fmQ_group(m, wb, wbk)
                    cq_conv()
                    r[0] = tmQ_mm(0)
                    r[1] = tmQ_mm(1)
                    rq[0] = qm_mm(0)
                    for b in range(4):
                        e[b] = tmQ_ew(b, *r[b])
                        eq = qm_ew(b, *rq[b])
                        if b + 1 < 4:
                            rq[b + 1] = qm_mm(b + 1)
                        tmQ_tail(b, *e[b])
                        if b + 2 < 4:
                            r[b + 2] = tmQ_mm(b + 2)
                        qm_tr(b, *eq)
                    if ada_defer and g == "A" and l + 1 < DEPTH:
                        for pcn in range(3 * j, 3 * j + 3):
                            ada_piece(l + 1, pcn)
                        if j == NT - 1:
                            ada_finish(l + 1)
                    ckpt(19)
                    set_psring([0, 1, 2, 7])
                    ckpt(20)
                    P.mark(f"{g}{l} Q{j} MLA")
                    acalls = []
                    if lat:
                        for h in range(4):
                            chunks = []
                            for kc in range(20):
                                jt = kc // 4
                                chunks.append((KT[:, h, kc * 128:(kc + 1) * 128], Vm[:, kc, h * 64:(h + 1) * 64], None,
                                               [f"KT{h}_{jt}", f"Vm_{jt}"]))
                            cc, e = divmod(h, 2)
                            acalls.append(((512, qh[:, h, :], [f"qh{h}"], chunks, MLA_SCALE, None,
                                            catj[e * 64:(e + 1) * 64, 4 + cc, :], [f"cat{4 + cc}"]), {}))
                    else:
                        for sl in range(2):
                            sidx = (j * 512) // 256 + sl
                            qsl = slice(sl * 256, (sl + 1) * 256)
                            for h in range(4):
                                chunks = []
                                for kc in (2 * sidx, 2 * sidx + 1):
                                    jt = kc // 4
                                    chunks.append((KT[:, h, kc * 128:(kc + 1) * 128], Vm[:, kc, h * 64:(h + 1) * 64], None,
                                                   [f"KT{h}_{jt}", f"Vm_{jt}"]))
                                cc, e = divmod(h, 2)
                                acalls.append(((256, qh[:, h, qsl], [f"qh{h}"], chunks, MLA_SCALE, None,
                                                catj[e * 64:(e + 1) * 64, 4 + cc, qsl], [f"cat{4 + cc}"]), {}))
                    ckpt(21)
                    if lat and j + 1 < NT:
                        norm_to(l, j + 1, 0)
                    P.mark(f"{g}{l} Q{j} SWA")
                    for n in range(2):
                        sink_l = [sinkexp[0:64, l * 4 + n * 2 + gq:l * 4 + n * 2 + gq + 1] for gq in range(2)]
                        if lat:
                            for bq in range(4):
                                blk = j * 4 + bq
                                tq = slice(bq * 128, (bq + 1) * 128)
                                chunks = []
                                if blk >= 1:
                                    chunks.append((ks[n * 64:(n + 1) * 64, (blk - 1) * 128:blk * 128],
                                                   Vs[:, blk - 1, n * 64:(n + 1) * 64], mprev,
                                                   [f"ks_{(blk - 1) // 4}", f"Vs_{(blk - 1) // 4}"]))
                                chunks.append((ks[n * 64:(n + 1) * 64, blk * 128:(blk + 1) * 128],
                                               Vs[:, blk, n * 64:(n + 1) * 64], None, [f"ks_{blk // 4}", f"Vs_{blk // 4}"]))
                                if blk <= 14:
                                    chunks.append((ks[n * 64:(n + 1) * 64, (blk + 1) * 128:(blk + 2) * 128],
                                                   Vs[:, blk + 1, n * 64:(n + 1) * 64], mnext,
                                                   [f"ks_{(blk + 1) // 4}", f"Vs_{(blk + 1) // 4}"]))
                                for kc in range(16, 20):
                                    chunks.append((ks[n * 64:(n + 1) * 64, kc * 128:(kc + 1) * 128],
                                                   Vs[:, kc, n * 64:(n + 1) * 64], None, [f"ks_{NT}", f"Vs_{NT}"]))
                                acalls.append(((128, qs[n * 64:(n + 1) * 64, :, tq], ["qs"], chunks, SWA_SCALE, sink_l,
                                                [catj[gq * 64:(gq + 1) * 64, 6 + n, tq] for gq in range(2)], [f"cat{6 + n}"]),
                                               {"G": 2}))
                        else:
                            for sl in range(2):
                                sidx = (j * 512) // 256 + sl
                                qsl = slice(sl * 256, (sl + 1) * 256)
                                chunks = []
                                for kc in (2 * sidx, 2 * sidx + 1):
                                    chunks.append((ks[n * 64:(n + 1) * 64, kc * 128:(kc + 1) * 128],
                                                   Vs[:, kc, n * 64:(n + 1) * 64], None, [f"ks_{kc // 4}", f"Vs_{kc // 4}"]))
                                acalls.append(((256, qs[n * 64:(n + 1) * 64, :, qsl], ["qs"], chunks, SWA_SCALE, sink_l,
                                                [catj[gq * 64:(gq + 1) * 64, 6 + n, qsl] for gq in range(2)], [f"cat{6 + n}"]),
                                               {"G": 2}))
                    attention_seq(acalls)
                    ckpt(22)
                    P.mark(f"{g}{l} Q{j} wout")
                    set_psring(range(8))
                    if debug and l == 0 and j == 0:
                        dma("sp", dbg_cat, catj, [f"cat{c}" for c in range(8)], [], is_out=True)
                    for half in range(2):
                        wt, wk_ = get_weight(f"wo{half}{g}{l}{j}")
                        for mi in range(4):
                            m = half * 4 + mi
                            ps, pk = psum()
                            for c in range(8):
                                mm(ps[:, :], wt[:, c, mi * 128:(mi + 1) * 128], catj[:, c, :], c == 0, c == 7, [wk_, f"cat{c}"], [pk])
                            xs_ = x[:, m, j * 512:(j + 1) * 512]
                            stt(xs_, ps[:, :], modv[:, l, 2, m, v:v + 1], xs_, ALU.mult, ALU.add, [pk, f"modt{l}", f"x{m}_{j}"], [f"x{m}_{j}"])
                P.barrier()
                if debug and l == 0:
                    dma("sp", dbg_x1[:, :, 0:T], x[:, :, 0:T], [], [], is_out=True)
                    P.barrier()
                ckpt(23)
                P.mark(f"{g}{l} MLP norm")
                set_psring(range(8))
                if l + 1 < depth:
                    small_weights(l + 1)
                ckpt(24)
                P.mark(f"{g}{l} MLP mm")
                ada_pc = [0]
                for jh in range(8):
                    wa, wak = get_weight(f"w1{g}{l}{jh}")
                    wb, wbk = get_weight(f"w2{g}{l}{jh}", ahead=NST - 2)
                    def mlp_up(j):
                        if jh == 0:
                            norm(l, 1, j, lambda c: h2[:, c, j * 512:(j + 1) * 512], lambda c: f"h2{c}_{j}")
                        u, uk = ring("ub", ub)
                        for hc in range(4):
                            ps, pk = psum()
                            for c in range(8):
                                mm(ps[:, :], wa[:, c, hc * 128:(hc + 1) * 128], h2[:, c, j * 512:(j + 1) * 512], c == 0, c == 7,
                                   [wak, f"h2{c}_{j}"], [pk])
                            r_, rk_ = ring("relu", relu_t)
                            act(r_[:], ps[:, :], AF.Relu, [pk], [rk_])
                            tt_op(u[:, hc, :], r_[:], r_[:], ALU.mult, [rk_], [f"{uk}_{hc}"])
                        return u, uk

                    def mlp_down(j, u, uk):
                        for m in range(8):
                            ps, pk = psum()
                            for hc in range(4):
                                mm(ps[:, :], wb[:, hc, m * 128:(m + 1) * 128], u[:, hc, :], hc == 0, hc == 3, [wbk, f"{uk}_{hc}"], [pk])
                            xs_ = x[:, m, j * 512:(j + 1) * 512]
                            stt(xs_, ps[:, :], modv[:, l, 5, m, v:v + 1], xs_, ALU.mult, ALU.add, [pk, f"modt{l}", f"x{m}_{j}"], [f"x{m}_{j}"])

                    us = {0: mlp_up(0)}
                    for j in range(NT):
                        if j + 1 < NT:
                            us[j + 1] = mlp_up(j + 1)
                        mlp_down(j, *us.pop(j))
                P.barrier()
                if debug and l == 0:
                    dma("sp", dbg_x2[:, :, 0:T], x[:, :, 0:T], [], [], is_out=True)
                    P.barrier()
            P.mark(f"{g} final")
            for j in range(NT):
                ps, pk = psum()
                for c in range(8):
                    sq, sqk = ring("sqt", sqt)
                    act(sq[:], x[:, c, j * 512:(j + 1) * 512], AF.Square, [f"x{c}_{j}"], [sqk])
                    mm(ps[:, :], ones_b, sq[:], c == 0, c == 7, [sqk, "cst"], [pk])
                act(s_t[:], ps[:, :], AF.Sqrt, [pk], ["s_t", "rstd"], bias=EPS, scale=1.0 / D)
                recip(rstd[:], s_t[:], ["s_t"], ["s_t", "rstd"])
                for c in range(8):
                    y_, yk = ring("tt", tt)
                    stt(y_[:], x[:, c, j * 512:(j + 1) * 512], vecs[:, 64 + c:65 + c], rstd[:], ALU.mult, ALU.mult,
                        [f"x{c}_{j}", "vecs", "rstd"], [yk])
                    dma("sp", yT[c * 128:(c + 1) * 128, j * 512:(j + 1) * 512], y_[:], [yk], [], is_out=True)
            P.barrier()

        try:
            for g_ in groups:
                run_group(g_)
            P.mark("end")
            P.finish()
        except StopBuild:
            pass
        MARKS.extend(P.marks)
    return nc


_CACHE = {}


def _consts():
    ident = np.eye(128, dtype=np.float32)
    ones = np.ones((128, 128), np.float32)
    kk = np.arange(128)[:, None]
    qq = np.arange(128)[None, :]
    mprev = np.where(kk >= qq, 0.0, NEG).astype(np.float32)
    mnext = np.where(kk <= qq, 0.0, NEG).astype(np.float32)
    c = np.concatenate([ident, ones, mprev, mnext, np.zeros((128, 128), np.float32)], axis=1)
    def tables(rot_dim):
        half = rot_dim // 2
        inv = (10000.0 ** (-np.arange(0, half, 2, dtype=np.float32) / half)).astype(np.float32)
        t = np.arange(2048)
        row = (t // 64).astype(np.float32)
        col = (t % 64).astype(np.float32)
        ar = row[:, None] * inv[None, :]
        ac = col[:, None] * inv[None, :]
        ang = np.concatenate([ar, ar, ac, ac], axis=-1).astype(np.float32)
        cos = np.cos(ang).astype(np.float32)
        sin = np.sin(ang).astype(np.float32)
        q = rot_dim // 4
        sgn = np.concatenate([-np.ones(q), np.ones(q), -np.ones(q), np.ones(q)]).astype(np.float32)
        return cos, sin * sgn[None, :]
    cs, ss = tables(64)
    cm, sm_ = tables(32)
    r = np.concatenate([cs, ss, cm, sm_], axis=1)
    r = r.reshape(16, 128, 192).transpose(1, 0, 2).reshape(128, 16 * 192)
    return np.ascontiguousarray(c), np.ascontiguousarray(r.astype(np.float32))


def kernel(x_prompt, x_sample, cache_mla_ckv, cache_mla_kpe, cache_swa_k, cache_swa_v, c, c_ctx,
           w_ada, b_ada, norm1, norm2, w_in, conv_w, sgu_norm, sgu_w, sgu_b, mla_q_norm, mla_w_q_up,
           mla_kv_norm, mla_w_kv_up, swa_sink, w_out, mlp_w1, mlp_w2, final_norm):
    in_maps = pack_inputs(x_prompt, x_sample, cache_mla_ckv, cache_mla_kpe, cache_swa_k, cache_swa_v, c, c_ctx,
                          w_ada, b_ada, norm1, norm2, w_in, conv_w, sgu_norm, sgu_w, sgu_b, mla_q_norm, mla_w_q_up,
                          mla_kv_norm, mla_w_kv_up, swa_sink, w_out, mlp_w1, mlp_w2, final_norm)
    if "nc" not in _CACHE:
        _CACHE["nc"] = build_program()
    nc = _CACHE["nc"]
    res = run_bass_kernel_spmd(nc, in_maps, core_ids=list(range(NCORES)))
    return unpack_outputs(res.results)


def pack_inputs(x_prompt, x_sample, cache_mla_ckv, cache_mla_kpe, cache_swa_k, cache_swa_v, c, c_ctx,
                w_ada, b_ada, norm1, norm2, w_in, conv_w, sgu_norm, sgu_w, sgu_b, mla_q_norm, mla_w_q_up,
                mla_kv_norm, mla_w_kv_up, swa_sink, w_out, mlp_w1, mlp_w2, final_norm, cores=range(NCORES)):
    f = lambda a: np.ascontiguousarray(np.asarray(a, dtype=np.float32))
    x_prompt, x_sample = f(x_prompt), f(x_sample)
    consts, ropes = _consts()
    w_in = f(w_in)
    a_b, a_c, a_x = w_in[:, :, 0:256], w_in[:, :, 256:512], w_in[:, :, 512:768]
    u_, v_ = w_in[:, :, 768:1024], w_in[:, :, 1024:1280]
    cq, ckv, kpe = w_in[:, :, 1280:1536], w_in[:, :, 1536:1664], w_in[:, :, 1664:1696]
    sq, sk, sv = w_in[:, :, 1696:1952], w_in[:, :, 1952:2080], w_in[:, :, 2080:2208]
    sq_g = sq.reshape(DEPTH, D, 2, 2, 64).transpose(0, 1, 3, 2, 4).reshape(DEPTH, D, 256)
    w_fm = f(np.concatenate([a_c, a_x, a_b, u_, cq], axis=2))
    w_tm = f(np.concatenate([ckv, kpe, sk, sv, v_, sq_g], axis=2))
    bada_fm = f(np.asarray(b_ada).reshape(DEPTH, 48, 128).transpose(2, 0, 1).reshape(128, DEPTH * 48))
    vecs = np.zeros((128, 128), np.float32)
    vecs[:, 0:32] = np.asarray(norm1).reshape(DEPTH, 8, 128).transpose(2, 0, 1).reshape(128, 32)
    vecs[:, 32:64] = np.asarray(norm2).reshape(DEPTH, 8, 128).transpose(2, 0, 1).reshape(128, 32)
    vecs[:, 64:72] = np.asarray(final_norm).reshape(8, 128).T
    vecs[:, 72:80] = np.asarray(mla_q_norm).reshape(DEPTH, 2, 128).transpose(2, 0, 1).reshape(128, 8)
    vecs[:, 80:104] = np.asarray(conv_w).reshape(DEPTH, 3, 2, 128).transpose(3, 0, 2, 1).reshape(128, 24)
    sgunorm_bc = f(np.broadcast_to(np.asarray(sgu_norm).reshape(1, DEPTH * 256), (128, DEPTH * 256)))
    kvnorm_bc = f(np.broadcast_to(np.asarray(mla_kv_norm).reshape(1, DEPTH * 128), (128, DEPTH * 128)))
    sink_bc = f(np.broadcast_to(np.asarray(swa_sink).reshape(1, 16), (128, 16)))
    sgub = f(np.asarray(sgu_b).reshape(1, DEPTH * 512))
    sgu_wT = f(np.asarray(sgu_w).transpose(0, 3, 1, 2).reshape(DEPTH, 128, 512))
    shared = dict(w_ada=f(w_ada), bada_fm=bada_fm, vecs_fm=vecs, sgunorm_bc=sgunorm_bc, kvnorm_bc=kvnorm_bc,
                  sink_bc=sink_bc, sgub=sgub, sgu_wT=sgu_wT, w_fm=w_fm, w_tm=w_tm, w_out=f(w_out), w1=f(mlp_w1),
                  w2=f(mlp_w2), wq=f(mla_w_q_up), wkv=f(mla_w_kv_up), ropes=ropes, consts=consts)
    c = np.asarray(c, np.float32)
    c_ctx = np.asarray(c_ctx, np.float32)
    in_maps = []
    for i in cores:
        cv = np.stack([c_ctx, c[i]], axis=0)
        cfm = f(cv.reshape(2, 8, 128).transpose(2, 1, 0).reshape(128, 16))
        m = dict(shared)
        m.update(
            xsT=f(x_sample[i].T),
            xpT=f(x_prompt[4 * i:4 * i + 4].reshape(1024, D).T),
            ckvT_c=f(np.asarray(cache_mla_ckv[i]).transpose(0, 2, 1)),
            kpeT_c=f(np.asarray(cache_mla_kpe[i]).transpose(0, 2, 1)),
            skT_c=f(np.asarray(cache_swa_k[i]).reshape(DEPTH, 512, 128).transpose(0, 2, 1)),
            sv_c=f(np.asarray(cache_swa_v[i]).reshape(DEPTH, 512, 128)),
            cfm=cfm,
        )
        in_maps.append(m)
    return in_maps


def unpack_outputs(rs):
    y_prompt = np.concatenate([r["ypT"].T.reshape(4, 256, D) for r in rs], axis=0).astype(np.float32)
    y_sample = np.stack([r["ysT"].T for r in rs], axis=0).astype(np.float32)
    new_ckv = np.concatenate([r["o_ckv"] for r in rs], axis=0).astype(np.float32)
    new_kpe = np.concatenate([r["o_kpe"] for r in rs], axis=0).astype(np.float32)
    new_k = np.concatenate([r["o_k"] for r in rs], axis=0).reshape(-1, DEPTH, 256, 2, 64).astype(np.float32)
    new_v = np.concatenate([r["o_v"] for r in rs], axis=0).reshape(-1, DEPTH, 256, 2, 64).astype(np.float32)
    return (np.ascontiguousarray(y_prompt), np.ascontiguousarray(y_sample), new_ckv, new_kpe, new_k, new_v)
```

```python
import numpy as np
import concourse.bass as bass
import concourse.mybir as mybir
from concourse.bass_utils import run_bass_kernel_spmd

F32, BF16 = mybir.dt.float32, mybir.dt.bfloat16
AF = mybir.ActivationFunctionType
ALU = mybir.AluOpType

D = 1024
DEPTH = 4
EPS = 1e-6
MLA_SCALE = 96 ** -0.5
SWA_SCALE = 0.125
NEG = -30000.0
NCORES = 8


class Info:
    __slots__ = ("sem", "val", "clock", "eng")

    def __init__(self, eng):
        self.sem = None
        self.val = 0
        self.clock = None
        self.eng = eng


class Prog:
    COMPUTE = ("pe", "act", "dve")
    QUEUES = ("sp", "pool")
    NSL = 6

    def __init__(self, nc):
        self.nc = nc
        self.eng = {"pe": nc.tensor, "act": nc.scalar, "dve": nc.vector, "pool": nc.gpsimd, "sp": nc.sync}
        self.sems = {}
        self.epoch = 0
        self.csem = {}
        self.cnt = {}
        self._new_epoch_sems()
        self.slots = {}
        self.nd = {q: 0 for q in self.QUEUES}
        for q in self.QUEUES:
            self.slots[q] = []
            for i in range(self.NSL):
                nm = f"d_{q}{i}"
                self.sems[nm] = nc.alloc_semaphore(name=nm)
                self.slots[q].append([nm, 0])
        self.clock = {e: {} for e in self.eng}
        self.last_w = {}
        self.readers = {}
        self.pending = {e: [] for e in self.COMPUTE}
        self.out_infos = []
        self.total = {e: 0 for e in self.eng}
        self.ps_open = {}
        self.marks = []

    def _new_epoch_sems(self):
        for e in self.COMPUTE:
            nm = f"c_{e}{self.epoch}"
            self.sems[nm] = self.nc.alloc_semaphore(name=nm)
            self.csem[e] = nm
            self.cnt[e] = 0
        self.epoch += 1

    def _wait(self, e, info):
        if info.sem is None:
            raise RuntimeError("dependency on unsignaled op")
        ck = self.clock[e]
        if ck.get(info.sem, 0) >= info.val:
            return
        self.eng[e].wait_ge(self.sems[info.sem], info.val)
        for s, v in info.clock.items():
            if ck.get(s, 0) < v:
                ck[s] = v

    def op(self, e, fn, reads=(), writes=(), sig=True, dma=False, is_out=False):
        psr = [k for k in reads if k.startswith("ps")]
        for k in psr:
            self.ps_open[k] = False
        if psr:
            writes = list(writes) + [k for k in psr if k not in writes]
        deps = []
        for k in reads:
            w = self.last_w.get(k)
            if w is not None:
                deps.append((w, True))
        for k in writes:
            w = self.last_w.get(k)
            if w is not None:
                deps.append((w, False))
            rd = self.readers.get(k)
            if rd:
                for r in rd.values():
                    deps.append((r, False))
        for info, raw in deps:
            if info.eng == e and not dma:
                if e == "pe":
                    continue
            self._wait(e, info)
        self.total[e] += 1
        info = Info(e)
        if dma:
            n = self.nd[e]
            self.nd[e] += 1
            slot = self.slots[e][n % self.NSL]
            ck = self.clock[e]
            if slot[1] > 0 and ck.get(slot[0], 0) < slot[1]:
                self.eng[e].wait_ge(self.sems[slot[0]], slot[1])
                ck[slot[0]] = slot[1]
            slot[1] += 16
            inst = fn()
            inst.then_inc(self.sems[slot[0]], 16)
            info.sem, info.val = slot[0], slot[1]
            info.clock = dict(ck)
            info.clock[info.sem] = info.val
            info.eng = e + "_dma%d" % n
            if is_out:
                self.out_infos.append(info)
        else:
            inst = fn()
            if e == "pe" and self.marks and self.marks[-1][1] is None:
                nm_ = inst.ins.name
                for mk in reversed(self.marks):
                    if mk[1] is not None:
                        break
                    mk[1] = nm_
            if sig:
                self.cnt[e] += 1
                inst.then_inc(self.sems[self.csem[e]], 1)
                info.sem, info.val = self.csem[e], self.cnt[e]
                info.clock = dict(self.clock[e])
                info.clock[info.sem] = info.val
                for p in self.pending[e]:
                    p.sem, p.val, p.clock = info.sem, info.val, info.clock
                self.pending[e] = []
            else:
                self.pending[e].append(info)
        for k in writes:
            self.last_w[k] = info
            self.readers[k] = {}
        for k in reads:
            self.readers.setdefault(k, {})[info.eng] = info
        return info

    def barrier(self):
        for e in self.COMPUTE:
            assert not self.pending[e]
        for f in self.eng:
            ck = self.clock[f]
            for e in self.COMPUTE:
                nm, v = self.csem[e], self.cnt[e]
                if v > 0 and e != f and ck.get(nm, 0) < v:
                    self.eng[f].wait_ge(self.sems[nm], v)
                if v > 0:
                    ck[nm] = v
            for q in self.QUEUES:
                for nm, v in self.slots[q]:
                    if v > 0 and ck.get(nm, 0) < v:
                        self.eng[f].wait_ge(self.sems[nm], v)
                        ck[nm] = v
        for e in self.COMPUTE:
            if self.cnt[e] > 0:
                self.eng[e].wait_ge(self.sems[self.csem[e]], self.cnt[e])
        self.last_w = {}
        self.readers = {}
        self._new_epoch_sems()

    def mark(self, label):
        self.marks.append([label, None])

    def finish(self):
        for info in self.out_infos:
            self._wait("sp", info)


MARKS = []


class StopBuild(Exception):
    pass


def build_program(depth=DEPTH, groups="AB", debug=False, p0=9):
    nc = bass.Bass("TRN2", target_bir_lowering=False)
    P = Prog(nc)

    def din(name, shape):
        return nc.dram_tensor(name, list(shape), F32, kind="ExternalInput").ap()

    def dout(name, shape):
        return nc.dram_tensor(name, list(shape), F32, kind="ExternalOutput").ap()

    xsT = din("xsT", (D, 2048))
    xpT = din("xpT", (D, 1024))
    ckvT_c = din("ckvT_c", (DEPTH, 128, 512))
    kpeT_c = din("kpeT_c", (DEPTH, 32, 512))
    skT_c = din("skT_c", (DEPTH, 128, 512))
    sv_c = din("sv_c", (DEPTH, 512, 128))
    cfm = din("cfm", (128, 16))
    w_ada = din("w_ada", (DEPTH, D, 6 * D))
    bada_fm = din("bada_fm", (128, DEPTH * 48))
    vecs_fm = din("vecs_fm", (128, 128))
    sgunorm_bc = din("sgunorm_bc", (128, DEPTH * 256))
    kvnorm_bc = din("kvnorm_bc", (128, DEPTH * 128))
    sink_bc = din("sink_bc", (128, 16))
    sgub = din("sgub", (1, DEPTH * 512))
    sgu_wT = din("sgu_wT", (DEPTH, 128, 512))
    w_fm = din("w_fm", (DEPTH, D, 1280))
    w_tm = din("w_tm", (DEPTH, D, 928))
    w_out = din("w_out", (DEPTH, D, D))
    w1 = din("w1", (DEPTH, D, 4096))
    w2 = din("w2", (DEPTH, 4096, D))
    wq = din("wq", (DEPTH, 256, 384))
    wkv = din("wkv", (DEPTH, 128, 512))
    ropes = din("ropes", (128, 16 * 192))
    consts = din("consts", (128, 640))
    ysT = dout("ysT", (D, 2048))
    ypT = dout("ypT", (D, 1024))
    o_ckv = dout("o_ckv", (4, DEPTH, 256, 128))
    o_kpe = dout("o_kpe", (4, DEPTH, 256, 32))
    o_k = dout("o_k", (4, DEPTH, 256, 128))
    o_v = dout("o_v", (4, DEPTH, 256, 128))
    if debug:
        dbg_h = nc.dram_tensor("dbg_h", [128, 8, 512], BF16, kind="ExternalOutput").ap()
        dbg_cat = nc.dram_tensor("dbg_cat", [128, 8, 512], BF16, kind="ExternalOutput").ap()
        dbg_x1 = dout("dbg_x1", (128, 8, 2048))
        dbg_x2 = dout("dbg_x2", (128, 8, 2048))

    A = nc.alloc_sbuf_tensor
    ident_f = A("ident_f", [128, 128], F32)
    cst = A("cst", [128, 512], BF16)
    ident_b, ones_b, mprev, mnext = cst[:, 0:128], cst[:, 128:256], cst[:, 256:384], cst[:, 384:512]
    modt = A("modt", [128, DEPTH * 48 * 2], F32)
    badat = A("badat", [128, DEPTH * 48], F32)
    vecs = A("vecs", [128, 128], F32)
    gs = A("gs", [128, DEPTH * 2 * 8 * 2], F32)
    csil = A("csil", [128, 16], F32)
    csil_b = A("csil_b", [128, 16], BF16)
    sinkexp = A("sinkexp", [128, 16], F32)
    ropet = A("ropet", [128, 16 * 192], BF16)
    sgn = A("sgn", [128, 256], F32)
    kvn = A("kvn", [128, 128], F32)
    wq_t = A("wq_t", [128, 2, 384], BF16)
    wkv_t = A("wkv_t", [128, 512], BF16)
    wsT_t = A("wsT_t", [128, 512], BF16)
    sgub_t = A("sgub_t", [1, 512], BF16)
    x = A("x", [128, 8, 2048], F32)
    R = A("R", [128, 36352], BF16)
    NST = 3
    wst = [A(f"wst{i}", [128, 4096], BF16) for i in range(NST)]
    sqt = [A(f"sqt{i}", [128, 512], BF16) for i in range(2)]
    tt = [A(f"tt{i}", [128, 512], F32) for i in range(2)]
    s_t = A("s_t", [128, 512], F32)
    rstd = s_t
    tmf = [A(f"tmf{i}", [128, 512], F32) for i in range(2)]
    tm2 = [A(f"tm2{i}", [128, 256], F32) for i in range(4)]
    tm3 = [A(f"tm3{i}", [128, 256], F32) for i in range(2)]
    sm = [A(f"sm{i}", [128, 4], F32) for i in range(4)]
    vn_t = [A(f"vn{i}", [128, 256], BF16) for i in range(2)]
    pt = [A(f"pt{i}", [128, 512], BF16) for i in range(4)]
    rden = [A(f"rden{i}", [64, 512], F32) for i in range(1)]
    qh = A("qh", [128, 4, 512], BF16)
    relu_t = [A(f"relu{i}", [128, 512], BF16) for i in range(2)]
    cqraw = relu_t
    yo = tt
    PS = [nc.alloc_psum_tensor(f"ps{i}", [128, 512], F32) for i in range(8)]

    rr = {}

    def ring(name, lst):
        i = rr.get(name, 0)
        rr[name] = i + 1
        j = i % len(lst)
        return lst[j], f"{name}{j}"

    psring = {"lst": list(range(8)), "i": 0}

    def psum():
        lst = psring["lst"]
        b = lst[psring["i"] % len(lst)]
        psring["i"] += 1
        assert not P.ps_open.get(f"ps{b}", False), f"psum bank ps{b} re-allocated before its previous contents were read"
        P.ps_open[f"ps{b}"] = True
        return PS[b], f"ps{b}"

    def set_psring(lst):
        psring["lst"] = list(lst)
        psring["i"] = 0

    def mm(out, lhsT, rhs, start, stop, reads, writes, sig=None):
        if sig is None:
            sig = stop
        return P.op("pe", lambda: nc.tensor.matmul(out, lhsT=lhsT, rhs=rhs, start=start, stop=stop),
                    reads=reads, writes=writes, sig=sig)

    def act(out, in_, func, reads, writes, bias=0.0, scale=1.0, accum_out=None):
        kw = {}
        if accum_out is not None:
            kw["accum_out"] = accum_out
        return P.op("act", lambda: nc.scalar.activation(out=out, in_=in_, func=func, bias=bias, scale=scale, **kw),
                    reads=reads, writes=writes)

    def tt_op(out, in0, in1, op, reads, writes):
        return P.op("dve", lambda: nc.vector.tensor_tensor(out=out, in0=in0, in1=in1, op=op), reads=reads, writes=writes)

    def ts_op(out, in0, s1, op0, reads, writes, s2=None, op1=None):
        if op1 is None:
            return P.op("dve", lambda: nc.vector.tensor_scalar(out=out, in0=in0, scalar1=s1, scalar2=None, op0=op0),
                        reads=reads, writes=writes)
        return P.op("dve", lambda: nc.vector.tensor_scalar(out=out, in0=in0, scalar1=s1, scalar2=s2, op0=op0, op1=op1),
                    reads=reads, writes=writes)

    def stt(out, in0, scalar, in1, op0, op1, reads, writes):
        return P.op("dve", lambda: nc.vector.scalar_tensor_tensor(out=out, in0=in0, scalar=scalar, in1=in1, op0=op0, op1=op1),
                    reads=reads, writes=writes)

    def vcopy(out, in_, reads, writes):
        return P.op("dve", lambda: nc.vector.tensor_copy(out=out, in_=in_), reads=reads, writes=writes)

    def recip(out, in_, reads, writes):
        return P.op("dve", lambda: nc.vector.reciprocal(out=out, in_=in_), reads=reads, writes=writes)

    def dma(q, out, in_, reads, writes, is_out=False):
        e = nc.sync if q == "sp" else nc.gpsimd
        return P.op(q, lambda: e.dma_start(out=out, in_=in_), reads=reads, writes=writes, dma=True, is_out=is_out)

    wplan = []
    wstate = {"issued": 0, "used": 0}

    ada_defer = (len(groups) > 0 and groups[0] == "A")
    ada_phase0 = [0] if ada_defer else list(range(DEPTH))
    ADA_SPLIT = [2, 2, 2, 2, 1, 1, 1, 1]

    def plan_weights():
        for l in ada_phase0:
            for pc in range(12):
                wplan.append((f"ada{l}_{pc}", w_ada[l, :, pc * 512:(pc + 1) * 512], (8, 512)))
        for g in groups:
            NT = 4 if g == "A" else 2
            for l in range(depth):
                for j in range(NT):
                    wplan.append((f"fmK{g}{l}{j}", w_fm[l, :, 0:512], (8, 512)))
                    wplan.append((f"tmK{g}{l}{j}", w_tm[l, :, 0:416], (8, 416)))
                for j in range(NT):
                    wplan.append((f"fmQa{g}{l}{j}", w_fm[l, :, 512:896], (8, 384)))
                    wplan.append((f"fmQb{g}{l}{j}", w_fm[l, :, 896:1280], (8, 384)))
                    wplan.append((f"tmQ{g}{l}{j}", w_tm[l, :, 416:928], (8, 512)))
                    if ada_defer and g == "A" and l + 1 < DEPTH:
                        for pcn in range(3 * j, 3 * j + 3):
                            wplan.append((f"ada{l + 1}_{pcn}", w_ada[l + 1, :, pcn * 512:(pcn + 1) * 512], (8, 512)))
                    wplan.append((f"wo0{g}{l}{j}", w_out[l, :, 0:512], (8, 512)))
                    wplan.append((f"wo1{g}{l}{j}", w_out[l, :, 512:1024], (8, 512)))
                pcn = 0
                for jh in range(8):
                    wplan.append((f"w1{g}{l}{jh}", w1[l, :, jh * 512:(jh + 1) * 512], (8, 512)))
                    wplan.append((f"w2{g}{l}{jh}", w2[l, jh * 512:(jh + 1) * 512, :], (4, 1024)))


    def issue_weight():
        i = wstate["issued"]
        if i >= len(wplan):
            return
        name, ap, (nc_, ncol) = wplan[i]
        slot = wst[i % NST]
        dst = slot[:, 0:nc_ * ncol].rearrange("p (c n) -> p c n", c=nc_)
        src = ap.rearrange("(c p) n -> p c n", p=128)
        dma("pool", dst, src, reads=[], writes=[f"wst{i % NST}"])
        wstate["issued"] += 1

    def get_weight(name, ahead=NST - 1):
        i = wstate["used"]
        assert wplan[i][0] == name, (wplan[i][0], name)
        while wstate["issued"] < min(i + ahead + 1, len(wplan)):
            issue_weight()
        wstate["used"] += 1
        nc_, ncol = wplan[i][2]
        return wst[i % NST][:, 0:nc_ * ncol].rearrange("p (c n) -> p c n", c=nc_), f"wst{i % NST}"

    plan_weights()
    MARKS.clear()

    def ckpt(n):
        if p0 == n:
            P.barrier()
            P.finish()
            raise StopBuild()

    with nc.allow_low_precision("bf16 matmuls"), nc.allow_non_contiguous_dma("small strided loads"):
        dma("sp", ident_f[:], consts[:, 0:128], [], ["ident_f"])
        dma("pool", cst[:], consts[:, 0:512], [], ["cst"])
        dma("sp", badat[:], bada_fm[:, :], [], ["badat"])
        dma("sp", vecs[:], vecs_fm[:, :], [], ["vecs"])
        dma("sp", csil[:], cfm[:, :], [], ["csil"])
        dma("sp", sinkexp[:], sink_bc[:, :], [], ["sinkexp"])
        dma("pool", ropet[:], ropes[:, :], [], ["ropet"])
        act(csil[:], csil[:], AF.Silu, ["csil"], ["csil"])
        act(sinkexp[:], sinkexp[:], AF.Exp, ["sinkexp"], ["sinkexp"])
        if p0 == 1:
            P.barrier()
            P.finish()
            return nc
        modv = modt[:].rearrange("p (l k c v) -> p l k c v", l=DEPTH, k=6, c=8)
        gsv = gs[:].rearrange("p (l n c v) -> p l n c v", l=DEPTH, n=2, c=8)
        modt3 = modt[:].rearrange("p (m v) -> p m v", v=2)
        vcopy(csil_b[:], csil[:], ["csil"], ["csil_b"])
        P.op("dve", lambda: nc.vector.memset(qh[96:128, :, :], 0.0), [], [f"qh{h}" for h in range(4)])

        def ada_piece(l, pc):
            wt, wk_ = get_weight(f"ada{l}_{pc}")
            ps, pk = psum()
            for c in range(8):
                mm(ps[0:2, :], csil_b[:, c * 2:c * 2 + 2], wt[:, c, :], c == 0, c == 7, [wk_, "csil_b"], [pk])
            t_, tk_ = ring("tt", tt)
            t_ = t_[0:2, :]
            act(t_, ps[0:2, :], AF.Copy, [pk], [tk_])
            ps2, pk2 = psum()
            for mi in range(4):
                P.op("pe", lambda: nc.tensor.transpose(ps2[:, mi * 2:mi * 2 + 2], t_[:, mi * 128:(mi + 1) * 128], ident_f[0:2, 0:2]),
                     [tk_, "ident_f"], [pk2])
            m0 = l * 48 + pc * 4
            tt_op(modt3[:, m0:m0 + 4, :], ps2[:, 0:8].rearrange("p (m v) -> p m v", v=2),
                  badat[:, m0:m0 + 4].unsqueeze(2).to_broadcast([128, 4, 2]), ALU.add, [pk2, "badat"], [f"modt{l}"])

        def ada_finish(l):
            for n in range(2):
                nv = vecs[:, n * 32 + l * 8:n * 32 + l * 8 + 8]
                ts_op(gsv[:, l, n], modv[:, l, 3 * n + 1], 1.0, ALU.add, [f"modt{l}"], [f"gs{l}"])
                tt_op(gsv[:, l, n], gsv[:, l, n], nv.unsqueeze(2).to_broadcast([128, 8, 2]), ALU.mult, [f"gs{l}", "vecs"], [f"gs{l}"])

        for l in ada_phase0:
            for pc in range(12):
                ada_piece(l, pc)
            ada_finish(l)
        P.barrier()

        def run_group(g):
            lat = (g == "A")
            T = 2048 if lat else 1024
            S = 2048 if lat else 256
            NT = T // 512
            NK = 2560 if lat else 1024
            NB = NK // 128
            v = 1 if lat else 0
            xT = xsT if lat else xpT
            yT = ysT if lat else ypT
            o = 0

            def carve(n, c=None):
                nonlocal o
                ap = R[:, o:o + n]
                o += n
                if c is not None:
                    ap = ap.rearrange("p (c t) -> p c t", c=c)
                return ap
            KT = carve(4 * NK, 4)
            Vm = carve(NB * 256, NB)
            ks = carve(NK)
            Vs = carve(NB * 128, NB)
            pbuf = carve(2 * T, 2)
            hjb = [carve(8 * 512, 8)]
            if not lat:
                hjb.append(carve(8 * 512, 8))
            H = {"ap": hjb[0], "k": "hj0_"}

            def use_buf(bi):
                H["ap"] = hjb[bi]
                H["k"] = f"hj{bi}_"

            def norm_to(l, j, bi):
                norm(l, 0, j, lambda c: hjb[bi][:, c, :], lambda c: f"hj{bi}_{c}")
            catj = carve(8 * 512, 8)
            cqn = carve(2 * 512, 2)
            qs = carve(2 * 512, 2)
            acc = carve(2 * 512, 2)
            ckT = carve(512)
            assert o <= 36352, o
            h2 = R[:, 0:8 * T].rearrange("p (c t) -> p c t", c=8)
            ub = [R[:, 8 * T + i * 2048:8 * T + (i + 1) * 2048].rearrange("p (c t) -> p c t", c=4) for i in range(2)]

            for c in range(8):
                dma("sp", x[:, c, 0:T], xT[c * 128:(c + 1) * 128, :], [], [f"x{c}_{j}" for j in range(NT)])

            def norm(l, n, j, dst, dkeys):
                ps, pk = psum()
                for c in range(8):
                    sq, sqk = ring("sqt", sqt)
                    act(sq[:], x[:, c, j * 512:(j + 1) * 512], AF.Square, [f"x{c}_{j}"], [sqk])
                    mm(ps[:, :], ones_b, sq[:], c == 0, c == 7, [sqk, "cst"], [pk], sig=True)
                act(s_t[:], ps[:, :], AF.Sqrt, [pk], ["s_t", "rstd"], bias=EPS, scale=1.0 / D)
                recip(rstd[:], s_t[:], ["s_t"], ["s_t", "rstd"])
                for c in range(8):
                    t, tk = ring("tt", tt)
                    tt_op(t[:], x[:, c, j * 512:(j + 1) * 512], rstd[:], ALU.mult, [f"x{c}_{j}", "rstd"], [tk])
                    act(dst(c), t[:], AF.Identity, [tk, f"gs{l}", f"modt{l}"], [dkeys(c)],
                        bias=modv[:, l, 3 * n, c, v:v + 1], scale=gsv[:, l, n, c, v:v + 1])

            def attention_start(NQ, qT, qkeys, chunks, scale, sink_ap, out_ap, out_keys, G=1, LA=2):
                W = G * NQ
                CH = 512 // W
                ai = rr.get("acc", 0) % 2
                rr["acc"] = rr.get("acc", 0) + 1
                nump, numk = PS[3 + ai * 2], f"ps{3 + ai * 2}"
                denp, denk = PS[4 + ai * 2], f"ps{4 + ai * 2}"
                ones64 = ones_b[:, 0:64]
                n = len(chunks)
                groups_ = [chunks[g0:g0 + CH] for g0 in range(0, n, CH)]

                def view(ap2d):
                    return ap2d if G == 1 else ap2d.rearrange("p (g t) -> p g t", g=G)

                def emit_S(grp):
                    sb, sbk = psum()
                    for i, (kT, vv, mask, keys) in enumerate(grp):
                        mm(view(sb[:, i * W:(i + 1) * W]), kT, qT, True, mask is None, keys + qkeys, [sbk])
                        if mask is not None:
                            mrhs = mask if G == 1 else mask.unsqueeze(1).to_broadcast([128, G, NQ])
                            mm(view(sb[:, i * W:(i + 1) * W]), ident_b, mrhs, False, True, ["cst"], [sbk])
                    return sb, sbk

                ng = len(groups_)
                sbs = {}
                for gi in range(min(LA, ng)):
                    sbs[gi] = emit_S(groups_[gi])

                def run(next_start=None):
                    nxt_run = None
                    idx = 0
                    for g0 in range(0, ng, LA):
                        cur = list(range(g0, min(g0 + LA, ng)))
                        for gi in range(g0 + LA, min(g0 + 2 * LA, ng)):
                            sbs[gi] = emit_S(groups_[gi])
                        if g0 + LA >= ng and next_start is not None:
                            nxt_run = next_start()
                        pts = {}
                        for gi in cur:
                            sb, sbk = sbs.pop(gi)
                            p_, pk_ = ring("pt", pt)
                            w = len(groups_[gi]) * W
                            act(p_[:, 0:w], sb[:, 0:w], AF.Exp, [sbk], [pk_], scale=scale)
                            pts[gi] = (p_, pk_)
                        for gi in cur:
                            p_, pk_ = pts[gi]
                            for i, (kT, vv, mask, keys) in enumerate(groups_[gi]):
                                first = (idx == 0)
                                last = (idx == n - 1)
                                idx += 1
                                end_batch = (gi == cur[-1] and i == len(groups_[gi]) - 1)
                                mm(nump[0:64, 0:W], vv, p_[:, i * W:(i + 1) * W], first, last, keys + [pk_], [numk], sig=False)
                                mm(denp[0:64, 0:W], ones64, p_[:, i * W:(i + 1) * W], first, last, [pk_, "cst"], [denk],
                                   sig=(end_batch or last))
                    rd, rdk = ring("rden", rden)
                    sinks = sink_ap if isinstance(sink_ap, list) else [sink_ap] * G
                    outs = out_ap if isinstance(out_ap, list) else [out_ap]
                    if sinks[0] is not None:
                        for gg in range(G):
                            ts_op(rd[:, gg * NQ:(gg + 1) * NQ], denp[0:64, gg * NQ:(gg + 1) * NQ], sinks[gg], ALU.add,
                                  [denk, "sinkexp"], [rdk])
                        recip(rd[:, 0:W], rd[:, 0:W], [rdk], [rdk])
                    else:
                        recip(rd[:, 0:W], denp[0:64, 0:W], [denk], [rdk])
                    for gg in range(G):
                        tt_op(outs[gg], nump[0:64, gg * NQ:(gg + 1) * NQ], rd[:, gg * NQ:(gg + 1) * NQ], ALU.mult,
                              [numk, rdk], out_keys)
                    return nxt_run

                return run

            def attention_seq(calls):
                run = attention_start(*calls[0][0], **calls[0][1])
                for i in range(len(calls)):
                    if i + 1 < len(calls):
                        na, nk = calls[i + 1]
                        run = run(lambda na=na, nk=nk: attention_start(*na, **nk))
                    else:
                        run(None)

            def rope(dst, src, H, Dh, gb, off, rk, wk):
                Q = Dh // 4
                cos = ropet[:, gb * 192 + off:gb * 192 + off + Dh]
                sin = ropet[:, gb * 192 + off + Dh:gb * 192 + off + 2 * Dh]
                t2, t2k = ring("tm3", tm3)
                s4 = src.rearrange("p (h a b q) -> p h a b q", h=H, a=2, b=2)
                d4 = t2[:, 0:H * Dh].rearrange("p (h a b q) -> p h a b q", h=H, a=2, b=2)
                sn = sin.rearrange("p (a b q) -> p a b q", a=2, b=2)
                for bb in range(2):
                    tt_op(d4[:, :, :, bb, :], s4[:, :, :, 1 - bb, :],
                          sn[:, :, bb, :].unsqueeze(1).to_broadcast([128, H, 2, Q]), ALU.mult,
                          rk + ["ropet"] + ([t2k] if bb else []), [t2k])
                s3 = src.rearrange("p (h d) -> p h d", h=H)
                d3 = dst.rearrange("p (h d) -> p h d", h=H)
                tt_op(d3, s3, cos.unsqueeze(1).to_broadcast([128, H, Dh]), ALU.mult, rk + ["ropet"], wk)
                tt_op(dst, dst, t2[:, 0:H * Dh], ALU.add, wk + [t2k], wk)

            def small_weights(l_):
                dma("pool", wq_t[:], wq[l_].rearrange("(c p) n -> p c n", p=128), [], ["wq_t"])
                dma("pool", wkv_t[:], wkv[l_], [], ["wkv_t"])
                dma("pool", wsT_t[:], sgu_wT[l_], [], ["wsT_t"])
                dma("pool", sgub_t[:], sgub[:, l_ * 512:(l_ + 1) * 512], [], ["sgub_t"])
                dma("sp", sgn[:], sgunorm_bc[:, l_ * 256:(l_ + 1) * 256], [], ["sgn"])
                dma("sp", kvn[:], kvnorm_bc[:, l_ * 128:(l_ + 1) * 128], [], ["kvn"])

            for l in range(depth):
                set_psring(range(8))
                if l == 0:
                    small_weights(0)
                P.op("dve", lambda: nc.vector.memset(KT[96:128, :, :], 0.0), [],
                     [f"KT{h}_{jj}" for h in range(4) for jj in range(NT + 1)])
                if lat:
                    dma("pool", ckT[:], ckvT_c[l], [], ["ckT"])
                    for h in range(4):
                        dma("pool", KT[64:96, h, T:T + 512], kpeT_c[l], [], [f"KT{h}_{NT}"])
                    dma("pool", ks[:, T:T + 512], skT_c[l], [], [f"ks_{NT}"])
                    dma("pool", Vs[:, 16:20, :], sv_c[l].rearrange("(b p) d -> p b d", p=128), [], [f"Vs_{NT}"])

                def kv_up(j, nblk):
                    for h in range(4):
                        ps, pk = psum()
                        mm(ps[0:64, 0:nblk * 128], wkv_t[:, h * 128:h * 128 + 64], ckT[:, 0:nblk * 128], True, True,
                           ["wkv_t", "ckT"], [pk])
                        act(KT[0:64, h, j * 512:j * 512 + nblk * 128], ps[0:64, 0:nblk * 128], AF.Copy, [pk], [f"KT{h}_{j}"])
                    for b in range(nblk):
                        ps, pk = psum()
                        mm(ps[:, :], ckT[:, b * 128:(b + 1) * 128], wkv_t[:], True, True, ["wkv_t", "ckT"], [pk])
                        vcopy(Vm[:, j * 4 + b, :].rearrange("p (h d) -> p h d", h=4),
                              ps[:, :].rearrange("p (h t d) -> p h t d", h=4, t=2)[:, :, 1, :], [pk], [f"Vm_{j}"])

                if lat:
                    kv_up(NT, 4)
                ckpt(10)

                for j in range(NT):
                    P.mark(f"{g}{l} K{j} norm")
                    if lat:
                        use_buf(0)
                        norm_to(l, j, 0)
                    else:
                        if j == 0:
                            norm_to(l, 0, 0)
                        if j + 1 < NT:
                            norm_to(l, j + 1, (j + 1) % 2)
                        use_buf(j % 2)
                    P.mark(f"{g}{l} K{j} fm")
                    wtF, wkF = get_weight(f"fmK{g}{l}{j}")
                    wtT, wkT = get_weight(f"tmK{g}{l}{j}", ahead=NST - 2)

                    def fmK_group(m):
                        ps, pk = psum()
                        for c in range(8):
                            mm(ps[:, :], wtF[:, c, m * 128:(m + 1) * 128], H["ap"][:, c, :], c == 0, c == 7, [wkF, f"{H['k']}{c}"], [pk])
                        if m < 2:
                            act(pbuf[:, m, j * 512:(j + 1) * 512], ps[:, :], AF.Copy, [pk], [f"p{m}_{j}"])
                        else:
                            tt_op(pbuf[:, m - 2, j * 512:(j + 1) * 512], ps[:, :], pbuf[:, m - 2, j * 512:(j + 1) * 512],
                                  ALU.mult, [pk, f"p{m - 2}_{j}"], [f"p{m - 2}_{j}"])

                    def tmK_mm(b):
                        tok = slice(b * 128, (b + 1) * 128)
                        ps, pk = psum()
                        for c in range(8):
                            mm(ps[:, 0:416], H["ap"][:, c, tok], wtT[:, c, :], c == 0, c == 7, [wkT, f"{H['k']}{c}"], [pk])
                        return ps, pk

                    def tmK_ew(b, ps, pk):
                        gb = j * 4 + b
                        st, stk = ring("tmf", tmf)
                        act(st[:, 0:416], ps[:, 0:416], AF.Copy, [pk], [stk])
                        smt, smk = ring("sm", sm)
                        t2, t2k = ring("tm3", tm3)
                        P.op("dve", lambda: nc.vector.memset(smt[:, 0:1], 0.0), [], [smk])
                        act(t2[:, 0:128], st[:, 0:128], AF.Square, [stk, smk], [t2k, smk], accum_out=smt[:, 0:1])
                        act(smt[:, 1:2], smt[:, 0:1], AF.Sqrt, [smk], [smk], bias=EPS, scale=1.0 / 128)
                        recip(smt[:, 2:3], smt[:, 1:2], [smk], [smk])
                        stt(st[:, 0:128], st[:, 0:128], smt[:, 2:3], kvn[:], ALU.mult, ALU.mult, [stk, smk, "kvn"], [stk])
                        if lat:
                            t3, t3k = ring("tm2", tm2)
                            rope(t3[:, 0:32], st[:, 128:160], 1, 32, gb, 128, [stk], [t3k])
                            kpe_src, kpek = t3[:, 0:32], t3k
                            t4, t4k = ring("tm2", tm2)
                            rope(t4[:, 0:128], st[:, 160:288], 2, 64, gb, 0, [stk], [t4k])
                            sk_src, skk = t4[:, 0:128], t4k
                        else:
                            kpe_src, kpek = st[:, 128:160], stk
                            sk_src, skk = st[:, 160:288], stk
                            sq_, r0 = divmod(gb * 128, 256)
                            dma("sp", o_ckv[sq_, l, r0:r0 + 128, :], st[:, 0:128], [stk], [], is_out=True)
                            dma("sp", o_kpe[sq_, l, r0:r0 + 128, :], st[:, 128:160], [stk], [], is_out=True)
                            dma("sp", o_k[sq_, l, r0:r0 + 128, :], st[:, 160:288], [stk], [], is_out=True)
                            dma("sp", o_v[sq_, l, r0:r0 + 128, :], st[:, 288:416], [stk], [], is_out=True)
                        vcopy(Vs[:, gb, :], st[:, 288:416], [stk], [f"Vs_{j}"])
                        return st, stk, kpe_src, kpek, sk_src, skk

                    def tmK_tr(b, st, stk, kpe_src, kpek, sk_src, skk):
                        gb = j * 4 + b
                        tok = slice(b * 128, (b + 1) * 128)
                        ps2, pk2 = psum()
                        P.op("pe", lambda: nc.tensor.transpose(ps2[:, 0:128], st[:, 0:128], ident_f[:]), [stk, "ident_f"], [pk2])
                        P.op("pe", lambda: nc.tensor.transpose(ps2[:, 128:256], sk_src, ident_f[:]), [skk, "ident_f"], [pk2])
                        P.op("pe", lambda: nc.tensor.transpose(ps2[0:32, 256:384], kpe_src, ident_f[:]), [kpek, "ident_f"], [pk2])
                        act(ckT[:, tok], ps2[:, 0:128], AF.Copy, [pk2], ["ckT"])
                        vcopy(ks[:, gb * 128:(gb + 1) * 128], ps2[:, 128:256], [pk2], [f"ks_{j}"])
                        act(KT[64:96, 0:2, gb * 128:(gb + 1) * 128], ps2[0:32, 256:384].unsqueeze(1).to_broadcast([32, 2, 128]),
                            AF.Copy, [pk2], [f"KT0_{j}", f"KT1_{j}"])
                        vcopy(KT[64:96, 2:4, gb * 128:(gb + 1) * 128], ps2[0:32, 256:384].unsqueeze(1).to_broadcast([32, 2, 128]),
                              [pk2], [f"KT2_{j}", f"KT3_{j}"])

                    P.mark(f"{g}{l} K{j} tm")
                    r = {}
                    e = {}
                    r[0] = tmK_mm(0)
                    r[1] = tmK_mm(1)
                    for b in range(4):
                        e[b] = tmK_ew(b, *r[b])
                        fmK_group(b)
                        tmK_tr(b, *e[b])
                        if b + 2 < 4:
                            r[b + 2] = tmK_mm(b + 2)
                    P.mark(f"{g}{l} K{j} kvup")
                    kv_up(j, 4)
                    ckpt(14)

                nseq_t = 512 // min(S, 512)
                for j in range(NT):
                    set_psring(range(8))
                    if lat:
                        use_buf(0)
                        if j == 0:
                            P.mark(f"{g}{l} Q{j} norm")
                            norm_to(l, 0, 0)
                    else:
                        use_buf(j % 2)
                    P.mark(f"{g}{l} Q{j} fm")
                    if debug and l == 0 and j == 0:
                        dma("sp", dbg_h, H["ap"], [f"{H['k']}{c}" for c in range(8)], [], is_out=True)
                    wa, wak = get_weight(f"fmQa{g}{l}{j}")

                    def fmQ_group(m, wt, wk_):
                        mi = m % 3
                        ps, pk = psum()
                        for c in range(8):
                            mm(ps[:, :], wt[:, c, mi * 128:(mi + 1) * 128], H["ap"][:, c, :], c == 0, c == 7, [wk_, f"{H['k']}{c}"], [pk])
                        if m < 2:
                            act(catj[:, m, :], ps[:, :], AF.Copy, [pk], [f"cat{m}"])
                        elif m < 4:
                            act(catj[:, m, :], ps[:, :], AF.Gelu_apprx_tanh, [pk], [f"cat{m}"])
                        else:
                            cr, crk = cqraw[m - 4], f"cqraw{m - 4}"
                            act(cr[:], ps[:, :], AF.Copy, [pk], [crk])

                    for m in range(3):
                        fmQ_group(m, wa, wak)
                    wb, wbk = get_weight(f"fmQb{g}{l}{j}")
                    wtT, wkT = get_weight(f"tmQ{g}{l}{j}", ahead=NST - 2)

                    def cq_conv():
                        ps, pk = psum()
                        for c in range(2):
                            sq, sqk = ring("sqt", sqt)
                            act(sq[:], cqraw[c][:], AF.Square, [f"cqraw{c}"], [sqk])
                            mm(ps[:, :], ones_b, sq[:], c == 0, c == 1, [sqk, "cst"], [pk], sig=True)
                        act(s_t[:], ps[:, :], AF.Sqrt, [pk], ["s_t", "rstd"], bias=EPS, scale=1.0 / 256)
                        recip(rstd[:], s_t[:], ["s_t"], ["s_t", "rstd"])
                        for c in range(2):
                            stt(cqn[:, c, :], cqraw[c][:], vecs[:, 72 + l * 2 + c:72 + l * 2 + c + 1], rstd[:], ALU.mult, ALU.mult,
                                [f"cqraw{c}", "vecs", "rstd"], [f"cqn{c}"])
                        Sq = min(S, 512)
                        for c in range(2):
                            cw = lambda k: vecs[:, 80 + (l * 2 + c) * 3 + k:80 + (l * 2 + c) * 3 + k + 1]
                            lo = j * 512
                            pk_all = [f"p{c}_{jj}" for jj in range(NT)]
                            ts_op(acc[:, c, :], pbuf[:, c, lo:lo + 512], cw(1), ALU.mult, pk_all + ["vecs"], [f"acc{c}"])
                            for sidx in range(nseq_t):
                                a0 = sidx * Sq
                                g0 = lo + a0
                                first_in_seq = (g0 % S == 0)
                                last_in_seq = ((g0 + Sq) % S == 0)
                                s0 = 1 if first_in_seq else 0
                                stt(acc[:, c, a0 + s0:a0 + Sq], pbuf[:, c, g0 + s0 - 1:g0 + Sq - 1], cw(0), acc[:, c, a0 + s0:a0 + Sq],
                                    ALU.mult, ALU.add, pk_all + ["vecs", f"acc{c}"], [f"acc{c}"])
                                e0 = 1 if last_in_seq else 0
                                stt(acc[:, c, a0:a0 + Sq - e0], pbuf[:, c, g0 + 1:g0 + Sq - e0 + 1], cw(2), acc[:, c, a0:a0 + Sq - e0],
                                    ALU.mult, ALU.add, pk_all + ["vecs", f"acc{c}"], [f"acc{c}"])
                            tt_op(catj[:, c, :], catj[:, c, :], acc[:, c, :], ALU.mult, [f"cat{c}", f"acc{c}"], [f"cat{c}"])

                    def tmQ_mm(b):
                        tok = slice(b * 128, (b + 1) * 128)
                        ps, pk = psum()
                        for c in range(8):
                            mm(ps[:, :], H["ap"][:, c, tok], wtT[:, c, :], c == 0, c == 7, [wkT, f"{H['k']}{c}"], [pk])
                        return ps, pk

                    def tmQ_ew(b, ps, pk):
                        gb = j * 4 + b
                        st, stk = ring("tmf", tmf)
                        act(st[:, 0:256], ps[:, 0:256], AF.Gelu_apprx_tanh, [pk], [stk])
                        act(st[:, 256:512], ps[:, 256:512], AF.Copy, [pk], [stk])
                        smt, smk = ring("sm", sm)
                        t2, t2k = ring("tm3", tm3)
                        P.op("dve", lambda: nc.vector.memset(smt[:, 0:1], 0.0), [], [smk])
                        act(t2[:, 0:256], st[:, 0:256], AF.Square, [stk, smk], [t2k, smk], accum_out=smt[:, 0:1])
                        act(smt[:, 1:2], smt[:, 0:1], AF.Sqrt, [smk], [smk], bias=EPS, scale=1.0 / 256)
                        recip(smt[:, 2:3], smt[:, 1:2], [smk], [smk])
                        vn, vnk = ring("vn", vn_t)
                        stt(vn[:], st[:, 0:256], smt[:, 2:3], sgn[:], ALU.mult, ALU.mult, [stk, smk, "sgn"], [vnk])
                        if lat:
                            t4, t4k = ring("tm2", tm2)
                            rope(t4[:, 0:256], st[:, 256:512], 4, 64, gb, 0, [stk], [t4k])
                            return st, stk, vn, vnk, t4, t4k
                        return st, stk, vn, vnk, None, None

                    def tmQ_tail(b, st, stk, vn, vnk, t4, t4k):
                        tok = slice(b * 128, (b + 1) * 128)
                        ps2, pk2 = psum()
                        for hd in range(4):
                            cc, e_ = divmod(hd, 2)
                            mm(ps2[:, hd * 128:(hd + 1) * 128], vn[:, cc * 128:(cc + 1) * 128], wsT_t[:, hd * 128:(hd + 1) * 128],
                               True, False, [vnk, "wsT_t"], [pk2], sig=False)
                            mm(ps2[:, hd * 128:(hd + 1) * 128], ones_b[0:1, 0:128], sgub_t[0:1, hd * 128:(hd + 1) * 128],
                               False, True, ["cst", "sgub_t"], [pk2], sig=True)
                        ps3, pk3 = psum()
                        for gg in range(2):
                            src_ap = (t4[:, gg * 128:(gg + 1) * 128] if lat else st[:, 256 + gg * 128:256 + (gg + 1) * 128])
                            P.op("pe", lambda: nc.tensor.transpose(ps3[:, gg * 128:(gg + 1) * 128], src_ap, ident_f[:]),
                                 [t4k if lat else stk, "ident_f"], [pk3])
                        for hd in range(4):
                            cc, e_ = divmod(hd, 2)
                            tt_op(catj[e_ * 64:(e_ + 1) * 64, 2 + cc, tok], catj[e_ * 64:(e_ + 1) * 64, 2 + cc, tok],
                                  ps2[e_ * 64:(e_ + 1) * 64, hd * 128:(hd + 1) * 128], ALU.mult, [f"cat{2 + cc}", pk2], [f"cat{2 + cc}"])
                        act(qs[:, :, tok], ps3[:, 0:256].rearrange("p (g t) -> p g t", g=2), AF.Copy, [pk3], ["qs"])

                    def qm_mm(b):
                        tokq = slice(b * 128, (b + 1) * 128)
                        ps, pk = psum()
                        for c in range(2):
                            mm(ps[:, 0:384], cqn[:, c, tokq], wq_t[:, c, :], c == 0, c == 1, [f"cqn{c}", "wq_t"], [pk])
                        return ps, pk

                    def qm_ew(b, ps, pk):
                        gb = j * 4 + b
                        st, stk = ring("tmf", tmf)
                        act(st[:, 0:384], ps[:, 0:384], AF.Copy, [pk], [stk])
                        if lat:
                            t3, t3k = ring("tm2", tm2)
                            src4 = st[:, 0:384].rearrange("p (h d) -> p h d", h=4)[:, :, 64:96]
                            t5, t5k = ring("tm2", tm2)
                            vcopy(t5[:, 0:128].rearrange("p (h d) -> p h d", h=4), src4, [stk], [t5k])
                            rope(t3[:, 0:128], t5[:, 0:128], 4, 32, gb, 128, [t5k], [t3k])
                            vcopy(src4, t3[:, 0:128].rearrange("p (h d) -> p h d", h=4), [t3k], [stk])
                        return st, stk

                    def qm_tr(b, st, stk):
                        tokq = slice(b * 128, (b + 1) * 128)
                        ps, pk = psum()
                        for h in range(4):
                            P.op("pe", lambda: nc.tensor.transpose(ps[0:96, h * 128:(h + 1) * 128], st[:, h * 96:(h + 1) * 96], ident_f[:]),
                                 [stk, "ident_f"], [pk])
                        src = ps[0:96, :].rearrange("p (h t) -> p h t", h=4)
                        if b % 2 == 0:
                            act(qh[0:96, :, tokq], src, AF.Copy, [pk], [f"qh{h}" for h in range(4)])
                        else:
                            vcopy(qh[0:96, :, tokq], src, [pk], [f"qh{h}" for h in range(4)])

                    P.mark(f"{g}{l} Q{j} tm")
                    r = {}
                    e = {}
                    rq = {}
                    for m in range(3, 6):
                        fmQ_group(m, wb, wbk)
                    cq_conv()
                    r[0] = tmQ_mm(0)
                    r[1] = tmQ_mm(1)
                    rq[0] = qm_mm(0)
                    for b in range(4):
                        e[b] = tmQ_ew(b, *r[b])
                        eq = qm_ew(b, *rq[b])
                        if b + 1 < 4:
                            rq[b + 1] = qm_mm(b + 1)
                        tmQ_tail(b, *e[b])
                        if b + 2 < 4:
                            r[b + 2] = tmQ_mm(b + 2)
                        qm_tr(b, *eq)
                    if ada_defer and g == "A" and l + 1 < DEPTH:
                        for pcn in range(3 * j, 3 * j + 3):
                            ada_piece(l + 1, pcn)
                        if j == NT - 1:
                            ada_finish(l + 1)
                    ckpt(19)
                    set_psring([0, 1, 2, 7])
                    ckpt(20)
                    P.mark(f"{g}{l} Q{j} MLA")
                    acalls = []
                    if lat:
                        for h in range(4):
                            chunks = []
                            for kc in range(20):
                                jt = kc // 4
                                chunks.append((KT[:, h, kc * 128:(kc + 1) * 128], Vm[:, kc, h * 64:(h + 1) * 64], None,
                                               [f"KT{h}_{jt}", f"Vm_{jt}"]))
                            cc, e = divmod(h, 2)
                            acalls.append(((512, qh[:, h, :], [f"qh{h}"], chunks, MLA_SCALE, None,
                                            catj[e * 64:(e + 1) * 64, 4 + cc, :], [f"cat{4 + cc}"]), {}))
                    else:
                        for sl in range(2):
                            sidx = (j * 512) // 256 + sl
                            qsl = slice(sl * 256, (sl + 1) * 256)
                            for h in range(4):
                                chunks = []
                                for kc in (2 * sidx, 2 * sidx + 1):
                                    jt = kc // 4
                                    chunks.append((KT[:, h, kc * 128:(kc + 1) * 128], Vm[:, kc, h * 64:(h + 1) * 64], None,
                                                   [f"KT{h}_{jt}", f"Vm_{jt}"]))
                                cc, e = divmod(h, 2)
                                acalls.append(((256, qh[:, h, qsl], [f"qh{h}"], chunks, MLA_SCALE, None,
                                                catj[e * 64:(e + 1) * 64, 4 + cc, qsl], [f"cat{4 + cc}"]), {}))
                    ckpt(21)
                    if lat and j + 1 < NT:
                        norm_to(l, j + 1, 0)
                    P.mark(f"{g}{l} Q{j} SWA")
                    for n in range(2):
                        sink_l = [sinkexp[0:64, l * 4 + n * 2 + gq:l * 4 + n * 2 + gq + 1] for gq in range(2)]
                        if lat:
                            for bq in range(4):
                                blk = j * 4 + bq
                                tq = slice(bq * 128, (bq + 1) * 128)
                                chunks = []
                                if blk >= 1:
                                    chunks.append((ks[n * 64:(n + 1) * 64, (blk - 1) * 128:blk * 128],
                                                   Vs[:, blk - 1, n * 64:(n + 1) * 64], mprev,
                                                   [f"ks_{(blk - 1) // 4}", f"Vs_{(blk - 1) // 4}"]))
                                chunks.append((ks[n * 64:(n + 1) * 64, blk * 128:(blk + 1) * 128],
                                               Vs[:, blk, n * 64:(n + 1) * 64], None, [f"ks_{blk // 4}", f"Vs_{blk // 4}"]))
                                if blk <= 14:
                                    chunks.append((ks[n * 64:(n + 1) * 64, (blk + 1) * 128:(blk + 2) * 128],
                                                   Vs[:, blk + 1, n * 64:(n + 1) * 64], mnext,
                                                   [f"ks_{(blk + 1) // 4}", f"Vs_{(blk + 1) // 4}"]))
                                for kc in range(16, 20):
                                    chunks.append((ks[n * 64:(n + 1) * 64, kc * 128:(kc + 1) * 128],
                                                   Vs[:, kc, n * 64:(n + 1) * 64], None, [f"ks_{NT}", f"Vs_{NT}"]))
                                acalls.append(((128, qs[n * 64:(n + 1) * 64, :, tq], ["qs"], chunks, SWA_SCALE, sink_l,
                                                [catj[gq * 64:(gq + 1) * 64, 6 + n, tq] for gq in range(2)], [f"cat{6 + n}"]),
                                               {"G": 2}))
                        else:
                            for sl in range(2):
                                sidx = (j * 512) // 256 + sl
                                qsl = slice(sl * 256, (sl + 1) * 256)
                                chunks = []
                                for kc in (2 * sidx, 2 * sidx + 1):
                                    chunks.append((ks[n * 64:(n + 1) * 64, kc * 128:(kc + 1) * 128],
                                                   Vs[:, kc, n * 64:(n + 1) * 64], None, [f"ks_{kc // 4}", f"Vs_{kc // 4}"]))
                                acalls.append(((256, qs[n * 64:(n + 1) * 64, :, qsl], ["qs"], chunks, SWA_SCALE, sink_l,
                                                [catj[gq * 64:(gq + 1) * 64, 6 + n, qsl] for gq in range(2)], [f"cat{6 + n}"]),
                                               {"G": 2}))
                    attention_seq(acalls)
                    ckpt(22)
                    P.mark(f"{g}{l} Q{j} wout")
                    set_psring(range(8))
                    if debug and l == 0 and j == 0:
                        dma("sp", dbg_cat, catj, [f"cat{c}" for c in range(8)], [], is_out=True)
                    for half in range(2):
                        wt, wk_ = get_weight(f"wo{half}{g}{l}{j}")
                        for mi in range(4):
                            m = half * 4 + mi
                            ps, pk = psum()
                            for c in range(8):
                                mm(ps[:, :], wt[:, c, mi * 128:(mi + 1) * 128], catj[:, c, :], c == 0, c == 7, [wk_, f"cat{c}"], [pk])
                            xs_ = x[:, m, j * 512:(j + 1) * 512]
                            stt(xs_, ps[:, :], modv[:, l, 2, m, v:v + 1], xs_, ALU.mult, ALU.add, [pk, f"modt{l}", f"x{m}_{j}"], [f"x{m}_{j}"])
                P.barrier()
                if debug and l == 0:
                    dma("sp", dbg_x1[:, :, 0:T], x[:, :, 0:T], [], [], is_out=True)
                    P.barrier()
                ckpt(23)
                P.mark(f"{g}{l} MLP norm")
                set_psring(range(8))
                if l + 1 < depth:
                    small_weights(l + 1)
                ckpt(24)
                P.mark(f"{g}{l} MLP mm")
                ada_pc = [0]
                for jh in range(8):
                    wa, wak = get_weight(f"w1{g}{l}{jh}")
                    wb, wbk = get_weight(f"w2{g}{l}{jh}", ahead=NST - 2)
                    def mlp_up(j):
                        if jh == 0:
                            norm(l, 1, j, lambda c: h2[:, c, j * 512:(j + 1) * 512], lambda c: f"h2{c}_{j}")
                        u, uk = ring("ub", ub)
                        for hc in range(4):
                            ps, pk = psum()
                            for c in range(8):
                                mm(ps[:, :], wa[:, c, hc * 128:(hc + 1) * 128], h2[:, c, j * 512:(j + 1) * 512], c == 0, c == 7,
                                   [wak, f"h2{c}_{j}"], [pk])
                            r_, rk_ = ring("relu", relu_t)
                            act(r_[:], ps[:, :], AF.Relu, [pk], [rk_])
                            tt_op(u[:, hc, :], r_[:], r_[:], ALU.mult, [rk_], [f"{uk}_{hc}"])
                        return u, uk

                    def mlp_down(j, u, uk):
                        for m in range(8):
                            ps, pk = psum()
                            for hc in range(4):
                                mm(ps[:, :], wb[:, hc, m * 128:(m + 1) * 128], u[:, hc, :], hc == 0, hc == 3, [wbk, f"{uk}_{hc}"], [pk])
                            xs_ = x[:, m, j * 512:(j + 1) * 512]
                            stt(xs_, ps[:, :], modv[:, l, 5, m, v:v + 1], xs_, ALU.mult, ALU.add, [pk, f"modt{l}", f"x{m}_{j}"], [f"x{m}_{j}"])

                    us = {0: mlp_up(0)}
                    for j in range(NT):
                        if j + 1 < NT:
                            us[j + 1] = mlp_up(j + 1)
                        mlp_down(j, *us.pop(j))
                P.barrier()
                if debug and l == 0:
                    dma("sp", dbg_x2[:, :, 0:T], x[:, :, 0:T], [], [], is_out=True)
                    P.barrier()
            P.mark(f"{g} final")
            for j in range(NT):
                ps, pk = psum()
                for c in range(8):
                    sq, sqk = ring("sqt", sqt)
                    act(sq[:], x[:, c, j * 512:(j + 1) * 512], AF.Square, [f"x{c}_{j}"], [sqk])
                    mm(ps[:, :], ones_b, sq[:], c == 0, c == 7, [sqk, "cst"], [pk], sig=True)
                act(s_t[:], ps[:, :], AF.Sqrt, [pk], ["s_t", "rstd"], bias=EPS, scale=1.0 / D)
                recip(rstd[:], s_t[:], ["s_t"], ["s_t", "rstd"])
                for c in range(8):
                    y_, yk = ring("tt", tt)
                    stt(y_[:], x[:, c, j * 512:(j + 1) * 512], vecs[:, 64 + c:65 + c], rstd[:], ALU.mult, ALU.mult,
                        [f"x{c}_{j}", "vecs", "rstd"], [yk])
                    dma("sp", yT[c * 128:(c + 1) * 128, j * 512:(j + 1) * 512], y_[:], [yk], [], is_out=True)
            P.barrier()

        try:
            for g_ in groups:
                run_group(g_)
            P.mark("end")
            P.finish()
        except StopBuild:
            pass
        MARKS.extend(P.marks)
    return nc


_CACHE = {}


def _consts():
    ident = np.eye(128, dtype=np.float32)
    ones = np.ones((128, 128), np.float32)
    kk = np.arange(128)[:, None]
    qq = np.arange(128)[None, :]
    mprev = np.where(kk >= qq, 0.0, NEG).astype(np.float32)
    mnext = np.where(kk <= qq, 0.0, NEG).astype(np.float32)
    c = np.concatenate([ident, ones, mprev, mnext, np.zeros((128, 128), np.float32)], axis=1)
    def tables(rot_dim):
        half = rot_dim // 2
        inv = (10000.0 ** (-np.arange(0, half, 2, dtype=np.float32) / half)).astype(np.float32)
        t = np.arange(2048)
        row = (t // 64).astype(np.float32)
        col = (t % 64).astype(np.float32)
        ar = row[:, None] * inv[None, :]
        ac = col[:, None] * inv[None, :]
        ang = np.concatenate([ar, ar, ac, ac], axis=-1).astype(np.float32)
        cos = np.cos(ang).astype(np.float32)
        sin = np.sin(ang).astype(np.float32)
        q = rot_dim // 4
        sgn = np.concatenate([-np.ones(q), np.ones(q), -np.ones(q), np.ones(q)]).astype(np.float32)
        return cos, sin * sgn[None, :]
    cs, ss = tables(64)
    cm, sm_ = tables(32)
    r = np.concatenate([cs, ss, cm, sm_], axis=1)
    r = r.reshape(16, 128, 192).transpose(1, 0, 2).reshape(128, 16 * 192)
    return np.ascontiguousarray(c), np.ascontiguousarray(r.astype(np.float32))


def kernel(x_prompt, x_sample, cache_mla_ckv, cache_mla_kpe, cache_swa_k, cache_swa_v, c, c_ctx,
           w_ada, b_ada, norm1, norm2, w_in, conv_w, sgu_norm, sgu_w, sgu_b, mla_q_norm, mla_w_q_up,
           mla_kv_norm, mla_w_kv_up, swa_sink, w_out, mlp_w1, mlp_w2, final_norm):
    in_maps = pack_inputs(x_prompt, x_sample, cache_mla_ckv, cache_mla_kpe, cache_swa_k, cache_swa_v, c, c_ctx,
                          w_ada, b_ada, norm1, norm2, w_in, conv_w, sgu_norm, sgu_w, sgu_b, mla_q_norm, mla_w_q_up,
                          mla_kv_norm, mla_w_kv_up, swa_sink, w_out, mlp_w1, mlp_w2, final_norm)
    if "nc" not in _CACHE:
        _CACHE["nc"] = build_program()
    nc = _CACHE["nc"]
    res = run_bass_kernel_spmd(nc, in_maps, core_ids=list(range(NCORES)))
    return unpack_outputs(res.results)


def pack_inputs(x_prompt, x_sample, cache_mla_ckv, cache_mla_kpe, cache_swa_k, cache_swa_v, c, c_ctx,
                w_ada, b_ada, norm1, norm2, w_in, conv_w, sgu_norm, sgu_w, sgu_b, mla_q_norm, mla_w_q_up,
                mla_kv_norm, mla_w_kv_up, swa_sink, w_out, mlp_w1, mlp_w2, final_norm, cores=range(NCORES)):
    f = lambda a: np.ascontiguousarray(np.asarray(a, dtype=np.float32))
    x_prompt, x_sample = f(x_prompt), f(x_sample)
    consts, ropes = _consts()
    w_in = f(w_in)
    a_b, a_c, a_x = w_in[:, :, 0:256], w_in[:, :, 256:512], w_in[:, :, 512:768]
    u_, v_ = w_in[:, :, 768:1024], w_in[:, :, 1024:1280]
    cq, ckv, kpe = w_in[:, :, 1280:1536], w_in[:, :, 1536:1664], w_in[:, :, 1664:1696]
    sq, sk, sv = w_in[:, :, 1696:1952], w_in[:, :, 1952:2080], w_in[:, :, 2080:2208]
    sq_g = sq.reshape(DEPTH, D, 2, 2, 64).transpose(0, 1, 3, 2, 4).reshape(DEPTH, D, 256)
    w_fm = f(np.concatenate([a_c, a_x, a_b, u_, cq], axis=2))
    w_tm = f(np.concatenate([ckv, kpe, sk, sv, v_, sq_g], axis=2))
    bada_fm = f(np.asarray(b_ada).reshape(DEPTH, 48, 128).transpose(2, 0, 1).reshape(128, DEPTH * 48))
    vecs = np.zeros((128, 128), np.float32)
    vecs[:, 0:32] = np.asarray(norm1).reshape(DEPTH, 8, 128).transpose(2, 0, 1).reshape(128, 32)
    vecs[:, 32:64] = np.asarray(norm2).reshape(DEPTH, 8, 128).transpose(2, 0, 1).reshape(128, 32)
    vecs[:, 64:72] = np.asarray(final_norm).reshape(8, 128).T
    vecs[:, 72:80] = np.asarray(mla_q_norm).reshape(DEPTH, 2, 128).transpose(2, 0, 1).reshape(128, 8)
    vecs[:, 80:104] = np.asarray(conv_w).reshape(DEPTH, 3, 2, 128).transpose(3, 0, 2, 1).reshape(128, 24)
    sgunorm_bc = f(np.broadcast_to(np.asarray(sgu_norm).reshape(1, DEPTH * 256), (128, DEPTH * 256)))
    kvnorm_bc = f(np.broadcast_to(np.asarray(mla_kv_norm).reshape(1, DEPTH * 128), (128, DEPTH * 128)))
    sink_bc = f(np.broadcast_to(np.asarray(swa_sink).reshape(1, 16), (128, 16)))
    sgub = f(np.asarray(sgu_b).reshape(1, DEPTH * 512))
    sgu_wT = f(np.asarray(sgu_w).transpose(0, 3, 1, 2).reshape(DEPTH, 128, 512))
    shared = dict(w_ada=f(w_ada), bada_fm=bada_fm, vecs_fm=vecs, sgunorm_bc=sgunorm_bc, kvnorm_bc=kvnorm_bc,
                  sink_bc=sink_bc, sgub=sgub, sgu_wT=sgu_wT, w_fm=w_fm, w_tm=w_tm, w_out=f(w_out), w1=f(mlp_w1),
                  w2=f(mlp_w2), wq=f(mla_w_q_up), wkv=f(mla_w_kv_up), ropes=ropes, consts=consts)
    c = np.asarray(c, np.float32)
    c_ctx = np.asarray(c_ctx, np.float32)
    in_maps = []
    for i in cores:
        cv = np.stack([c_ctx, c[i]], axis=0)
        cfm = f(cv.reshape(2, 8, 128).transpose(2, 1, 0).reshape(128, 16))
        m = dict(shared)
        m.update(
            xsT=f(x_sample[i].T),
            xpT=f(x_prompt[4 * i:4 * i + 4].reshape(1024, D).T),
            ckvT_c=f(np.asarray(cache_mla_ckv[i]).transpose(0, 2, 1)),
            kpeT_c=f(np.asarray(cache_mla_kpe[i]).transpose(0, 2, 1)),
            skT_c=f(np.asarray(cache_swa_k[i]).reshape(DEPTH, 512, 128).transpose(0, 2, 1)),
            sv_c=f(np.asarray(cache_swa_v[i]).reshape(DEPTH, 512, 128)),
            cfm=cfm,
        )
        in_maps.append(m)
    return in_maps


def unpack_outputs(rs):
    y_prompt = np.concatenate([r["ypT"].T.reshape(4, 256, D) for r in rs], axis=0).astype(np.float32)
    y_sample = np.stack([r["ysT"].T for r in rs], axis=0).astype(np.float32)
    new_ckv = np.concatenate([r["o_ckv"] for r in rs], axis=0).astype(np.float32)
    new_kpe = np.concatenate([r["o_kpe"] for r in rs], axis=0).astype(np.float32)
    new_k = np.concatenate([r["o_k"] for r in rs], axis=0).reshape(-1, DEPTH, 256, 2, 64).astype(np.float32)
    new_v = np.concatenate([r["o_v"] for r in rs], axis=0).reshape(-1, DEPTH, 256, 2, 64).astype(np.float32)
    return (np.ascontiguousarray(y_prompt), np.ascontiguousarray(y_sample), new_ckv, new_kpe, new_k, new_v)
```

```python
import numpy as np
import concourse.bass as bass
import concourse.mybir as mybir
from concourse.bass_utils import run_bass_kernel_spmd

F32, BF16 = mybir.dt.float32, mybir.dt.bfloat16
AF = mybir.ActivationFunctionType
ALU = mybir.AluOpType

D = 1024
DEPTH = 4
EPS = 1e-6
MLA_SCALE = 96 ** -0.5
SWA_SCALE = 0.125
NEG = -30000.0
NCORES = 8


class Info:
    __slots__ = ("sem", "val", "clock", "eng")

    def __init__(self, eng):
        self.sem = None
        self.val = 0
        self.clock = None
        self.eng = eng


class Prog:
    COMPUTE = ("pe", "act", "dve")
    QUEUES = ("sp", "pool")
    NSL = 6

    def __init__(self, nc):
        self.nc = nc
        self.eng = {"pe": nc.tensor, "act": nc.scalar, "dve": nc.vector, "pool": nc.gpsimd, "sp": nc.sync}
        self.sems = {}
        self.epoch = 0
        self.csem = {}
        self.cnt = {}
        self._new_epoch_sems()
        self.slots = {}
        self.nd = {q: 0 for q in self.QUEUES}
        for q in self.QUEUES:
            self.slots[q] = []
            for i in range(self.NSL):
                nm = f"d_{q}{i}"
                self.sems[nm] = nc.alloc_semaphore(name=nm)
                self.slots[q].append([nm, 0])
        self.clock = {e: {} for e in self.eng}
        self.last_w = {}
        self.readers = {}
        self.pending = {e: [] for e in self.COMPUTE}
        self.out_infos = []
        self.total = {e: 0 for e in self.eng}
        self.ps_open = {}
        self.marks = []

    def _new_epoch_sems(self):
        for e in self.COMPUTE:
            nm = f"c_{e}{self.epoch}"
            self.sems[nm] = self.nc.alloc_semaphore(name=nm)
            self.csem[e] = nm
            self.cnt[e] = 0
        self.epoch += 1

    def _wait(self, e, info):
        if info.sem is None:
            raise RuntimeError("dependency on unsignaled op")
        ck = self.clock[e]
        if ck.get(info.sem, 0) >= info.val:
            return
        self.eng[e].wait_ge(self.sems[info.sem], info.val)
        for s, v in info.clock.items():
            if ck.get(s, 0) < v:
                ck[s] = v

    def op(self, e, fn, reads=(), writes=(), sig=True, dma=False, is_out=False):
        psr = [k for k in reads if k.startswith("ps")]
        for k in psr:
            self.ps_open[k] = False
        if psr:
            writes = list(writes) + [k for k in psr if k not in writes]
        deps = []
        for k in reads:
            w = self.last_w.get(k)
            if w is not None:
                deps.append((w, True))
        for k in writes:
            w = self.last_w.get(k)
            if w is not None:
                deps.append((w, False))
            rd = self.readers.get(k)
            if rd:
                for r in rd.values():
                    deps.append((r, False))
        for info, raw in deps:
            if info.eng == e and not dma:
                if e == "pe":
                    continue
            self._wait(e, info)
        self.total[e] += 1
        info = Info(e)
        if dma:
            n = self.nd[e]
            self.nd[e] += 1
            slot = self.slots[e][n % self.NSL]
            ck = self.clock[e]
            if slot[1] > 0 and ck.get(slot[0], 0) < slot[1]:
                self.eng[e].wait_ge(self.sems[slot[0]], slot[1])
                ck[slot[0]] = slot[1]
            slot[1] += 16
            inst = fn()
            inst.then_inc(self.sems[slot[0]], 16)
            info.sem, info.val = slot[0], slot[1]
            info.clock = dict(ck)
            info.clock[info.sem] = info.val
            info.eng = e + "_dma%d" % n
            if is_out:
                self.out_infos.append(info)
        else:
            inst = fn()
            if e == "pe" and self.marks and self.marks[-1][1] is None:
                nm_ = inst.ins.name
                for mk in reversed(self.marks):
                    if mk[1] is not None:
                        break
                    mk[1] = nm_
            if sig:
                self.cnt[e] += 1
                inst.then_inc(self.sems[self.csem[e]], 1)
                info.sem, info.val = self.csem[e], self.cnt[e]
                info.clock = dict(self.clock[e])
                info.clock[info.sem] = info.val
                for p in self.pending[e]:
                    p.sem, p.val, p.clock = info.sem, info.val, info.clock
                self.pending[e] = []
            else:
                self.pending[e].append(info)
        for k in writes:
            self.last_w[k] = info
            self.readers[k] = {}
        for k in reads:
            self.readers.setdefault(k, {})[info.eng] = info
        return info

    def barrier(self):
        for e in self.COMPUTE:
            assert not self.pending[e]
        for f in self.eng:
            ck = self.clock[f]
            for e in self.COMPUTE:
                nm, v = self.csem[e], self.cnt[e]
                if v > 0 and e != f and ck.get(nm, 0) < v:
                    self.eng[f].wait_ge(self.sems[nm], v)
                if v > 0:
                    ck[nm] = v
            for q in self.QUEUES:
                for nm, v in self.slots[q]:
                    if v > 0 and ck.get(nm, 0) < v:
                        self.eng[f].wait_ge(self.sems[nm], v)
                        ck[nm] = v
        for e in self.COMPUTE:
            if self.cnt[e] > 0:
                self.eng[e].wait_ge(self.sems[self.csem[e]], self.cnt[e])
        self.last_w = {}
        self.readers = {}
        self._new_epoch_sems()

    def mark(self, label):
        self.marks.append([label, None])

    def finish(self):
        for info in self.out_infos:
            self._wait("sp", info)


MARKS = []


class StopBuild(Exception):
    pass


def build_program(depth=DEPTH, groups="AB", debug=False, p0=9):
    nc = bass.Bass("TRN2", target_bir_lowering=False)
    P = Prog(nc)

    def din(name, shape):
        return nc.dram_tensor(name, list(shape), F32, kind="ExternalInput").ap()

    def dout(name, shape):
        return nc.dram_tensor(name, list(shape), F32, kind="ExternalOutput").ap()

    xsT = din("xsT", (D, 2048))
    xpT = din("xpT", (D, 1024))
    ckvT_c = din("ckvT_c", (DEPTH, 128, 512))
    kpeT_c = din("kpeT_c", (DEPTH, 32, 512))
    skT_c = din("skT_c", (DEPTH, 128, 512))
    sv_c = din("sv_c", (DEPTH, 512, 128))
    cfm = din("cfm", (128, 16))
    w_ada = din("w_ada", (DEPTH, D, 6 * D))
    bada_fm = din("bada_fm", (128, DEPTH * 48))
    vecs_fm = din("vecs_fm", (128, 128))
    sgunorm_bc = din("sgunorm_bc", (128, DEPTH * 256))
    kvnorm_bc = din("kvnorm_bc", (128, DEPTH * 128))
    sink_bc = din("sink_bc", (128, 16))
    sgub = din("sgub", (1, DEPTH * 512))
    sgu_wT = din("sgu_wT", (DEPTH, 128, 512))
    w_fm = din("w_fm", (DEPTH, D, 1280))
    w_tm = din("w_tm", (DEPTH, D, 928))
    w_out = din("w_out", (DEPTH, D, D))
    w1 = din("w1", (DEPTH, D, 4096))
    w2 = din("w2", (DEPTH, 4096, D))
    wq = din("wq", (DEPTH, 256, 384))
    wkv = din("wkv", (DEPTH, 128, 512))
    ropes = din("ropes", (128, 16 * 192))
    consts = din("consts", (128, 640))
    ysT = dout("ysT", (D, 2048))
    ypT = dout("ypT", (D, 1024))
    o_ckv = dout("o_ckv", (4, DEPTH, 256, 128))
    o_kpe = dout("o_kpe", (4, DEPTH, 256, 32))
    o_k = dout("o_k", (4, DEPTH, 256, 128))
    o_v = dout("o_v", (4, DEPTH, 256, 128))
    if debug:
        dbg_h = nc.dram_tensor("dbg_h", [128, 8, 512], BF16, kind="ExternalOutput").ap()
        dbg_cat = nc.dram_tensor("dbg_cat", [128, 8, 512], BF16, kind="ExternalOutput").ap()
        dbg_x1 = dout("dbg_x1", (128, 8, 2048))
        dbg_x2 = dout("dbg_x2", (128, 8, 2048))

    A = nc.alloc_sbuf_tensor
    ident_f = A("ident_f", [128, 128], F32)
    cst = A("cst", [128, 512], BF16)
    ident_b, ones_b, mprev, mnext = cst[:, 0:128], cst[:, 128:256], cst[:, 256:384], cst[:, 384:512]
    modt = A("modt", [128, DEPTH * 48 * 2], F32)
    badat = A("badat", [128, DEPTH * 48], F32)
    vecs = A("vecs", [128, 128], F32)
    gs = A("gs", [128, DEPTH * 2 * 8 * 2], F32)
    csil = A("csil", [128, 16], F32)
    csil_b = A("csil_b", [128, 16], BF16)
    sinkexp = A("sinkexp", [128, 16], F32)
    ropet = A("ropet", [128, 16 * 192], BF16)
    sgn = A("sgn", [128, 256], F32)
    kvn = A("kvn", [128, 128], F32)
    wq_t = A("wq_t", [128, 2, 384], BF16)
    wkv_t = A("wkv_t", [128, 512], BF16)
    wsT_t = A("wsT_t", [128, 512], BF16)
    sgub_t = A("sgub_t", [1, 512], BF16)
    x = A("x", [128, 8, 2048], F32)
    R = A("R", [128, 36352], BF16)
    NST = 3
    wst = [A(f"wst{i}", [128, 4096], BF16) for i in range(NST)]
    sqt = [A(f"sqt{i}", [128, 512], BF16) for i in range(2)]
    tt = [A(f"tt{i}", [128, 512], F32) for i in range(2)]
    s_t = A("s_t", [128, 512], F32)
    rstd = s_t
    tmf = [A(f"tmf{i}", [128, 512], F32) for i in range(2)]
    tm2 = [A(f"tm2{i}", [128, 256], F32) for i in range(4)]
    tm3 = [A(f"tm3{i}", [128, 256], F32) for i in range(2)]
    sm = [A(f"sm{i}", [128, 4], F32) for i in range(4)]
    vn_t = [A(f"vn{i}", [128, 256], BF16) for i in range(2)]
    pt = [A(f"pt{i}", [128, 512], BF16) for i in range(4)]
    rden = [A(f"rden{i}", [64, 512], F32) for i in range(1)]
    qh = A("qh", [128, 4, 512], BF16)
    relu_t = [A(f"relu{i}", [128, 512], BF16) for i in range(2)]
    cqraw = relu_t
    yo = tt
    PS = [nc.alloc_psum_tensor(f"ps{i}", [128, 512], F32) for i in range(8)]

    rr = {}

    def ring(name, lst):
        i = rr.get(name, 0)
        rr[name] = i + 1
        j = i % len(lst)
        return lst[j], f"{name}{j}"

    psring = {"lst": list(range(8)), "i": 0}

    def psum():
        lst = psring["lst"]
        b = lst[psring["i"] % len(lst)]
        psring["i"] += 1
        assert not P.ps_open.get(f"ps{b}", False), f"psum bank ps{b} re-allocated before its previous contents were read"
        P.ps_open[f"ps{b}"] = True
        return PS[b], f"ps{b}"

    def set_psring(lst):
        psring["lst"] = list(lst)
        psring["i"] = 0

    def mm(out, lhsT, rhs, start, stop, reads, writes, sig=None):
        if sig is None:
            sig = stop
        return P.op("pe", lambda: nc.tensor.matmul(out, lhsT=lhsT, rhs=rhs, start=start, stop=stop),
                    reads=reads, writes=writes, sig=sig)

    def act(out, in_, func, reads, writes, bias=0.0, scale=1.0, accum_out=None):
        kw = {}
        if accum_out is not None:
            kw["accum_out"] = accum_out
        return P.op("act", lambda: nc.scalar.activation(out=out, in_=in_, func=func, bias=bias, scale=scale, **kw),
                    reads=reads, writes=writes)

    def tt_op(out, in0, in1, op, reads, writes):
        return P.op("dve", lambda: nc.vector.tensor_tensor(out=out, in0=in0, in1=in1, op=op), reads=reads, writes=writes)

    def ts_op(out, in0, s1, op0, reads, writes, s2=None, op1=None):
        if op1 is None:
            return P.op("dve", lambda: nc.vector.tensor_scalar(out=out, in0=in0, scalar1=s1, scalar2=None, op0=op0),
                        reads=reads, writes=writes)
        return P.op("dve", lambda: nc.vector.tensor_scalar(out=out, in0=in0, scalar1=s1, scalar2=s2, op0=op0, op1=op1),
                    reads=reads, writes=writes)

    def stt(out, in0, scalar, in1, op0, op1, reads, writes):
        return P.op("dve", lambda: nc.vector.scalar_tensor_tensor(out=out, in0=in0, scalar=scalar, in1=in1, op0=op0, op1=op1),
                    reads=reads, writes=writes)

    def vcopy(out, in_, reads, writes):
        return P.op("dve", lambda: nc.vector.tensor_copy(out=out, in_=in_), reads=reads, writes=writes)

    def recip(out, in_, reads, writes):
        return P.op("dve", lambda: nc.vector.reciprocal(out=out, in_=in_), reads=reads, writes=writes)

    def dma(q, out, in_, reads, writes, is_out=False):
        e = nc.sync if q == "sp" else nc.gpsimd
        return P.op(q, lambda: e.dma_start(out=out, in_=in_), reads=reads, writes=writes, dma=True, is_out=is_out)

    wplan = []
    wstate = {"issued": 0, "used": 0}

    ada_defer = (len(groups) > 0 and groups[0] == "A")
    ada_phase0 = [0] if ada_defer else list(range(DEPTH))
    ADA_SPLIT = [2, 2, 2, 2, 1, 1, 1, 1]

    def plan_weights():
        for l in ada_phase0:
            for pc in range(12):
                wplan.append((f"ada{l}_{pc}", w_ada[l, :, pc * 512:(pc + 1) * 512], (8, 512)))
        for g in groups:
            NT = 4 if g == "A" else 2
            for l in range(depth):
                for j in range(NT):
                    wplan.append((f"fmK{g}{l}{j}", w_fm[l, :, 0:512], (8, 512)))
                    wplan.append((f"tmK{g}{l}{j}", w_tm[l, :, 0:416], (8, 416)))
                for j in range(NT):
                    wplan.append((f"fmQa{g}{l}{j}", w_fm[l, :, 512:896], (8, 384)))
                    wplan.append((f"fmQb{g}{l}{j}", w_fm[l, :, 896:1280], (8, 384)))
                    wplan.append((f"tmQ{g}{l}{j}", w_tm[l, :, 416:928], (8, 512)))
                    if ada_defer and g == "A" and l + 1 < DEPTH:
                        for pcn in range(3 * j, 3 * j + 3):
                            wplan.append((f"ada{l + 1}_{pcn}", w_ada[l + 1, :, pcn * 512:(pcn + 1) * 512], (8, 512)))
                    wplan.append((f"wo0{g}{l}{j}", w_out[l, :, 0:512], (8, 512)))
                    wplan.append((f"wo1{g}{l}{j}", w_out[l, :, 512:1024], (8, 512)))
                pcn = 0
                for jh in range(8):
                    wplan.append((f"w1{g}{l}{jh}", w1[l, :, jh * 512:(jh + 1) * 512], (8, 512)))
                    wplan.append((f"w2{g}{l}{jh}", w2[l, jh * 512:(jh + 1) * 512, :], (4, 1024)))


    def issue_weight():
        i = wstate["issued"]
        if i >= len(wplan):
            return
        name, ap, (nc_, ncol) = wplan[i]
        slot = wst[i % NST]
        dst = slot[:, 0:nc_ * ncol].rearrange("p (c n) -> p c n", c=nc_)
        src = ap.rearrange("(c p) n -> p c n", p=128)
        dma("pool", dst, src, reads=[], writes=[f"wst{i % NST}"])
        wstate["issued"] += 1

    def get_weight(name, ahead=NST - 1):
        i = wstate["used"]
        assert wplan[i][0] == name, (wplan[i][0], name)
        while wstate["issued"] < min(i + ahead + 1, len(wplan)):
            issue_weight()
        wstate["used"] += 1
        nc_, ncol = wplan[i][2]
        return wst[i % NST][:, 0:nc_ * ncol].rearrange("p (c n) -> p c n", c=nc_), f"wst{i % NST}"

    plan_weights()
    MARKS.clear()

    def ckpt(n):
        if p0 == n:
            P.barrier()
            P.finish()
            raise StopBuild()

    with nc.allow_low_precision("bf16 matmuls"), nc.allow_non_contiguous_dma("small strided loads"):
        dma("sp", ident_f[:], consts[:, 0:128], [], ["ident_f"])
        dma("pool", cst[:], consts[:, 0:512], [], ["cst"])
        dma("sp", badat[:], bada_fm[:, :], [], ["badat"])
        dma("sp", vecs[:], vecs_fm[:, :], [], ["vecs"])
        dma("sp", csil[:], cfm[:, :], [], ["csil"])
        dma("sp", sinkexp[:], sink_bc[:, :], [], ["sinkexp"])
        dma("pool", ropet[:], ropes[:, :], [], ["ropet"])
        act(csil[:], csil[:], AF.Silu, ["csil"], ["csil"])
        act(sinkexp[:], sinkexp[:], AF.Exp, ["sinkexp"], ["sinkexp"])
        if p0 == 1:
            P.barrier()
            P.finish()
            return nc
        modv = modt[:].rearrange("p (l k c v) -> p l k c v", l=DEPTH, k=6, c=8)
        gsv = gs[:].rearrange("p (l n c v) -> p l n c v", l=DEPTH, n=2, c=8)
        modt3 = modt[:].rearrange("p (m v) -> p m v", v=2)
        vcopy(csil_b[:], csil[:], ["csil"], ["csil_b"])
        P.op("dve", lambda: nc.vector.memset(qh[96:128, :, :], 0.0), [], [f"qh{h}" for h in range(4)])

        def ada_piece(l, pc):
            wt, wk_ = get_weight(f"ada{l}_{pc}")
            ps, pk = psum()
            for c in range(8):
                mm(ps[0:2, :], csil_b[:, c * 2:c * 2 + 2], wt[:, c, :], c == 0, c == 7, [wk_, "csil_b"], [pk])
            t_, tk_ = ring("tt", tt)
            t_ = t_[0:2, :]
            act(t_, ps[0:2, :], AF.Copy, [pk], [tk_])
            ps2, pk2 = psum()
            for mi in range(4):
                P.op("pe", lambda: nc.tensor.transpose(ps2[:, mi * 2:mi * 2 + 2], t_[:, mi * 128:(mi + 1) * 128], ident_f[0:2, 0:2]),
                     [tk_, "ident_f"], [pk2])
            m0 = l * 48 + pc * 4
            tt_op(modt3[:, m0:m0 + 4, :], ps2[:, 0:8].rearrange("p (m v) -> p m v", v=2),
                  badat[:, m0:m0 + 4].unsqueeze(2).to_broadcast([128, 4, 2]), ALU.add, [pk2, "badat"], [f"modt{l}"])

        def ada_finish(l):
            for n in range(2):
                nv = vecs[:, n * 32 + l * 8:n * 32 + l * 8 + 8]
                ts_op(gsv[:, l, n], modv[:, l, 3 * n + 1], 1.0, ALU.add, [f"modt{l}"], [f"gs{l}"])
                tt_op(gsv[:, l, n], gsv[:, l, n], nv.unsqueeze(2).to_broadcast([128, 8, 2]), ALU.mult, [f"gs{l}", "vecs"], [f"gs{l}"])

        for l in ada_phase0:
            for pc in range(12):
                ada_piece(l, pc)
            ada_finish(l)
        P.barrier()

        def run_group(g):
            lat = (g == "A")
            T = 2048 if lat else 1024
            S = 2048 if lat else 256
            NT = T // 512
            NK = 2560 if lat else 1024
            NB = NK // 128
            v = 1 if lat else 0
            xT = xsT if lat else xpT
            yT = ysT if lat else ypT
            o = 0

            def carve(n, c=None):
                nonlocal o
                ap = R[:, o:o + n]
                o += n
                if c is not None:
                    ap = ap.rearrange("p (c t) -> p c t", c=c)
                return ap
            KT = carve(4 * NK, 4)
            Vm = carve(NB * 256, NB)
            ks = carve(NK)
            Vs = carve(NB * 128, NB)
            pbuf = carve(2 * T, 2)
            hjb = [carve(8 * 512, 8)]
            if not lat:
                hjb.append(carve(8 * 512, 8))
            H = {"ap": hjb[0], "k": "hj0_"}

            def use_buf(bi):
                H["ap"] = hjb[bi]
                H["k"] = f"hj{bi}_"

            def norm_to(l, j, bi):
                norm(l, 0, j, lambda c: hjb[bi][:, c, :], lambda c: f"hj{bi}_{c}")
            catj = carve(8 * 512, 8)
            cqn = carve(2 * 512, 2)
            qs = carve(2 * 512, 2)
            acc = carve(2 * 512, 2)
            ckT = carve(512)
            assert o <= 36352, o
            h2 = R[:, 0:8 * T].rearrange("p (c t) -> p c t", c=8)
            ub = [R[:, 8 * T + i * 2048:8 * T + (i + 1) * 2048].rearrange("p (c t) -> p c t", c=4) for i in range(2)]

            for c in range(8):
                dma("sp", x[:, c, 0:T], xT[c * 128:(c + 1) * 128, :], [], [f"x{c}_{j}" for j in range(NT)])

            def norm(l, n, j, dst, dkeys):
                ps, pk = psum()
                for c in range(8):
                    sq, sqk = ring("sqt", sqt)
                    act(sq[:], x[:, c, j * 512:(j + 1) * 512], AF.Square, [f"x{c}_{j}"], [sqk])
                    mm(ps[:, :], ones_b, sq[:], c == 0, c == 7, [sqk, "cst"], [pk], sig=True)
                act(s_t[:], ps[:, :], AF.Sqrt, [pk], ["s_t", "rstd"], bias=EPS, scale=1.0 / D)
                recip(rstd[:], s_t[:], ["s_t"], ["s_t", "rstd"])
                for c in range(8):
                    t, tk = ring("tt", tt)
                    tt_op(t[:], x[:, c, j * 512:(j + 1) * 512], rstd[:], ALU.mult, [f"x{c}_{j}", "rstd"], [tk])
                    act(dst(c), t[:], AF.Identity, [tk, f"gs{l}", f"modt{l}"], [dkeys(c)],
                        bias=modv[:, l, 3 * n, c, v:v + 1], scale=gsv[:, l, n, c, v:v + 1])

            def attention_start(NQ, qT, qkeys, chunks, scale, sink_ap, out_ap, out_keys, G=1, LA=2):
                W = G * NQ
                CH = 512 // W
                ai = rr.get("acc", 0) % 2
                rr["acc"] = rr.get("acc", 0) + 1
                nump, numk = PS[3 + ai * 2], f"ps{3 + ai * 2}"
                denp, denk = PS[4 + ai * 2], f"ps{4 + ai * 2}"
                ones64 = ones_b[:, 0:64]
                n = len(chunks)
                groups_ = [chunks[g0:g0 + CH] for g0 in range(0, n, CH)]

                def view(ap2d):
                    return ap2d if G == 1 else ap2d.rearrange("p (g t) -> p g t", g=G)

                def emit_S(grp):
                    sb, sbk = psum()
                    for i, (kT, vv, mask, keys) in enumerate(grp):
                        mm(view(sb[:, i * W:(i + 1) * W]), kT, qT, True, mask is None, keys + qkeys, [sbk])
                        if mask is not None:
                            mrhs = mask if G == 1 else mask.unsqueeze(1).to_broadcast([128, G, NQ])
                            mm(view(sb[:, i * W:(i + 1) * W]), ident_b, mrhs, False, True, ["cst"], [sbk])
                    return sb, sbk

                ng = len(groups_)
                sbs = {}
                for gi in range(min(LA, ng)):
                    sbs[gi] = emit_S(groups_[gi])

                def run(next_start=None):
                    nxt_run = None
                    idx = 0
                    for g0 in range(0, ng, LA):
                        cur = list(range(g0, min(g0 + LA, ng)))
                        for gi in range(g0 + LA, min(g0 + 2 * LA, ng)):
                            sbs[gi] = emit_S(groups_[gi])
                        if g0 + LA >= ng and next_start is not None:
                            nxt_run = next_start()
                        pts = {}
                        for gi in cur:
                            sb, sbk = sbs.pop(gi)
                            p_, pk_ = ring("pt", pt)
                            w = len(groups_[gi]) * W
                            act(p_[:, 0:w], sb[:, 0:w], AF.Exp, [sbk], [pk_], scale=scale)
                            pts[gi] = (p_, pk_)
                        for gi in cur:
                            p_, pk_ = pts[gi]
                            for i, (kT, vv, mask, keys) in enumerate(groups_[gi]):
                                first = (idx == 0)
                                last = (idx == n - 1)
                                idx += 1
                                end_batch = (gi == cur[-1] and i == len(groups_[gi]) - 1)
                                mm(nump[0:64, 0:W], vv, p_[:, i * W:(i + 1) * W], first, last, keys + [pk_], [numk], sig=False)
                                mm(denp[0:64, 0:W], ones64, p_[:, i * W:(i + 1) * W], first, last, [pk_, "cst"], [denk],
                                   sig=(end_batch or last))
                    rd, rdk = ring("rden", rden)
                    sinks = sink_ap if isinstance(sink_ap, list) else [sink_ap] * G
                    outs = out_ap if isinstance(out_ap, list) else [out_ap]
                    if sinks[0] is not None:
                        for gg in range(G):
                            ts_op(rd[:, gg * NQ:(gg + 1) * NQ], denp[0:64, gg * NQ:(gg + 1) * NQ], sinks[gg], ALU.add,
                                  [denk, "sinkexp"], [rdk])
                        recip(rd[:, 0:W], rd[:, 0:W], [rdk], [rdk])
                    else:
                        recip(rd[:, 0:W], denp[0:64, 0:W], [denk], [rdk])
                    for gg in range(G):
                        tt_op(outs[gg], nump[0:64, gg * NQ:(gg + 1) * NQ], rd[:, gg * NQ:(gg + 1) * NQ], ALU.mult,
                              [numk, rdk], out_keys)
                    return nxt_run

                return run

            def attention_seq(calls):
                run = attention_start(*calls[0][0], **calls[0][1])
                for i in range(len(calls)):
                    if i + 1 < len(calls):
                        na, nk = calls[i + 1]
                        run = run(lambda na=na, nk=nk: attention_start(*na, **nk))
                    else:
                        run(None)

            def rope(dst, src, H, Dh, gb, off, rk, wk):
                Q = Dh // 4
                cos = ropet[:, gb * 192 + off:gb * 192 + off + Dh]
                sin = ropet[:, gb * 192 + off + Dh:gb * 192 + off + 2 * Dh]
                t2, t2k = ring("tm3", tm3)
                s4 = src.rearrange("p (h a b q) -> p h a b q", h=H, a=2, b=2)
                d4 = t2[:, 0:H * Dh].rearrange("p (h a b q) -> p h a b q", h=H, a=2, b=2)
                sn = sin.rearrange("p (a b q) -> p a b q", a=2, b=2)
                for bb in range(2):
                    tt_op(d4[:, :, :, bb, :], s4[:, :, :, 1 - bb, :],
                          sn[:, :, bb, :].unsqueeze(1).to_broadcast([128, H, 2, Q]), ALU.mult,
                          rk + ["ropet"] + ([t2k] if bb else []), [t2k])
                s3 = src.rearrange("p (h d) -> p h d", h=H)
                d3 = dst.rearrange("p (h d) -> p h d", h=H)
                tt_op(d3, s3, cos.unsqueeze(1).to_broadcast([128, H, Dh]), ALU.mult, rk + ["ropet"], wk)
                tt_op(dst, dst, t2[:, 0:H * Dh], ALU.add, wk + [t2k], wk)

            def small_weights(l_):
                dma("pool", wq_t[:], wq[l_].rearrange("(c p) n -> p c n", p=128), [], ["wq_t"])
                dma("pool", wkv_t[:], wkv[l_], [], ["wkv_t"])
                dma("pool", wsT_t[:], sgu_wT[l_], [], ["wsT_t"])
                dma("pool", sgub_t[:], sgub[:, l_ * 512:(l_ + 1) * 512], [], ["sgub_t"])
                dma("sp", sgn[:], sgunorm_bc[:, l_ * 256:(l_ + 1) * 256], [], ["sgn"])
                dma("sp", kvn[:], kvnorm_bc[:, l_ * 128:(l_ + 1) * 128], [], ["kvn"])

            for l in range(depth):
                set_psring(range(8))
                if l == 0:
                    small_weights(0)
                P.op("dve", lambda: nc.vector.memset(KT[96:128, :, :], 0.0), [],
                     [f"KT{h}_{jj}" for h in range(4) for jj in range(NT + 1)])
                if lat:
                    dma("pool", ckT[:], ckvT_c[l], [], ["ckT"])
                    for h in range(4):
                        dma("pool", KT[64:96, h, T:T + 512], kpeT_c[l], [], [f"KT{h}_{NT}"])
                    dma("pool", ks[:, T:T + 512], skT_c[l], [], [f"ks_{NT}"])
                    dma("pool", Vs[:, 16:20, :], sv_c[l].rearrange("(b p) d -> p b d", p=128), [], [f"Vs_{NT}"])

                def kv_up(j, nblk):
                    for h in range(4):
                        ps, pk = psum()
                        mm(ps[0:64, 0:nblk * 128], wkv_t[:, h * 128:h * 128 + 64], ckT[:, 0:nblk * 128], True, True,
                           ["wkv_t", "ckT"], [pk])
                        act(KT[0:64, h, j * 512:j * 512 + nblk * 128], ps[0:64, 0:nblk * 128], AF.Copy, [pk], [f"KT{h}_{j}"])
                    for b in range(nblk):
                        ps, pk = psum()
                        mm(ps[:, :], ckT[:, b * 128:(b + 1) * 128], wkv_t[:], True, True, ["wkv_t", "ckT"], [pk])
                        vcopy(Vm[:, j * 4 + b, :].rearrange("p (h d) -> p h d", h=4),
                              ps[:, :].rearrange("p (h t d) -> p h t d", h=4, t=2)[:, :, 1, :], [pk], [f"Vm_{j}"])

                if lat:
                    kv_up(NT, 4)
                ckpt(10)

                for j in range(NT):
                    P.mark(f"{g}{l} K{j} norm")
                    if lat:
                        use_buf(0)
                        if j == 0:
                            norm_to(l, j, 0)
                    else:
                        if j == 0:
                            norm_to(l, 0, 0)
                        if j + 1 < NT:
                            norm_to(l, j + 1, (j + 1) % 2)
                        use_buf(j % 2)
                    P.mark(f"{g}{l} K{j} fm")
                    wtF, wkF = get_weight(f"fmK{g}{l}{j}")
                    wtT, wkT = get_weight(f"tmK{g}{l}{j}", ahead=NST - 2)

                    def fmK_group(m):
                        ps, pk = psum()
                        for c in range(8):
                            mm(ps[:, :], wtF[:, c, m * 128:(m + 1) * 128], H["ap"][:, c, :], c == 0, c == 7, [wkF, f"{H['k']}{c}"], [pk])
                        if m < 2:
                            act(pbuf[:, m, j * 512:(j + 1) * 512], ps[:, :], AF.Copy, [pk], [f"p{m}_{j}"])
                        else:
                            tt_op(pbuf[:, m - 2, j * 512:(j + 1) * 512], ps[:, :], pbuf[:, m - 2, j * 512:(j + 1) * 512],
                                  ALU.mult, [pk, f"p{m - 2}_{j}"], [f"p{m - 2}_{j}"])

                    def tmK_mm(b):
                        tok = slice(b * 128, (b + 1) * 128)
                        ps, pk = psum()
                        for c in range(8):
                            mm(ps[:, 0:416], H["ap"][:, c, tok], wtT[:, c, :], c == 0, c == 7, [wkT, f"{H['k']}{c}"], [pk])
                        return ps, pk

                    def tmK_ew(b, ps, pk):
                        gb = j * 4 + b
                        st, stk = ring("tmf", tmf)
                        act(st[:, 0:416], ps[:, 0:416], AF.Copy, [pk], [stk])
                        smt, smk = ring("sm", sm)
                        t2, t2k = ring("tm3", tm3)
                        P.op("dve", lambda: nc.vector.memset(smt[:, 0:1], 0.0), [], [smk])
                        act(t2[:, 0:128], st[:, 0:128], AF.Square, [stk, smk], [t2k, smk], accum_out=smt[:, 0:1])
                        act(smt[:, 1:2], smt[:, 0:1], AF.Sqrt, [smk], [smk], bias=EPS, scale=1.0 / 128)
                        recip(smt[:, 2:3], smt[:, 1:2], [smk], [smk])
                        stt(st[:, 0:128], st[:, 0:128], smt[:, 2:3], kvn[:], ALU.mult, ALU.mult, [stk, smk, "kvn"], [stk])
                        if lat:
                            t3, t3k = ring("tm2", tm2)
                            rope(t3[:, 0:32], st[:, 128:160], 1, 32, gb, 128, [stk], [t3k])
                            kpe_src, kpek = t3[:, 0:32], t3k
                            t4, t4k = ring("tm2", tm2)
                            rope(t4[:, 0:128], st[:, 160:288], 2, 64, gb, 0, [stk], [t4k])
                            sk_src, skk = t4[:, 0:128], t4k
                        else:
                            kpe_src, kpek = st[:, 128:160], stk
                            sk_src, skk = st[:, 160:288], stk
                            sq_, r0 = divmod(gb * 128, 256)
                            dma("sp", o_ckv[sq_, l, r0:r0 + 128, :], st[:, 0:128], [stk], [], is_out=True)
                            dma("sp", o_kpe[sq_, l, r0:r0 + 128, :], st[:, 128:160], [stk], [], is_out=True)
                            dma("sp", o_k[sq_, l, r0:r0 + 128, :], st[:, 160:288], [stk], [], is_out=True)
                            dma("sp", o_v[sq_, l, r0:r0 + 128, :], st[:, 288:416], [stk], [], is_out=True)
                        vcopy(Vs[:, gb, :], st[:, 288:416], [stk], [f"Vs_{j}"])
                        return st, stk, kpe_src, kpek, sk_src, skk

                    def tmK_tr(b, st, stk, kpe_src, kpek, sk_src, skk):
                        gb = j * 4 + b
                        tok = slice(b * 128, (b + 1) * 128)
                        ps2, pk2 = psum()
                        P.op("pe", lambda: nc.tensor.transpose(ps2[:, 0:128], st[:, 0:128], ident_f[:]), [stk, "ident_f"], [pk2])
                        P.op("pe", lambda: nc.tensor.transpose(ps2[:, 128:256], sk_src, ident_f[:]), [skk, "ident_f"], [pk2])
                        P.op("pe", lambda: nc.tensor.transpose(ps2[0:32, 256:384], kpe_src, ident_f[:]), [kpek, "ident_f"], [pk2])
                        act(ckT[:, tok], ps2[:, 0:128], AF.Copy, [pk2], ["ckT"])
                        vcopy(ks[:, gb * 128:(gb + 1) * 128], ps2[:, 128:256], [pk2], [f"ks_{j}"])
                        act(KT[64:96, 0:2, gb * 128:(gb + 1) * 128], ps2[0:32, 256:384].unsqueeze(1).to_broadcast([32, 2, 128]),
                            AF.Copy, [pk2], [f"KT0_{j}", f"KT1_{j}"])
                        vcopy(KT[64:96, 2:4, gb * 128:(gb + 1) * 128], ps2[0:32, 256:384].unsqueeze(1).to_broadcast([32, 2, 128]),
                              [pk2], [f"KT2_{j}", f"KT3_{j}"])

                    P.mark(f"{g}{l} K{j} tm")
                    r = {}
                    e = {}
                    r[0] = tmK_mm(0)
                    r[1] = tmK_mm(1)
                    for b in range(4):
                        e[b] = tmK_ew(b, *r[b])
                        fmK_group(b)
                        if lat and b == 3:
                            norm_to(l, j + 1 if j + 1 < NT else 0, 0)
                        tmK_tr(b, *e[b])
                        if b + 2 < 4:
                            r[b + 2] = tmK_mm(b + 2)
                    P.mark(f"{g}{l} K{j} kvup")
                    kv_up(j, 4)
                    ckpt(14)

                nseq_t = 512 // min(S, 512)
                for j in range(NT):
                    set_psring(range(8))
                    if lat:
                        use_buf(0)
                    else:
                        use_buf(j % 2)
                    P.mark(f"{g}{l} Q{j} fm")
                    if debug and l == 0 and j == 0:
                        dma("sp", dbg_h, H["ap"], [f"{H['k']}{c}" for c in range(8)], [], is_out=True)
                    wa, wak = get_weight(f"fmQa{g}{l}{j}")

                    def fmQ_group(m, wt, wk_):
                        mi = m % 3
                        ps, pk = psum()
                        for c in range(8):
                            mm(ps[:, :], wt[:, c, mi * 128:(mi + 1) * 128], H["ap"][:, c, :], c == 0, c == 7, [wk_, f"{H['k']}{c}"], [pk])
                        if m < 2:
                            act(catj[:, m, :], ps[:, :], AF.Copy, [pk], [f"cat{m}"])
                        elif m < 4:
                            act(catj[:, m, :], ps[:, :], AF.Gelu_apprx_tanh, [pk], [f"cat{m}"])
                        else:
                            cr, crk = cqraw[m - 4], f"cqraw{m - 4}"
                            act(cr[:], ps[:, :], AF.Copy, [pk], [crk])

                    for m in range(3):
                        fmQ_group(m, wa, wak)
                    wb, wbk = get_weight(f"fmQb{g}{l}{j}")
                    wtT, wkT = get_weight(f"tmQ{g}{l}{j}", ahead=NST - 2)

                    def cq_conv():
                        ps, pk = psum()
                        for c in range(2):
                            sq, sqk = ring("sqt", sqt)
                            act(sq[:], cqraw[c][:], AF.Square, [f"cqraw{c}"], [sqk])
                            mm(ps[:, :], ones_b, sq[:], c == 0, c == 1, [sqk, "cst"], [pk], sig=True)
                        act(s_t[:], ps[:, :], AF.Sqrt, [pk], ["s_t", "rstd"], bias=EPS, scale=1.0 / 256)
                        recip(rstd[:], s_t[:], ["s_t"], ["s_t", "rstd"])
                        for c in range(2):
                            stt(cqn[:, c, :], cqraw[c][:], vecs[:, 72 + l * 2 + c:72 + l * 2 + c + 1], rstd[:], ALU.mult, ALU.mult,
                                [f"cqraw{c}", "vecs", "rstd"], [f"cqn{c}"])
                        Sq = min(S, 512)
                        for c in range(2):
                            cw = lambda k: vecs[:, 80 + (l * 2 + c) * 3 + k:80 + (l * 2 + c) * 3 + k + 1]
                            lo = j * 512
                            pk_all = [f"p{c}_{jj}" for jj in range(NT)]
                            ts_op(acc[:, c, :], pbuf[:, c, lo:lo + 512], cw(1), ALU.mult, pk_all + ["vecs"], [f"acc{c}"])
                            for sidx in range(nseq_t):
                                a0 = sidx * Sq
                                g0 = lo + a0
                                first_in_seq = (g0 % S == 0)
                                last_in_seq = ((g0 + Sq) % S == 0)
                                s0 = 1 if first_in_seq else 0
                                stt(acc[:, c, a0 + s0:a0 + Sq], pbuf[:, c, g0 + s0 - 1:g0 + Sq - 1], cw(0), acc[:, c, a0 + s0:a0 + Sq],
                                    ALU.mult, ALU.add, pk_all + ["vecs", f"acc{c}"], [f"acc{c}"])
                                e0 = 1 if last_in_seq else 0
                                stt(acc[:, c, a0:a0 + Sq - e0], pbuf[:, c, g0 + 1:g0 + Sq - e0 + 1], cw(2), acc[:, c, a0:a0 + Sq - e0],
                                    ALU.mult, ALU.add, pk_all + ["vecs", f"acc{c}"], [f"acc{c}"])
                            tt_op(catj[:, c, :], catj[:, c, :], acc[:, c, :], ALU.mult, [f"cat{c}", f"acc{c}"], [f"cat{c}"])

                    def tmQ_mm(b):
                        tok = slice(b * 128, (b + 1) * 128)
                        ps, pk = psum()
                        for c in range(8):
                            mm(ps[:, :], H["ap"][:, c, tok], wtT[:, c, :], c == 0, c == 7, [wkT, f"{H['k']}{c}"], [pk])
                        return ps, pk

                    def tmQ_ew(b, ps, pk):
                        gb = j * 4 + b
                        st, stk = ring("tmf", tmf)
                        act(st[:, 0:256], ps[:, 0:256], AF.Gelu_apprx_tanh, [pk], [stk])
                        act(st[:, 256:512], ps[:, 256:512], AF.Copy, [pk], [stk])
                        smt, smk = ring("sm", sm)
                        t2, t2k = ring("tm3", tm3)
                        P.op("dve", lambda: nc.vector.memset(smt[:, 0:1], 0.0), [], [smk])
                        act(t2[:, 0:256], st[:, 0:256], AF.Square, [stk, smk], [t2k, smk], accum_out=smt[:, 0:1])
                        act(smt[:, 1:2], smt[:, 0:1], AF.Sqrt, [smk], [smk], bias=EPS, scale=1.0 / 256)
                        recip(smt[:, 2:3], smt[:, 1:2], [smk], [smk])
                        vn, vnk = ring("vn", vn_t)
                        stt(vn[:], st[:, 0:256], smt[:, 2:3], sgn[:], ALU.mult, ALU.mult, [stk, smk, "sgn"], [vnk])
                        if lat:
                            t4, t4k = ring("tm2", tm2)
                            rope(t4[:, 0:256], st[:, 256:512], 4, 64, gb, 0, [stk], [t4k])
                            return st, stk, vn, vnk, t4, t4k
                        return st, stk, vn, vnk, None, None

                    def tmQ_tail(b, st, stk, vn, vnk, t4, t4k):
                        tok = slice(b * 128, (b + 1) * 128)
                        ps2, pk2 = psum()
                        for hd in range(4):
                            cc, e_ = divmod(hd, 2)
                            mm(ps2[:, hd * 128:(hd + 1) * 128], vn[:, cc * 128:(cc + 1) * 128], wsT_t[:, hd * 128:(hd + 1) * 128],
                               True, False, [vnk, "wsT_t"], [pk2], sig=False)
                            mm(ps2[:, hd * 128:(hd + 1) * 128], ones_b[0:1, 0:128], sgub_t[0:1, hd * 128:(hd + 1) * 128],
                               False, True, ["cst", "sgub_t"], [pk2], sig=True)
                        ps3, pk3 = psum()
                        for gg in range(2):
                            src_ap = (t4[:, gg * 128:(gg + 1) * 128] if lat else st[:, 256 + gg * 128:256 + (gg + 1) * 128])
                            P.op("pe", lambda: nc.tensor.transpose(ps3[:, gg * 128:(gg + 1) * 128], src_ap, ident_f[:]),
                                 [t4k if lat else stk, "ident_f"], [pk3])
                        for hd in range(4):
                            cc, e_ = divmod(hd, 2)
                            tt_op(catj[e_ * 64:(e_ + 1) * 64, 2 + cc, tok], catj[e_ * 64:(e_ + 1) * 64, 2 + cc, tok],
                                  ps2[e_ * 64:(e_ + 1) * 64, hd * 128:(hd + 1) * 128], ALU.mult, [f"cat{2 + cc}", pk2], [f"cat{2 + cc}"])
                        act(qs[:, :, tok], ps3[:, 0:256].rearrange("p (g t) -> p g t", g=2), AF.Copy, [pk3], ["qs"])

                    def qm_mm(b):
                        tokq = slice(b * 128, (b + 1) * 128)
                        ps, pk = psum()
                        for c in range(2):
                            mm(ps[:, 0:384], cqn[:, c, tokq], wq_t[:, c, :], c == 0, c == 1, [f"cqn{c}", "wq_t"], [pk])
                        return ps, pk

                    def qm_ew(b, ps, pk):
                        gb = j * 4 + b
                        st, stk = ring("tmf", tmf)
                        act(st[:, 0:384], ps[:, 0:384], AF.Copy, [pk], [stk])
                        if lat:
                            t3, t3k = ring("tm2", tm2)
                            src4 = st[:, 0:384].rearrange("p (h d) -> p h d", h=4)[:, :, 64:96]
                            t5, t5k = ring("tm2", tm2)
                            vcopy(t5[:, 0:128].rearrange("p (h d) -> p h d", h=4), src4, [stk], [t5k])
                            rope(t3[:, 0:128], t5[:, 0:128], 4, 32, gb, 128, [t5k], [t3k])
                            vcopy(src4, t3[:, 0:128].rearrange("p (h d) -> p h d", h=4), [t3k], [stk])
                        return st, stk

                    def qm_tr(b, st, stk):
                        tokq = slice(b * 128, (b + 1) * 128)
                        ps, pk = psum()
                        for h in range(4):
                            P.op("pe", lambda: nc.tensor.transpose(ps[0:96, h * 128:(h + 1) * 128], st[:, h * 96:(h + 1) * 96], ident_f[:]),
                                 [stk, "ident_f"], [pk])
                        src = ps[0:96, :].rearrange("p (h t) -> p h t", h=4)
                        if b % 2 == 0:
                            act(qh[0:96, :, tokq], src, AF.Copy, [pk], [f"qh{h}" for h in range(4)])
                        else:
                            vcopy(qh[0:96, :, tokq], src, [pk], [f"qh{h}" for h in range(4)])

                    P.mark(f"{g}{l} Q{j} tm")
                    r = {}
                    e = {}
                    rq = {}
                    for m in range(3, 6):
                        fmQ_group(m, wb, wbk)
                    cq_conv()
                    r[0] = tmQ_mm(0)
                    r[1] = tmQ_mm(1)
                    rq[0] = qm_mm(0)
                    for b in range(4):
                        e[b] = tmQ_ew(b, *r[b])
                        eq = qm_ew(b, *rq[b])
                        if b + 1 < 4:
                            rq[b + 1] = qm_mm(b + 1)
                        tmQ_tail(b, *e[b])
                        if b + 2 < 4:
                            r[b + 2] = tmQ_mm(b + 2)
                        qm_tr(b, *eq)
                    if ada_defer and g == "A" and l + 1 < DEPTH:
                        for pcn in range(3 * j, 3 * j + 3):
                            ada_piece(l + 1, pcn)
                        if j == NT - 1:
                            ada_finish(l + 1)
                    ckpt(19)
                    set_psring([0, 1, 2, 7])
                    ckpt(20)
                    P.mark(f"{g}{l} Q{j} MLA")
                    acalls = []
                    if lat:
                        for h in range(4):
                            chunks = []
                            for kc in range(20):
                                jt = kc // 4
                                chunks.append((KT[:, h, kc * 128:(kc + 1) * 128], Vm[:, kc, h * 64:(h + 1) * 64], None,
                                               [f"KT{h}_{jt}", f"Vm_{jt}"]))
                            cc, e = divmod(h, 2)
                            acalls.append(((512, qh[:, h, :], [f"qh{h}"], chunks, MLA_SCALE, None,
                                            catj[e * 64:(e + 1) * 64, 4 + cc, :], [f"cat{4 + cc}"]), {}))
                    else:
                        for sl in range(2):
                            sidx = (j * 512) // 256 + sl
                            qsl = slice(sl * 256, (sl + 1) * 256)
                            for h in range(4):
                                chunks = []
                                for kc in (2 * sidx, 2 * sidx + 1):
                                    jt = kc // 4
                                    chunks.append((KT[:, h, kc * 128:(kc + 1) * 128], Vm[:, kc, h * 64:(h + 1) * 64], None,
                                                   [f"KT{h}_{jt}", f"Vm_{jt}"]))
                                cc, e = divmod(h, 2)
                                acalls.append(((256, qh[:, h, qsl], [f"qh{h}"], chunks, MLA_SCALE, None,
                                                catj[e * 64:(e + 1) * 64, 4 + cc, qsl], [f"cat{4 + cc}"]), {}))
                    ckpt(21)
                    if lat and j + 1 < NT:
                        norm_to(l, j + 1, 0)
                    P.mark(f"{g}{l} Q{j} SWA")
                    for n in range(2):
                        sink_l = [sinkexp[0:64, l * 4 + n * 2 + gq:l * 4 + n * 2 + gq + 1] for gq in range(2)]
                        if lat:
                            for bq in range(4):
                                blk = j * 4 + bq
                                tq = slice(bq * 128, (bq + 1) * 128)
                                chunks = []
                                if blk >= 1:
                                    chunks.append((ks[n * 64:(n + 1) * 64, (blk - 1) * 128:blk * 128],
                                                   Vs[:, blk - 1, n * 64:(n + 1) * 64], mprev,
                                                   [f"ks_{(blk - 1) // 4}", f"Vs_{(blk - 1) // 4}"]))
                                chunks.append((ks[n * 64:(n + 1) * 64, blk * 128:(blk + 1) * 128],
                                               Vs[:, blk, n * 64:(n + 1) * 64], None, [f"ks_{blk // 4}", f"Vs_{blk // 4}"]))
                                if blk <= 14:
                                    chunks.append((ks[n * 64:(n + 1) * 64, (blk + 1) * 128:(blk + 2) * 128],
                                                   Vs[:, blk + 1, n * 64:(n + 1) * 64], mnext,
                                                   [f"ks_{(blk + 1) // 4}", f"Vs_{(blk + 1) // 4}"]))
                                for kc in range(16, 20):
                                    chunks.append((ks[n * 64:(n + 1) * 64, kc * 128:(kc + 1) * 128],
                                                   Vs[:, kc, n * 64:(n + 1) * 64], None, [f"ks_{NT}", f"Vs_{NT}"]))
                                acalls.append(((128, qs[n * 64:(n + 1) * 64, :, tq], ["qs"], chunks, SWA_SCALE, sink_l,
                                                [catj[gq * 64:(gq + 1) * 64, 6 + n, tq] for gq in range(2)], [f"cat{6 + n}"]),
                                               {"G": 2}))
                        else:
                            for sl in range(2):
                                sidx = (j * 512) // 256 + sl
                                qsl = slice(sl * 256, (sl + 1) * 256)
                                chunks = []
                                for kc in (2 * sidx, 2 * sidx + 1):
                                    chunks.append((ks[n * 64:(n + 1) * 64, kc * 128:(kc + 1) * 128],
                                                   Vs[:, kc, n * 64:(n + 1) * 64], None, [f"ks_{kc // 4}", f"Vs_{kc // 4}"]))
                                acalls.append(((256, qs[n * 64:(n + 1) * 64, :, qsl], ["qs"], chunks, SWA_SCALE, sink_l,
                                                [catj[gq * 64:(gq + 1) * 64, 6 + n, qsl] for gq in range(2)], [f"cat{6 + n}"]),
                                               {"G": 2}))
                    attention_seq(acalls)
                    ckpt(22)
                    P.mark(f"{g}{l} Q{j} wout")
                    set_psring(range(8))
                    if debug and l == 0 and j == 0:
                        dma("sp", dbg_cat, catj, [f"cat{c}" for c in range(8)], [], is_out=True)
                    for half in range(2):
                        wt, wk_ = get_weight(f"wo{half}{g}{l}{j}")
                        for mi in range(4):
                            m = half * 4 + mi
                            ps, pk = psum()
                            for c in range(8):
                                mm(ps[:, :], wt[:, c, mi * 128:(mi + 1) * 128], catj[:, c, :], c == 0, c == 7, [wk_, f"cat{c}"], [pk])
                            xs_ = x[:, m, j * 512:(j + 1) * 512]
                            stt(xs_, ps[:, :], modv[:, l, 2, m, v:v + 1], xs_, ALU.mult, ALU.add, [pk, f"modt{l}", f"x{m}_{j}"], [f"x{m}_{j}"])
                P.barrier()
                if debug and l == 0:
                    dma("sp", dbg_x1[:, :, 0:T], x[:, :, 0:T], [], [], is_out=True)
                    P.barrier()
                ckpt(23)
                P.mark(f"{g}{l} MLP norm")
                set_psring(range(8))
                if l + 1 < depth:
                    small_weights(l + 1)
                ckpt(24)
                P.mark(f"{g}{l} MLP mm")
                ada_pc = [0]
                for jh in range(8):
                    wa, wak = get_weight(f"w1{g}{l}{jh}")
                    wb, wbk = get_weight(f"w2{g}{l}{jh}", ahead=NST - 2)
                    def mlp_up(j):
                        if jh == 0:
                            norm(l, 1, j, lambda c: h2[:, c, j * 512:(j + 1) * 512], lambda c: f"h2{c}_{j}")
                        u, uk = ring("ub", ub)
                        for hc in range(4):
                            ps, pk = psum()
                            for c in range(8):
                                mm(ps[:, :], wa[:, c, hc * 128:(hc + 1) * 128], h2[:, c, j * 512:(j + 1) * 512], c == 0, c == 7,
                                   [wak, f"h2{c}_{j}"], [pk])
                            r_, rk_ = ring("relu", relu_t)
                            act(r_[:], ps[:, :], AF.Relu, [pk], [rk_])
                            tt_op(u[:, hc, :], r_[:], r_[:], ALU.mult, [rk_], [f"{uk}_{hc}"])
                        return u, uk

                    def mlp_down(j, u, uk):
                        for m in range(8):
                            ps, pk = psum()
                            for hc in range(4):
                                mm(ps[:, :], wb[:, hc, m * 128:(m + 1) * 128], u[:, hc, :], hc == 0, hc == 3, [wbk, f"{uk}_{hc}"], [pk])
                            xs_ = x[:, m, j * 512:(j + 1) * 512]
                            stt(xs_, ps[:, :], modv[:, l, 5, m, v:v + 1], xs_, ALU.mult, ALU.add, [pk, f"modt{l}", f"x{m}_{j}"], [f"x{m}_{j}"])

                    us = {0: mlp_up(0)}
                    for j in range(NT):
                        if j + 1 < NT:
                            us[j + 1] = mlp_up(j + 1)
                        mlp_down(j, *us.pop(j))
                P.barrier()
                if debug and l == 0:
                    dma("sp", dbg_x2[:, :, 0:T], x[:, :, 0:T], [], [], is_out=True)
                    P.barrier()
            P.mark(f"{g} final")
            for j in range(NT):
                ps, pk = psum()
                for c in range(8):
                    sq, sqk = ring("sqt", sqt)
                    act(sq[:], x[:, c, j * 512:(j + 1) * 512], AF.Square, [f"x{c}_{j}"], [sqk])
                    mm(ps[:, :], ones_b, sq[:], c == 0, c == 7, [sqk, "cst"], [pk], sig=True)
                act(s_t[:], ps[:, :], AF.Sqrt, [pk], ["s_t", "rstd"], bias=EPS, scale=1.0 / D)
                recip(rstd[:], s_t[:], ["s_t"], ["s_t", "rstd"])
                for c in range(8):
                    y_, yk = ring("tt", tt)
                    stt(y_[:], x[:, c, j * 512:(j + 1) * 512], vecs[:, 64 + c:65 + c], rstd[:], ALU.mult, ALU.mult,
                        [f"x{c}_{j}", "vecs", "rstd"], [yk])
                    dma("sp", yT[c * 128:(c + 1) * 128, j * 512:(j + 1) * 512], y_[:], [yk], [], is_out=True)
            P.barrier()

        try:
            for g_ in groups:
                run_group(g_)
            P.mark("end")
            P.finish()
        except StopBuild:
            pass
        MARKS.extend(P.marks)
    return nc


_CACHE = {}


def _consts():
    ident = np.eye(128, dtype=np.float32)
    ones = np.ones((128, 128), np.float32)
    kk = np.arange(128)[:, None]
    qq = np.arange(128)[None, :]
    mprev = np.where(kk >= qq, 0.0, NEG).astype(np.float32)
    mnext = np.where(kk <= qq, 0.0, NEG).astype(np.float32)
    c = np.concatenate([ident, ones, mprev, mnext, np.zeros((128, 128), np.float32)], axis=1)
    def tables(rot_dim):
        half = rot_dim // 2
        inv = (10000.0 ** (-np.arange(0, half, 2, dtype=np.float32) / half)).astype(np.float32)
        t = np.arange(2048)
        row = (t // 64).astype(np.float32)
        col = (t % 64).astype(np.float32)
        ar = row[:, None] * inv[None, :]
        ac = col[:, None] * inv[None, :]
        ang = np.concatenate([ar, ar, ac, ac], axis=-1).astype(np.float32)
        cos = np.cos(ang).astype(np.float32)
        sin = np.sin(ang).astype(np.float32)
        q = rot_dim // 4
        sgn = np.concatenate([-np.ones(q), np.ones(q), -np.ones(q), np.ones(q)]).astype(np.float32)
        return cos, sin * sgn[None, :]
    cs, ss = tables(64)
    cm, sm_ = tables(32)
    r = np.concatenate([cs, ss, cm, sm_], axis=1)
    r = r.reshape(16, 128, 192).transpose(1, 0, 2).reshape(128, 16 * 192)
    return np.ascontiguousarray(c), np.ascontiguousarray(r.astype(np.float32))


def kernel(x_prompt, x_sample, cache_mla_ckv, cache_mla_kpe, cache_swa_k, cache_swa_v, c, c_ctx,
           w_ada, b_ada, norm1, norm2, w_in, conv_w, sgu_norm, sgu_w, sgu_b, mla_q_norm, mla_w_q_up,
           mla_kv_norm, mla_w_kv_up, swa_sink, w_out, mlp_w1, mlp_w2, final_norm):
    in_maps = pack_inputs(x_prompt, x_sample, cache_mla_ckv, cache_mla_kpe, cache_swa_k, cache_swa_v, c, c_ctx,
                          w_ada, b_ada, norm1, norm2, w_in, conv_w, sgu_norm, sgu_w, sgu_b, mla_q_norm, mla_w_q_up,
                          mla_kv_norm, mla_w_kv_up, swa_sink, w_out, mlp_w1, mlp_w2, final_norm)
    if "nc" not in _CACHE:
        _CACHE["nc"] = build_program()
    nc = _CACHE["nc"]
    res = run_bass_kernel_spmd(nc, in_maps, core_ids=list(range(NCORES)))
    return unpack_outputs(res.results)


def pack_inputs(x_prompt, x_sample, cache_mla_ckv, cache_mla_kpe, cache_swa_k, cache_swa_v, c, c_ctx,
                w_ada, b_ada, norm1, norm2, w_in, conv_w, sgu_norm, sgu_w, sgu_b, mla_q_norm, mla_w_q_up,
                mla_kv_norm, mla_w_kv_up, swa_sink, w_out, mlp_w1, mlp_w2, final_norm, cores=range(NCORES)):
    f = lambda a: np.ascontiguousarray(np.asarray(a, dtype=np.float32))
    x_prompt, x_sample = f(x_prompt), f(x_sample)
    consts, ropes = _consts()
    w_in = f(w_in)
    a_b, a_c, a_x = w_in[:, :, 0:256], w_in[:, :, 256:512], w_in[:, :, 512:768]
    u_, v_ = w_in[:, :, 768:1024], w_in[:, :, 1024:1280]
    cq, ckv, kpe = w_in[:, :, 1280:1536], w_in[:, :, 1536:1664], w_in[:, :, 1664:1696]
    sq, sk, sv = w_in[:, :, 1696:1952], w_in[:, :, 1952:2080], w_in[:, :, 2080:2208]
    sq_g = sq.reshape(DEPTH, D, 2, 2, 64).transpose(0, 1, 3, 2, 4).reshape(DEPTH, D, 256)
    w_fm = f(np.concatenate([a_c, a_x, a_b, u_, cq], axis=2))
    w_tm = f(np.concatenate([ckv, kpe, sk, sv, v_, sq_g], axis=2))
    bada_fm = f(np.asarray(b_ada).reshape(DEPTH, 48, 128).transpose(2, 0, 1).reshape(128, DEPTH * 48))
    vecs = np.zeros((128, 128), np.float32)
    vecs[:, 0:32] = np.asarray(norm1).reshape(DEPTH, 8, 128).transpose(2, 0, 1).reshape(128, 32)
    vecs[:, 32:64] = np.asarray(norm2).reshape(DEPTH, 8, 128).transpose(2, 0, 1).reshape(128, 32)
    vecs[:, 64:72] = np.asarray(final_norm).reshape(8, 128).T
    vecs[:, 72:80] = np.asarray(mla_q_norm).reshape(DEPTH, 2, 128).transpose(2, 0, 1).reshape(128, 8)
    vecs[:, 80:104] = np.asarray(conv_w).reshape(DEPTH, 3, 2, 128).transpose(3, 0, 2, 1).reshape(128, 24)
    sgunorm_bc = f(np.broadcast_to(np.asarray(sgu_norm).reshape(1, DEPTH * 256), (128, DEPTH * 256)))
    kvnorm_bc = f(np.broadcast_to(np.asarray(mla_kv_norm).reshape(1, DEPTH * 128), (128, DEPTH * 128)))
    sink_bc = f(np.broadcast_to(np.asarray(swa_sink).reshape(1, 16), (128, 16)))
    sgub = f(np.asarray(sgu_b).reshape(1, DEPTH * 512))
    sgu_wT = f(np.asarray(sgu_w).transpose(0, 3, 1, 2).reshape(DEPTH, 128, 512))
    shared = dict(w_ada=f(w_ada), bada_fm=bada_fm, vecs_fm=vecs, sgunorm_bc=sgunorm_bc, kvnorm_bc=kvnorm_bc,
                  sink_bc=sink_bc, sgub=sgub, sgu_wT=sgu_wT, w_fm=w_fm, w_tm=w_tm, w_out=f(w_out), w1=f(mlp_w1),
                  w2=f(mlp_w2), wq=f(mla_w_q_up), wkv=f(mla_w_kv_up), ropes=ropes, consts=consts)
    c = np.asarray(c, np.float32)
    c_ctx = np.asarray(c_ctx, np.float32)
    in_maps = []
    for i in cores:
        cv = np.stack([c_ctx, c[i]], axis=0)
        cfm = f(cv.reshape(2, 8, 128).transpose(2, 1, 0).reshape(128, 16))
        m = dict(shared)
        m.update(
            xsT=f(x_sample[i].T),
            xpT=f(x_prompt[4 * i:4 * i + 4].reshape(1024, D).T),
            ckvT_c=f(np.asarray(cache_mla_ckv[i]).transpose(0, 2, 1)),
            kpeT_c=f(np.asarray(cache_mla_kpe[i]).transpose(0, 2, 1)),
            skT_c=f(np.asarray(cache_swa_k[i]).reshape(DEPTH, 512, 128).transpose(0, 2, 1)),
            sv_c=f(np.asarray(cache_swa_v[i]).reshape(DEPTH, 512, 128)),
            cfm=cfm,
        )
        in_maps.append(m)
    return in_maps


def unpack_outputs(rs):
    y_prompt = np.concatenate([r["ypT"].T.reshape(4, 256, D) for r in rs], axis=0).astype(np.float32)
    y_sample = np.stack([r["ysT"].T for r in rs], axis=0).astype(np.float32)
    new_ckv = np.concatenate([r["o_ckv"] for r in rs], axis=0).astype(np.float32)
    new_kpe = np.concatenate([r["o_kpe"] for r in rs], axis=0).astype(np.float32)
    new_k = np.concatenate([r["o_k"] for r in rs], axis=0).reshape(-1, DEPTH, 256, 2, 64).astype(np.float32)
    new_v = np.concatenate([r["o_v"] for r in rs], axis=0).reshape(-1, DEPTH, 256, 2, 64).astype(np.float32)
    return (np.ascontiguousarray(y_prompt), np.ascontiguousarray(y_sample), new_ckv, new_kpe, new_k, new_v)
```
